# Optimizing a Trainium2 kernel written in Bass

```python
import jax, jax.numpy as jnp
from jax import lax
import numpy as np

D_MODEL = 1024
BATCH = 4
SEQ = 8192
DEPTH = 1
DEC_BATCH = 8
DEC_SEQ = 16
PAST_LEN = 2048

CHUNK = 64
M_HEADS = 4
M_HEAD_DIM = 128
M_WIDTH = M_HEADS * M_HEAD_DIM
A_Q_HEADS = 8
A_KV_HEADS = 2
A_HEAD_DIM = 64
A_GROUP = A_Q_HEADS // A_KV_HEADS
A_WIDTH = A_Q_HEADS * A_HEAD_DIM
KV_WIDTH = A_KV_HEADS * A_HEAD_DIM
MIX_WIDTH = M_WIDTH + A_WIDTH
IN_WIDTH = 4 * M_WIDTH + 2 * M_HEADS + A_WIDTH + 2 * KV_WIDTH
WINDOW = 128
WIN_CHUNKS = WINDOW // CHUNK
ROPE_THETA = 10000.0
D_FF = 2816
UP_WIDTH = 2 * D_FF
CONV_W = 3
LN_EPS = 1e-5
DEEPNORM_ALPHA = (2 * DEPTH) ** 0.25
DEEPNORM_BETA = (8 * DEPTH) ** -0.25

kernel_name = 'hybrid_mlstm_swa_convffn_stream_step'


def layer_norm(x, g, b):
    xf = x.astype(jnp.float32)
    mu = jnp.mean(xf, axis=-1, keepdims=True)
    var = jnp.mean(jnp.square(xf - mu), axis=-1, keepdims=True)
    y = (xf - mu) * lax.rsqrt(var + LN_EPS) * g.astype(jnp.float32) + b.astype(jnp.float32)
    return y.astype(x.dtype)


def head_norm(h, g):
    mu = jnp.mean(h, axis=-1, keepdims=True)
    var = jnp.mean(jnp.square(h - mu), axis=-1, keepdims=True)
    return (h - mu) * lax.rsqrt(var + LN_EPS) * g.astype(jnp.float32).reshape(M_HEADS, M_HEAD_DIM)


def rope(x, pos):
    half = x.shape[-1] // 2
    inv = ROPE_THETA ** (-jnp.arange(half, dtype=jnp.float32) / half)
    ang = pos.astype(jnp.float32)[:, None] * inv[None, :]
    cos = jnp.cos(ang)[:, None, :]
    sin = jnp.sin(ang)[:, None, :]
    xf = x.astype(jnp.float32)
    x1, x2 = xf[..., :half], xf[..., half:]
    return jnp.concatenate([x1 * cos - x2 * sin, x2 * cos + x1 * sin], axis=-1).astype(x.dtype)


def split_in(z):
    sizes = (M_WIDTH, M_WIDTH, M_WIDTH, M_WIDTH, M_HEADS, M_HEADS, A_WIDTH, KV_WIDTH, KV_WIDTH)
    idx = []
    acc = 0
    for s in sizes[:-1]:
        acc += s
        idx.append(acc)
    return jnp.split(z, idx, axis=-1)


def mlstm_chunkwise(q, k, v, logi, logf, C0, n0, m0, chunk):
    B, T, H, D = q.shape
    nc = T // chunk

    def vec_chunks(a):
        return a.astype(jnp.float32).reshape(B, nc, chunk, H, D).transpose(1, 0, 3, 2, 4)

    def gate_chunks(a):
        return a.reshape(B, nc, chunk, H).transpose(1, 0, 3, 2)

    causal = jnp.tril(jnp.ones((chunk, chunk), dtype=bool))

    def step(carry, inp):
        C, n, m = carry
        qc, kc, vc, li, lf = inp
        b = jnp.cumsum(lf, axis=-1)
        inter = b + m[..., None]
        dmat = jnp.where(causal, b[..., :, None] - b[..., None, :] + li[..., None, :], -jnp.inf)
        mt = jnp.maximum(inter, jnp.max(dmat, axis=-1))
        w_inter = jnp.exp(inter - mt)
        s = jnp.einsum('bhld,bhsd->bhls', qc, kc) * jnp.exp(dmat - mt[..., None])
        num = w_inter[..., None] * jnp.einsum('bhld,bhde->bhle', qc, C) + jnp.einsum('bhls,bhse->bhle', s, vc)
        den = w_inter * jnp.einsum('bhld,bhd->bhl', qc, n) + jnp.sum(s, axis=-1)
        h = num / jnp.maximum(jnp.abs(den), jnp.exp(-mt))[..., None]
        b_last = b[..., -1]
        g = b_last[..., None] - b + li
        m_new = jnp.maximum(b_last + m, jnp.max(g, axis=-1))
        decay = jnp.exp(b_last + m - m_new)
        wk = jnp.exp(g - m_new[..., None])
        C_new = decay[..., None, None] * C + jnp.einsum('bhl,bhld,bhle->bhde', wk, kc, vc)
        n_new = decay[..., None] * n + jnp.einsum('bhl,bhld->bhd', wk, kc)
        return (C_new, n_new, m_new), h

    carry0 = (C0.astype(jnp.float32), n0.astype(jnp.float32), m0.astype(jnp.float32))
    (C1, n1, m1), hs = lax.scan(step, carry0, (vec_chunks(q), vec_chunks(k), vec_chunks(v), gate_chunks(logi), gate_chunks(logf)))
    h = hs.transpose(1, 0, 3, 2, 4).reshape(B, T, H, D)
    return h, C1, n1, m1


def band_attend(q, k, v, mask, sinks):
    s = jnp.einsum('bnlkgd,bnskd->bnkgls', q, k, preferred_element_type=jnp.float32) * (A_HEAD_DIM ** -0.5)
    if mask is not None:
        s = jnp.where(mask, s, -jnp.inf)
    sink = sinks.astype(jnp.float32).reshape(A_KV_HEADS, A_GROUP)[None, None, :, :, None, None]
    mx = jnp.maximum(jnp.max(s, axis=-1, keepdims=True), sink)
    p = jnp.exp(s - mx)
    p = p / (jnp.sum(p, axis=-1, keepdims=True) + jnp.exp(sink - mx))
    o = jnp.einsum('bnkgls,bnskd->bnlkgd', p.astype(v.dtype), v, preferred_element_type=jnp.float32)
    return o.astype(v.dtype)


def swa_prompt(q, k, v, sinks):
    B, T = q.shape[0], q.shape[1]
    nc = T // CHUNK
    qc = q.reshape(B, nc, CHUNK, A_KV_HEADS, A_GROUP, A_HEAD_DIM)
    pad = ((0, 0), (WIN_CHUNKS * CHUNK, 0), (0, 0), (0, 0))
    kp = jnp.pad(k, pad).reshape(B, nc + WIN_CHUNKS, CHUNK, A_KV_HEADS, A_HEAD_DIM)
    vp = jnp.pad(v, pad).reshape(B, nc + WIN_CHUNKS, CHUNK, A_KV_HEADS, A_HEAD_DIM)
    kb = jnp.concatenate([kp[:, j:j + nc] for j in range(WIN_CHUNKS + 1)], axis=2)
    vb = jnp.concatenate([vp[:, j:j + nc] for j in range(WIN_CHUNKS + 1)], axis=2)
    band = (WIN_CHUNKS + 1) * CHUNK
    valid = (jnp.arange(nc)[:, None] + jnp.arange(band)[None, :] // CHUNK - WIN_CHUNKS) >= 0
    o = band_attend(qc, kb, vb, valid[None, :, None, None, None, :], sinks)
    return o.reshape(B, T, A_WIDTH)


def swa_step(q, k, v, k_cache, v_cache, sinks):
    B, T = q.shape[0], q.shape[1]
    qc = q.reshape(B, 1, T, A_KV_HEADS, A_GROUP, A_HEAD_DIM)
    kb = jnp.concatenate([k_cache.astype(k.dtype), k], axis=1)[:, None]
    vb = jnp.concatenate([v_cache.astype(v.dtype), v], axis=1)[:, None]
    o = band_attend(qc, kb, vb, None, sinks)
    return o.reshape(B, T, A_WIDTH)


def conv_ffn(h, conv_prev, w_up, w_conv, b_conv, w_down):
    T = h.shape[1]
    u = h @ w_up
    ext = jnp.concatenate([conv_prev.astype(u.dtype), u], axis=1)
    c = b_conv + sum(ext[:, j:j + T] * w_conv[j] for j in range(CONV_W))
    a, g = jnp.split(c, 2, axis=-1)
    return (jax.nn.silu(g) * a) @ w_down, ext[:, -(CONV_W - 1):]


def trunk_layer(x, pos, C0, n0, m0, k_cache, v_cache, conv_prev, mchunk,
                w_in, b_igate, b_fgate, g_mlstm_norm, attn_sinks, w_out, ln1_g, ln1_b,
                w_up, w_conv, b_conv, w_down, ln2_g, ln2_b):
    B, T, _ = x.shape
    qm, km, vm, om, ip, fp, qa, ka, va = split_in(x @ w_in)
    qm = qm.reshape(B, T, M_HEADS, M_HEAD_DIM)
    km = km.reshape(B, T, M_HEADS, M_HEAD_DIM) * (M_HEAD_DIM ** -0.5)
    vm = vm.reshape(B, T, M_HEADS, M_HEAD_DIM)
    logi = ip.astype(jnp.float32) + b_igate.astype(jnp.float32)
    logf = jax.nn.log_sigmoid(fp.astype(jnp.float32) + b_fgate.astype(jnp.float32))
    hm, C1, n1, m1 = mlstm_chunkwise(qm, km, vm, logi, logf, C0, n0, m0, mchunk)
    og = jax.nn.sigmoid(om.astype(jnp.float32)).reshape(B, T, M_HEADS, M_HEAD_DIM)
    hm = (og * head_norm(hm, g_mlstm_norm)).reshape(B, T, M_WIDTH).astype(x.dtype)
    qa = rope(qa.reshape(B, T, A_Q_HEADS, A_HEAD_DIM), pos)
    ka = rope(ka.reshape(B, T, A_KV_HEADS, A_HEAD_DIM), pos)
    va = va.reshape(B, T, A_KV_HEADS, A_HEAD_DIM)
    if k_cache is None:
        ha = swa_prompt(qa, ka, va, attn_sinks)
        k_rows, v_rows = ka[:, -WINDOW:], va[:, -WINDOW:]
    else:
        ha = swa_step(qa, ka, va, k_cache, v_cache, attn_sinks)
        k_rows, v_rows = ka, va
    mix = jnp.concatenate([hm, ha], axis=-1) @ w_out
    h = layer_norm(DEEPNORM_ALPHA * x + mix, ln1_g, ln1_b)
    f, conv_rows = conv_ffn(h, conv_prev, w_up, w_conv, b_conv, w_down)
    y = layer_norm(DEEPNORM_ALPHA * h + f, ln2_g, ln2_b)
    return y, (C1, n1, m1, k_rows, v_rows, conv_rows)


def setup_inputs(seed: int = 0) -> dict:
    key = jax.random.key(seed)
    ks = jax.random.split(key, 24)

    def nrm(k, shape, scale):
        return jax.random.normal(k, shape, jnp.float32) * scale

    return {
        'x_prompt': nrm(ks[0], (BATCH, SEQ, D_MODEL), 1.0),
        'x_sample': nrm(ks[1], (DEC_BATCH, DEC_SEQ, D_MODEL), 1.0),
        'state_mlstm_C': nrm(ks[2], (DEPTH, DEC_BATCH, M_HEADS, M_HEAD_DIM, M_HEAD_DIM), 0.3),
        'state_mlstm_n': nrm(ks[3], (DEPTH, DEC_BATCH, M_HEADS, M_HEAD_DIM), 0.3),
        'state_mlstm_m': nrm(ks[4], (DEPTH, DEC_BATCH, M_HEADS), 1.0),
        'cache_swa_k': nrm(ks[5], (DEPTH, DEC_BATCH, WINDOW, A_KV_HEADS, A_HEAD_DIM), 1.0),
        'cache_swa_v': nrm(ks[6], (DEPTH, DEC_BATCH, WINDOW, A_KV_HEADS, A_HEAD_DIM), 1.0),
        'state_conv': nrm(ks[7], (DEPTH, DEC_BATCH, CONV_W - 1, UP_WIDTH), 1.0),
        'w_in': nrm(ks[8], (DEPTH, D_MODEL, IN_WIDTH), D_MODEL ** -0.5),
        'b_igate': nrm(ks[9], (DEPTH, M_HEADS), 0.1),
        'b_fgate': jnp.linspace(3.0, 6.0, M_HEADS, dtype=jnp.float32)[None, :] + nrm(ks[10], (DEPTH, M_HEADS), 0.1),
        'g_mlstm_norm': 1.0 + nrm(ks[11], (DEPTH, M_WIDTH), 0.02),
        'attn_sinks': nrm(ks[12], (DEPTH, A_Q_HEADS), 0.5),
        'w_out': nrm(ks[13], (DEPTH, MIX_WIDTH, D_MODEL), MIX_WIDTH ** -0.5 * DEEPNORM_BETA),
        'ln1_g': 1.0 + nrm(ks[14], (DEPTH, D_MODEL), 0.02),
        'ln1_b': nrm(ks[15], (DEPTH, D_MODEL), 0.02),
        'w_up': nrm(ks[16], (DEPTH, D_MODEL, UP_WIDTH), D_MODEL ** -0.5),
        'w_conv': nrm(ks[17], (DEPTH, CONV_W, UP_WIDTH), CONV_W ** -0.5),
        'b_conv': nrm(ks[18], (DEPTH, UP_WIDTH), 0.02),
        'w_down': nrm(ks[19], (DEPTH, D_FF, D_MODEL), D_FF ** -0.5 * DEEPNORM_BETA),
        'ln2_g': 1.0 + nrm(ks[20], (DEPTH, D_MODEL), 0.02),
        'ln2_b': nrm(ks[21], (DEPTH, D_MODEL), 0.02),
    }


def reference(x_prompt, x_sample, state_mlstm_C, state_mlstm_n, state_mlstm_m, cache_swa_k, cache_swa_v,
              state_conv, w_in, b_igate, b_fgate, g_mlstm_norm, attn_sinks, w_out, ln1_g, ln1_b,
              w_up, w_conv, b_conv, w_down, ln2_g, ln2_b):
    bp, tp = x_prompt.shape[0], x_prompt.shape[1]
    bs, ts = x_sample.shape[0], x_sample.shape[1]
    pos_p = jnp.arange(tp, dtype=jnp.int32)
    pos_s = PAST_LEN + jnp.arange(ts, dtype=jnp.int32)
    hp, hs = x_prompt, x_sample
    sp, ss = [], []
    for l in range(DEPTH):
        w = (w_in[l], b_igate[l], b_fgate[l], g_mlstm_norm[l], attn_sinks[l], w_out[l], ln1_g[l], ln1_b[l],
             w_up[l], w_conv[l], b_conv[l], w_down[l], ln2_g[l], ln2_b[l])
        hp, st_p = trunk_layer(hp, pos_p,
                               jnp.zeros((bp, M_HEADS, M_HEAD_DIM, M_HEAD_DIM), jnp.float32),
                               jnp.zeros((bp, M_HEADS, M_HEAD_DIM), jnp.float32),
                               jnp.zeros((bp, M_HEADS), jnp.float32),
                               None, None,
                               jnp.zeros((bp, CONV_W - 1, UP_WIDTH), hp.dtype),
                               CHUNK, *w)
        hs, st_s = trunk_layer(hs, pos_s, state_mlstm_C[l], state_mlstm_n[l], state_mlstm_m[l],
                               cache_swa_k[l], cache_swa_v[l], state_conv[l], ts, *w)
        sp.append(st_p)
        ss.append(st_s)
    P = [jnp.stack([s[i] for s in sp]) for i in range(6)]
    S = [jnp.stack([s[i] for s in ss]) for i in range(6)]
    return (hp, hs, P[0], P[1], P[2], P[3], P[4], P[5], S[0], S[1], S[2], S[3], S[4], S[5])
```

```python
import os
from contextlib import ExitStack
import numpy as np
import concourse.bass as bass
import concourse.mybir as mybir
from concourse.bass_utils import run_bass_kernel_spmd

F32 = mybir.dt.float32
BF16 = mybir.dt.bfloat16
AF = mybir.ActivationFunctionType
ALU = mybir.AluOpType
AX = mybir.AxisListType

D = 1024
NT_PRE = 30
NT_FULL = 34
DFF = 2816
NU = 22
WCOLS = 3464
C_QM, C_KM, C_VM, C_OM, C_IP, C_FP, C_VA, C_QAP, C_QAR, C_KAP, C_KAR = 0, 512, 1024, 1536, 2048, 2052, 2056, 2184, 2696, 3208, 3336
ALPHA = float(2.0 ** 0.25)
EPS = 1e-5
KSCALE = float(128.0 ** -0.5)
NEG = -30000.0


class Tracker:
    HOP = 0.4
    DEF_COST = {'pe': 0.7, 'act': 0.45, 'dve': 0.3, 'pool': 0.8, 'sp': 0.08}

    def __init__(self, nc, es):
        self.nc = nc
        self.es = es
        self.engs = {}
        self.last_w = {}
        self.readers = {}
        self.streams = {}
        self.ops = []
        self.thr_hist = {}
        self.cur_tag = 'setup'
        self.filler_fn = None
        self.pe_scale = float(os.environ.get('K_PE_SCALE', '0.65'))
        self.ad_scale = float(os.environ.get('K_AD_SCALE', '1.1'))
        self.HOP = float(os.environ.get('K_HOP', '0.9'))
        self.hop_same = float(os.environ.get('K_HOP_SAME', '0.3')); self.lc_scale = float(os.environ.get('K_LC_SCALE', '1.0'))
        self.filler_cost = 0.43
        self.fill_min = 1.5
        self.fill_frac = 0.6

    def add_engine(self, name, eng):
        sem = self.es.enter_context(self.nc.semaphore("s_" + name))
        self.engs[name] = dict(eng=eng, sem=sem, name=name, order=[])

    def stream(self, name, group=False):
        if name not in self.streams:
            sem = self.es.enter_context(self.nc.semaphore("d_" + name))
            self.streams[name] = dict(sem=sem, count=0, group=group, name=name)
        return self.streams[name]

    def _deps(self, reads, writes, extra, group):
        deps = set()
        for k in reads:
            deps.update(self.last_w.get(k, ()))
        if not group:
            for k in writes:
                deps.update(self.last_w.get(k, ()))
                deps.update(self.readers.get(k, ()))
        deps.update(extra)
        return deps

    def _commit(self, oid, reads, writes, group):
        for k in reads:
            self.readers.setdefault(k, []).append(oid)
        for k in writes:
            if group:
                self.last_w.setdefault(k, []).append(oid)
            else:
                self.last_w[k] = [oid]
                self.readers[k] = []

    def op(self, ename, fn, r=(), w=(), extra=(), c=None):
        deps = self._deps(r, w, extra, False)
        oid = len(self.ops)
        cc = self.DEF_COST[ename] if c is None else c
        if ename == 'pe':
            cc *= self.pe_scale
        elif ename in ('act', 'dve', 'pool'):
            cc = LINECOST.get((ename, fn.__code__.co_firstlineno), cc * self.ad_scale) * self.lc_scale
        self.ops.append(dict(id=oid, eng=ename, fn=fn, deps=deps, dma=None, tag=self.cur_tag, cost=cc))
        self._commit(oid, r, w, False)
        return oid

    def dma(self, qname, stream, fn, r=(), w=(), extra=(), group=False, c=4.0):
        S = self.stream(stream, group)
        deps = self._deps(r, w, extra, group)
        oid = len(self.ops)
        thr = ()
        if group:
            hist = self.thr_hist.setdefault(qname, [])
            B = 3 if qname == 'pool' else 4
            nb = len(hist) // B
            if nb > 0:
                thr = tuple(hist[(nb - 1) * B:nb * B])
                deps.update(thr)
            hist.append(oid)
        self.ops.append(dict(id=oid, eng=qname, fn=fn, deps=deps, dma=S, cost=c, thr=thr, tag=self.cur_tag))
        self._commit(oid, r, w, group)
        return oid

    def wait_all(self, ename, toks):
        oid = len(self.ops)
        self.ops.append(dict(id=oid, eng=ename, fn=None, deps=set(toks), dma=None, cost=0.05, tag='final'))
        return oid

    def schedule(self):
        import heapq
        ops = self.ops
        n = len(ops)
        ndeps = [len(o['deps']) for o in ops]
        users = [[] for _ in range(n)]
        for o in ops:
            for d in o['deps']:
                users[d].append(o['id'])
        ready_t = [0.0] * n
        fin = [0.0] * n
        bl = [0.0] * n
        for o in reversed(ops):
            i = o['id']
            m = 0.0
            for u in users[i]:
                if bl[u] > m:
                    m = bl[u]
            bl[i] = m + o['cost'] + self.HOP
        mode = os.environ.get('K_PRIO', 'id')
        if mode == 'bl':
            prio = [(-bl[i], i) for i in range(n)]
        else:
            prio = [(i, i) for i in range(n)]
        ready = {e: [] for e in self.engs}
        free_t = {e: 0.0 for e in self.engs}
        events = []
        for o in ops:
            if ndeps[o['id']] == 0:
                heapq.heappush(ready[o['eng']], (prio[o['id']], o['id']))
        for e in self.engs:
            heapq.heappush(events, (0.0, 0, e))
        seq = 1
        done = 0
        pending_wake = {e: True for e in self.engs}
        while done < n:
            if not events:
                raise RuntimeError("scheduler stuck")
            t, _, e = heapq.heappop(events)
            pending_wake[e] = False
            if free_t[e] > t + 1e-9:
                heapq.heappush(events, (free_t[e], seq, e)); seq += 1; pending_wake[e] = True
                continue
            cand = None
            tmp = []
            while ready[e]:
                item = heapq.heappop(ready[e])
                oid = item[1]
                if ready_t[oid] <= t + 1e-9:
                    cand = oid
                    break
                tmp.append(item)
            for x in tmp:
                heapq.heappush(ready[e], x)
            if cand is None:
                if ready[e]:
                    tn = min(ready_t[x[1]] for x in ready[e])
                    heapq.heappush(events, (tn, seq, e)); seq += 1; pending_wake[e] = True
                continue
            o = ops[cand]
            o['t0'] = t
            self.engs[e]['order'].append(cand)
            if o['dma'] is not None:
                free_t[e] = t + (0.6 if e == 'pool' else 0.08)
                fin[cand] = t + o['cost']
            else:
                free_t[e] = t + o['cost']
                fin[cand] = free_t[e]
            done += 1
            for u in users[cand]:
                ndeps[u] -= 1
                if ops[u]['eng'] == e and o['dma'] is None:
                    rt = fin[cand] + (0.0 if e == 'pe' else self.hop_same)
                else:
                    rt = fin[cand] + self.HOP
                if rt > ready_t[u]:
                    ready_t[u] = rt
                if ndeps[u] == 0:
                    ue = ops[u]['eng']
                    heapq.heappush(ready[ue], (prio[u], u))
                    if not pending_wake[ue]:
                        heapq.heappush(events, (max(ready_t[u], free_t[ue]), seq, ue)); seq += 1; pending_wake[ue] = True
            heapq.heappush(events, (free_t[e], seq, e)); seq += 1; pending_wake[e] = True
        self.sim_time = max(fin)
        self.fin = fin
        if self.filler_fn is not None:
            fc = self.filler_cost
            new_order = []
            prev_end = None
            nfill = 0
            armed = False
            for oid in self.engs['pe']['order']:
                o = ops[oid]
                if not armed:
                    if any(ops[d]['dma'] is not None and ops[d]['dma']['name'] == 'win' for d in o['deps']):
                        armed = True
                    new_order.append(oid)
                    prev_end = fin[oid]
                    continue
                if prev_end is not None:
                    gap = o['t0'] - prev_end
                    if gap > self.fill_min:
                        k = int((gap * self.fill_frac) / fc)
                        for _ in range(k):
                            fid = len(ops)
                            ops.append(dict(id=fid, eng='pe', fn=self.filler_fn, deps=set(), dma=None, cost=fc, tag='fill', filler=True))
                            new_order.append(fid)
                            nfill += 1
                new_order.append(oid)
                prev_end = fin[oid]
            self.engs['pe']['order'] = new_order
            self.nfill = nfill
        for e, E in self.engs.items():
            pos = 0
            for oid in E['order']:
                o = ops[oid]
                if o['dma'] is not None:
                    o['dma']['count'] += 16
                    o['val'] = o['dma']['count']
                elif o['fn'] is not None and not o.get('filler'):
                    pos += 1
                    o['val'] = pos

    def replay(self, ename, eng):
        E = self.engs[ename]
        ops = self.ops
        seen = {}
        for oid in E['order']:
            o = ops[oid]
            need = {}
            for d in o['deps']:
                D = ops[d]
                if D['dma'] is not None:
                    S = D['dma']
                    if d in o.get('thr', ()):
                        sem, val = S['sem'], D['val']
                    else:
                        sem, val = S['sem'], (S['count'] if S['group'] else D['val'])
                else:
                    if D['fn'] is None:
                        continue
                    if D['eng'] == 'pe' and ename == 'pe':
                        continue
                    sem, val = self.engs[D['eng']]['sem'], D['val']
                if need.get(sem, 0) < val:
                    need[sem] = val
            for sem, val in need.items():
                if seen.get(sem, 0) < val:
                    eng.wait_ge(sem, val)
                    seen[sem] = val
            if o['fn'] is None:
                continue
            ins = o['fn'](eng)
            if o.get('filler'):
                continue
            if o['dma'] is not None:
                ins.then_inc(o['dma']['sem'], 16)
            else:
                ins.then_inc(E['sem'], 1)


def build_program(nt_pre=NT_PRE, nt_full=NT_FULL):
    nc = bass.Bass("TRN2", target_bir_lowering=False)
    es = ExitStack()

    def din(name, shape, dt=F32):
        return nc.dram_tensor(name, list(shape), dt, kind="ExternalInput").ap()

    def dout(name, shape, dt=F32):
        return nc.dram_tensor(name, list(shape), dt, kind="ExternalOutput").ap()

    xall = din("xall", [8192, D])
    xs_d = din("xs", [16, D])
    C0_d = din("C0", [4, 128, 128])
    n0_d = din("n0", [4, 128])
    m0_d = din("m0", [4, 1])
    ck_d = din("ck", [128, 128])
    cv_d = din("cv", [128, 128])
    sconv_d = din("sconv", [2, 2 * DFF])
    win_d = din("w_in", [D, WCOLS])
    wout_d = din("w_out", [D, D])
    wup_d = din("w_up", [D, 2 * DFF])
    wdn_d = din("w_down", [DFF, D])
    wconv_d = din("w_conv", [3, 2 * DFF])
    bconv_d = din("b_conv", [1, 2 * DFF])
    bi_d = din("b_i", [4, 1])
    bf_d = din("b_f", [4, 1])
    gn_d = din("g_norm", [1, 512])
    sink_d = din("sinks", [1, 8])
    ln_d = din("ln", [4, D])
    cos_d = din("cosT", [NT_FULL + 1, 128, 128])
    sin_d = din("sinT", [NT_FULL + 1, 128, 128])
    ident_d = din("ident", [128, 128])
    maskT_d = din("maskT", [128, 128])
    amask_d = din("amask", [128, 2, 128])
    sel4_d = din("sel4", [4, 4 * 128])
    hs_d = din("hs", [128, 2])

    y_d = dout("y", [4096, D])
    ys_d = dout("ys", [16, D])
    Cp_d = dout("Cp", [4, 128, 128])
    np_d = dout("np", [4, 128])
    mp_d = dout("mp", [4, 1])
    kp_d = dout("kp", [128, 128])
    vp_d = dout("vp", [128, 128])
    cvp_d = dout("cvp", [2, 2 * DFF])
    Cs_d = dout("Cs", [4, 128, 128])
    ns_d = dout("ns", [4, 128])
    ms_d = dout("ms", [4, 1])
    ks_d = dout("ks", [16, 128])
    vs_d = dout("vs", [16, 128])
    cvs_d = dout("cvs", [2, 2 * DFF])
    wupbf = nc.dram_tensor("wupbf", [NU, 128, 8 * 256], BF16, kind="Internal").ap()
    wdnbf = nc.dram_tensor("wdnbf", [NU // 2, 128, 2 * D], BF16, kind="Internal").ap()

    def sb(name, shape, dt=F32):
        return es.enter_context(nc.sbuf_tensor(name, list(shape), dt))

    def pt(name, shape, dt=F32):
        return es.enter_context(nc.psum_tensor(name, list(shape), dt))

    win = sb("win", [128, 8, WCOLS], BF16)
    wout = sb("wout", [128, 8, D], BF16)
    wdr = [sb("wdr%d" % i, [128, 2, 512], BF16) for i in range(5)]
    NS = 2
    ustS = [sb("ustS%d" % i, [128, 2, 258]) for i in range(NS)]
    ctS = [sb("ctS%d" % i, [128, 2, 256]) for i in range(NS)]
    s2b0 = sb("s2b0", [128, D])
    wur = [sb("wur%d" % i, [128, 8, 256], BF16) for i in range(2)]
    lnp = sb("lnp", [128, 4, D])
    gnb = sb("gnb", [128, 512])
    esink = sb("esink", [128, 8])
    ident = sb("ident_s", [128, 128])
    maskT = sb("maskT_s", [128, 128])
    amask = sb("amask_s", [128, 2, 128], BF16)
    sel4 = sb("sel4_s", [4, 4 * 128])
    hs = sb("hs_s", [128, 2])
    bi = sb("bi_s", [4, 1])
    nbf = sb("nbf_s", [4, 1])
    wcv = sb("wcv", [128, 3, 2 * NU])
    bcv = sb("bcv", [128, 2 * NU])
    cvh = sb("cvh", [128, 2 * NU, 2])
    cvsb = sb("cvsb", [128, 2 * NU, 2])
    ones4 = sb("ones4", [4, 128])
    onesr = ones4

    xin = [sb("xin%d" % i, [128, D]) for i in range(2)]
    xT = sb("xT", [128, 8, 128], BF16)
    mixT = xT
    kmtok = sb("kmtok", [128, 4, 128], BF16)
    kw = kmtok
    v1 = sb("v1", [128, 4, 130], BF16)
    qmT = sb("qmT", [128, 4, 128], BF16)
    qaT = sb("qaT", [128, 4, 128], BF16)
    kmT = sb("kmT", [128, 4, 128], BF16)
    cosb = [sb("cos%d" % i, [128, 128]) for i in range(2)]
    sinb = [sb("sin%d" % i, [128, 128]) for i in range(2)]
    nd1 = sb("nd1", [128, 4, 130])
    nd = nd1
    rt1 = sb("rt1", [128, 4, 128])
    Ebuf = sb("Ebuf", [128, 4, 128])
    hmr = Ebuf
    rt2 = sb("rt2", [128, 4, 128])
    krope = sb("krope", [128, 128])
    kTb = [sb("kTb%d" % i, [128, 128], BF16) for i in range(2)]
    va1 = [sb("va1%d" % i, [128, 2, 66], BF16) for i in range(2)]
    kvout = sb("kvout", [128, 2, 128])
    ST = sb("ST", [128, 4, 128], BF16)
    Cst = [sb("Cst0", [128, 4, 130])] * 2
    Cb = [sb("Cb0", [128, 4, 130], BF16)] * 2
    mst = [sb("mst0", [4, 1])] * 2
    bst = sb("bst", [128, 4, 6])
    mv = sb("mv", [128, 4, 2])
    sm = sb("sm", [128, 32])
    pTraw = sb("pTraw", [128, 1024])
    pT = pTraw[:].bitcast(BF16).rearrange("p (b h t) -> p b h t", b=2, h=8)
    og = sb("og", [128, 512])
    mix = sb("mix", [128, D])
    mixb = sb("mixb", [128, D], BF16)
    identb = sb("identb", [128, 128], BF16)
    s1 = mix
    s2b = [s2b0, mix]
    MIXK = ['mix']
    S2K = [['s2b0a', 's2b0b', 's2b0'], MIXK]
    hresL = [sb("hres%d" % i, [128, 2, D]) for i in range(2)]
    hTL = [sb("hT%d" % i, [128, 8, 256], BF16) for i in range(2)]
    actT = sb("actT", [128, NU, 256], BF16)
    G = sb("G", [4, 7, 128])
    tm = sb("tm", [128, 16])
    dbc = sb("dbc", [128, 4])
    gsm = sb("gsm", [4, 8])

    P01 = pt("P01", [128, 1024])
    P23 = pt("P23", [128, 1024])
    P4 = pt("P4", [128, 512])
    P5 = pt("P5", [128, 512])
    P6 = pt("P6", [128, 512])
    P7 = pt("P7", [128, 512])
    PA0 = P01[:, 0:512]
    PA1 = P01[:, 512:1024]
    P0b = PA0.bitcast(BF16)
    P1b = PA1.bitcast(BF16)

    T = Tracker(nc, es)
    FILL = int(os.environ.get('K_FILL', '1'))
    T.add_engine('sp', nc.sync)
    T.add_engine('pe', nc.tensor)
    T.add_engine('act', nc.scalar)
    T.add_engine('dve', nc.vector)
    T.add_engine('pool', nc.gpsimd)

    def ld(q, stream, out, in_, w, r=()):
        return T.dma(q, stream, lambda e, o=out, i=in_: e.dma_start(out=o, in_=i), r=r, w=w, group=True, c=6.0)

    for k in range(8):
        for c0 in (0, 1732):
            ld('pool', 'win', win[:, k, c0:c0 + 1732], win_d[k * 128:(k + 1) * 128, c0:c0 + 1732], w=['win'])
    ld('sp', 'small', ident[:], ident_d[:, :], w=['const'])
    ld('sp', 'small', maskT[:], maskT_d[:, :], w=['const'])
    ld('pool', 'small2', amask[:], amask_d[:, :, :], w=['c5'])
    ld('sp', 'small', sel4[:], sel4_d[:, :], w=['const'])
    ld('sp', 'small', hs[:], hs_d[:, :], w=['const'])
    ld('sp', 'small', bi[:], bi_d[:, :], w=['const'])
    ld('sp', 'small', nbf[:], bf_d[:, :], w=['const'])
    ld('sp', 'small', gnb[:], gn_d[0, :].partition_broadcast(128), w=['const'])
    ld('sp', 'small', esink[:], sink_d[0, :].partition_broadcast(128), w=['const'])
    for j in range(4):
        ld('sp', 'small', lnp[:, j, :], ln_d[j, :].partition_broadcast(128), w=['const'])
    for j in range(3):
        T.dma('sp', 'small', lambda e, j=j: e.dma_start(
            out=wcv[:, j, :], in_=wconv_d[j, :].rearrange("(c p) -> p c", p=128), allow_slow_non_contiguous=True), w=['const'], group=True, c=20.0)
    T.dma('sp', 'small', lambda e: e.dma_start(
        out=bcv[:], in_=bconv_d[0, :].rearrange("(c p) -> p c", p=128), allow_slow_non_contiguous=True), w=['const'], group=True, c=20.0)
    for j in range(2):
        T.dma('sp', 'small', lambda e, j=j: e.dma_start(
            out=cvsb[:, :, j], in_=sconv_d[j, :].rearrange("(c p) -> p c", p=128), allow_slow_non_contiguous=True), w=['cvsb'], group=True, c=20.0)
    for k in range(8):
        ld('pool', 'wout', wout[:, k, :], wout_d[k * 128:(k + 1) * 128, :], w=['wout'])
    for u in range(NU):
        for j in range(2):
            c0 = j * DFF + u * 128
            T.dma('pool', 'wupc', lambda e, u=u, j=j, c0=c0: e.dma_start(
                out=wupbf[u].rearrange("p (k c) -> p k c", k=8)[:, :, j * 128:(j + 1) * 128],
                in_=wup_d[:, c0:c0 + 128].rearrange("(k p) c -> p k c", p=128)), w=['wupbf'], group=True, c=8.0)
    for c in range(NU // 2):
        T.dma('pool', 'wdnc', lambda e, c=c: e.dma_start(
            out=wdnbf[c].rearrange("p (k n) -> p k n", k=2),
            in_=wdn_d[c * 256:(c + 1) * 256, :].rearrange("(k p) n -> p k n", p=128)), w=['wdnbf'], group=True, c=8.0)

    T.op('dve', lambda e: e.memset(ones4[:], 1.0), w=['c2'])
    T.op('dve', lambda e: e.tensor_scalar(nbf[:], nbf[:], -1.0, None, ALU.mult), r=['const'], w=['c3'])
    T.op('act', lambda e: e.activation(esink[:], esink[:], AF.Exp), r=['const'], w=['c4'])
    T.op('dve', lambda e: e.tensor_copy(identb[:], ident[:]), r=['const'], w=['c6'])
    for i in range(2):
        T.op('dve', lambda e, i=i: e.memset(v1[:, :, 128:130], 1.0), w=['v1'])
    for i in range(2):
        T.op('dve', lambda e, i=i: e.memset(va1[i][:], 1.0), w=['va1_%d' % i])
        T.op('dve', lambda e, i=i: e.memset(kTb[i][:], 0.0), w=['kT_%d' % i])
    T.op('dve', lambda e: e.memset(cvh[:], 0.0), w=['cvh'])
    CONSTS = ['const', 'c2', 'c3', 'c4', 'c5', 'c6']

    def transposes8(src, Tn, dst_ps, rkeys):
        def fn(e):
            ins = None
            for k in range(8):
                ins = e.transpose(dst_ps[:, k * 128:k * 128 + Tn], src[:Tn, k * 128:(k + 1) * 128], ident[:Tn, :Tn])
            return ins
        T.op('pe', fn, r=list(rkeys) + CONSTS, w=['P0', 'P1'], c=2.0)

    def load_x(slot, src_ap, Tn):
        return T.dma('sp', 'xin%d' % slot, lambda e: e.dma_start(out=xin[slot][:Tn, :], in_=src_ap), w=['xin%d' % slot])

    def make_xT(slot, Tn):
        transposes8(xin[slot], Tn, P01, ['xin%d' % slot])
        for b in range(2):
            T.op('act', lambda e, b=b: e.copy(
                xT[:, 4 * b:4 * b + 4, :Tn],
                P01[:, 512 * b:512 * b + 512].rearrange("p (k t) -> p k t", k=4)[:, :, :Tn]),
                r=['P%d' % b], w=['xT'])

    def mm_A(ps, col0, ncols, Tn, wkey, bankkeys):
        def fn(e):
            ins = None
            for k in range(8):
                ins = e.matmul(ps[:Tn, :ncols], xT[:, k, :Tn], win[:, k, col0:col0 + ncols], start=(k == 0), stop=(k == 7))
            return ins
        T.op('pe', fn, r=['xT', 'win'], w=bankkeys, c=8 * (0.09 + max(ncols, 64) / 2400.0) + 0.1)

    def mm_B(ps, col0, mcols, Tn, bankkeys):
        def fn(e):
            ins = None
            for k in range(8):
                ins = e.matmul(ps[:mcols, :Tn], win[:, k, col0:col0 + mcols], xT[:, k, :Tn], start=(k == 0), stop=(k == 7))
            return ins
        T.op('pe', fn, r=['xT', 'win'], w=bankkeys, c=8 * 0.13 + 0.1)

    def gate_chain(Tn, si, full):
        m = mst[si]
        mk = 'm%d' % si
        ipT = P4[0:4, 128:128 + Tn]
        fpT = P4[0:4, 256:256 + Tn]
        R = lambda j: G[:, j, :Tn]
        T.op('act', lambda e: e.activation(R(0), fpT, AF.Exp, bias=nbf[:], scale=-1.0), r=['P4', 'c3'], w=['G0'])
        T.op('act', lambda e: e.activation(R(0), R(0), AF.Ln, bias=1.0), r=['G0'], w=['G0'])
        T.op('dve', lambda e: e.tensor_scalar(R(0), R(0), -1.0, None, ALU.mult), r=['G0'], w=['G0'])
        T.op('dve', lambda e: e.tensor_tensor_scan(R(1), onesr[:, :Tn], R(0), 0.0, ALU.mult, ALU.add), r=['G0', 'c2'], w=['G1'])
        T.op('dve', lambda e: e.scalar_tensor_tensor(R(2), ipT, bi[:], R(1), ALU.add, ALU.subtract), r=['P4', 'G1', 'const'], w=['G2'])
        T.op('dve', lambda e: e.tensor_tensor_scan(R(3), R(2), R(2), m[:], ALU.max, ALU.max), r=['G2', mk], w=['G3'])
        T.op('act', lambda e: e.activation(R(4), R(3), AF.Exp, bias=m[:], scale=-1.0), r=['G3', mk], w=['G4'])
        T.op('dve', lambda e: e.tensor_scalar(gsm[:, 0:1], G[:, 3, Tn - 1:Tn], -1.0, None, ALU.mult), r=['G3'], w=['gsm0'])
        T.op('act', lambda e: e.activation(R(5), R(2), AF.Exp, bias=gsm[:, 0:1]), r=['G2', 'gsm0'], w=['G5'])
        T.op('dve', lambda e: e.tensor_tensor(R(6), R(1), R(3), ALU.add), r=['G1', 'G3'], w=['G6'])
        if full:
            T.op('act', lambda e: e.activation(R(0), R(6), AF.Exp, scale=-1.0), r=['G6'], w=['G0'])
            T.op('dve', lambda e: e.tensor_scalar(R(1), R(3), -1.0, None, ALU.mult), r=['G3'], w=['G1'])
        T.op('dve', lambda e: e.tensor_scalar(gsm[:, 4:8], ident[0:4, 0:4], G[:, 4, Tn - 1:Tn], None, ALU.mult), r=['G4', 'const'], w=['gsm1'])
        T.op('dve', lambda e: e.tensor_copy(m[:], G[:, 6, Tn - 1:Tn]), r=['G6'], w=[mk])

        def fn(e):
            ins = None
            rows = [(2, 0), (4, 4), (0, 8), (5, 12)] if full else [(5, 12)]
            for (rj, c) in rows:
                ins = e.matmul(P4[:Tn, 384 + c:384 + c + 4], G[:, rj, :Tn], ident[0:4, 0:4], start=True, stop=True)
            ins = e.matmul(P4[:, 448:452], ones4[:, :], gsm[:, 4:8], start=True, stop=True)
            return ins
        T.op('pe', fn, r=['G2', 'G4', 'G0', 'G5', 'gsm1', 'c2', 'const'], w=['P4'])
        if full:
            T.op('dve', lambda e: e.tensor_copy(tm[:Tn, :], P4[:Tn, 384:400]), r=['P4'], w=['tm'])
        else:
            T.op('dve', lambda e: e.tensor_copy(tm[:Tn, 12:16], P4[:Tn, 396:400]), r=['P4'], w=['tm'])
        T.op('dve', lambda e: e.tensor_copy(dbc[:], P4[:, 448:452]), r=['P4'], w=['dbc'])

    def state_update(Tn, si):
        C = Cst[si]
        ck = 'C%d' % si
        T.op('dve', lambda e: e.tensor_tensor(kw[:Tn], kmtok[:Tn], tm[:Tn, 12:16].unsqueeze(2).to_broadcast([Tn, 4, 128]), ALU.mult),
             r=['kmtok', 'tm'], w=['kmtok'])

        def fn(e):
            ins = None
            for h in range(4):
                ins = e.matmul(P23[:, h * 256:h * 256 + 129], kw[:Tn, h, :], v1[:Tn, h, 0:129], start=True, stop=True)
            return ins
        T.op('pe', fn, r=['kmtok', 'v1'], w=['P2', 'P3'])
        for h in range(4):
            T.op('dve', lambda e, h=h: e.scalar_tensor_tensor(C[:, h, 0:129], C[:, h, 0:129], dbc[:, h:h + 1],
                                                              P23[:, h * 256:h * 256 + 129], ALU.mult, ALU.add),
                 r=['dbc', 'P2', 'P3', ck], w=[ck])

    out_toks = []

    def do_prefix():
        tok_x = {}
        if nt_pre > 0:
            tok_x[0] = load_x(0, xall[0:128, :], 128)
        for i in range(nt_pre):
            T.cur_tag = 'pre'
            slot = i % 2
            if i + 1 < nt_pre:
                load_x((i + 1) % 2, xall[(i + 1) * 128:(i + 2) * 128, :], 128)
            make_xT(slot, 128)
            mm_A(P23[:, 0:512], C_KM, 512, 128, 'win', ['P2'])
            mm_A(P23[:, 512:1024], C_VM, 512, 128, 'win', ['P3'])
            mm_B(P4[:, 128:256], C_IP, 4, 128, ['P4'])
            mm_B(P4[:, 256:384], C_FP, 4, 128, ['P4'])
            T.op('act', lambda e: e.activation(kmtok[:].rearrange("p h d -> p (h d)"), P23[:, 0:512], AF.Copy, scale=KSCALE), r=['P2'], w=['kmtok'])
            T.op('act', lambda e: e.copy(v1[:, :, 0:128], P23[:, 512:1024].rearrange("p (h d) -> p h d", h=4)), r=['P3'], w=['v1'])
            gate_chain(128, 0, False)
            state_update(128, 0)


    def stage1(Tn, xslot, si, kslot, pslot, prev_bias, cs_slot, hcol, htile, save_kv, is_sample, par=0):
        hres = hresL[par]
        hT = hTL[par]
        hrk = 'hres%d_%d' % (par, htile)
        htk = 'hT%d' % par
        C = Cst[si]
        ck = 'C%d' % si
        cbk = 'Cb%d' % si
        make_xT(xslot, Tn)
        mm_A(P23[:, 0:512], C_KM, 512, Tn, 'win', ['P2'])
        mm_A(P23[:, 512:1024], C_VM, 512, Tn, 'win', ['P3'])
        mm_A(PA1[:, 0:512], C_OM, 512, Tn, 'win', ['P1'])
        mm_A(P4[:, 0:128], C_VA, 128, Tn, 'win', ['P4'])
        mm_B(P4[:, 128:256], C_IP, 4, Tn, ['P4'])
        mm_B(P4[:, 256:384], C_FP, 4, Tn, ['P4'])
        T.op('act', lambda e: e.activation(kmtok[:Tn].rearrange("p h d -> p (h d)"), P23[:Tn, 0:512], AF.Copy, scale=KSCALE), r=['P2'], w=['kmtok'])
        T.op('act', lambda e: e.copy(v1[:Tn, :, 0:128], P23[:Tn, 512:1024].rearrange("p (h d) -> p h d", h=4)), r=['P3'], w=['v1'])
        T.op('act', lambda e: e.activation(og[:Tn], PA1[:Tn, :], AF.Exp, scale=-1.0), r=['P1'], w=['og'])
        T.op('act', lambda e: e.activation(og[:Tn], og[:Tn], AF.Ln, bias=1.0), r=['og'], w=['og'], c=0.55)
        T.op('act', lambda e: e.activation(og[:Tn], og[:Tn], AF.Exp, scale=-1.0), r=['og'], w=['og'], c=0.55)
        T.op('act', lambda e: e.copy(va1[kslot][:Tn, :, 0:64], P4[:Tn, 0:128].rearrange("p (k d) -> p k d", k=2)), r=['P4'], w=['va1_%d' % kslot])
        if save_kv:
            T.op('dve', lambda e: e.tensor_copy(kvout[:Tn, 1, :], P4[:Tn, 0:128]), r=['P4'], w=['kvout'])
        gate_chain(Tn, si, True)
        def fn_rb(e):
            ins = None
            for h in range(4):
                ins = e.matmul(P4[:Tn, h * 128:h * 128 + Tn], sel4[:, h * 128:h * 128 + Tn], G[:, 1, :Tn], start=True, stop=True)
            return ins
        T.op('pe', fn_rb, r=['G1', 'const'], w=['P4'], c=1.0)
        T.op('dve', lambda e: e.tensor_tensor(Ebuf[:Tn, :, :Tn], P4[:Tn, :].rearrange("p (h t) -> p h t", h=4)[:, :, :Tn],
                                              maskT[:Tn, :Tn].unsqueeze(1).to_broadcast([Tn, 4, Tn]), ALU.add),
             r=['P4', 'const'], w=['Ebuf'])
        for h in range(4):
            T.op('act', lambda e, h=h: e.activation(Ebuf[:Tn, h, :Tn], Ebuf[:Tn, h, :Tn], AF.Exp, bias=tm[:Tn, h:h + 1]), r=['Ebuf', 'tm'], w=['Ebuf'])
        for mt_ in range(4):
            mm_B(PA0[:, mt_ * 128:mt_ * 128 + 128], C_QM + mt_ * 128, 128, Tn, ['P0'])
        T.op('act', lambda e: e.copy(qmT[:, :, :Tn], PA0[:, :].rearrange("p (h t) -> p h t", h=4)[:, :, :Tn]), r=['P0'], w=['qmT'])
        def fn_kt(e):
            ins = None
            for h in range(4):
                ins = e.transpose(P1b[:, h * 128:h * 128 + Tn], kmtok[:Tn, h, :], identb[:Tn, :Tn])
            return ins
        T.op('pe', fn_kt, r=['kmtok', 'c6'], w=['P1'], c=0.7)
        T.op('act', lambda e: e.copy(kmT[:, :, :Tn], P1b[:, 0:512].rearrange("p (h t) -> p h t", h=4)[:, :, :Tn]), r=['P1'], w=['kmT'])
        def fn_s(e):
            ins = None
            for h in range(4):
                ins = e.matmul(P4[:Tn, h * 128:h * 128 + Tn], kmT[:, h, :Tn], qmT[:, h, :Tn], start=True, stop=True)
            return ins
        T.op('pe', fn_s, r=['kmT', 'qmT'], w=['P4'])
        T.op('dve', lambda e: e.tensor_tensor(ST[:Tn, :, :Tn], P4[:Tn, :].rearrange("p (h t) -> p h t", h=4)[:, :, :Tn], Ebuf[:Tn, :, :Tn], ALU.mult),
             r=['P4', 'Ebuf'], w=['ST'])
        def fn_qc(e):
            ins = None
            for h in range(4):
                ins = e.matmul(P23[:Tn, h * 256:h * 256 + 129], qmT[:, h, :Tn], Cb[si][:, h, 0:129], start=True, stop=True)
            return ins
        T.op('pe', fn_qc, r=['qmT', cbk], w=['P2', 'P3'])
        def fn_sv(e):
            ins = None
            for h in range(4):
                ins = e.matmul(P01[:Tn, h * 256:h * 256 + 129], ST[:Tn, h, :Tn], v1[:Tn, h, 0:129], start=True, stop=True)
            return ins
        T.op('pe', fn_sv, r=['ST', 'v1'], w=['P0', 'P1'])
        for h in range(4):
            T.op('act', lambda e, h=h: e.activation(nd1[:Tn, h, 0:129], P23[:Tn, h * 256:h * 256 + 129], AF.Copy, scale=tm[:Tn, 4 + h:5 + h]),
                 r=['P2', 'P3', 'tm'], w=['nd1'])
        T.op('dve', lambda e: e.tensor_tensor(nd[:Tn, :, 0:129], nd1[:Tn, :, 0:129],
                                              P01[:Tn, :].rearrange("p (h c) -> p h c", h=4)[:, :, 0:129], ALU.add),
             r=['nd1', 'P0', 'P1'], w=['nd1'])
        state_update(Tn, si)
        T.op('act', lambda e: e.copy(Cb[si][:], C[:]), r=[ck], w=[cbk])
        T.op('dve', lambda e: e.tensor_scalar(sm[:Tn, 20:24], nd[:Tn, :, 128], -1.0, None, ALU.mult), r=['nd1'], w=['sm'])
        T.op('dve', lambda e: e.tensor_tensor(sm[:Tn, 20:24], sm[:Tn, 20:24], nd[:Tn, :, 128], ALU.max), r=['nd1', 'sm'], w=['sm'])
        T.op('dve', lambda e: e.tensor_tensor(sm[:Tn, 0:4], sm[:Tn, 20:24], tm[:Tn, 8:12], ALU.max), r=['sm', 'tm'], w=['sm'])
        T.op('dve', lambda e: e.reciprocal(sm[:Tn, 0:4], sm[:Tn, 0:4]), r=['sm'], w=['sm'])
        T.op('dve', lambda e: e.tensor_tensor(hmr[:Tn], nd[:Tn, :, 0:128], sm[:Tn, 0:4].unsqueeze(2).to_broadcast([Tn, 4, 128]), ALU.mult),
             r=['nd1', 'sm'], w=['Ebuf'])
        for h in range(4):
            T.op('dve', lambda e, h=h: e.bn_stats(bst[:Tn, h, :], hmr[:Tn, h, :]), r=['Ebuf'], w=['bst'])
        for h in range(4):
            T.op('dve', lambda e, h=h: e.bn_aggr(mv[:Tn, h, :], bst[:Tn, h, :]), r=['bst'], w=['mv'])
        T.op('act', lambda e: e.activation(sm[:Tn, 4:8], mv[:Tn, :, 1], AF.Ln, bias=EPS), r=['mv'], w=['sm'])
        T.op('act', lambda e: e.activation(sm[:Tn, 4:8], sm[:Tn, 4:8], AF.Exp, scale=-0.5), r=['sm'], w=['sm'])
        for h in range(4):
            T.op('dve', lambda e, h=h: e.tensor_scalar(hmr[:Tn, h, :], hmr[:Tn, h, :], mv[:Tn, h, 0:1], sm[:Tn, 4 + h:5 + h], ALU.subtract, ALU.mult),
                 r=['Ebuf', 'mv', 'sm'], w=['Ebuf'])
        T.op('dve', lambda e: e.tensor_tensor(hmr[:Tn].rearrange("p h d -> p (h d)"), hmr[:Tn].rearrange("p h d -> p (h d)"), gnb[:Tn], ALU.mult),
             r=['Ebuf', 'const'], w=['Ebuf'])
        T.op('dve', lambda e: e.tensor_tensor(mixb[:Tn, 0:512], hmr[:Tn].rearrange("p h d -> p (h d)"), og[:Tn], ALU.mult),
             r=['Ebuf', 'og'], w=['mbA'])
        cosv, sinv = cosb[cs_slot], sinb[cs_slot]
        csk = 'cs%d' % cs_slot
        for mt_ in range(4):
            mm_B(P4[:, mt_ * 128:mt_ * 128 + 128], C_QAP + mt_ * 128, 128, Tn, ['P4'])
        for mt_ in range(4):
            mm_B(PA1[:, mt_ * 128:mt_ * 128 + 128], C_QAR + mt_ * 128, 128, Tn, ['P1'])
        T.op('dve', lambda e: e.tensor_tensor(rt1[:, :, :Tn], P4[:, :].rearrange("p (h t) -> p h t", h=4)[:, :, :Tn],
                                              cosv[:, :Tn].unsqueeze(1).to_broadcast([128, 4, Tn]), ALU.mult), r=['P4', csk], w=['rt1'])
        T.op('dve', lambda e: e.tensor_tensor(rt2[:, :, :Tn], PA1[:, :].rearrange("p (h t) -> p h t", h=4)[:, :, :Tn],
                                              sinv[:, :Tn].unsqueeze(1).to_broadcast([128, 4, Tn]), ALU.mult), r=['P1', csk], w=['rt2'])
        T.op('dve', lambda e: e.tensor_tensor(qaT[:, :, :Tn], rt1[:, :, :Tn], rt2[:, :, :Tn], ALU.add), r=['rt1', 'rt2'], w=['qaT'])
        mm_B(PA0[:, 0:128], C_KAP, 128, Tn, ['P0'])
        mm_B(PA0[:, 128:256], C_KAR, 128, Tn, ['P0'])
        T.op('dve', lambda e: e.tensor_tensor(rt1[:, 0, :Tn], PA0[:, 0:Tn], cosv[:, :Tn], ALU.mult), r=['P0', csk], w=['rt1'])
        T.op('dve', lambda e: e.tensor_tensor(rt2[:, 0, :Tn], PA0[:, 128:128 + Tn], sinv[:, :Tn], ALU.mult), r=['P0', csk], w=['rt2'])
        T.op('dve', lambda e: e.tensor_tensor(krope[:, :Tn], rt1[:, 0, :Tn], rt2[:, 0, :Tn], ALU.add), r=['rt1', 'rt2'], w=['krope'])
        T.op('act', lambda e: e.copy(kTb[kslot][:, :Tn], krope[:, :Tn]), r=['krope'], w=['kT_%d' % kslot])
        if save_kv:
            T.op('pe', lambda e: e.transpose(PA0[:Tn, 256:384], krope[:, :Tn], ident[:, :]), r=['krope', 'const'], w=['P0'])
            T.op('dve', lambda e: e.tensor_copy(kvout[:Tn, 0, :], PA0[:Tn, 256:384]), r=['P0'], w=['kvout'])
        blocks = [(pslot, 128, prev_bias), (kslot, Tn, 0.0)]
        psb = [(P23, ['P2', 'P3']), (P01, ['P0', 'P1'])]
        for bi_, (ks_, Sb, bias_) in enumerate(blocks):
            ps_, keys_ = psb[bi_]
            def fn_sc(e, ks_=ks_, Sb=Sb, ps_=ps_):
                ins = None
                for j in range(4):
                    for half in range(2):
                        hd = half * 4 + j
                        ins = e.matmul(ps_[:Sb, hd * 128:hd * 128 + Tn], kTb[ks_][half * 64:(half + 1) * 64, :Sb],
                                       qaT[half * 64:(half + 1) * 64, j, :Tn], start=True, stop=True)
                return ins
            T.op('pe', fn_sc, r=['kT_%d' % ks_, 'qaT'], w=keys_)
            for b2 in range(2):
                T.op('act', lambda e, b2=b2, Sb=Sb, ps_=ps_, bias_=bias_, bi_=bi_: e.activation(
                    pT[:Sb, bi_, 4 * b2:4 * b2 + 4, :Tn],
                    ps_[:Sb, 512 * b2:512 * b2 + 512].rearrange("p (h t) -> p h t", h=4)[:, :, :Tn],
                    AF.Exp, bias=bias_, scale=0.125), r=[keys_[b2], 'c5', 'const'], w=['pT%d' % bi_])
        def fn_pv(e):
            ins = None
            for hd in range(8):
                if is_sample:
                    for bi_, (ks_, Sb, _) in enumerate(blocks):
                        ins = e.matmul(P45(hd)[:Tn, :], pT[:Sb, bi_, hd, :Tn], va1[ks_][:Sb, hd // 4, 0:65],
                                       start=(bi_ == 0), stop=(bi_ == 1))
                else:
                    for qc, rng in ((0, ((0, 0, 128), (1, 0, 64))), (1, ((0, 64, 128), (1, 0, 128)))):
                        for ii, (bi_, s0, s1_) in enumerate(rng):
                            ks_ = blocks[bi_][0]
                            ins = e.matmul(P45(hd)[qc * 64:(qc + 1) * 64, :], pT[s0:s1_, bi_, hd, qc * 64:(qc + 1) * 64],
                                           va1[ks_][s0:s1_, hd // 4, 0:65], start=(ii == 0), stop=(ii == 1))
            return ins
        def P45(hd):
            base = P4 if hd < 4 else PA1
            c = (hd % 4) * 65
            return base[:, c:c + 65]
        T.op('pe', fn_pv, r=['pT0', 'pT1', 'va1_%d' % pslot, 'va1_%d' % kslot], w=['P4', 'P1'])
        for half, base, bk in ((0, P4, 'P4'), (1, PA1, 'P1')):
            o3 = base[:Tn, 0:260].rearrange("p (h c) -> p h c", h=4)
            T.op('dve', lambda e, o3=o3, half=half: e.tensor_tensor(sm[:Tn, 8 + 4 * half:12 + 4 * half], o3[:, :, 64], esink[:Tn, 4 * half:4 * half + 4], ALU.add),
                 r=[bk, 'c4'], w=['sm'])
            T.op('dve', lambda e, half=half: e.reciprocal(sm[:Tn, 8 + 4 * half:12 + 4 * half], sm[:Tn, 8 + 4 * half:12 + 4 * half]), r=['sm'], w=['sm'])
            T.op('dve', lambda e, o3=o3, half=half: e.tensor_tensor(
                mixb[:Tn, 512 + 256 * half:768 + 256 * half].rearrange("p (h d) -> p h d", h=4), o3[:, :, 0:64],
                sm[:Tn, 8 + 4 * half:12 + 4 * half].unsqueeze(2).to_broadcast([Tn, 4, 64]), ALU.mult),
                r=[bk, 'sm'], w=['mbB%d' % half])
        def fn_mt(e):
            ins = None
            for k in range(8):
                ins = e.transpose(P0b[:, k * 128:k * 128 + Tn], mixb[:Tn, k * 128:(k + 1) * 128], identb[:Tn, :Tn])
            return ins
        T.op('pe', fn_mt, r=['mbA', 'mbB0', 'mbB1', 'c6'], w=['P0'], c=1.1)
        T.op('act', lambda e: e.copy(mixT[:, :, :Tn], P0b[:, :].rearrange("p (k t) -> p k t", k=8)[:, :, :Tn]), r=['P0'], w=['xT'], c=0.9)
        for n in range(2):
            def fn_o(e, n=n):
                ins = None
                for k in range(8):
                    ins = e.matmul(P23[:Tn, n * 512:(n + 1) * 512], mixT[:, k, :Tn], wout[:, k, n * 512:(n + 1) * 512], start=(k == 0), stop=(k == 7))
                return ins
            T.op('pe', fn_o, r=['xT', 'wout'], w=['P%d' % (2 + n)], c=2.5)
        xk = 'xin%d' % xslot
        T.op('dve', lambda e: e.scalar_tensor_tensor(s1[:Tn], xin[xslot][:Tn], ALPHA, P23[:Tn, :], ALU.mult, ALU.add), r=[xk, 'P2', 'P3'], w=['mix'], c=1.3)
        layer_norm(s1, MIXK, Tn, 0, hres[:, htile, :], hrk)
        transposes8(hres[:, htile, :], Tn, P01, [hrk])
        for b in range(2):
            T.op('act', lambda e, b=b: e.copy(hT[:, 4 * b:4 * b + 4, hcol:hcol + Tn],
                                              P01[:, 512 * b:512 * b + 512].rearrange("p (k t) -> p k t", k=4)[:, :, :Tn]),
                 r=['P%d' % b], w=[htk])

    def layer_norm(src, skey, Tn, which, dst, dkey):
        skeys = list(skey) if isinstance(skey, (list, tuple)) else [skey]
        skey = skeys[-1]
        for c in range(2):
            T.op('dve', lambda e, c=c: e.bn_stats(bst[:Tn, c, :], src[:Tn, c * 512:(c + 1) * 512]), r=skeys, w=['bst'])
        T.op('dve', lambda e: e.bn_aggr(mv[:Tn, 0, :], bst[:Tn, 0:2, :].rearrange("p a b -> p (a b)")), r=['bst'], w=['mv'])
        T.op('act', lambda e: e.activation(sm[:Tn, 16:17], mv[:Tn, 0, 1:2], AF.Ln, bias=EPS), r=['mv'], w=['sm'])
        T.op('act', lambda e: e.activation(sm[:Tn, 16:17], sm[:Tn, 16:17], AF.Exp, scale=-0.5), r=['sm'], w=['sm'])
        T.op('dve', lambda e: e.scalar_tensor_tensor(src[:Tn], src[:Tn], mv[:Tn, 0, 0:1], lnp[:Tn, 2 * which, :], ALU.subtract, ALU.mult),
             r=skeys + ['mv', 'const'], w=skeys, c=1.25)
        T.op('dve', lambda e: e.scalar_tensor_tensor(dst[:Tn], src[:Tn], sm[:Tn, 16:17], lnp[:Tn, 2 * which + 1, :], ALU.mult, ALU.add),
             r=skeys + ['sm', 'const'], w=[dkey], c=1.25)

    ring_ctr = [0]
    dring_ctr = [0]
    set_ctr = [0]

    def stage2(N, segs, cvbuf, cvkey, first_macro, out_fn, par=0):
        hres = hresL[par]
        hT = hTL[par]
        htk = 'hT%d' % par
        for u in range(NU):
            rs = ring_ctr[0] % 2
            ring_ctr[0] += 1
            bs = u % 2
            PSU = (P6, P7)[bs]
            pk = ('P6', 'P7')[bs]
            si_ = set_ctr[0] % NS
            set_ctr[0] += 1
            ustv = ustS[si_]
            uk = ['ustS%d' % si_]
            ctv = ctS[si_]
            ckk = 'ctS%d' % si_
            T.dma('sp', 'wur%d' % rs, lambda e, rs=rs, u=u: e.dma_start(out=wur[rs][:].rearrange("p k c -> p (k c)"), in_=wupbf[u]),
                  r=['wupbf'], w=['wur%d' % rs])
            def fn_u(e, rs=rs, PSU=PSU):
                ins = None
                for j in range(2):
                    for k in range(8):
                        ins = e.matmul(PSU[:, j * 256:j * 256 + N], wur[rs][:, k, j * 128:(j + 1) * 128], hT[:, k, :N], start=(k == 0), stop=(k == 7))
                return ins
            T.op('pe', fn_u, r=['wur%d' % rs, htk], w=[pk], c=16 * (0.09 + max(N, 64) / 2400.0) + 0.1)
            pu = PSU[:, :].rearrange("p (j t) -> p j t", j=2)
            cv4 = cvbuf[:].rearrange("p (j u) t -> p j u t", j=2)
            T.op('act', lambda e, ustv=ustv, pu=pu: e.copy(ustv[:, :, 2:2 + N], pu[:, :, :N]), r=[pk], w=uk, c=0.7)
            T.op('dve', lambda e, u=u, ustv=ustv: e.tensor_copy(ustv[:, :, 0:2], cv4[:, :, u, :]), r=[cvkey], w=uk)
            if first_macro:
                T.op('dve', lambda e, u=u, ustv=ustv: e.tensor_scalar(cv4[:, :, u, :], ustv[:, :, N:N + 2], hs[:, 0:1], None, ALU.mult), r=uk + ['const'], w=[cvkey])
                continue
            T.op('dve', lambda e, u=u, ustv=ustv: e.tensor_copy(cv4[:, :, u, :], ustv[:, :, N:N + 2]), r=uk, w=[cvkey])
            for j in range(2):
                ci = j * NU + u
                T.op('act', lambda e, j=j, ci=ci, ctv=ctv, pu=pu: e.activation(ctv[:, j, :N], pu[:, j, :N], AF.Identity, bias=bcv[:, ci:ci + 1], scale=wcv[:, 2, ci:ci + 1]),
                     r=[pk, 'const'], w=[ckk], c=0.45)
                T.op('dve', lambda e, j=j, ci=ci, ctv=ctv, ustv=ustv: e.scalar_tensor_tensor(ctv[:, j, :N], ustv[:, j, 1:1 + N], wcv[:, 1, ci:ci + 1], ctv[:, j, :N], ALU.mult, ALU.add),
                     r=uk + [ckk, 'const'], w=[ckk], c=0.45)
                T.op('dve', lambda e, j=j, ci=ci, ctv=ctv, ustv=ustv: e.scalar_tensor_tensor(ctv[:, j, :N], ustv[:, j, 0:N], wcv[:, 0, ci:ci + 1], ctv[:, j, :N], ALU.mult, ALU.add),
                     r=uk + [ckk, 'const'], w=[ckk], c=0.45)
            T.op('act', lambda e, ctv=ctv: e.activation(ctv[:, 1, :N], ctv[:, 1, :N], AF.Silu), r=[ckk], w=[ckk], c=0.45)
            T.op('pool', lambda e, u=u, ctv=ctv: e.tensor_tensor(actT[:, u, :N], ctv[:, 1, :N], ctv[:, 0, :N], ALU.mult), r=[ckk], w=['actT'], c=0.7)
        if first_macro:
            return
        accs = [(P6, 'P6'), (P7, 'P7')]
        NCH = NU // 2
        for si_, (col0, Tn, htile) in enumerate(segs):
            pass
        for n in range(2):
            for c in range(NCH):
                ds = dring_ctr[0] % 5
                dring_ctr[0] += 1
                T.dma('sp', 'wdr%d' % ds, lambda e, ds=ds, c=c, n=n: e.dma_start(
                    out=wdr[ds][:], in_=wdnbf[c].rearrange("p (k n) -> p k n", k=2)[:, :, n * 512:(n + 1) * 512]),
                    r=['wdnbf'], w=['wdr%d' % ds])
                def fn_d(e, ds=ds, c=c):
                    ins = None
                    for si_, (col0, Tn, htile) in enumerate(segs):
                        acc = accs[si_][0]
                        for kk in range(2):
                            kc = 2 * c + kk
                            ins = e.matmul(acc[:Tn, :], actT[:, kc, col0:col0 + Tn], wdr[ds][:, kk, :],
                                           start=(kc == 0), stop=(kc == NU - 1))
                    return ins
                T.op('pe', fn_d, r=['actT', 'wdr%d' % ds], w=[accs[i][1] for i in range(len(segs))], c=len(segs) * 2 * 0.31 + 0.1)
            for si_, (col0, Tn, htile) in enumerate(segs):
                acc, akey = accs[si_]
                s2 = s2b[si_]
                wk_ = [S2K[0][n]] if si_ == 0 else MIXK
                T.op('dve', lambda e, Tn=Tn, htile=htile, acc=acc, s2=s2, n=n: e.scalar_tensor_tensor(
                    s2[:Tn, n * 512:(n + 1) * 512], hres[:Tn, htile, n * 512:(n + 1) * 512], ALPHA, acc[:Tn, :], ALU.mult, ALU.add),
                    r=['hres%d_%d' % (par, htile), akey], w=wk_, c=0.7)
        for si_, (col0, Tn, htile) in enumerate(segs):
            s2 = s2b[si_]
            layer_norm(s2, S2K[si_], Tn, 1, s2, S2K[si_][-1] if si_ == 0 else 'mix')
            out_fn(Tn, si_)

    def load_cs(slot, idx):
        T.dma('sp', 'cs%d' % slot, lambda e: e.dma_start(out=cosb[slot][:], in_=cos_d[idx]), w=['cs%d' % slot])
        T.dma('sp', 'cs%d' % slot, lambda e: e.dma_start(out=sinb[slot][:], in_=sin_d[idx]), w=['cs%d' % slot])

    def do_main():
        n_macro = nt_full // 2
        xbase = 8192 - NT_FULL * 128
        yrow = [0]
        for mi in range(n_macro):
            for tt in range(2):
                i = 2 * mi + tt
                slot = i % 2
                if i == 0:
                    load_x(0, xall[xbase: xbase + 128, :], 128)
                    load_cs(0, 0)
                if i + 1 < nt_full:
                    load_x((i + 1) % 2, xall[xbase + (i + 1) * 128: xbase + (i + 2) * 128, :], 128)
                    load_cs((i + 1) % 2, i + 1)
                T.cur_tag = 's1_%02d' % mi
                kslot = i % 2
                pslot = (i + 1) % 2
                pbias = hs[:, 1:2] if i == 2 else (NEG if i == 0 else 0.0)
                last = (i == nt_full - 1)
                stage1(128, slot, 0, kslot, pslot, pbias, slot, tt * 128, tt, last, False, par=mi % 2)
                if i == 1:
                    T.op('dve', lambda e: e.tensor_scalar(mst[0][:], mst[0][:], hs[0:4, 0:1], None, ALU.mult), r=['m0', 'const'], w=['m0'])
                if last:
                    out_toks.append(T.dma('sp', 'okvp', lambda e: e.dma_start(out=kp_d[:, :], in_=kvout[:, 0, :]), r=['kvout']))
                    out_toks.append(T.dma('sp', 'okvp', lambda e: e.dma_start(out=vp_d[:, :], in_=kvout[:, 1, :]), r=['kvout']))

            def out_y(Tn, yslot):
                r0 = yrow[0]
                yrow[0] += Tn
                out_toks.append(T.dma('sp', 'oy%d' % yslot, lambda e: e.dma_start(out=y_d[r0:r0 + Tn, :], in_=s2b[yslot][:Tn, :]), r=S2K[yslot]))
            T.cur_tag = 's2_%02d' % mi
            stage2(256, [(0, 128, 0), (128, 128, 1)], cvh, 'cvh', mi == 0, out_y, par=mi % 2)

        out_toks.append(T.dma('sp', 'ostC', lambda e: e.dma_start(out=Cp_d.rearrange("h d e -> d h e"), in_=Cst[0][:, :, 0:128]), r=['C0']))
        out_toks.append(T.dma('sp', 'ostC', lambda e: e.dma_start(out=np_d.rearrange("h d -> d h"), in_=Cst[0][:, :, 128], allow_slow_non_contiguous=True), r=['C0']))
        out_toks.append(T.dma('sp', 'ostm', lambda e: e.dma_start(out=mp_d[:, :], in_=mst[0][:]), r=['m0']))
        for j in range(2):
            out_toks.append(T.dma('sp', 'ostcv', lambda e, j=j: e.dma_start(out=cvp_d[j, :].rearrange("(c p) -> p c", p=128), in_=cvh[:, :, j],
                                                                           allow_slow_non_contiguous=True), r=['cvh']))

    def do_sample():
        T.cur_tag = 'sample'
        load_x(0, xs_d[:, :], 16)
        load_cs(0, NT_FULL)
        T.dma('sp', 'xin1', lambda e: e.dma_start(out=xin[1][:, 0:128], in_=ck_d[:, :]), w=['xin1'])
        T.dma('sp', 'xin1', lambda e: e.dma_start(out=xin[1][:, 128:256], in_=cv_d[:, :]), w=['xin1'])
        T.op('pe', lambda e: e.transpose(P4[:, 0:128], xin[1][:, 0:128], ident[:, :]), r=['xin1', 'const'], w=['P4'])
        T.op('act', lambda e: e.copy(kTb[1][:, :], P4[:, 0:128]), r=['P4'], w=['kT_1'])
        T.op('act', lambda e: e.copy(va1[1][:, :, 0:64], xin[1][:, 128:256].rearrange("p (k d) -> p k d", k=2)), r=['xin1'], w=['va1_1'])
        T.op('dve', lambda e: e.memset(Cst[0][:], 0.0), w=['C0'])
        T.dma('sp', 'sstC', lambda e: e.dma_start(out=Cst[0][:, :, 0:128], in_=C0_d.rearrange("h d e -> d h e")), w=['C0'])
        T.dma('sp', 'sstC', lambda e: e.dma_start(
            out=Cst[0][:, :, 128], in_=n0_d.rearrange("h d -> d h"), allow_slow_non_contiguous=True), w=['C0'])
        T.dma('sp', 'sstm', lambda e: e.dma_start(out=mst[0][:], in_=m0_d[:, :]), w=['m0'])
        T.op('act', lambda e: e.copy(Cb[0][:], Cst[0][:]), r=['C0'], w=['Cb0'])
        stage1(16, 0, 0, 0, 1, 0.0, 0, 0, 0, True, True)
        out_toks.append(T.dma('sp', 'okvs', lambda e: e.dma_start(out=ks_d[:, :], in_=kvout[:16, 0, :]), r=['kvout']))
        out_toks.append(T.dma('sp', 'okvs', lambda e: e.dma_start(out=vs_d[:, :], in_=kvout[:16, 1, :]), r=['kvout']))

        def out_ys(Tn, yslot):
            out_toks.append(T.dma('sp', 'oy%d' % yslot, lambda e: e.dma_start(out=ys_d[:, :], in_=s2b[yslot][:Tn, :]), r=S2K[yslot]))
        stage2(16, [(0, 16, 0)], cvsb, 'cvsb', False, out_ys, par=0)
        out_toks.append(T.dma('sp', 'ossC', lambda e: e.dma_start(out=Cs_d.rearrange("h d e -> d h e"), in_=Cst[0][:, :, 0:128]), r=['C0']))
        out_toks.append(T.dma('sp', 'ossC', lambda e: e.dma_start(out=ns_d.rearrange("h d -> d h"), in_=Cst[0][:, :, 128], allow_slow_non_contiguous=True), r=['C0']))
        out_toks.append(T.dma('sp', 'ossm', lambda e: e.dma_start(out=ms_d[:, :], in_=mst[0][:]), r=['m0']))
        for j in range(2):
            out_toks.append(T.dma('sp', 'osscv', lambda e, j=j: e.dma_start(out=cvs_d[j, :].rearrange("(c p) -> p c", p=128), in_=cvsb[:, :, j],
                                                                           allow_slow_non_contiguous=True), r=['cvsb']))
    do_sample()
    T.cur_tag = 'setup2'
    T.op('dve', lambda e: e.memset(Cst[0][:], 0.0), w=['C0'])
    T.op('dve', lambda e: e.memset(Cb[0][:], 0.0), w=['Cb0'])
    T.op('dve', lambda e: e.memset(mst[0][:], 0.0), w=['m0'])
    do_prefix()
    do_main()
    T.wait_all('sp', out_toks)
    if FILL:
        T.filler_fn = lambda e: e.matmul(P5[:, 0:512], identb[:, :], win[:, 0, 0:512], start=True, stop=True)
        T.fill_frac = float(os.environ.get('K_FILL_FRAC', '0.5'))
        T.filler_cost = float(os.environ.get('K_FILL_COST', '0.3'))
        T.fill_min = float(os.environ.get('K_FILL_MIN', '1.5'))
    T.schedule()

    with nc.Block() as block:
        @block.sync
        def _(e):
            T.replay('sp', e)

        @block.tensor
        def _(e):
            T.replay('pe', e)

        @block.scalar
        def _(e):
            T.replay('act', e)

        @block.vector
        def _(e):
            T.replay('dve', e)

        @block.gpsimd
        def _(e):
            T.replay('pool', e)
    es.close()
    return nc


def _host_consts():
    ident = np.eye(128, dtype=np.float32)
    s = np.arange(128)[:, None]
    l = np.arange(128)[None, :]
    maskT = np.where(l >= s, 0.0, NEG).astype(np.float32)
    amask = np.ones((128, 2, 128), np.float32)
    amask[:, 0, :] = np.where((s < 64) & (l >= 64), 0.0, 1.0)
    amask[:, 1, :] = np.where((s >= 64) & (l < 64), 0.0, 1.0)
    sel4 = np.zeros((4, 4, 128), np.float32)
    for h in range(4):
        sel4[h, h, :] = 1.0
    return ident, maskT, amask, sel4.reshape(4, 512)


def _rope_tables(pos):
    half = 32
    inv = (np.float32(10000.0) ** (-np.arange(half, dtype=np.float32) / np.float32(half))).astype(np.float32)
    d = np.arange(128) % 64
    f = inv[d % 32]
    ang = (pos.astype(np.float32)[None, :] * f[:, None]).astype(np.float32)
    c = np.cos(ang).astype(np.float32)
    sn = np.sin(ang).astype(np.float32)
    sign = np.where(d < 32, -1.0, 1.0).astype(np.float32)[:, None]
    return c, (sn * sign).astype(np.float32)


_NC_CACHE = {}


def kernel(x_prompt, x_sample, state_mlstm_C, state_mlstm_n, state_mlstm_m, cache_swa_k, cache_swa_v,
           state_conv, w_in, b_igate, b_fgate, g_mlstm_norm, attn_sinks, w_out, ln1_g, ln1_b,
           w_up, w_conv, b_conv, w_down, ln2_g, ln2_b):
    f = lambda a: np.ascontiguousarray(np.asarray(a, dtype=np.float32))
    x_prompt, x_sample = f(x_prompt), f(x_sample)
    w_in0 = f(w_in)[0]
    rot = (np.arange(64) + 32) % 64
    qa0, ka0, va0 = 2056, 2568, 2696
    qap, qar = [], []
    for j in range(4):
        for hd in (j, 4 + j):
            qap.extend(qa0 + hd * 64 + np.arange(64))
            qar.extend(qa0 + hd * 64 + rot)
    kap = list(ka0 + np.arange(128))
    kar = list(ka0 + np.concatenate([rot, 64 + rot]))
    cols = list(range(0, 2056)) + list(range(va0, va0 + 128)) + qap + qar + kap + kar
    w_in_aug = np.ascontiguousarray(w_in0[:, np.array(cols, dtype=np.int64)])
    assert w_in_aug.shape[1] == WCOLS
    ident, maskT, amask, sel4 = _host_consts()
    ln = np.stack([f(ln1_g)[0], f(ln1_b)[0], f(ln2_g)[0], f(ln2_b)[0]], 0)

    nt_pre = int(os.environ.get("K_NT_PRE", NT_PRE))
    nt_full = int(os.environ.get("K_NT_FULL", NT_FULL))
    key = (nt_pre, nt_full)
    if key not in _NC_CACHE:
        _NC_CACHE[key] = build_program(nt_pre, nt_full)
    nc = _NC_CACHE[key]

    in_maps = []
    for c in range(8):
        b, half = c // 2, c % 2
        if half == 1:
            xall = x_prompt[b]
        else:
            xall = np.concatenate([np.zeros((4096, D), np.float32), x_prompt[b, :4096]], 0)
        pos0 = half * 4096 - 256
        cosT = np.zeros((NT_FULL + 1, 128, 128), np.float32)
        sinT = np.zeros((NT_FULL + 1, 128, 128), np.float32)
        for i in range(NT_FULL):
            cc, ss = _rope_tables(pos0 + i * 128 + np.arange(128))
            cosT[i], sinT[i] = cc, ss
        cc, ss = _rope_tables(2048 + np.arange(16))
        cosT[NT_FULL, :, :16], sinT[NT_FULL, :, :16] = cc, ss
        hsv = np.zeros((128, 2), np.float32)
        hsv[:, 0] = float(half)
        hsv[:, 1] = 0.0 if half == 1 else NEG
        in_maps.append({
            "xall": np.ascontiguousarray(xall), "xs": x_sample[c],
            "C0": f(state_mlstm_C)[0, c], "n0": f(state_mlstm_n)[0, c], "m0": f(state_mlstm_m)[0, c].reshape(4, 1),
            "ck": f(cache_swa_k)[0, c].reshape(128, 128), "cv": f(cache_swa_v)[0, c].reshape(128, 128),
            "sconv": f(state_conv)[0, c],
            "w_in": w_in_aug, "w_out": f(w_out)[0], "w_up": f(w_up)[0], "w_down": f(w_down)[0],
            "w_conv": f(w_conv)[0], "b_conv": f(b_conv)[0].reshape(1, -1),
            "b_i": f(b_igate)[0].reshape(4, 1), "b_f": f(b_fgate)[0].reshape(4, 1),
            "g_norm": f(g_mlstm_norm)[0].reshape(1, 512), "sinks": f(attn_sinks)[0].reshape(1, 8),
            "ln": ln, "cosT": cosT, "sinT": sinT, "ident": ident, "maskT": maskT, "amask": amask,
            "sel4": sel4, "hs": hsv,
        })
    res = run_bass_kernel_spmd(nc, in_maps, core_ids=list(range(8)))
    R = res.results
    y_p = np.zeros((4, 8192, D), np.float32)
    for c in range(8):
        y_p[c // 2, (c % 2) * 4096:(c % 2 + 1) * 4096] = R[c]["y"]
    y_s = np.stack([R[c]["ys"] for c in range(8)], 0)
    odd = [1, 3, 5, 7]
    C_p = np.stack([R[c]["Cp"] for c in odd], 0)[None]
    n_p = np.stack([R[c]["np"] for c in odd], 0)[None]
    m_p = np.stack([R[c]["mp"].reshape(4) for c in odd], 0)[None]
    k_p = np.stack([R[c]["kp"].reshape(128, 2, 64) for c in odd], 0)[None]
    v_p = np.stack([R[c]["vp"].reshape(128, 2, 64) for c in odd], 0)[None]
    cv_p = np.stack([R[c]["cvp"] for c in odd], 0)[None]
    C_s = np.stack([R[c]["Cs"] for c in range(8)], 0)[None]
    n_s = np.stack([R[c]["ns"] for c in range(8)], 0)[None]
    m_s = np.stack([R[c]["ms"].reshape(4) for c in range(8)], 0)[None]
    k_s = np.stack([R[c]["ks"].reshape(16, 2, 64) for c in range(8)], 0)[None]
    v_s = np.stack([R[c]["vs"].reshape(16, 2, 64) for c in range(8)], 0)[None]
    cv_s = np.stack([R[c]["cvs"] for c in range(8)], 0)[None]
    return (y_p, y_s, C_p, n_p, m_p, k_p, v_p, cv_p, C_s, n_s, m_s, k_s, v_s, cv_s)


LINECOST = {
    ('act', 513): 0.586,
    ('act', 544): 0.311,
    ('act', 545): 0.308,
    ('act', 554): 0.4,
    ('act', 558): 0.292,
    ('act', 562): 0.2,
    ('act', 618): 0.687,
    ('act', 619): 0.586,
    ('act', 641): 0.687,
    ('act', 642): 0.586,
    ('act', 644): 0.597,
    ('act', 645): 0.629,
    ('act', 646): 0.63,
    ('act', 648): 0.155,
    ('act', 663): 0.401,
    ('act', 667): 0.586,
    ('act', 674): 0.488,
    ('act', 698): 0.46,
    ('act', 705): 0.625,
    ('act', 717): 0.095,
    ('act', 718): 0.205,
    ('act', 743): 0.248,
    ('act', 762): 0.499,
    ('act', 802): 1.012,
    ('act', 816): 0.586,
    ('act', 826): 0.203,
    ('act', 827): 0.203,
    ('act', 865): 0.586,
    ('act', 873): 0.467,
    ('act', 879): 0.416,
    ('dve', 546): 0.214,
    ('dve', 548): 0.412,
    ('dve', 550): 0.339,
    ('dve', 552): 0.472,
    ('dve', 556): 0.143,
    ('dve', 560): 0.199,
    ('dve', 563): 0.216,
    ('dve', 565): 0.217,
    ('dve', 567): 0.055,
    ('dve', 579): 0.175,
    ('dve', 581): 0.163,
    ('dve', 582): 0.068,
    ('dve', 588): 0.692,
    ('dve', 598): 0.349,
    ('dve', 659): 0.692,
    ('dve', 682): 0.601,
    ('dve', 700): 0.692,
    ('dve', 707): 0.17,
    ('dve', 708): 0.165,
    ('dve', 709): 0.164,
    ('dve', 710): 0.085,
    ('dve', 711): 0.693,
    ('dve', 714): 0.294,
    ('dve', 716): 0.181,
    ('dve', 720): 0.346,
    ('dve', 722): 0.692,
    ('dve', 724): 0.678,
    ('dve', 733): 0.691,
    ('dve', 735): 0.592,
    ('dve', 737): 0.626,
    ('dve', 740): 0.221,
    ('dve', 741): 0.175,
    ('dve', 742): 0.292,
    ('dve', 788): 0.162,
    ('dve', 790): 0.171,
    ('dve', 791): 0.425,
    ('dve', 811): 1.221,
    ('dve', 824): 0.694,
    ('dve', 825): 0.209,
    ('dve', 828): 1.284,
    ('dve', 830): 1.285,
    ('dve', 866): 0.17,
    ('dve', 868): 0.225,
    ('dve', 870): 0.17,
    ('dve', 875): 0.396,
    ('dve', 877): 0.484,
    ('dve', 910): 0.675,
    ('pool', 447): 0.639,
    ('pool', 476): 1.067,
    ('pool', 480): 0.763,
    ('pool', 880): 0.729,
}
```

```python
import os
from contextlib import ExitStack
import numpy as np
import concourse.bass as bass
import concourse.mybir as mybir
from concourse.bass_utils import run_bass_kernel_spmd

F32 = mybir.dt.float32
BF16 = mybir.dt.bfloat16
AF = mybir.ActivationFunctionType
ALU = mybir.AluOpType
AX = mybir.AxisListType

D = 1024
NT_PRE = 30
NT_FULL = 34
DFF = 2816
NU = 22
WCOLS = 3464
C_QM, C_KM, C_VM, C_OM, C_IP, C_FP, C_VA, C_QAP, C_QAR, C_KAP, C_KAR = 0, 512, 1024, 1536, 2048, 2052, 2056, 2184, 2696, 3208, 3336
ALPHA = float(2.0 ** 0.25)
EPS = 1e-5
KSCALE = float(128.0 ** -0.5)
NEG = -30000.0


SKIP_WAR = int(os.environ.get('K_SKIP_WAR', '1'))


class Tracker:
    HOP = 0.4
    DEF_COST = {'pe': 0.7, 'act': 0.45, 'dve': 0.3, 'pool': 0.8, 'sp': 0.08}

    def __init__(self, nc, es):
        self.nc = nc
        self.es = es
        self.engs = {}
        self.last_w = {}
        self.readers = {}
        self.streams = {}
        self.ops = []
        self.thr_hist = {}
        self.cur_tag = 'setup'
        self.filler_fn = None
        self.pe_scale = float(os.environ.get('K_PE_SCALE', '0.65'))
        self.ad_scale = float(os.environ.get('K_AD_SCALE', '1.1'))
        self.HOP = float(os.environ.get('K_HOP', '0.9'))
        self.hop_same = float(os.environ.get('K_HOP_SAME', '0.3'))
        self.filler_cost = 0.43
        self.fill_min = 1.5
        self.fill_frac = 0.6

    def add_engine(self, name, eng):
        sem = self.es.enter_context(self.nc.semaphore("s_" + name))
        self.engs[name] = dict(eng=eng, sem=sem, name=name, order=[])

    def stream(self, name, group=False):
        if name not in self.streams:
            sem = self.es.enter_context(self.nc.semaphore("d_" + name))
            self.streams[name] = dict(sem=sem, count=0, group=group, name=name)
        return self.streams[name]

    def _deps(self, reads, writes, extra, group):
        deps = set()
        for k in reads:
            deps.update(self.last_w.get(k, ()))
        self._raw = set(deps)
        if not group:
            for k in writes:
                deps.update(self.last_w.get(k, ()))
                deps.update(self.readers.get(k, ()))
        deps.update(extra)
        return deps

    def _commit(self, oid, reads, writes, group):
        for k in reads:
            self.readers.setdefault(k, []).append(oid)
        for k in writes:
            if group:
                self.last_w.setdefault(k, []).append(oid)
            else:
                self.last_w[k] = [oid]
                self.readers[k] = []

    def op(self, ename, fn, r=(), w=(), extra=(), c=None):
        deps = self._deps(r, w, extra, False)
        oid = len(self.ops)
        cc = self.DEF_COST[ename] if c is None else c
        if ename == 'pe':
            cc *= self.pe_scale
        elif ename in ('act', 'dve', 'pool'):
            cc *= self.ad_scale
        self.ops.append(dict(id=oid, eng=ename, fn=fn, deps=deps, dma=None, tag=self.cur_tag, cost=cc, raw=self._raw | set(extra)))
        self._commit(oid, r, w, False)
        return oid

    def dma(self, qname, stream, fn, r=(), w=(), extra=(), group=False, c=4.0):
        S = self.stream(stream, group)
        deps = self._deps(r, w, extra, group)
        oid = len(self.ops)
        thr = ()
        if group:
            hist = self.thr_hist.setdefault(qname, [])
            B = 3 if qname == 'pool' else 4
            nb = len(hist) // B
            if nb > 0:
                thr = tuple(hist[(nb - 1) * B:nb * B])
                deps.update(thr)
            hist.append(oid)
        self.ops.append(dict(id=oid, eng=qname, fn=fn, deps=deps, dma=S, cost=c, thr=thr, tag=self.cur_tag))
        self._commit(oid, r, w, group)
        return oid

    def wait_all(self, ename, toks):
        oid = len(self.ops)
        self.ops.append(dict(id=oid, eng=ename, fn=None, deps=set(toks), dma=None, cost=0.05, tag='final'))
        return oid

    def schedule(self):
        import heapq
        ops = self.ops
        n = len(ops)
        ndeps = [len(o['deps']) for o in ops]
        users = [[] for _ in range(n)]
        for o in ops:
            for d in o['deps']:
                users[d].append(o['id'])
        ready_t = [0.0] * n
        fin = [0.0] * n
        bl = [0.0] * n
        for o in reversed(ops):
            i = o['id']
            m = 0.0
            for u in users[i]:
                if bl[u] > m:
                    m = bl[u]
            bl[i] = m + o['cost'] + self.HOP
        mode = os.environ.get('K_PRIO', 'id')
        if mode == 'bl':
            prio = [(-bl[i], i) for i in range(n)]
        else:
            prio = [(i, i) for i in range(n)]
        ready = {e: [] for e in self.engs}
        free_t = {e: 0.0 for e in self.engs}
        events = []
        for o in ops:
            if ndeps[o['id']] == 0:
                heapq.heappush(ready[o['eng']], (prio[o['id']], o['id']))
        for e in self.engs:
            heapq.heappush(events, (0.0, 0, e))
        seq = 1
        done = 0
        pending_wake = {e: True for e in self.engs}
        while done < n:
            if not events:
                raise RuntimeError("scheduler stuck")
            t, _, e = heapq.heappop(events)
            pending_wake[e] = False
            if free_t[e] > t + 1e-9:
                heapq.heappush(events, (free_t[e], seq, e)); seq += 1; pending_wake[e] = True
                continue
            cand = None
            tmp = []
            while ready[e]:
                item = heapq.heappop(ready[e])
                oid = item[1]
                if ready_t[oid] <= t + 1e-9:
                    cand = oid
                    break
                tmp.append(item)
            for x in tmp:
                heapq.heappush(ready[e], x)
            if cand is None:
                if ready[e]:
                    tn = min(ready_t[x[1]] for x in ready[e])
                    heapq.heappush(events, (tn, seq, e)); seq += 1; pending_wake[e] = True
                continue
            o = ops[cand]
            o['t0'] = t
            self.engs[e]['order'].append(cand)
            if o['dma'] is not None:
                free_t[e] = t + (0.6 if e == 'pool' else 0.08)
                fin[cand] = t + o['cost']
            else:
                free_t[e] = t + o['cost']
                fin[cand] = free_t[e]
            done += 1
            for u in users[cand]:
                ndeps[u] -= 1
                if ops[u]['eng'] == e and o['dma'] is None:
                    if e == 'pe' or (SKIP_WAR and e in ('act', 'dve') and cand not in ops[u].get('raw', ops[u]['deps'])):
                        rt = fin[cand]
                    else:
                        rt = fin[cand] + self.hop_same
                else:
                    rt = fin[cand] + self.HOP
                if rt > ready_t[u]:
                    ready_t[u] = rt
                if ndeps[u] == 0:
                    ue = ops[u]['eng']
                    heapq.heappush(ready[ue], (prio[u], u))
                    if not pending_wake[ue]:
                        heapq.heappush(events, (max(ready_t[u], free_t[ue]), seq, ue)); seq += 1; pending_wake[ue] = True
            heapq.heappush(events, (free_t[e], seq, e)); seq += 1; pending_wake[e] = True
        self.sim_time = max(fin)
        self.fin = fin
        if self.filler_fn is not None:
            fc = self.filler_cost
            new_order = []
            prev_end = None
            nfill = 0
            armed = False
            for oid in self.engs['pe']['order']:
                o = ops[oid]
                if not armed:
                    if any(ops[d]['dma'] is not None and ops[d]['dma']['name'] == 'win' for d in o['deps']):
                        armed = True
                    new_order.append(oid)
                    prev_end = fin[oid]
                    continue
                if prev_end is not None:
                    gap = o['t0'] - prev_end
                    if gap > self.fill_min:
                        k = int((gap * self.fill_frac) / fc)
                        for _ in range(k):
                            fid = len(ops)
                            ops.append(dict(id=fid, eng='pe', fn=self.filler_fn, deps=set(), dma=None, cost=fc, tag='fill', filler=True))
                            new_order.append(fid)
                            nfill += 1
                new_order.append(oid)
                prev_end = fin[oid]
            self.engs['pe']['order'] = new_order
            self.nfill = nfill
        for e, E in self.engs.items():
            pos = 0
            for oid in E['order']:
                o = ops[oid]
                if o['dma'] is not None:
                    o['dma']['count'] += 16
                    o['val'] = o['dma']['count']
                elif o['fn'] is not None and not o.get('filler'):
                    pos += 1
                    o['val'] = pos

    def replay(self, ename, eng):
        E = self.engs[ename]
        ops = self.ops
        seen = {}
        for oid in E['order']:
            o = ops[oid]
            need = {}
            for d in o['deps']:
                D = ops[d]
                if D['dma'] is not None:
                    S = D['dma']
                    if d in o.get('thr', ()):
                        sem, val = S['sem'], D['val']
                    else:
                        sem, val = S['sem'], (S['count'] if S['group'] else D['val'])
                else:
                    if D['fn'] is None:
                        continue
                    if D['eng'] == 'pe' and ename == 'pe':
                        continue
                    if SKIP_WAR and D['eng'] == ename and ename in ('act', 'dve') and d not in o.get('raw', o['deps']):
                        continue
                    sem, val = self.engs[D['eng']]['sem'], D['val']
                if need.get(sem, 0) < val:
                    need[sem] = val
            for sem, val in need.items():
                if seen.get(sem, 0) < val:
                    eng.wait_ge(sem, val)
                    seen[sem] = val
            if o['fn'] is None:
                continue
            ins = o['fn'](eng)
            if o.get('filler'):
                continue
            if o['dma'] is not None:
                ins.then_inc(o['dma']['sem'], 16)
            else:
                ins.then_inc(E['sem'], 1)


def build_program(nt_pre=NT_PRE, nt_full=NT_FULL):
    nc = bass.Bass("TRN2", target_bir_lowering=False)
    es = ExitStack()

    def din(name, shape, dt=F32):
        return nc.dram_tensor(name, list(shape), dt, kind="ExternalInput").ap()

    def dout(name, shape, dt=F32):
        return nc.dram_tensor(name, list(shape), dt, kind="ExternalOutput").ap()

    xall = din("xall", [8192, D])
    xs_d = din("xs", [16, D])
    C0_d = din("C0", [4, 128, 128])
    n0_d = din("n0", [4, 128])
    m0_d = din("m0", [4, 1])
    ck_d = din("ck", [128, 128])
    cv_d = din("cv", [128, 128])
    sconv_d = din("sconv", [2, 2 * DFF])
    win_d = din("w_in", [D, WCOLS])
    wout_d = din("w_out", [D, D])
    wup_d = din("w_up", [D, 2 * DFF])
    wdn_d = din("w_down", [DFF, D])
    wconv_d = din("w_conv", [3, 2 * DFF])
    bconv_d = din("b_conv", [1, 2 * DFF])
    bi_d = din("b_i", [4, 1])
    bf_d = din("b_f", [4, 1])
    gn_d = din("g_norm", [1, 512])
    sink_d = din("sinks", [1, 8])
    ln_d = din("ln", [4, D])
    cos_d = din("cosT", [NT_FULL + 1, 128, 128])
    sin_d = din("sinT", [NT_FULL + 1, 128, 128])
    ident_d = din("ident", [128, 128])
    maskT_d = din("maskT", [128, 128])
    amask_d = din("amask", [128, 2, 128])
    sel4_d = din("sel4", [4, 4 * 128])
    hs_d = din("hs", [128, 2])

    y_d = dout("y", [4096, D])
    ys_d = dout("ys", [16, D])
    Cp_d = dout("Cp", [4, 128, 128])
    np_d = dout("np", [4, 128])
    mp_d = dout("mp", [4, 1])
    kp_d = dout("kp", [128, 128])
    vp_d = dout("vp", [128, 128])
    cvp_d = dout("cvp", [2, 2 * DFF])
    Cs_d = dout("Cs", [4, 128, 128])
    ns_d = dout("ns", [4, 128])
    ms_d = dout("ms", [4, 1])
    ks_d = dout("ks", [16, 128])
    vs_d = dout("vs", [16, 128])
    cvs_d = dout("cvs", [2, 2 * DFF])
    wupbf = nc.dram_tensor("wupbf", [NU, 128, 8 * 256], BF16, kind="Internal").ap()
    wdnbf = nc.dram_tensor("wdnbf", [NU // 2, 128, 2 * D], BF16, kind="Internal").ap()

    def sb(name, shape, dt=F32):
        return es.enter_context(nc.sbuf_tensor(name, list(shape), dt))

    def pt(name, shape, dt=F32):
        return es.enter_context(nc.psum_tensor(name, list(shape), dt))

    win = sb("win", [128, 8, WCOLS], BF16)
    wout = sb("wout", [128, 8, D], BF16)
    wdr = [sb("wdr%d" % i, [128, 2, 512], BF16) for i in range(5)]
    NS = 2
    ustS = [sb("ustS%d" % i, [128, 2, 258]) for i in range(NS)]
    ctS = [sb("ctS%d" % i, [128, 2, 256]) for i in range(NS)]
    s2b0 = sb("s2b0", [128, D])
    wur = [sb("wur%d" % i, [128, 8, 256], BF16) for i in range(2)]
    lnp = sb("lnp", [128, 4, D])
    gnb = sb("gnb", [128, 512])
    esink = sb("esink", [128, 8])
    ident = sb("ident_s", [128, 128])
    maskT = sb("maskT_s", [128, 128])
    amask = sb("amask_s", [128, 2, 128], BF16)
    sel4 = sb("sel4_s", [4, 4 * 128])
    hs = sb("hs_s", [128, 2])
    bi = sb("bi_s", [4, 1])
    nbf = sb("nbf_s", [4, 1])
    wcv = sb("wcv", [128, 3, 2 * NU])
    bcv = sb("bcv", [128, 2 * NU])
    cvh = sb("cvh", [128, 2 * NU, 2])
    cvsb = sb("cvsb", [128, 2 * NU, 2])
    ones4 = sb("ones4", [4, 128])
    onesr = ones4

    xin = [sb("xin%d" % i, [128, D]) for i in range(2)]
    xT = sb("xT", [128, 8, 128], BF16)
    mixT = xT
    kmtok = sb("kmtok", [128, 4, 128], BF16)
    kw = kmtok
    v1 = sb("v1", [128, 4, 130], BF16)
    qmT = sb("qmT", [128, 4, 128], BF16)
    qaT = sb("qaT", [128, 4, 128], BF16)
    kmT = sb("kmT", [128, 4, 128], BF16)
    cosb = [sb("cos%d" % i, [128, 128]) for i in range(2)]
    sinb = [sb("sin%d" % i, [128, 128]) for i in range(2)]
    nd1 = sb("nd1", [128, 4, 130])
    nd = nd1
    rt1 = sb("rt1", [128, 4, 128])
    Ebuf = sb("Ebuf", [128, 4, 128])
    hmr = Ebuf
    rt2 = sb("rt2", [128, 4, 128])
    krope = sb("krope", [128, 128])
    kTb = [sb("kTb%d" % i, [128, 128], BF16) for i in range(2)]
    va1 = [sb("va1%d" % i, [128, 2, 66], BF16) for i in range(2)]
    kvout = sb("kvout", [128, 2, 128])
    ST = sb("ST", [128, 4, 128], BF16)
    Cst = [sb("Cst0", [128, 4, 130])] * 2
    Cb = [sb("Cb0", [128, 4, 130], BF16)] * 2
    mst = [sb("mst0", [4, 1])] * 2
    bst = sb("bst", [128, 4, 6])
    mv = sb("mv", [128, 4, 2])
    sm = sb("sm", [128, 32])
    pTraw = sb("pTraw", [128, 1024])
    pT = pTraw[:].bitcast(BF16).rearrange("p (b h t) -> p b h t", b=2, h=8)
    og = sb("og", [128, 512])
    mix = sb("mix", [128, D])
    mixb = sb("mixb", [128, D], BF16)
    identb = sb("identb", [128, 128], BF16)
    s1 = mix
    s2b = [s2b0, mix]
    MIXK = ['mix']
    S2K = [['s2b0a', 's2b0b', 's2b0'], MIXK]
    hresL = [sb("hres%d" % i, [128, 2, D]) for i in range(2)]
    hTL = [sb("hT%d" % i, [128, 8, 256], BF16) for i in range(2)]
    actT = sb("actT", [128, NU, 256], BF16)
    G = sb("G", [4, 7, 128])
    tm = sb("tm", [128, 16])
    dbc = sb("dbc", [128, 4])
    gsm = sb("gsm", [4, 8])

    P01 = pt("P01", [128, 1024])
    P23 = pt("P23", [128, 1024])
    P4 = pt("P4", [128, 512])
    P5 = pt("P5", [128, 512])
    P6 = pt("P6", [128, 512])
    P7 = pt("P7", [128, 512])
    PA0 = P01[:, 0:512]
    PA1 = P01[:, 512:1024]
    P0b = PA0.bitcast(BF16)
    P1b = PA1.bitcast(BF16)

    T = Tracker(nc, es)
    FILL = int(os.environ.get('K_FILL', '1'))
    T.add_engine('sp', nc.sync)
    T.add_engine('pe', nc.tensor)
    T.add_engine('act', nc.scalar)
    T.add_engine('dve', nc.vector)
    T.add_engine('pool', nc.gpsimd)

    def ld(q, stream, out, in_, w, r=()):
        return T.dma(q, stream, lambda e, o=out, i=in_: e.dma_start(out=o, in_=i), r=r, w=w, group=True, c=6.0)

    for k in range(8):
        for c0 in (0, 1732):
            ld('pool', 'win', win[:, k, c0:c0 + 1732], win_d[k * 128:(k + 1) * 128, c0:c0 + 1732], w=['win'])
    ld('sp', 'small', ident[:], ident_d[:, :], w=['const'])
    ld('sp', 'small', maskT[:], maskT_d[:, :], w=['const'])
    ld('pool', 'small2', amask[:], amask_d[:, :, :], w=['c5'])
    ld('sp', 'small', sel4[:], sel4_d[:, :], w=['const'])
    ld('sp', 'small', hs[:], hs_d[:, :], w=['const'])
    ld('sp', 'small', bi[:], bi_d[:, :], w=['const'])
    ld('sp', 'small', nbf[:], bf_d[:, :], w=['const'])
    ld('sp', 'small', gnb[:], gn_d[0, :].partition_broadcast(128), w=['const'])
    ld('sp', 'small', esink[:], sink_d[0, :].partition_broadcast(128), w=['const'])
    for j in range(4):
        ld('sp', 'small', lnp[:, j, :], ln_d[j, :].partition_broadcast(128), w=['const'])
    for j in range(3):
        T.dma('sp', 'small', lambda e, j=j: e.dma_start(
            out=wcv[:, j, :], in_=wconv_d[j, :].rearrange("(c p) -> p c", p=128), allow_slow_non_contiguous=True), w=['const'], group=True, c=20.0)
    T.dma('sp', 'small', lambda e: e.dma_start(
        out=bcv[:], in_=bconv_d[0, :].rearrange("(c p) -> p c", p=128), allow_slow_non_contiguous=True), w=['const'], group=True, c=20.0)
    for j in range(2):
        T.dma('sp', 'small', lambda e, j=j: e.dma_start(
            out=cvsb[:, :, j], in_=sconv_d[j, :].rearrange("(c p) -> p c", p=128), allow_slow_non_contiguous=True), w=['cvsb'], group=True, c=20.0)
    for k in range(8):
        ld('pool', 'wout', wout[:, k, :], wout_d[k * 128:(k + 1) * 128, :], w=['wout'])
    for u in range(NU):
        for j in range(2):
            c0 = j * DFF + u * 128
            T.dma('pool', 'wupc', lambda e, u=u, j=j, c0=c0: e.dma_start(
                out=wupbf[u].rearrange("p (k c) -> p k c", k=8)[:, :, j * 128:(j + 1) * 128],
                in_=wup_d[:, c0:c0 + 128].rearrange("(k p) c -> p k c", p=128)), w=['wupbf'], group=True, c=8.0)
    for c in range(NU // 2):
        T.dma('pool', 'wdnc', lambda e, c=c: e.dma_start(
            out=wdnbf[c].rearrange("p (k n) -> p k n", k=2),
            in_=wdn_d[c * 256:(c + 1) * 256, :].rearrange("(k p) n -> p k n", p=128)), w=['wdnbf'], group=True, c=8.0)

    T.op('dve', lambda e: e.memset(ones4[:], 1.0), w=['c2'])
    T.op('dve', lambda e: e.tensor_scalar(nbf[:], nbf[:], -1.0, None, ALU.mult), r=['const'], w=['c3'])
    T.op('act', lambda e: e.activation(esink[:], esink[:], AF.Exp), r=['const'], w=['c4'])
    T.op('dve', lambda e: e.tensor_copy(identb[:], ident[:]), r=['const'], w=['c6'])
    for i in range(2):
        T.op('dve', lambda e, i=i: e.memset(v1[:, :, 128:130], 1.0), w=['v1'])
    for i in range(2):
        T.op('dve', lambda e, i=i: e.memset(va1[i][:], 1.0), w=['va1_%d' % i])
        T.op('dve', lambda e, i=i: e.memset(kTb[i][:], 0.0), w=['kT_%d' % i])
    T.op('dve', lambda e: e.memset(cvh[:], 0.0), w=['cvh'])
    CONSTS = ['const', 'c2', 'c3', 'c4', 'c5', 'c6']

    def transposes8(src, Tn, dst_ps, rkeys):
        def fn(e):
            ins = None
            for k in range(8):
                ins = e.transpose(dst_ps[:, k * 128:k * 128 + Tn], src[:Tn, k * 128:(k + 1) * 128], ident[:Tn, :Tn])
            return ins
        T.op('pe', fn, r=list(rkeys) + CONSTS, w=['P0', 'P1'], c=2.0)

    def load_x(slot, src_ap, Tn):
        return T.dma('sp', 'xin%d' % slot, lambda e: e.dma_start(out=xin[slot][:Tn, :], in_=src_ap), w=['xin%d' % slot])

    def make_xT(slot, Tn):
        transposes8(xin[slot], Tn, P01, ['xin%d' % slot])
        for b in range(2):
            T.op('act', lambda e, b=b: e.copy(
                xT[:, 4 * b:4 * b + 4, :Tn],
                P01[:, 512 * b:512 * b + 512].rearrange("p (k t) -> p k t", k=4)[:, :, :Tn]),
                r=['P%d' % b], w=['xT'])

    def mm_A(ps, col0, ncols, Tn, wkey, bankkeys):
        def fn(e):
            ins = None
            for k in range(8):
                ins = e.matmul(ps[:Tn, :ncols], xT[:, k, :Tn], win[:, k, col0:col0 + ncols], start=(k == 0), stop=(k == 7))
            return ins
        T.op('pe', fn, r=['xT', 'win'], w=bankkeys, c=8 * (0.09 + max(ncols, 64) / 2400.0) + 0.1)

    def mm_B(ps, col0, mcols, Tn, bankkeys):
        def fn(e):
            ins = None
            for k in range(8):
                ins = e.matmul(ps[:mcols, :Tn], win[:, k, col0:col0 + mcols], xT[:, k, :Tn], start=(k == 0), stop=(k == 7))
            return ins
        T.op('pe', fn, r=['xT', 'win'], w=bankkeys, c=8 * 0.13 + 0.1)

    def gate_chain(Tn, si, full):
        m = mst[si]
        mk = 'm%d' % si
        ipT = P4[0:4, 128:128 + Tn]
        fpT = P4[0:4, 256:256 + Tn]
        R = lambda j: G[:, j, :Tn]
        T.op('act', lambda e: e.activation(R(0), fpT, AF.Exp, bias=nbf[:], scale=-1.0), r=['P4', 'c3'], w=['G0'])
        T.op('act', lambda e: e.activation(R(0), R(0), AF.Ln, bias=1.0), r=['G0'], w=['G0'])
        T.op('dve', lambda e: e.tensor_scalar(R(0), R(0), -1.0, None, ALU.mult), r=['G0'], w=['G0'])
        T.op('dve', lambda e: e.tensor_tensor_scan(R(1), onesr[:, :Tn], R(0), 0.0, ALU.mult, ALU.add), r=['G0', 'c2'], w=['G1'])
        T.op('dve', lambda e: e.scalar_tensor_tensor(R(2), ipT, bi[:], R(1), ALU.add, ALU.subtract), r=['P4', 'G1', 'const'], w=['G2'])
        T.op('dve', lambda e: e.tensor_tensor_scan(R(3), R(2), R(2), m[:], ALU.max, ALU.max), r=['G2', mk], w=['G3'])
        T.op('act', lambda e: e.activation(R(4), R(3), AF.Exp, bias=m[:], scale=-1.0), r=['G3', mk], w=['G4'])
        T.op('dve', lambda e: e.tensor_scalar(gsm[:, 0:1], G[:, 3, Tn - 1:Tn], -1.0, None, ALU.mult), r=['G3'], w=['gsm0'])
        T.op('act', lambda e: e.activation(R(5), R(2), AF.Exp, bias=gsm[:, 0:1]), r=['G2', 'gsm0'], w=['G5'])
        T.op('dve', lambda e: e.tensor_tensor(R(6), R(1), R(3), ALU.add), r=['G1', 'G3'], w=['G6'])
        if full:
            T.op('act', lambda e: e.activation(R(0), R(6), AF.Exp, scale=-1.0), r=['G6'], w=['G0'])
            T.op('dve', lambda e: e.tensor_scalar(R(1), R(3), -1.0, None, ALU.mult), r=['G3'], w=['G1'])
        T.op('dve', lambda e: e.tensor_scalar(gsm[:, 4:8], ident[0:4, 0:4], G[:, 4, Tn - 1:Tn], None, ALU.mult), r=['G4', 'const'], w=['gsm1'])
        T.op('dve', lambda e: e.tensor_copy(m[:], G[:, 6, Tn - 1:Tn]), r=['G6'], w=[mk])

        def fn(e):
            ins = None
            rows = [(2, 0), (4, 4), (0, 8), (5, 12)] if full else [(5, 12)]
            for (rj, c) in rows:
                ins = e.matmul(P4[:Tn, 384 + c:384 + c + 4], G[:, rj, :Tn], ident[0:4, 0:4], start=True, stop=True)
            ins = e.matmul(P4[:, 448:452], ones4[:, :], gsm[:, 4:8], start=True, stop=True)
            return ins
        T.op('pe', fn, r=['G2', 'G4', 'G0', 'G5', 'gsm1', 'c2', 'const'], w=['P4'])
        if full:
            T.op('dve', lambda e: e.tensor_copy(tm[:Tn, :], P4[:Tn, 384:400]), r=['P4'], w=['tm'])
        else:
            T.op('dve', lambda e: e.tensor_copy(tm[:Tn, 12:16], P4[:Tn, 396:400]), r=['P4'], w=['tm'])
        T.op('dve', lambda e: e.tensor_copy(dbc[:], P4[:, 448:452]), r=['P4'], w=['dbc'])

    def state_update(Tn, si):
        C = Cst[si]
        ck = 'C%d' % si
        T.op('dve', lambda e: e.tensor_tensor(kw[:Tn], kmtok[:Tn], tm[:Tn, 12:16].unsqueeze(2).to_broadcast([Tn, 4, 128]), ALU.mult),
             r=['kmtok', 'tm'], w=['kmtok'])

        def fn(e):
            ins = None
            for h in range(4):
                ins = e.matmul(P23[:, h * 256:h * 256 + 129], kw[:Tn, h, :], v1[:Tn, h, 0:129], start=True, stop=True)
            return ins
        T.op('pe', fn, r=['kmtok', 'v1'], w=['P2', 'P3'])
        for h in range(4):
            T.op('dve', lambda e, h=h: e.scalar_tensor_tensor(C[:, h, 0:129], C[:, h, 0:129], dbc[:, h:h + 1],
                                                              P23[:, h * 256:h * 256 + 129], ALU.mult, ALU.add),
                 r=['dbc', 'P2', 'P3', ck], w=[ck])

    out_toks = []

    def do_prefix():
        tok_x = {}
        if nt_pre > 0:
            tok_x[0] = load_x(0, xall[0:128, :], 128)
        for i in range(nt_pre):
            T.cur_tag = 'pre'
            slot = i % 2
            if i + 1 < nt_pre:
                load_x((i + 1) % 2, xall[(i + 1) * 128:(i + 2) * 128, :], 128)
            make_xT(slot, 128)
            mm_A(P23[:, 0:512], C_KM, 512, 128, 'win', ['P2'])
            mm_A(P23[:, 512:1024], C_VM, 512, 128, 'win', ['P3'])
            mm_B(P4[:, 128:256], C_IP, 4, 128, ['P4'])
            mm_B(P4[:, 256:384], C_FP, 4, 128, ['P4'])
            T.op('act', lambda e: e.activation(kmtok[:].rearrange("p h d -> p (h d)"), P23[:, 0:512], AF.Copy, scale=KSCALE), r=['P2'], w=['kmtok'])
            T.op('act', lambda e: e.copy(v1[:, :, 0:128], P23[:, 512:1024].rearrange("p (h d) -> p h d", h=4)), r=['P3'], w=['v1'])
            gate_chain(128, 0, False)
            state_update(128, 0)


    def stage1(Tn, xslot, si, kslot, pslot, prev_bias, cs_slot, hcol, htile, save_kv, is_sample, par=0):
        hres = hresL[par]
        hT = hTL[par]
        hrk = 'hres%d_%d' % (par, htile)
        htk = 'hT%d' % par
        C = Cst[si]
        ck = 'C%d' % si
        cbk = 'Cb%d' % si
        make_xT(xslot, Tn)
        mm_A(P23[:, 0:512], C_KM, 512, Tn, 'win', ['P2'])
        mm_A(P23[:, 512:1024], C_VM, 512, Tn, 'win', ['P3'])
        mm_A(PA1[:, 0:512], C_OM, 512, Tn, 'win', ['P1'])
        mm_A(P4[:, 0:128], C_VA, 128, Tn, 'win', ['P4'])
        mm_B(P4[:, 128:256], C_IP, 4, Tn, ['P4'])
        mm_B(P4[:, 256:384], C_FP, 4, Tn, ['P4'])
        T.op('act', lambda e: e.activation(kmtok[:Tn].rearrange("p h d -> p (h d)"), P23[:Tn, 0:512], AF.Copy, scale=KSCALE), r=['P2'], w=['kmtok'])
        T.op('act', lambda e: e.copy(v1[:Tn, :, 0:128], P23[:Tn, 512:1024].rearrange("p (h d) -> p h d", h=4)), r=['P3'], w=['v1'])
        T.op('act', lambda e: e.activation(og[:Tn], PA1[:Tn, :], AF.Exp, scale=-1.0), r=['P1'], w=['og'])
        T.op('act', lambda e: e.activation(og[:Tn], og[:Tn], AF.Ln, bias=1.0), r=['og'], w=['og'], c=0.55)
        T.op('act', lambda e: e.activation(og[:Tn], og[:Tn], AF.Exp, scale=-1.0), r=['og'], w=['og'], c=0.55)
        T.op('act', lambda e: e.copy(va1[kslot][:Tn, :, 0:64], P4[:Tn, 0:128].rearrange("p (k d) -> p k d", k=2)), r=['P4'], w=['va1_%d' % kslot])
        if save_kv:
            T.op('dve', lambda e: e.tensor_copy(kvout[:Tn, 1, :], P4[:Tn, 0:128]), r=['P4'], w=['kvout'])
        gate_chain(Tn, si, True)
        def fn_rb(e):
            ins = None
            for h in range(4):
                ins = e.matmul(P4[:Tn, h * 128:h * 128 + Tn], sel4[:, h * 128:h * 128 + Tn], G[:, 1, :Tn], start=True, stop=True)
            return ins
        T.op('pe', fn_rb, r=['G1', 'const'], w=['P4'], c=1.0)
        T.op('dve', lambda e: e.tensor_tensor(Ebuf[:Tn, :, :Tn], P4[:Tn, :].rearrange("p (h t) -> p h t", h=4)[:, :, :Tn],
                                              maskT[:Tn, :Tn].unsqueeze(1).to_broadcast([Tn, 4, Tn]), ALU.add),
             r=['P4', 'const'], w=['Ebuf'])
        for h in range(4):
            T.op('act', lambda e, h=h: e.activation(Ebuf[:Tn, h, :Tn], Ebuf[:Tn, h, :Tn], AF.Exp, bias=tm[:Tn, h:h + 1]), r=['Ebuf', 'tm'], w=['Ebuf'])
        for mt_ in range(4):
            mm_B(PA0[:, mt_ * 128:mt_ * 128 + 128], C_QM + mt_ * 128, 128, Tn, ['P0'])
        T.op('act', lambda e: e.copy(qmT[:, :, :Tn], PA0[:, :].rearrange("p (h t) -> p h t", h=4)[:, :, :Tn]), r=['P0'], w=['qmT'])
        def fn_kt(e):
            ins = None
            for h in range(4):
                ins = e.transpose(P1b[:, h * 128:h * 128 + Tn], kmtok[:Tn, h, :], identb[:Tn, :Tn])
            return ins
        T.op('pe', fn_kt, r=['kmtok', 'c6'], w=['P1'], c=0.7)
        T.op('act', lambda e: e.copy(kmT[:, :, :Tn], P1b[:, 0:512].rearrange("p (h t) -> p h t", h=4)[:, :, :Tn]), r=['P1'], w=['kmT'])
        def fn_s(e):
            ins = None
            for h in range(4):
                ins = e.matmul(P4[:Tn, h * 128:h * 128 + Tn], kmT[:, h, :Tn], qmT[:, h, :Tn], start=True, stop=True)
            return ins
        T.op('pe', fn_s, r=['kmT', 'qmT'], w=['P4'])
        T.op('dve', lambda e: e.tensor_tensor(ST[:Tn, :, :Tn], P4[:Tn, :].rearrange("p (h t) -> p h t", h=4)[:, :, :Tn], Ebuf[:Tn, :, :Tn], ALU.mult),
             r=['P4', 'Ebuf'], w=['ST'])
        def fn_qc(e):
            ins = None
            for h in range(4):
                ins = e.matmul(P23[:Tn, h * 256:h * 256 + 129], qmT[:, h, :Tn], Cb[si][:, h, 0:129], start=True, stop=True)
            return ins
        T.op('pe', fn_qc, r=['qmT', cbk], w=['P2', 'P3'])
        def fn_sv(e):
            ins = None
            for h in range(4):
                ins = e.matmul(P01[:Tn, h * 256:h * 256 + 129], ST[:Tn, h, :Tn], v1[:Tn, h, 0:129], start=True, stop=True)
            return ins
        T.op('pe', fn_sv, r=['ST', 'v1'], w=['P0', 'P1'])
        for h in range(4):
            T.op('act', lambda e, h=h: e.activation(nd1[:Tn, h, 0:129], P23[:Tn, h * 256:h * 256 + 129], AF.Copy, scale=tm[:Tn, 4 + h:5 + h]),
                 r=['P2', 'P3', 'tm'], w=['nd1'])
        T.op('dve', lambda e: e.tensor_tensor(nd[:Tn, :, 0:129], nd1[:Tn, :, 0:129],
                                              P01[:Tn, :].rearrange("p (h c) -> p h c", h=4)[:, :, 0:129], ALU.add),
             r=['nd1', 'P0', 'P1'], w=['nd1'])
        state_update(Tn, si)
        T.op('act', lambda e: e.copy(Cb[si][:], C[:]), r=[ck], w=[cbk])
        T.op('dve', lambda e: e.tensor_scalar(sm[:Tn, 20:24], nd[:Tn, :, 128], -1.0, None, ALU.mult), r=['nd1'], w=['sm'])
        T.op('dve', lambda e: e.tensor_tensor(sm[:Tn, 20:24], sm[:Tn, 20:24], nd[:Tn, :, 128], ALU.max), r=['nd1', 'sm'], w=['sm'])
        T.op('dve', lambda e: e.tensor_tensor(sm[:Tn, 0:4], sm[:Tn, 20:24], tm[:Tn, 8:12], ALU.max), r=['sm', 'tm'], w=['sm'])
        T.op('dve', lambda e: e.reciprocal(sm[:Tn, 0:4], sm[:Tn, 0:4]), r=['sm'], w=['sm'])
        T.op('dve', lambda e: e.tensor_tensor(hmr[:Tn], nd[:Tn, :, 0:128], sm[:Tn, 0:4].unsqueeze(2).to_broadcast([Tn, 4, 128]), ALU.mult),
             r=['nd1', 'sm'], w=['Ebuf'])
        for h in range(4):
            T.op('dve', lambda e, h=h: e.bn_stats(bst[:Tn, h, :], hmr[:Tn, h, :]), r=['Ebuf'], w=['bst'])
        for h in range(4):
            T.op('dve', lambda e, h=h: e.bn_aggr(mv[:Tn, h, :], bst[:Tn, h, :]), r=['bst'], w=['mv'])
        T.op('act', lambda e: e.activation(sm[:Tn, 4:8], mv[:Tn, :, 1], AF.Ln, bias=EPS), r=['mv'], w=['sm'])
        T.op('act', lambda e: e.activation(sm[:Tn, 4:8], sm[:Tn, 4:8], AF.Exp, scale=-0.5), r=['sm'], w=['sm'])
        for h in range(4):
            T.op('dve', lambda e, h=h: e.tensor_scalar(hmr[:Tn, h, :], hmr[:Tn, h, :], mv[:Tn, h, 0:1], sm[:Tn, 4 + h:5 + h], ALU.subtract, ALU.mult),
                 r=['Ebuf', 'mv', 'sm'], w=['Ebuf'])
        T.op('dve', lambda e: e.tensor_tensor(hmr[:Tn].rearrange("p h d -> p (h d)"), hmr[:Tn].rearrange("p h d -> p (h d)"), gnb[:Tn], ALU.mult),
             r=['Ebuf', 'const'], w=['Ebuf'])
        T.op('dve', lambda e: e.tensor_tensor(mixb[:Tn, 0:512], hmr[:Tn].rearrange("p h d -> p (h d)"), og[:Tn], ALU.mult),
             r=['Ebuf', 'og'], w=['mbA'])
        cosv, sinv = cosb[cs_slot], sinb[cs_slot]
        csk = 'cs%d' % cs_slot
        for mt_ in range(4):
            mm_B(P4[:, mt_ * 128:mt_ * 128 + 128], C_QAP + mt_ * 128, 128, Tn, ['P4'])
        for mt_ in range(4):
            mm_B(PA1[:, mt_ * 128:mt_ * 128 + 128], C_QAR + mt_ * 128, 128, Tn, ['P1'])
        T.op('dve', lambda e: e.tensor_tensor(rt1[:, :, :Tn], P4[:, :].rearrange("p (h t) -> p h t", h=4)[:, :, :Tn],
                                              cosv[:, :Tn].unsqueeze(1).to_broadcast([128, 4, Tn]), ALU.mult), r=['P4', csk], w=['rt1'])
        T.op('dve', lambda e: e.tensor_tensor(rt2[:, :, :Tn], PA1[:, :].rearrange("p (h t) -> p h t", h=4)[:, :, :Tn],
                                              sinv[:, :Tn].unsqueeze(1).to_broadcast([128, 4, Tn]), ALU.mult), r=['P1', csk], w=['rt2'])
        T.op('dve', lambda e: e.tensor_tensor(qaT[:, :, :Tn], rt1[:, :, :Tn], rt2[:, :, :Tn], ALU.add), r=['rt1', 'rt2'], w=['qaT'])
        mm_B(PA0[:, 0:128], C_KAP, 128, Tn, ['P0'])
        mm_B(PA0[:, 128:256], C_KAR, 128, Tn, ['P0'])
        T.op('dve', lambda e: e.tensor_tensor(rt1[:, 0, :Tn], PA0[:, 0:Tn], cosv[:, :Tn], ALU.mult), r=['P0', csk], w=['rt1'])
        T.op('dve', lambda e: e.tensor_tensor(rt2[:, 0, :Tn], PA0[:, 128:128 + Tn], sinv[:, :Tn], ALU.mult), r=['P0', csk], w=['rt2'])
        T.op('dve', lambda e: e.tensor_tensor(krope[:, :Tn], rt1[:, 0, :Tn], rt2[:, 0, :Tn], ALU.add), r=['rt1', 'rt2'], w=['krope'])
        T.op('act', lambda e: e.copy(kTb[kslot][:, :Tn], krope[:, :Tn]), r=['krope'], w=['kT_%d' % kslot])
        if save_kv:
            T.op('pe', lambda e: e.transpose(PA0[:Tn, 256:384], krope[:, :Tn], ident[:, :]), r=['krope', 'const'], w=['P0'])
            T.op('dve', lambda e: e.tensor_copy(kvout[:Tn, 0, :], PA0[:Tn, 256:384]), r=['P0'], w=['kvout'])
        blocks = [(pslot, 128, prev_bias), (kslot, Tn, 0.0)]
        psb = [(P23, ['P2', 'P3']), (P01, ['P0', 'P1'])]
        for bi_, (ks_, Sb, bias_) in enumerate(blocks):
            ps_, keys_ = psb[bi_]
            def fn_sc(e, ks_=ks_, Sb=Sb, ps_=ps_):
                ins = None
                for j in range(4):
                    for half in range(2):
                        hd = half * 4 + j
                        ins = e.matmul(ps_[:Sb, hd * 128:hd * 128 + Tn], kTb[ks_][half * 64:(half + 1) * 64, :Sb],
                                       qaT[half * 64:(half + 1) * 64, j, :Tn], start=True, stop=True)
                return ins
            T.op('pe', fn_sc, r=['kT_%d' % ks_, 'qaT'], w=keys_)
            for b2 in range(2):
                T.op('act', lambda e, b2=b2, Sb=Sb, ps_=ps_, bias_=bias_, bi_=bi_: e.activation(
                    pT[:Sb, bi_, 4 * b2:4 * b2 + 4, :Tn],
                    ps_[:Sb, 512 * b2:512 * b2 + 512].rearrange("p (h t) -> p h t", h=4)[:, :, :Tn],
                    AF.Exp, bias=bias_, scale=0.125), r=[keys_[b2], 'c5', 'const'], w=['pT%d' % bi_])
        def fn_pv(e):
            ins = None
            for hd in range(8):
                if is_sample:
                    for bi_, (ks_, Sb, _) in enumerate(blocks):
                        ins = e.matmul(P45(hd)[:Tn, :], pT[:Sb, bi_, hd, :Tn], va1[ks_][:Sb, hd // 4, 0:65],
                                       start=(bi_ == 0), stop=(bi_ == 1))
                else:
                    for qc, rng in ((0, ((0, 0, 128), (1, 0, 64))), (1, ((0, 64, 128), (1, 0, 128)))):
                        for ii, (bi_, s0, s1_) in enumerate(rng):
                            ks_ = blocks[bi_][0]
                            ins = e.matmul(P45(hd)[qc * 64:(qc + 1) * 64, :], pT[s0:s1_, bi_, hd, qc * 64:(qc + 1) * 64],
                                           va1[ks_][s0:s1_, hd // 4, 0:65], start=(ii == 0), stop=(ii == 1))
            return ins
        def P45(hd):
            base = P4 if hd < 4 else PA1
            c = (hd % 4) * 65
            return base[:, c:c + 65]
        T.op('pe', fn_pv, r=['pT0', 'pT1', 'va1_%d' % pslot, 'va1_%d' % kslot], w=['P4', 'P1'])
        for half, base, bk in ((0, P4, 'P4'), (1, PA1, 'P1')):
            o3 = base[:Tn, 0:260].rearrange("p (h c) -> p h c", h=4)
            T.op('dve', lambda e, o3=o3, half=half: e.tensor_tensor(sm[:Tn, 8 + 4 * half:12 + 4 * half], o3[:, :, 64], esink[:Tn, 4 * half:4 * half + 4], ALU.add),
                 r=[bk, 'c4'], w=['sm'])
            T.op('dve', lambda e, half=half: e.reciprocal(sm[:Tn, 8 + 4 * half:12 + 4 * half], sm[:Tn, 8 + 4 * half:12 + 4 * half]), r=['sm'], w=['sm'])
            T.op('dve', lambda e, o3=o3, half=half: e.tensor_tensor(
                mixb[:Tn, 512 + 256 * half:768 + 256 * half].rearrange("p (h d) -> p h d", h=4), o3[:, :, 0:64],
                sm[:Tn, 8 + 4 * half:12 + 4 * half].unsqueeze(2).to_broadcast([Tn, 4, 64]), ALU.mult),
                r=[bk, 'sm'], w=['mbB%d' % half])
        def fn_mt(e):
            ins = None
            for k in range(8):
                ins = e.transpose(P0b[:, k * 128:k * 128 + Tn], mixb[:Tn, k * 128:(k + 1) * 128], identb[:Tn, :Tn])
            return ins
        T.op('pe', fn_mt, r=['mbA', 'mbB0', 'mbB1', 'c6'], w=['P0'], c=1.1)
        T.op('act', lambda e: e.copy(mixT[:, :, :Tn], P0b[:, :].rearrange("p (k t) -> p k t", k=8)[:, :, :Tn]), r=['P0'], w=['xT'], c=0.9)
        for n in range(2):
            def fn_o(e, n=n):
                ins = None
                for k in range(8):
                    ins = e.matmul(P23[:Tn, n * 512:(n + 1) * 512], mixT[:, k, :Tn], wout[:, k, n * 512:(n + 1) * 512], start=(k == 0), stop=(k == 7))
                return ins
            T.op('pe', fn_o, r=['xT', 'wout'], w=['P%d' % (2 + n)], c=2.5)
        xk = 'xin%d' % xslot
        T.op('dve', lambda e: e.scalar_tensor_tensor(s1[:Tn], xin[xslot][:Tn], ALPHA, P23[:Tn, :], ALU.mult, ALU.add), r=[xk, 'P2', 'P3'], w=['mix'], c=1.3)
        layer_norm(s1, MIXK, Tn, 0, hres[:, htile, :], hrk)
        transposes8(hres[:, htile, :], Tn, P01, [hrk])
        for b in range(2):
            T.op('act', lambda e, b=b: e.copy(hT[:, 4 * b:4 * b + 4, hcol:hcol + Tn],
                                              P01[:, 512 * b:512 * b + 512].rearrange("p (k t) -> p k t", k=4)[:, :, :Tn]),
                 r=['P%d' % b], w=[htk])

    def layer_norm(src, skey, Tn, which, dst, dkey):
        skeys = list(skey) if isinstance(skey, (list, tuple)) else [skey]
        skey = skeys[-1]
        for c in range(2):
            T.op('dve', lambda e, c=c: e.bn_stats(bst[:Tn, c, :], src[:Tn, c * 512:(c + 1) * 512]), r=skeys, w=['bst'])
        T.op('dve', lambda e: e.bn_aggr(mv[:Tn, 0, :], bst[:Tn, 0:2, :].rearrange("p a b -> p (a b)")), r=['bst'], w=['mv'])
        T.op('act', lambda e: e.activation(sm[:Tn, 16:17], mv[:Tn, 0, 1:2], AF.Ln, bias=EPS), r=['mv'], w=['sm'])
        T.op('act', lambda e: e.activation(sm[:Tn, 16:17], sm[:Tn, 16:17], AF.Exp, scale=-0.5), r=['sm'], w=['sm'])
        T.op('dve', lambda e: e.scalar_tensor_tensor(src[:Tn], src[:Tn], mv[:Tn, 0, 0:1], lnp[:Tn, 2 * which, :], ALU.subtract, ALU.mult),
             r=skeys + ['mv', 'const'], w=skeys, c=1.25)
        T.op('dve', lambda e: e.scalar_tensor_tensor(dst[:Tn], src[:Tn], sm[:Tn, 16:17], lnp[:Tn, 2 * which + 1, :], ALU.mult, ALU.add),
             r=skeys + ['sm', 'const'], w=[dkey], c=1.25)

    ring_ctr = [0]
    dring_ctr = [0]
    set_ctr = [0]

    def stage2(N, segs, cvbuf, cvkey, first_macro, out_fn, par=0):
        hres = hresL[par]
        hT = hTL[par]
        htk = 'hT%d' % par
        for u in range(NU):
            rs = ring_ctr[0] % 2
            ring_ctr[0] += 1
            bs = u % 2
            PSU = (P6, P7)[bs]
            pk = ('P6', 'P7')[bs]
            si_ = set_ctr[0] % NS
            set_ctr[0] += 1
            ustv = ustS[si_]
            uk = ['ustS%d' % si_]
            ctv = ctS[si_]
            ckk = 'ctS%d' % si_
            T.dma('sp', 'wur%d' % rs, lambda e, rs=rs, u=u: e.dma_start(out=wur[rs][:].rearrange("p k c -> p (k c)"), in_=wupbf[u]),
                  r=['wupbf'], w=['wur%d' % rs])
            def fn_u(e, rs=rs, PSU=PSU):
                ins = None
                for j in range(2):
                    for k in range(8):
                        ins = e.matmul(PSU[:, j * 256:j * 256 + N], wur[rs][:, k, j * 128:(j + 1) * 128], hT[:, k, :N], start=(k == 0), stop=(k == 7))
                return ins
            T.op('pe', fn_u, r=['wur%d' % rs, htk], w=[pk], c=16 * (0.09 + max(N, 64) / 2400.0) + 0.1)
            pu = PSU[:, :].rearrange("p (j t) -> p j t", j=2)
            cv4 = cvbuf[:].rearrange("p (j u) t -> p j u t", j=2)
            T.op('act', lambda e, ustv=ustv, pu=pu: e.copy(ustv[:, :, 2:2 + N], pu[:, :, :N]), r=[pk], w=uk, c=0.7)
            T.op('dve', lambda e, u=u, ustv=ustv: e.tensor_copy(ustv[:, :, 0:2], cv4[:, :, u, :]), r=[cvkey], w=uk)
            if first_macro:
                T.op('dve', lambda e, u=u, ustv=ustv: e.tensor_scalar(cv4[:, :, u, :], ustv[:, :, N:N + 2], hs[:, 0:1], None, ALU.mult), r=uk + ['const'], w=[cvkey])
                continue
            T.op('dve', lambda e, u=u, ustv=ustv: e.tensor_copy(cv4[:, :, u, :], ustv[:, :, N:N + 2]), r=uk, w=[cvkey])
            for j in range(2):
                ci = j * NU + u
                T.op('act', lambda e, j=j, ci=ci, ctv=ctv, pu=pu: e.activation(ctv[:, j, :N], pu[:, j, :N], AF.Identity, bias=bcv[:, ci:ci + 1], scale=wcv[:, 2, ci:ci + 1]),
                     r=[pk, 'const'], w=[ckk], c=0.45)
                T.op('dve', lambda e, j=j, ci=ci, ctv=ctv, ustv=ustv: e.scalar_tensor_tensor(ctv[:, j, :N], ustv[:, j, 1:1 + N], wcv[:, 1, ci:ci + 1], ctv[:, j, :N], ALU.mult, ALU.add),
                     r=uk + [ckk, 'const'], w=[ckk], c=0.45)
                T.op('dve', lambda e, j=j, ci=ci, ctv=ctv, ustv=ustv: e.scalar_tensor_tensor(ctv[:, j, :N], ustv[:, j, 0:N], wcv[:, 0, ci:ci + 1], ctv[:, j, :N], ALU.mult, ALU.add),
                     r=uk + [ckk, 'const'], w=[ckk], c=0.45)
            T.op('act', lambda e, ctv=ctv: e.activation(ctv[:, 1, :N], ctv[:, 1, :N], AF.Silu), r=[ckk], w=[ckk], c=0.45)
            T.op('pool', lambda e, u=u, ctv=ctv: e.tensor_tensor(actT[:, u, :N], ctv[:, 1, :N], ctv[:, 0, :N], ALU.mult), r=[ckk], w=['actT'], c=0.7)
        if first_macro:
            return
        accs = [(P6, 'P6'), (P7, 'P7')]
        NCH = NU // 2
        for si_, (col0, Tn, htile) in enumerate(segs):
            pass
        for n in range(2):
            for c in range(NCH):
                ds = dring_ctr[0] % 5
                dring_ctr[0] += 1
                T.dma('sp', 'wdr%d' % ds, lambda e, ds=ds, c=c, n=n: e.dma_start(
                    out=wdr[ds][:], in_=wdnbf[c].rearrange("p (k n) -> p k n", k=2)[:, :, n * 512:(n + 1) * 512]),
                    r=['wdnbf'], w=['wdr%d' % ds])
                def fn_d(e, ds=ds, c=c):
                    ins = None
                    for si_, (col0, Tn, htile) in enumerate(segs):
                        acc = accs[si_][0]
                        for kk in range(2):
                            kc = 2 * c + kk
                            ins = e.matmul(acc[:Tn, :], actT[:, kc, col0:col0 + Tn], wdr[ds][:, kk, :],
                                           start=(kc == 0), stop=(kc == NU - 1))
                    return ins
                T.op('pe', fn_d, r=['actT', 'wdr%d' % ds], w=[accs[i][1] for i in range(len(segs))], c=len(segs) * 2 * 0.31 + 0.1)
            for si_, (col0, Tn, htile) in enumerate(segs):
                acc, akey = accs[si_]
                s2 = s2b[si_]
                wk_ = [S2K[0][n]] if si_ == 0 else MIXK
                T.op('dve', lambda e, Tn=Tn, htile=htile, acc=acc, s2=s2, n=n: e.scalar_tensor_tensor(
                    s2[:Tn, n * 512:(n + 1) * 512], hres[:Tn, htile, n * 512:(n + 1) * 512], ALPHA, acc[:Tn, :], ALU.mult, ALU.add),
                    r=['hres%d_%d' % (par, htile), akey], w=wk_, c=0.7)
        for si_, (col0, Tn, htile) in enumerate(segs):
            s2 = s2b[si_]
            layer_norm(s2, S2K[si_], Tn, 1, s2, S2K[si_][-1] if si_ == 0 else 'mix')
            out_fn(Tn, si_)

    def load_cs(slot, idx):
        T.dma('sp', 'cs%d' % slot, lambda e: e.dma_start(out=cosb[slot][:], in_=cos_d[idx]), w=['cs%d' % slot])
        T.dma('sp', 'cs%d' % slot, lambda e: e.dma_start(out=sinb[slot][:], in_=sin_d[idx]), w=['cs%d' % slot])

    def do_main():
        n_macro = nt_full // 2
        xbase = 8192 - NT_FULL * 128
        yrow = [0]
        for mi in range(n_macro):
            for tt in range(2):
                i = 2 * mi + tt
                slot = i % 2
                if i == 0:
                    load_x(0, xall[xbase: xbase + 128, :], 128)
                    load_cs(0, 0)
                if i + 1 < nt_full:
                    load_x((i + 1) % 2, xall[xbase + (i + 1) * 128: xbase + (i + 2) * 128, :], 128)
                    load_cs((i + 1) % 2, i + 1)
                T.cur_tag = 's1_%02d' % mi
                kslot = i % 2
                pslot = (i + 1) % 2
                pbias = hs[:, 1:2] if i == 2 else (NEG if i == 0 else 0.0)
                last = (i == nt_full - 1)
                stage1(128, slot, 0, kslot, pslot, pbias, slot, tt * 128, tt, last, False, par=mi % 2)
                if i == 1:
                    T.op('dve', lambda e: e.tensor_scalar(mst[0][:], mst[0][:], hs[0:4, 0:1], None, ALU.mult), r=['m0', 'const'], w=['m0'])
                if last:
                    out_toks.append(T.dma('sp', 'okvp', lambda e: e.dma_start(out=kp_d[:, :], in_=kvout[:, 0, :]), r=['kvout']))
                    out_toks.append(T.dma('sp', 'okvp', lambda e: e.dma_start(out=vp_d[:, :], in_=kvout[:, 1, :]), r=['kvout']))

            def out_y(Tn, yslot):
                r0 = yrow[0]
                yrow[0] += Tn
                out_toks.append(T.dma('sp', 'oy%d' % yslot, lambda e: e.dma_start(out=y_d[r0:r0 + Tn, :], in_=s2b[yslot][:Tn, :]), r=S2K[yslot]))
            T.cur_tag = 's2_%02d' % mi
            stage2(256, [(0, 128, 0), (128, 128, 1)], cvh, 'cvh', mi == 0, out_y, par=mi % 2)

        out_toks.append(T.dma('sp', 'ostC', lambda e: e.dma_start(out=Cp_d.rearrange("h d e -> d h e"), in_=Cst[0][:, :, 0:128]), r=['C0']))
        out_toks.append(T.dma('sp', 'ostC', lambda e: e.dma_start(out=np_d.rearrange("h d -> d h"), in_=Cst[0][:, :, 128], allow_slow_non_contiguous=True), r=['C0']))
        out_toks.append(T.dma('sp', 'ostm', lambda e: e.dma_start(out=mp_d[:, :], in_=mst[0][:]), r=['m0']))
        for j in range(2):
            out_toks.append(T.dma('sp', 'ostcv', lambda e, j=j: e.dma_start(out=cvp_d[j, :].rearrange("(c p) -> p c", p=128), in_=cvh[:, :, j],
                                                                           allow_slow_non_contiguous=True), r=['cvh']))

    def do_sample():
        T.cur_tag = 'sample'
        load_x(0, xs_d[:, :], 16)
        load_cs(0, NT_FULL)
        T.dma('sp', 'xin1', lambda e: e.dma_start(out=xin[1][:, 0:128], in_=ck_d[:, :]), w=['xin1'])
        T.dma('sp', 'xin1', lambda e: e.dma_start(out=xin[1][:, 128:256], in_=cv_d[:, :]), w=['xin1'])
        T.op('pe', lambda e: e.transpose(P4[:, 0:128], xin[1][:, 0:128], ident[:, :]), r=['xin1', 'const'], w=['P4'])
        T.op('act', lambda e: e.copy(kTb[1][:, :], P4[:, 0:128]), r=['P4'], w=['kT_1'])
        T.op('act', lambda e: e.copy(va1[1][:, :, 0:64], xin[1][:, 128:256].rearrange("p (k d) -> p k d", k=2)), r=['xin1'], w=['va1_1'])
        T.op('dve', lambda e: e.memset(Cst[0][:], 0.0), w=['C0'])
        T.dma('sp', 'sstC', lambda e: e.dma_start(out=Cst[0][:, :, 0:128], in_=C0_d.rearrange("h d e -> d h e")), w=['C0'])
        T.dma('sp', 'sstC', lambda e: e.dma_start(
            out=Cst[0][:, :, 128], in_=n0_d.rearrange("h d -> d h"), allow_slow_non_contiguous=True), w=['C0'])
        T.dma('sp', 'sstm', lambda e: e.dma_start(out=mst[0][:], in_=m0_d[:, :]), w=['m0'])
        T.op('act', lambda e: e.copy(Cb[0][:], Cst[0][:]), r=['C0'], w=['Cb0'])
        stage1(16, 0, 0, 0, 1, 0.0, 0, 0, 0, True, True)
        out_toks.append(T.dma('sp', 'okvs', lambda e: e.dma_start(out=ks_d[:, :], in_=kvout[:16, 0, :]), r=['kvout']))
        out_toks.append(T.dma('sp', 'okvs', lambda e: e.dma_start(out=vs_d[:, :], in_=kvout[:16, 1, :]), r=['kvout']))

        def out_ys(Tn, yslot):
            out_toks.append(T.dma('sp', 'oy%d' % yslot, lambda e: e.dma_start(out=ys_d[:, :], in_=s2b[yslot][:Tn, :]), r=S2K[yslot]))
        stage2(16, [(0, 16, 0)], cvsb, 'cvsb', False, out_ys, par=0)
        out_toks.append(T.dma('sp', 'ossC', lambda e: e.dma_start(out=Cs_d.rearrange("h d e -> d h e"), in_=Cst[0][:, :, 0:128]), r=['C0']))
        out_toks.append(T.dma('sp', 'ossC', lambda e: e.dma_start(out=ns_d.rearrange("h d -> d h"), in_=Cst[0][:, :, 128], allow_slow_non_contiguous=True), r=['C0']))
        out_toks.append(T.dma('sp', 'ossm', lambda e: e.dma_start(out=ms_d[:, :], in_=mst[0][:]), r=['m0']))
        for j in range(2):
            out_toks.append(T.dma('sp', 'osscv', lambda e, j=j: e.dma_start(out=cvs_d[j, :].rearrange("(c p) -> p c", p=128), in_=cvsb[:, :, j],
                                                                           allow_slow_non_contiguous=True), r=['cvsb']))
    do_sample()
    T.cur_tag = 'setup2'
    T.op('dve', lambda e: e.memset(Cst[0][:], 0.0), w=['C0'])
    T.op('dve', lambda e: e.memset(Cb[0][:], 0.0), w=['Cb0'])
    T.op('dve', lambda e: e.memset(mst[0][:], 0.0), w=['m0'])
    do_prefix()
    do_main()
    T.wait_all('sp', out_toks)
    if FILL:
        T.filler_fn = lambda e: e.matmul(P5[:, 0:512], identb[:, :], win[:, 0, 0:512], start=True, stop=True)
        T.fill_frac = float(os.environ.get('K_FILL_FRAC', '0.5'))
        T.filler_cost = float(os.environ.get('K_FILL_COST', '0.3'))
        T.fill_min = float(os.environ.get('K_FILL_MIN', '1.5'))
    T.schedule()

    with nc.Block() as block:
        @block.sync
        def _(e):
            T.replay('sp', e)

        @block.tensor
        def _(e):
            T.replay('pe', e)

        @block.scalar
        def _(e):
            T.replay('act', e)

        @block.vector
        def _(e):
            T.replay('dve', e)

        @block.gpsimd
        def _(e):
            T.replay('pool', e)
    es.close()
    return nc


def _host_consts():
    ident = np.eye(128, dtype=np.float32)
    s = np.arange(128)[:, None]
    l = np.arange(128)[None, :]
    maskT = np.where(l >= s, 0.0, NEG).astype(np.float32)
    amask = np.ones((128, 2, 128), np.float32)
    amask[:, 0, :] = np.where((s < 64) & (l >= 64), 0.0, 1.0)
    amask[:, 1, :] = np.where((s >= 64) & (l < 64), 0.0, 1.0)
    sel4 = np.zeros((4, 4, 128), np.float32)
    for h in range(4):
        sel4[h, h, :] = 1.0
    return ident, maskT, amask, sel4.reshape(4, 512)


def _rope_tables(pos):
    half = 32
    inv = (np.float32(10000.0) ** (-np.arange(half, dtype=np.float32) / np.float32(half))).astype(np.float32)
    d = np.arange(128) % 64
    f = inv[d % 32]
    ang = (pos.astype(np.float32)[None, :] * f[:, None]).astype(np.float32)
    c = np.cos(ang).astype(np.float32)
    sn = np.sin(ang).astype(np.float32)
    sign = np.where(d < 32, -1.0, 1.0).astype(np.float32)[:, None]
    return c, (sn * sign).astype(np.float32)


_NC_CACHE = {}


def kernel(x_prompt, x_sample, state_mlstm_C, state_mlstm_n, state_mlstm_m, cache_swa_k, cache_swa_v,
           state_conv, w_in, b_igate, b_fgate, g_mlstm_norm, attn_sinks, w_out, ln1_g, ln1_b,
           w_up, w_conv, b_conv, w_down, ln2_g, ln2_b):
    f = lambda a: np.ascontiguousarray(np.asarray(a, dtype=np.float32))
    x_prompt, x_sample = f(x_prompt), f(x_sample)
    w_in0 = f(w_in)[0]
    rot = (np.arange(64) + 32) % 64
    qa0, ka0, va0 = 2056, 2568, 2696
    qap, qar = [], []
    for j in range(4):
        for hd in (j, 4 + j):
            qap.extend(qa0 + hd * 64 + np.arange(64))
            qar.extend(qa0 + hd * 64 + rot)
    kap = list(ka0 + np.arange(128))
    kar = list(ka0 + np.concatenate([rot, 64 + rot]))
    cols = list(range(0, 2056)) + list(range(va0, va0 + 128)) + qap + qar + kap + kar
    w_in_aug = np.ascontiguousarray(w_in0[:, np.array(cols, dtype=np.int64)])
    assert w_in_aug.shape[1] == WCOLS
    ident, maskT, amask, sel4 = _host_consts()
    ln = np.stack([f(ln1_g)[0], f(ln1_b)[0], f(ln2_g)[0], f(ln2_b)[0]], 0)

    nt_pre = int(os.environ.get("K_NT_PRE", NT_PRE))
    nt_full = int(os.environ.get("K_NT_FULL", NT_FULL))
    key = (nt_pre, nt_full)
    if key not in _NC_CACHE:
        _NC_CACHE[key] = build_program(nt_pre, nt_full)
    nc = _NC_CACHE[key]

    in_maps = []
    for c in range(8):
        b, half = c // 2, c % 2
        if half == 1:
            xall = x_prompt[b]
        else:
            xall = np.concatenate([np.zeros((4096, D), np.float32), x_prompt[b, :4096]], 0)
        pos0 = half * 4096 - 256
        cosT = np.zeros((NT_FULL + 1, 128, 128), np.float32)
        sinT = np.zeros((NT_FULL + 1, 128, 128), np.float32)
        for i in range(NT_FULL):
            cc, ss = _rope_tables(pos0 + i * 128 + np.arange(128))
            cosT[i], sinT[i] = cc, ss
        cc, ss = _rope_tables(2048 + np.arange(16))
        cosT[NT_FULL, :, :16], sinT[NT_FULL, :, :16] = cc, ss
        hsv = np.zeros((128, 2), np.float32)
        hsv[:, 0] = float(half)
        hsv[:, 1] = 0.0 if half == 1 else NEG
        in_maps.append({
            "xall": np.ascontiguousarray(xall), "xs": x_sample[c],
            "C0": f(state_mlstm_C)[0, c], "n0": f(state_mlstm_n)[0, c], "m0": f(state_mlstm_m)[0, c].reshape(4, 1),
            "ck": f(cache_swa_k)[0, c].reshape(128, 128), "cv": f(cache_swa_v)[0, c].reshape(128, 128),
            "sconv": f(state_conv)[0, c],
            "w_in": w_in_aug, "w_out": f(w_out)[0], "w_up": f(w_up)[0], "w_down": f(w_down)[0],
            "w_conv": f(w_conv)[0], "b_conv": f(b_conv)[0].reshape(1, -1),
            "b_i": f(b_igate)[0].reshape(4, 1), "b_f": f(b_fgate)[0].reshape(4, 1),
            "g_norm": f(g_mlstm_norm)[0].reshape(1, 512), "sinks": f(attn_sinks)[0].reshape(1, 8),
            "ln": ln, "cosT": cosT, "sinT": sinT, "ident": ident, "maskT": maskT, "amask": amask,
            "sel4": sel4, "hs": hsv,
        })
    res = run_bass_kernel_spmd(nc, in_maps, core_ids=list(range(8)))
    R = res.results
    y_p = np.zeros((4, 8192, D), np.float32)
    for c in range(8):
        y_p[c // 2, (c % 2) * 4096:(c % 2 + 1) * 4096] = R[c]["y"]
    y_s = np.stack([R[c]["ys"] for c in range(8)], 0)
    odd = [1, 3, 5, 7]
    C_p = np.stack([R[c]["Cp"] for c in odd], 0)[None]
    n_p = np.stack([R[c]["np"] for c in odd], 0)[None]
    m_p = np.stack([R[c]["mp"].reshape(4) for c in odd], 0)[None]
    k_p = np.stack([R[c]["kp"].reshape(128, 2, 64) for c in odd], 0)[None]
    v_p = np.stack([R[c]["vp"].reshape(128, 2, 64) for c in odd], 0)[None]
    cv_p = np.stack([R[c]["cvp"] for c in odd], 0)[None]
    C_s = np.stack([R[c]["Cs"] for c in range(8)], 0)[None]
    n_s = np.stack([R[c]["ns"] for c in range(8)], 0)[None]
    m_s = np.stack([R[c]["ms"].reshape(4) for c in range(8)], 0)[None]
    k_s = np.stack([R[c]["ks"].reshape(16, 2, 64) for c in range(8)], 0)[None]
    v_s = np.stack([R[c]["vs"].reshape(16, 2, 64) for c in range(8)], 0)[None]
    cv_s = np.stack([R[c]["cvs"] for c in range(8)], 0)[None]
    return (y_p, y_s, C_p, n_p, m_p, k_p, v_p, cv_p, C_s, n_s, m_s, k_s, v_s, cv_s)
```

```python
import os
from contextlib import ExitStack
import numpy as np
import concourse.bass as bass
import concourse.mybir as mybir
from concourse.bass_utils import run_bass_kernel_spmd

F32 = mybir.dt.float32
BF16 = mybir.dt.bfloat16
AF = mybir.ActivationFunctionType
ALU = mybir.AluOpType
AX = mybir.AxisListType

D = 1024
NT_PRE = 30
NT_FULL = 34
DFF = 2816
NU = 22
WCOLS = 3464
C_QM, C_KM, C_VM, C_OM, C_IP, C_FP, C_VA, C_QAP, C_QAR, C_KAP, C_KAR = 0, 512, 1024, 1536, 2048, 2052, 2056, 2184, 2696, 3208, 3336
ALPHA = float(2.0 ** 0.25)
EPS = 1e-5
KSCALE = float(128.0 ** -0.5)
NEG = -30000.0


SKIP_WAR = int(os.environ.get('K_SKIP_WAR', '1'))


class Tracker:
    HOP = 0.4
    DEF_COST = {'pe': 0.7, 'act': 0.45, 'dve': 0.3, 'pool': 0.8, 'sp': 0.08}

    def __init__(self, nc, es):
        self.nc = nc
        self.es = es
        self.engs = {}
        self.last_w = {}
        self.readers = {}
        self.streams = {}
        self.ops = []
        self.thr_hist = {}
        self.cur_tag = 'setup'
        self.filler_fn = None
        self.pe_scale = float(os.environ.get('K_PE_SCALE', '0.65'))
        self.ad_scale = float(os.environ.get('K_AD_SCALE', '1.1'))
        self.HOP = float(os.environ.get('K_HOP', '0.9'))
        self.hop_same = float(os.environ.get('K_HOP_SAME', '0.3'))
        self.filler_cost = 0.43
        self.fill_min = 1.5
        self.fill_frac = 0.6

    def add_engine(self, name, eng):
        sem = self.es.enter_context(self.nc.semaphore("s_" + name))
        self.engs[name] = dict(eng=eng, sem=sem, name=name, order=[])

    def stream(self, name, group=False):
        if name not in self.streams:
            sem = self.es.enter_context(self.nc.semaphore("d_" + name))
            self.streams[name] = dict(sem=sem, count=0, group=group, name=name)
        return self.streams[name]

    def _deps(self, reads, writes, extra, group):
        deps = set()
        for k in reads:
            deps.update(self.last_w.get(k, ()))
        self._raw = set(deps)
        if not group:
            for k in writes:
                deps.update(self.last_w.get(k, ()))
                deps.update(self.readers.get(k, ()))
        deps.update(extra)
        return deps

    def _commit(self, oid, reads, writes, group):
        for k in reads:
            self.readers.setdefault(k, []).append(oid)
        for k in writes:
            if group:
                self.last_w.setdefault(k, []).append(oid)
            else:
                self.last_w[k] = [oid]
                self.readers[k] = []

    def op(self, ename, fn, r=(), w=(), extra=(), c=None):
        deps = self._deps(r, w, extra, False)
        oid = len(self.ops)
        cc = self.DEF_COST[ename] if c is None else c
        if ename == 'pe':
            cc *= self.pe_scale
        elif ename in ('act', 'dve', 'pool'):
            cc *= self.ad_scale
        self.ops.append(dict(id=oid, eng=ename, fn=fn, deps=deps, dma=None, tag=self.cur_tag, cost=cc, raw=self._raw | set(extra)))
        self._commit(oid, r, w, False)
        return oid

    def dma(self, qname, stream, fn, r=(), w=(), extra=(), group=False, c=4.0):
        S = self.stream(stream, group)
        deps = self._deps(r, w, extra, group)
        oid = len(self.ops)
        thr = ()
        if group:
            hist = self.thr_hist.setdefault(qname, [])
            B = 3 if qname == 'pool' else 4
            nb = len(hist) // B
            if nb > 0:
                thr = tuple(hist[(nb - 1) * B:nb * B])
                deps.update(thr)
            hist.append(oid)
        self.ops.append(dict(id=oid, eng=qname, fn=fn, deps=deps, dma=S, cost=c, thr=thr, tag=self.cur_tag))
        self._commit(oid, r, w, group)
        return oid

    def wait_all(self, ename, toks):
        oid = len(self.ops)
        self.ops.append(dict(id=oid, eng=ename, fn=None, deps=set(toks), dma=None, cost=0.05, tag='final'))
        return oid

    def schedule(self):
        import heapq
        ops = self.ops
        n = len(ops)
        ndeps = [len(o['deps']) for o in ops]
        users = [[] for _ in range(n)]
        for o in ops:
            for d in o['deps']:
                users[d].append(o['id'])
        ready_t = [0.0] * n
        fin = [0.0] * n
        bl = [0.0] * n
        for o in reversed(ops):
            i = o['id']
            m = 0.0
            for u in users[i]:
                if bl[u] > m:
                    m = bl[u]
            bl[i] = m + o['cost'] + self.HOP
        mode = os.environ.get('K_PRIO', 'id')
        if mode == 'bl':
            prio = [(-bl[i], i) for i in range(n)]
        else:
            prio = [(i, i) for i in range(n)]
        ready = {e: [] for e in self.engs}
        free_t = {e: 0.0 for e in self.engs}
        events = []
        for o in ops:
            if ndeps[o['id']] == 0:
                heapq.heappush(ready[o['eng']], (prio[o['id']], o['id']))
        for e in self.engs:
            heapq.heappush(events, (0.0, 0, e))
        seq = 1
        done = 0
        pending_wake = {e: True for e in self.engs}
        while done < n:
            if not events:
                raise RuntimeError("scheduler stuck")
            t, _, e = heapq.heappop(events)
            pending_wake[e] = False
            if free_t[e] > t + 1e-9:
                heapq.heappush(events, (free_t[e], seq, e)); seq += 1; pending_wake[e] = True
                continue
            cand = None
            tmp = []
            while ready[e]:
                item = heapq.heappop(ready[e])
                oid = item[1]
                if ready_t[oid] <= t + 1e-9:
                    cand = oid
                    break
                tmp.append(item)
            for x in tmp:
                heapq.heappush(ready[e], x)
            if cand is None:
                if ready[e]:
                    tn = min(ready_t[x[1]] for x in ready[e])
                    heapq.heappush(events, (tn, seq, e)); seq += 1; pending_wake[e] = True
                continue
            o = ops[cand]
            o['t0'] = t
            self.engs[e]['order'].append(cand)
            if o['dma'] is not None:
                free_t[e] = t + (0.6 if e == 'pool' else 0.08)
                fin[cand] = t + o['cost']
            else:
                free_t[e] = t + o['cost']
                fin[cand] = free_t[e]
            done += 1
            for u in users[cand]:
                ndeps[u] -= 1
                if ops[u]['eng'] == e and o['dma'] is None:
                    if e == 'pe' or (SKIP_WAR and e in ('act', 'dve') and cand not in ops[u].get('raw', ops[u]['deps'])):
                        rt = fin[cand]
                    else:
                        rt = fin[cand] + self.hop_same
                else:
                    rt = fin[cand] + self.HOP
                if rt > ready_t[u]:
                    ready_t[u] = rt
                if ndeps[u] == 0:
                    ue = ops[u]['eng']
                    heapq.heappush(ready[ue], (prio[u], u))
                    if not pending_wake[ue]:
                        heapq.heappush(events, (max(ready_t[u], free_t[ue]), seq, ue)); seq += 1; pending_wake[ue] = True
            heapq.heappush(events, (free_t[e], seq, e)); seq += 1; pending_wake[e] = True
        self.sim_time = max(fin)
        self.fin = fin
        if self.filler_fn is not None:
            fc = self.filler_cost
            new_order = []
            prev_end = None
            nfill = 0
            armed = False
            for oid in self.engs['pe']['order']:
                o = ops[oid]
                if not armed:
                    if any(ops[d]['dma'] is not None and ops[d]['dma']['name'] == 'win' for d in o['deps']):
                        armed = True
                    new_order.append(oid)
                    prev_end = fin[oid]
                    continue
                if prev_end is not None:
                    gap = o['t0'] - prev_end
                    if gap > self.fill_min:
                        k = int((gap * self.fill_frac) / fc)
                        for _ in range(k):
                            fid = len(ops)
                            ops.append(dict(id=fid, eng='pe', fn=self.filler_fn, deps=set(), dma=None, cost=fc, tag='fill', filler=True))
                            new_order.append(fid)
                            nfill += 1
                new_order.append(oid)
                prev_end = fin[oid]
            self.engs['pe']['order'] = new_order
            self.nfill = nfill
        for e, E in self.engs.items():
            pos = 0
            for oid in E['order']:
                o = ops[oid]
                if o['dma'] is not None:
                    o['dma']['count'] += 16
                    o['val'] = o['dma']['count']
                elif o['fn'] is not None and not o.get('filler'):
                    pos += 1
                    o['val'] = pos

    def replay(self, ename, eng):
        E = self.engs[ename]
        ops = self.ops
        seen = {}
        for oid in E['order']:
            o = ops[oid]
            need = {}
            for d in o['deps']:
                D = ops[d]
                if D['dma'] is not None:
                    S = D['dma']
                    if d in o.get('thr', ()):
                        sem, val = S['sem'], D['val']
                    else:
                        sem, val = S['sem'], (S['count'] if S['group'] else D['val'])
                else:
                    if D['fn'] is None:
                        continue
                    if D['eng'] == 'pe' and ename == 'pe':
                        continue
                    if SKIP_WAR and D['eng'] == ename and ename in ('act', 'dve') and d not in o.get('raw', o['deps']):
                        continue
                    sem, val = self.engs[D['eng']]['sem'], D['val']
                if need.get(sem, 0) < val:
                    need[sem] = val
            for sem, val in need.items():
                if seen.get(sem, 0) < val:
                    eng.wait_ge(sem, val)
                    seen[sem] = val
            if o['fn'] is None:
                continue
            ins = o['fn'](eng)
            if o.get('filler'):
                continue
            if o['dma'] is not None:
                ins.then_inc(o['dma']['sem'], 16)
            else:
                ins.then_inc(E['sem'], 1)


def build_program(nt_pre=NT_PRE, nt_full=NT_FULL):
    nc = bass.Bass("TRN2", target_bir_lowering=False)
    es = ExitStack()

    def din(name, shape, dt=F32):
        return nc.dram_tensor(name, list(shape), dt, kind="ExternalInput").ap()

    def dout(name, shape, dt=F32):
        return nc.dram_tensor(name, list(shape), dt, kind="ExternalOutput").ap()

    xall = din("xall", [8192, D])
    xs_d = din("xs", [16, D])
    C0_d = din("C0", [4, 128, 128])
    n0_d = din("n0", [4, 128])
    m0_d = din("m0", [4, 1])
    ck_d = din("ck", [128, 128])
    cv_d = din("cv", [128, 128])
    sconv_d = din("sconv", [2, 2 * DFF])
    win_d = din("w_in", [D, WCOLS])
    wout_d = din("w_out", [D, D])
    wup_d = din("w_up", [D, 2 * DFF])
    wdn_d = din("w_down", [DFF, D])
    wconv_d = din("w_conv", [3, 2 * DFF])
    bconv_d = din("b_conv", [1, 2 * DFF])
    bi_d = din("b_i", [4, 1])
    bf_d = din("b_f", [4, 1])
    gn_d = din("g_norm", [1, 512])
    sink_d = din("sinks", [1, 8])
    ln_d = din("ln", [4, D])
    cos_d = din("cosT", [NT_FULL + 1, 128, 128])
    sin_d = din("sinT", [NT_FULL + 1, 128, 128])
    ident_d = din("ident", [128, 128])
    maskT_d = din("maskT", [128, 128])
    amask_d = din("amask", [128, 2, 128])
    sel4_d = din("sel4", [4, 4 * 128])
    hs_d = din("hs", [128, 2])

    y_d = dout("y", [4096, D])
    ys_d = dout("ys", [16, D])
    Cp_d = dout("Cp", [4, 128, 128])
    np_d = dout("np", [4, 128])
    mp_d = dout("mp", [4, 1])
    kp_d = dout("kp", [128, 128])
    vp_d = dout("vp", [128, 128])
    cvp_d = dout("cvp", [2, 2 * DFF])
    Cs_d = dout("Cs", [4, 128, 128])
    ns_d = dout("ns", [4, 128])
    ms_d = dout("ms", [4, 1])
    ks_d = dout("ks", [16, 128])
    vs_d = dout("vs", [16, 128])
    cvs_d = dout("cvs", [2, 2 * DFF])
    wupbf = nc.dram_tensor("wupbf", [NU, 128, 8 * 256], BF16, kind="Internal").ap()
    wdnbf = nc.dram_tensor("wdnbf", [NU // 2, 128, 2 * D], BF16, kind="Internal").ap()

    def sb(name, shape, dt=F32):
        return es.enter_context(nc.sbuf_tensor(name, list(shape), dt))

    def pt(name, shape, dt=F32):
        return es.enter_context(nc.psum_tensor(name, list(shape), dt))

    win = sb("win", [128, 8, WCOLS], BF16)
    wout = sb("wout", [128, 8, D], BF16)
    wdr = [sb("wdr%d" % i, [128, 2, 512], BF16) for i in range(5)]
    NS = 2
    ustS = [sb("ustS%d" % i, [128, 2, 258]) for i in range(NS)]
    ctS = [sb("ctS%d" % i, [128, 2, 256]) for i in range(NS)]
    s2b0 = sb("s2b0", [128, D])
    wur = [sb("wur%d" % i, [128, 8, 256], BF16) for i in range(2)]
    lnp = sb("lnp", [128, 4, D])
    gnb = sb("gnb", [128, 512])
    esink = sb("esink", [128, 8])
    ident = sb("ident_s", [128, 128])
    maskT = sb("maskT_s", [128, 128])
    amask = sb("amask_s", [128, 2, 128], BF16)
    sel4 = sb("sel4_s", [4, 4 * 128])
    hs = sb("hs_s", [128, 2])
    bi = sb("bi_s", [4, 1])
    nbf = sb("nbf_s", [4, 1])
    wcv = sb("wcv", [128, 3, 2 * NU])
    bcv = sb("bcv", [128, 2 * NU])
    cvh = sb("cvh", [128, 2 * NU, 2])
    cvsb = sb("cvsb", [128, 2 * NU, 2])
    ones4 = sb("ones4", [4, 128])
    onesr = ones4

    xin = [sb("xin%d" % i, [128, D]) for i in range(2)]
    xT = sb("xT", [128, 8, 128], BF16)
    mixT = xT
    kmtok = sb("kmtok", [128, 4, 128], BF16)
    kw = kmtok
    v1 = sb("v1", [128, 4, 130], BF16)
    qmT = sb("qmT", [128, 4, 128], BF16)
    qaT = sb("qaT", [128, 4, 128], BF16)
    kmT = sb("kmT", [128, 4, 128], BF16)
    cosb = [sb("cos%d" % i, [128, 128]) for i in range(2)]
    sinb = [sb("sin%d" % i, [128, 128]) for i in range(2)]
    nd1 = sb("nd1", [128, 4, 130])
    nd = nd1
    rt1 = sb("rt1", [128, 4, 128])
    Ebuf = sb("Ebuf", [128, 4, 128])
    hmr = Ebuf
    rt2 = sb("rt2", [128, 4, 128])
    krope = sb("krope", [128, 128])
    kTb = [sb("kTb%d" % i, [128, 128], BF16) for i in range(2)]
    va1 = [sb("va1%d" % i, [128, 2, 66], BF16) for i in range(2)]
    kvout = sb("kvout", [128, 2, 128])
    ST = sb("ST", [128, 4, 128], BF16)
    Cst = [sb("Cst0", [128, 4, 130])] * 2
    Cb = [sb("Cb0", [128, 4, 130], BF16)] * 2
    mst = [sb("mst0", [4, 1])] * 2
    bst = sb("bst", [128, 4, 6])
    mv = sb("mv", [128, 4, 2])
    sm = sb("sm", [128, 32])
    lnscr = [(sb("bstL%d" % i, [128, 2, 6]), sb("mvL%d" % i, [128, 2]), sb("smL%d" % i, [128, 2])) for i in range(2)]
    pTraw = sb("pTraw", [128, 1024])
    pT = pTraw[:].bitcast(BF16).rearrange("p (b h t) -> p b h t", b=2, h=8)
    og = sb("og", [128, 512])
    mix = sb("mix", [128, D])
    mixb = sb("mixb", [128, D], BF16)
    identb = sb("identb", [128, 128], BF16)
    s1 = mix
    s2b = [s2b0, mix]
    MIXK = ['mix']
    S2K = [['s2b0a', 's2b0b', 's2b0'], MIXK]
    hresL = [sb("hres%d" % i, [128, 2, D]) for i in range(2)]
    hTL = [sb("hT%d" % i, [128, 8, 256], BF16) for i in range(2)]
    actT = sb("actT", [128, NU, 256], BF16)
    G = sb("G", [4, 7, 128])
    tm = sb("tm", [128, 16])
    dbc = sb("dbc", [128, 4])
    gsm = sb("gsm", [4, 8])

    P01 = pt("P01", [128, 1024])
    P23 = pt("P23", [128, 1024])
    P4 = pt("P4", [128, 512])
    P5 = pt("P5", [128, 512])
    P6 = pt("P6", [128, 512])
    P7 = pt("P7", [128, 512])
    PA0 = P01[:, 0:512]
    PA1 = P01[:, 512:1024]
    P0b = PA0.bitcast(BF16)
    P1b = PA1.bitcast(BF16)

    T = Tracker(nc, es)
    FILL = int(os.environ.get('K_FILL', '1'))
    T.add_engine('sp', nc.sync)
    T.add_engine('pe', nc.tensor)
    T.add_engine('act', nc.scalar)
    T.add_engine('dve', nc.vector)
    T.add_engine('pool', nc.gpsimd)

    def ld(q, stream, out, in_, w, r=()):
        return T.dma(q, stream, lambda e, o=out, i=in_: e.dma_start(out=o, in_=i), r=r, w=w, group=True, c=6.0)

    for k in range(8):
        for c0 in (0, 1732):
            ld('pool', 'win', win[:, k, c0:c0 + 1732], win_d[k * 128:(k + 1) * 128, c0:c0 + 1732], w=['win'])
    ld('sp', 'small', ident[:], ident_d[:, :], w=['const'])
    ld('sp', 'small', maskT[:], maskT_d[:, :], w=['const'])
    ld('pool', 'small2', amask[:], amask_d[:, :, :], w=['c5'])
    ld('sp', 'small', sel4[:], sel4_d[:, :], w=['const'])
    ld('sp', 'small', hs[:], hs_d[:, :], w=['const'])
    ld('sp', 'small', bi[:], bi_d[:, :], w=['const'])
    ld('sp', 'small', nbf[:], bf_d[:, :], w=['const'])
    ld('sp', 'small', gnb[:], gn_d[0, :].partition_broadcast(128), w=['const'])
    ld('sp', 'small', esink[:], sink_d[0, :].partition_broadcast(128), w=['const'])
    for j in range(4):
        ld('sp', 'small', lnp[:, j, :], ln_d[j, :].partition_broadcast(128), w=['const'])
    for j in range(3):
        T.dma('sp', 'small', lambda e, j=j: e.dma_start(
            out=wcv[:, j, :], in_=wconv_d[j, :].rearrange("(c p) -> p c", p=128), allow_slow_non_contiguous=True), w=['const'], group=True, c=20.0)
    T.dma('sp', 'small', lambda e: e.dma_start(
        out=bcv[:], in_=bconv_d[0, :].rearrange("(c p) -> p c", p=128), allow_slow_non_contiguous=True), w=['const'], group=True, c=20.0)
    for j in range(2):
        T.dma('sp', 'small', lambda e, j=j: e.dma_start(
            out=cvsb[:, :, j], in_=sconv_d[j, :].rearrange("(c p) -> p c", p=128), allow_slow_non_contiguous=True), w=['cvsb'], group=True, c=20.0)
    for k in range(8):
        ld('pool', 'wout', wout[:, k, :], wout_d[k * 128:(k + 1) * 128, :], w=['wout'])
    for u in range(NU):
        for j in range(2):
            c0 = j * DFF + u * 128
            T.dma('pool', 'wupc', lambda e, u=u, j=j, c0=c0: e.dma_start(
                out=wupbf[u].rearrange("p (k c) -> p k c", k=8)[:, :, j * 128:(j + 1) * 128],
                in_=wup_d[:, c0:c0 + 128].rearrange("(k p) c -> p k c", p=128)), w=['wupbf'], group=True, c=8.0)
    for c in range(NU // 2):
        T.dma('pool', 'wdnc', lambda e, c=c: e.dma_start(
            out=wdnbf[c].rearrange("p (k n) -> p k n", k=2),
            in_=wdn_d[c * 256:(c + 1) * 256, :].rearrange("(k p) n -> p k n", p=128)), w=['wdnbf'], group=True, c=8.0)

    T.op('dve', lambda e: e.memset(ones4[:], 1.0), w=['c2'])
    T.op('dve', lambda e: e.tensor_scalar(nbf[:], nbf[:], -1.0, None, ALU.mult), r=['const'], w=['c3'])
    T.op('act', lambda e: e.activation(esink[:], esink[:], AF.Exp), r=['const'], w=['c4'])
    T.op('dve', lambda e: e.tensor_copy(identb[:], ident[:]), r=['const'], w=['c6'])
    for i in range(2):
        T.op('dve', lambda e, i=i: e.memset(v1[:, :, 128:130], 1.0), w=['v1'])
    for i in range(2):
        T.op('dve', lambda e, i=i: e.memset(va1[i][:], 1.0), w=['va1_%d' % i])
        T.op('dve', lambda e, i=i: e.memset(kTb[i][:], 0.0), w=['kT_%d' % i])
    T.op('dve', lambda e: e.memset(cvh[:], 0.0), w=['cvh'])
    CONSTS = ['const', 'c2', 'c3', 'c4', 'c5', 'c6']

    def transposes8(src, Tn, dst_ps, rkeys):
        def fn(e):
            ins = None
            for k in range(8):
                ins = e.transpose(dst_ps[:, k * 128:k * 128 + Tn], src[:Tn, k * 128:(k + 1) * 128], ident[:Tn, :Tn])
            return ins
        T.op('pe', fn, r=list(rkeys) + CONSTS, w=['P0', 'P1'], c=2.0)

    def load_x(slot, src_ap, Tn):
        return T.dma('sp', 'xin%d' % slot, lambda e: e.dma_start(out=xin[slot][:Tn, :], in_=src_ap), w=['xin%d' % slot])

    def make_xT(slot, Tn):
        transposes8(xin[slot], Tn, P01, ['xin%d' % slot])
        for b in range(2):
            T.op('act', lambda e, b=b: e.copy(
                xT[:, 4 * b:4 * b + 4, :Tn],
                P01[:, 512 * b:512 * b + 512].rearrange("p (k t) -> p k t", k=4)[:, :, :Tn]),
                r=['P%d' % b], w=['xT'])

    def mm_A(ps, col0, ncols, Tn, wkey, bankkeys):
        def fn(e):
            ins = None
            for k in range(8):
                ins = e.matmul(ps[:Tn, :ncols], xT[:, k, :Tn], win[:, k, col0:col0 + ncols], start=(k == 0), stop=(k == 7))
            return ins
        T.op('pe', fn, r=['xT', 'win'], w=bankkeys, c=8 * (0.09 + max(ncols, 64) / 2400.0) + 0.1)

    def mm_B(ps, col0, mcols, Tn, bankkeys):
        def fn(e):
            ins = None
            for k in range(8):
                ins = e.matmul(ps[:mcols, :Tn], win[:, k, col0:col0 + mcols], xT[:, k, :Tn], start=(k == 0), stop=(k == 7))
            return ins
        T.op('pe', fn, r=['xT', 'win'], w=bankkeys, c=8 * 0.13 + 0.1)

    def gate_chain(Tn, si, full):
        m = mst[si]
        mk = 'm%d' % si
        ipT = P4[0:4, 128:128 + Tn]
        fpT = P4[0:4, 256:256 + Tn]
        R = lambda j: G[:, j, :Tn]
        T.op('act', lambda e: e.activation(R(0), fpT, AF.Exp, bias=nbf[:], scale=-1.0), r=['P4', 'c3'], w=['G0'])
        T.op('act', lambda e: e.activation(R(0), R(0), AF.Ln, bias=1.0), r=['G0'], w=['G0'])
        T.op('dve', lambda e: e.tensor_tensor_scan(R(1), onesr[:, :Tn], R(0), 0.0, ALU.mult, ALU.subtract), r=['G0', 'c2'], w=['G1'])
        T.op('dve', lambda e: e.scalar_tensor_tensor(R(2), ipT, bi[:], R(1), ALU.add, ALU.subtract), r=['P4', 'G1', 'const'], w=['G2'])
        T.op('dve', lambda e: e.tensor_tensor_scan(R(3), R(2), R(2), m[:], ALU.max, ALU.max), r=['G2', mk], w=['G3'])
        T.op('act', lambda e: e.activation(R(4), R(3), AF.Exp, bias=m[:], scale=-1.0), r=['G3', mk], w=['G4'])
        T.op('dve', lambda e: e.tensor_scalar(gsm[:, 0:1], G[:, 3, Tn - 1:Tn], -1.0, None, ALU.mult), r=['G3'], w=['gsm0'])
        T.op('act', lambda e: e.activation(R(5), R(2), AF.Exp, bias=gsm[:, 0:1]), r=['G2', 'gsm0'], w=['G5'])
        T.op('dve', lambda e: e.tensor_tensor(R(6), R(1), R(3), ALU.add), r=['G1', 'G3'], w=['G6'])
        if full:
            T.op('act', lambda e: e.activation(R(0), R(6), AF.Exp, scale=-1.0), r=['G6'], w=['G0'])
            T.op('dve', lambda e: e.tensor_scalar(R(1), R(3), -1.0, None, ALU.mult), r=['G3'], w=['G1'])
        T.op('dve', lambda e: e.tensor_scalar(gsm[:, 4:8], ident[0:4, 0:4], G[:, 4, Tn - 1:Tn], None, ALU.mult), r=['G4', 'const'], w=['gsm1'])
        T.op('dve', lambda e: e.tensor_copy(m[:], G[:, 6, Tn - 1:Tn]), r=['G6'], w=[mk])

        def fn(e):
            ins = None
            rows = [(2, 0), (4, 4), (0, 8), (5, 12)] if full else [(5, 12)]
            for (rj, c) in rows:
                ins = e.matmul(P4[:Tn, 384 + c:384 + c + 4], G[:, rj, :Tn], ident[0:4, 0:4], start=True, stop=True)
            ins = e.matmul(P4[:, 448:452], ones4[:, :], gsm[:, 4:8], start=True, stop=True)
            return ins
        T.op('pe', fn, r=['G2', 'G4', 'G0', 'G5', 'gsm1', 'c2', 'const'], w=['P4'])
        if full:
            T.op('dve', lambda e: e.tensor_copy(tm[:Tn, :], P4[:Tn, 384:400]), r=['P4'], w=['tm'])
        else:
            T.op('dve', lambda e: e.tensor_copy(tm[:Tn, 12:16], P4[:Tn, 396:400]), r=['P4'], w=['tm'])
        T.op('dve', lambda e: e.tensor_copy(dbc[:], P4[:, 448:452]), r=['P4'], w=['dbc'])

    def state_update(Tn, si):
        C = Cst[si]
        ck = 'C%d' % si
        T.op('dve', lambda e: e.tensor_tensor(kw[:Tn], kmtok[:Tn], tm[:Tn, 12:16].unsqueeze(2).to_broadcast([Tn, 4, 128]), ALU.mult),
             r=['kmtok', 'tm'], w=['kmtok'])

        def fn(e):
            ins = None
            for h in range(4):
                ins = e.matmul(P23[:, h * 256:h * 256 + 129], kw[:Tn, h, :], v1[:Tn, h, 0:129], start=True, stop=True)
            return ins
        T.op('pe', fn, r=['kmtok', 'v1'], w=['P2', 'P3'])
        for h in range(4):
            T.op('dve', lambda e, h=h: e.scalar_tensor_tensor(C[:, h, 0:129], C[:, h, 0:129], dbc[:, h:h + 1],
                                                              P23[:, h * 256:h * 256 + 129], ALU.mult, ALU.add),
                 r=['dbc', 'P2', 'P3', ck], w=[ck])

    out_toks = []

    def do_prefix():
        tok_x = {}
        if nt_pre > 0:
            tok_x[0] = load_x(0, xall[0:128, :], 128)
        for i in range(nt_pre):
            T.cur_tag = 'pre'
            slot = i % 2
            if i + 1 < nt_pre:
                load_x((i + 1) % 2, xall[(i + 1) * 128:(i + 2) * 128, :], 128)
            make_xT(slot, 128)
            mm_A(P23[:, 0:512], C_KM, 512, 128, 'win', ['P2'])
            mm_A(P23[:, 512:1024], C_VM, 512, 128, 'win', ['P3'])
            mm_B(P4[:, 128:256], C_IP, 4, 128, ['P4'])
            mm_B(P4[:, 256:384], C_FP, 4, 128, ['P4'])
            T.op('act', lambda e: e.activation(kmtok[:].rearrange("p h d -> p (h d)"), P23[:, 0:512], AF.Copy, scale=KSCALE), r=['P2'], w=['kmtok'])
            T.op('act', lambda e: e.copy(v1[:, :, 0:128], P23[:, 512:1024].rearrange("p (h d) -> p h d", h=4)), r=['P3'], w=['v1'])
            gate_chain(128, 0, False)
            state_update(128, 0)


    def stage1(Tn, xslot, si, kslot, pslot, prev_bias, cs_slot, hcol, htile, save_kv, is_sample, par=0):
        hres = hresL[par]
        hT = hTL[par]
        hrk = 'hres%d_%d' % (par, htile)
        htk = 'hT%d' % par
        C = Cst[si]
        ck = 'C%d' % si
        cbk = 'Cb%d' % si
        make_xT(xslot, Tn)
        mm_A(P23[:, 0:512], C_KM, 512, Tn, 'win', ['P2'])
        mm_A(P23[:, 512:1024], C_VM, 512, Tn, 'win', ['P3'])
        mm_A(PA1[:, 0:512], C_OM, 512, Tn, 'win', ['P1'])
        mm_A(P4[:, 0:128], C_VA, 128, Tn, 'win', ['P4'])
        mm_B(P4[:, 128:256], C_IP, 4, Tn, ['P4'])
        mm_B(P4[:, 256:384], C_FP, 4, Tn, ['P4'])
        T.op('act', lambda e: e.activation(kmtok[:Tn].rearrange("p h d -> p (h d)"), P23[:Tn, 0:512], AF.Copy, scale=KSCALE), r=['P2'], w=['kmtok'])
        T.op('act', lambda e: e.copy(v1[:Tn, :, 0:128], P23[:Tn, 512:1024].rearrange("p (h d) -> p h d", h=4)), r=['P3'], w=['v1'])
        T.op('act', lambda e: e.activation(og[:Tn], PA1[:Tn, :], AF.Exp, scale=-1.0), r=['P1'], w=['og'])
        T.op('act', lambda e: e.activation(og[:Tn], og[:Tn], AF.Ln, bias=1.0), r=['og'], w=['og'], c=0.55)
        T.op('act', lambda e: e.activation(og[:Tn], og[:Tn], AF.Exp, scale=-1.0), r=['og'], w=['og'], c=0.55)
        T.op('pool', lambda e: e.tensor_tensor(og[:Tn], og[:Tn], gnb[:Tn], ALU.mult), r=['og', 'const'], w=['og'], c=1.1)
        T.op('act', lambda e: e.copy(va1[kslot][:Tn, :, 0:64], P4[:Tn, 0:128].rearrange("p (k d) -> p k d", k=2)), r=['P4'], w=['va1_%d' % kslot])
        if save_kv:
            T.op('dve', lambda e: e.tensor_copy(kvout[:Tn, 1, :], P4[:Tn, 0:128]), r=['P4'], w=['kvout'])
        gate_chain(Tn, si, True)
        def fn_rb(e):
            ins = None
            for h in range(4):
                ins = e.matmul(P4[:Tn, h * 128:h * 128 + Tn], sel4[:, h * 128:h * 128 + Tn], G[:, 1, :Tn], start=True, stop=True)
            return ins
        T.op('pe', fn_rb, r=['G1', 'const'], w=['P4'], c=1.0)
        T.op('dve', lambda e: e.tensor_tensor(Ebuf[:Tn, :, :Tn], P4[:Tn, :].rearrange("p (h t) -> p h t", h=4)[:, :, :Tn],
                                              maskT[:Tn, :Tn].unsqueeze(1).to_broadcast([Tn, 4, Tn]), ALU.add),
             r=['P4', 'const'], w=['Ebuf'])
        for h in range(4):
            T.op('act', lambda e, h=h: e.activation(Ebuf[:Tn, h, :Tn], Ebuf[:Tn, h, :Tn], AF.Exp, bias=tm[:Tn, h:h + 1]), r=['Ebuf', 'tm'], w=['Ebuf'])
        for mt_ in range(4):
            mm_B(PA0[:, mt_ * 128:mt_ * 128 + 128], C_QM + mt_ * 128, 128, Tn, ['P0'])
        T.op('act', lambda e: e.copy(qmT[:, :, :Tn], PA0[:, :].rearrange("p (h t) -> p h t", h=4)[:, :, :Tn]), r=['P0'], w=['qmT'])
        def fn_kt(e):
            ins = None
            for h in range(4):
                ins = e.transpose(P1b[:, h * 128:h * 128 + Tn], kmtok[:Tn, h, :], identb[:Tn, :Tn])
            return ins
        T.op('pe', fn_kt, r=['kmtok', 'c6'], w=['P1'], c=0.7)
        T.op('act', lambda e: e.copy(kmT[:, :, :Tn], P1b[:, 0:512].rearrange("p (h t) -> p h t", h=4)[:, :, :Tn]), r=['P1'], w=['kmT'])
        def fn_s(e):
            ins = None
            for h in range(4):
                ins = e.matmul(P4[:Tn, h * 128:h * 128 + Tn], kmT[:, h, :Tn], qmT[:, h, :Tn], start=True, stop=True)
            return ins
        T.op('pe', fn_s, r=['kmT', 'qmT'], w=['P4'])
        T.op('dve', lambda e: e.tensor_tensor(ST[:Tn, :, :Tn], P4[:Tn, :].rearrange("p (h t) -> p h t", h=4)[:, :, :Tn], Ebuf[:Tn, :, :Tn], ALU.mult),
             r=['P4', 'Ebuf'], w=['ST'])
        def fn_qc(e):
            ins = None
            for h in range(4):
                ins = e.matmul(P23[:Tn, h * 256:h * 256 + 129], qmT[:, h, :Tn], Cb[si][:, h, 0:129], start=True, stop=True)
            return ins
        T.op('pe', fn_qc, r=['qmT', cbk], w=['P2', 'P3'])
        def fn_sv(e):
            ins = None
            for h in range(4):
                ins = e.matmul(P01[:Tn, h * 256:h * 256 + 129], ST[:Tn, h, :Tn], v1[:Tn, h, 0:129], start=True, stop=True)
            return ins
        T.op('pe', fn_sv, r=['ST', 'v1'], w=['P0', 'P1'])
        for h in range(4):
            T.op('act', lambda e, h=h: e.activation(nd1[:Tn, h, 0:129], P23[:Tn, h * 256:h * 256 + 129], AF.Copy, scale=tm[:Tn, 4 + h:5 + h]),
                 r=['P2', 'P3', 'tm'], w=['nd1'])
        T.op('dve', lambda e: e.tensor_tensor(nd[:Tn, :, 0:129], nd1[:Tn, :, 0:129],
                                              P01[:Tn, :].rearrange("p (h c) -> p h c", h=4)[:, :, 0:129], ALU.add),
             r=['nd1', 'P0', 'P1'], w=['nd1'])
        state_update(Tn, si)
        T.op('act', lambda e: e.copy(Cb[si][:], C[:]), r=[ck], w=[cbk])
        T.op('dve', lambda e: e.tensor_scalar(sm[:Tn, 20:24], nd[:Tn, :, 128], -1.0, None, ALU.mult), r=['nd1'], w=['sm_den'])
        T.op('dve', lambda e: e.tensor_tensor(sm[:Tn, 20:24], sm[:Tn, 20:24], nd[:Tn, :, 128], ALU.max), r=['nd1', 'sm_den'], w=['sm_den'])
        T.op('dve', lambda e: e.tensor_tensor(sm[:Tn, 0:4], sm[:Tn, 20:24], tm[:Tn, 8:12], ALU.max), r=['sm_den', 'tm'], w=['sm_den'])
        T.op('dve', lambda e: e.reciprocal(sm[:Tn, 0:4], sm[:Tn, 0:4]), r=['sm_den'], w=['sm_den'])
        T.op('dve', lambda e: e.tensor_tensor(hmr[:Tn], nd[:Tn, :, 0:128], sm[:Tn, 0:4].unsqueeze(2).to_broadcast([Tn, 4, 128]), ALU.mult),
             r=['nd1', 'sm_den'], w=['Ebuf'])
        for h in range(4):
            T.op('dve', lambda e, h=h: e.bn_stats(bst[:Tn, h, :], hmr[:Tn, h, :]), r=['Ebuf'], w=['bst'])
        for h in range(4):
            T.op('dve', lambda e, h=h: e.bn_aggr(mv[:Tn, h, :], bst[:Tn, h, :]), r=['bst'], w=['mv'])
        T.op('act', lambda e: e.activation(sm[:Tn, 4:8], mv[:Tn, :, 1], AF.Ln, bias=EPS), r=['mv'], w=['sm_hn'])
        T.op('act', lambda e: e.activation(sm[:Tn, 4:8], sm[:Tn, 4:8], AF.Exp, scale=-0.5), r=['sm_hn'], w=['sm_hn'])
        for h in range(4):
            T.op('dve', lambda e, h=h: e.tensor_scalar(hmr[:Tn, h, :], hmr[:Tn, h, :], mv[:Tn, h, 0:1], sm[:Tn, 4 + h:5 + h], ALU.subtract, ALU.mult),
                 r=['Ebuf', 'mv', 'sm_hn'], w=['Ebuf'])
        T.op('dve', lambda e: e.tensor_tensor(mixb[:Tn, 0:512], hmr[:Tn].rearrange("p h d -> p (h d)"), og[:Tn], ALU.mult),
             r=['Ebuf', 'og'], w=['mbA'])
        cosv, sinv = cosb[cs_slot], sinb[cs_slot]
        csk = 'cs%d' % cs_slot
        for mt_ in range(4):
            mm_B(P4[:, mt_ * 128:mt_ * 128 + 128], C_QAP + mt_ * 128, 128, Tn, ['P4'])
        for mt_ in range(4):
            mm_B(PA1[:, mt_ * 128:mt_ * 128 + 128], C_QAR + mt_ * 128, 128, Tn, ['P1'])
        T.op('dve', lambda e: e.tensor_tensor(rt1[:, :, :Tn], P4[:, :].rearrange("p (h t) -> p h t", h=4)[:, :, :Tn],
                                              cosv[:, :Tn].unsqueeze(1).to_broadcast([128, 4, Tn]), ALU.mult), r=['P4', csk], w=['rt1'])
        T.op('dve', lambda e: e.tensor_tensor(rt2[:, :, :Tn], PA1[:, :].rearrange("p (h t) -> p h t", h=4)[:, :, :Tn],
                                              sinv[:, :Tn].unsqueeze(1).to_broadcast([128, 4, Tn]), ALU.mult), r=['P1', csk], w=['rt2'])
        T.op('dve', lambda e: e.tensor_tensor(qaT[:, :, :Tn], rt1[:, :, :Tn], rt2[:, :, :Tn], ALU.add), r=['rt1', 'rt2'], w=['qaT'])
        mm_B(PA0[:, 0:128], C_KAP, 128, Tn, ['P0'])
        mm_B(PA0[:, 128:256], C_KAR, 128, Tn, ['P0'])
        T.op('dve', lambda e: e.tensor_tensor(rt1[:, 0, :Tn], PA0[:, 0:Tn], cosv[:, :Tn], ALU.mult), r=['P0', csk], w=['rt1'])
        T.op('dve', lambda e: e.tensor_tensor(rt2[:, 0, :Tn], PA0[:, 128:128 + Tn], sinv[:, :Tn], ALU.mult), r=['P0', csk], w=['rt2'])
        T.op('dve', lambda e: e.tensor_tensor(krope[:, :Tn], rt1[:, 0, :Tn], rt2[:, 0, :Tn], ALU.add), r=['rt1', 'rt2'], w=['krope'])
        T.op('act', lambda e: e.copy(kTb[kslot][:, :Tn], krope[:, :Tn]), r=['krope'], w=['kT_%d' % kslot])
        if save_kv:
            T.op('pe', lambda e: e.transpose(PA0[:Tn, 256:384], krope[:, :Tn], ident[:, :]), r=['krope', 'const'], w=['P0'])
            T.op('dve', lambda e: e.tensor_copy(kvout[:Tn, 0, :], PA0[:Tn, 256:384]), r=['P0'], w=['kvout'])
        blocks = [(pslot, 128, prev_bias), (kslot, Tn, 0.0)]
        psb = [(P23, ['P2', 'P3']), (P01, ['P0', 'P1'])]
        for bi_, (ks_, Sb, bias_) in enumerate(blocks):
            ps_, keys_ = psb[bi_]
            def fn_sc(e, ks_=ks_, Sb=Sb, ps_=ps_):
                ins = None
                for j in range(4):
                    for half in range(2):
                        hd = half * 4 + j
                        ins = e.matmul(ps_[:Sb, hd * 128:hd * 128 + Tn], kTb[ks_][half * 64:(half + 1) * 64, :Sb],
                                       qaT[half * 64:(half + 1) * 64, j, :Tn], start=True, stop=True)
                return ins
            T.op('pe', fn_sc, r=['kT_%d' % ks_, 'qaT'], w=keys_)
            for b2 in range(2):
                T.op('act', lambda e, b2=b2, Sb=Sb, ps_=ps_, bias_=bias_, bi_=bi_: e.activation(
                    pT[:Sb, bi_, 4 * b2:4 * b2 + 4, :Tn],
                    ps_[:Sb, 512 * b2:512 * b2 + 512].rearrange("p (h t) -> p h t", h=4)[:, :, :Tn],
                    AF.Exp, bias=bias_, scale=0.125), r=[keys_[b2], 'c5', 'const'], w=['pT%d' % bi_])
        def fn_pv(e):
            ins = None
            for hd in range(8):
                if is_sample:
                    for bi_, (ks_, Sb, _) in enumerate(blocks):
                        ins = e.matmul(P45(hd)[:Tn, :], pT[:Sb, bi_, hd, :Tn], va1[ks_][:Sb, hd // 4, 0:65],
                                       start=(bi_ == 0), stop=(bi_ == 1))
                else:
                    for qc, rng in ((0, ((0, 0, 128), (1, 0, 64))), (1, ((0, 64, 128), (1, 0, 128)))):
                        for ii, (bi_, s0, s1_) in enumerate(rng):
                            ks_ = blocks[bi_][0]
                            ins = e.matmul(P45(hd)[qc * 64:(qc + 1) * 64, :], pT[s0:s1_, bi_, hd, qc * 64:(qc + 1) * 64],
                                           va1[ks_][s0:s1_, hd // 4, 0:65], start=(ii == 0), stop=(ii == 1))
            return ins
        def P45(hd):
            base = P4 if hd < 4 else PA1
            c = (hd % 4) * 65
            return base[:, c:c + 65]
        T.op('pe', fn_pv, r=['pT0', 'pT1', 'va1_%d' % pslot, 'va1_%d' % kslot], w=['P4', 'P1'])
        for half, base, bk in ((0, P4, 'P4'), (1, PA1, 'P1')):
            o3 = base[:Tn, 0:260].rearrange("p (h c) -> p h c", h=4)
            T.op('dve', lambda e, o3=o3, half=half: e.tensor_tensor(sm[:Tn, 8 + 4 * half:12 + 4 * half], o3[:, :, 64], esink[:Tn, 4 * half:4 * half + 4], ALU.add),
                 r=[bk, 'c4'], w=['sm_pv'])
        T.op('dve', lambda e: e.reciprocal(sm[:Tn, 8:16], sm[:Tn, 8:16]), r=['sm_pv'], w=['sm_pv'])
        for half, base, bk in ((0, P4, 'P4'), (1, PA1, 'P1')):
            o3 = base[:Tn, 0:260].rearrange("p (h c) -> p h c", h=4)
            T.op('dve', lambda e, o3=o3, half=half: e.tensor_tensor(
                mixb[:Tn, 512 + 256 * half:768 + 256 * half].rearrange("p (h d) -> p h d", h=4), o3[:, :, 0:64],
                sm[:Tn, 8 + 4 * half:12 + 4 * half].unsqueeze(2).to_broadcast([Tn, 4, 64]), ALU.mult),
                r=[bk, 'sm_pv'], w=['mbB%d' % half])
        def fn_mt(e):
            ins = None
            for k in range(8):
                ins = e.transpose(P0b[:, k * 128:k * 128 + Tn], mixb[:Tn, k * 128:(k + 1) * 128], identb[:Tn, :Tn])
            return ins
        T.op('pe', fn_mt, r=['mbA', 'mbB0', 'mbB1', 'c6'], w=['P0'], c=1.1)
        T.op('act', lambda e: e.copy(mixT[:, :, :Tn], P0b[:, :].rearrange("p (k t) -> p k t", k=8)[:, :, :Tn]), r=['P0'], w=['xT'], c=0.9)
        for n in range(2):
            def fn_o(e, n=n):
                ins = None
                for k in range(8):
                    ins = e.matmul(P23[:Tn, n * 512:(n + 1) * 512], mixT[:, k, :Tn], wout[:, k, n * 512:(n + 1) * 512], start=(k == 0), stop=(k == 7))
                return ins
            T.op('pe', fn_o, r=['xT', 'wout'], w=['P%d' % (2 + n)], c=2.5)
        xk = 'xin%d' % xslot
        T.op('dve', lambda e: e.scalar_tensor_tensor(s1[:Tn], xin[xslot][:Tn], ALPHA, P23[:Tn, :], ALU.mult, ALU.add), r=[xk, 'P2', 'P3'], w=['mix'], c=1.3)
        layer_norm(s1, MIXK, Tn, 0, hres[:, htile, :], hrk)
        transposes8(hres[:, htile, :], Tn, P01, [hrk])
        for b in range(2):
            T.op('act', lambda e, b=b: e.copy(hT[:, 4 * b:4 * b + 4, hcol:hcol + Tn],
                                              P01[:, 512 * b:512 * b + 512].rearrange("p (k t) -> p k t", k=4)[:, :, :Tn]),
                 r=['P%d' % b], w=[htk])

    def layer_norm(src, skey, Tn, which, dst, dkey):
        skeys = list(skey) if isinstance(skey, (list, tuple)) else [skey]
        skey = skeys[-1]
        bstL, mvL, smL = lnscr[which]
        kb, km_, ks_ = 'bstL%d' % which, 'mvL%d' % which, 'smL%d' % which
        for c in range(2):
            T.op('dve', lambda e, c=c: e.bn_stats(bstL[:Tn, c, :], src[:Tn, c * 512:(c + 1) * 512]), r=skeys, w=[kb])
        T.op('dve', lambda e: e.bn_aggr(mvL[:Tn, :], bstL[:Tn, 0:2, :].rearrange("p a b -> p (a b)")), r=[kb], w=[km_])
        T.op('act', lambda e: e.activation(smL[:Tn, 0:1], mvL[:Tn, 1:2], AF.Ln, bias=EPS), r=[km_], w=[ks_])
        T.op('act', lambda e: e.activation(smL[:Tn, 0:1], smL[:Tn, 0:1], AF.Exp, scale=-0.5), r=[ks_], w=[ks_])
        T.op('dve', lambda e: e.scalar_tensor_tensor(src[:Tn], src[:Tn], mvL[:Tn, 0:1], lnp[:Tn, 2 * which, :], ALU.subtract, ALU.mult),
             r=skeys + [km_, 'const'], w=skeys, c=1.25)
        T.op('dve', lambda e: e.scalar_tensor_tensor(dst[:Tn], src[:Tn], smL[:Tn, 0:1], lnp[:Tn, 2 * which + 1, :], ALU.mult, ALU.add),
             r=skeys + [ks_, 'const'], w=[dkey], c=1.25)

    ring_ctr = [0]
    dring_ctr = [0]
    set_ctr = [0]

    def stage2(N, segs, cvbuf, cvkey, first_macro, out_fn, par=0):
        hres = hresL[par]
        hT = hTL[par]
        htk = 'hT%d' % par
        for u in range(NU):
            rs = ring_ctr[0] % 2
            ring_ctr[0] += 1
            bs = u % 2
            PSU = (P6, P7)[bs]
            pk = ('P6', 'P7')[bs]
            si_ = set_ctr[0] % NS
            set_ctr[0] += 1
            ustv = ustS[si_]
            uk = ['ustS%d' % si_]
            ctv = ctS[si_]
            ckk = 'ctS%d' % si_
            T.dma('sp', 'wur%d' % rs, lambda e, rs=rs, u=u: e.dma_start(out=wur[rs][:].rearrange("p k c -> p (k c)"), in_=wupbf[u]),
                  r=['wupbf'], w=['wur%d' % rs])
            def fn_u(e, rs=rs, PSU=PSU):
                ins = None
                for j in range(2):
                    for k in range(8):
                        ins = e.matmul(PSU[:, j * 256:j * 256 + N], wur[rs][:, k, j * 128:(j + 1) * 128], hT[:, k, :N], start=(k == 0), stop=(k == 7))
                return ins
            T.op('pe', fn_u, r=['wur%d' % rs, htk], w=[pk], c=16 * (0.09 + max(N, 64) / 2400.0) + 0.1)
            pu = PSU[:, :].rearrange("p (j t) -> p j t", j=2)
            cv4 = cvbuf[:].rearrange("p (j u) t -> p j u t", j=2)
            T.op('act', lambda e, ustv=ustv, pu=pu: e.copy(ustv[:, :, 2:2 + N], pu[:, :, :N]), r=[pk], w=uk, c=0.7)
            T.op('dve', lambda e, u=u, ustv=ustv: e.tensor_copy(ustv[:, :, 0:2], cv4[:, :, u, :]), r=[cvkey], w=uk)
            if first_macro:
                T.op('dve', lambda e, u=u, ustv=ustv: e.tensor_scalar(cv4[:, :, u, :], ustv[:, :, N:N + 2], hs[:, 0:1], None, ALU.mult), r=uk + ['const'], w=[cvkey])
                continue
            T.op('dve', lambda e, u=u, ustv=ustv: e.tensor_copy(cv4[:, :, u, :], ustv[:, :, N:N + 2]), r=uk, w=[cvkey])
            for j in range(2):
                ci = j * NU + u
                T.op('act', lambda e, j=j, ci=ci, ctv=ctv, pu=pu: e.activation(ctv[:, j, :N], pu[:, j, :N], AF.Identity, bias=bcv[:, ci:ci + 1], scale=wcv[:, 2, ci:ci + 1]),
                     r=[pk, 'const'], w=[ckk], c=0.45)
                T.op('dve', lambda e, j=j, ci=ci, ctv=ctv, ustv=ustv: e.scalar_tensor_tensor(ctv[:, j, :N], ustv[:, j, 1:1 + N], wcv[:, 1, ci:ci + 1], ctv[:, j, :N], ALU.mult, ALU.add),
                     r=uk + [ckk, 'const'], w=[ckk], c=0.45)
                T.op('dve', lambda e, j=j, ci=ci, ctv=ctv, ustv=ustv: e.scalar_tensor_tensor(ctv[:, j, :N], ustv[:, j, 0:N], wcv[:, 0, ci:ci + 1], ctv[:, j, :N], ALU.mult, ALU.add),
                     r=uk + [ckk, 'const'], w=[ckk], c=0.45)
            T.op('act', lambda e, ctv=ctv: e.activation(ctv[:, 1, :N], ctv[:, 1, :N], AF.Silu), r=[ckk], w=[ckk], c=0.45)
            T.op('pool', lambda e, u=u, ctv=ctv: e.tensor_tensor(actT[:, u, :N], ctv[:, 1, :N], ctv[:, 0, :N], ALU.mult), r=[ckk], w=['actT'], c=0.7)
        if first_macro:
            return
        accs = [(P6, 'P6'), (P7, 'P7')]
        NCH = NU // 2
        for si_, (col0, Tn, htile) in enumerate(segs):
            pass
        for n in range(2):
            for c in range(NCH):
                ds = dring_ctr[0] % 5
                dring_ctr[0] += 1
                T.dma('sp', 'wdr%d' % ds, lambda e, ds=ds, c=c, n=n: e.dma_start(
                    out=wdr[ds][:], in_=wdnbf[c].rearrange("p (k n) -> p k n", k=2)[:, :, n * 512:(n + 1) * 512]),
                    r=['wdnbf'], w=['wdr%d' % ds])
                def fn_d(e, ds=ds, c=c):
                    ins = None
                    for si_, (col0, Tn, htile) in enumerate(segs):
                        acc = accs[si_][0]
                        for kk in range(2):
                            kc = 2 * c + kk
                            ins = e.matmul(acc[:Tn, :], actT[:, kc, col0:col0 + Tn], wdr[ds][:, kk, :],
                                           start=(kc == 0), stop=(kc == NU - 1))
                    return ins
                T.op('pe', fn_d, r=['actT', 'wdr%d' % ds], w=[accs[i][1] for i in range(len(segs))], c=len(segs) * 2 * 0.31 + 0.1)
            for si_, (col0, Tn, htile) in enumerate(segs):
                acc, akey = accs[si_]
                s2 = s2b[si_]
                wk_ = [S2K[0][n]] if si_ == 0 else MIXK
                T.op('dve', lambda e, Tn=Tn, htile=htile, acc=acc, s2=s2, n=n: e.scalar_tensor_tensor(
                    s2[:Tn, n * 512:(n + 1) * 512], hres[:Tn, htile, n * 512:(n + 1) * 512], ALPHA, acc[:Tn, :], ALU.mult, ALU.add),
                    r=['hres%d_%d' % (par, htile), akey], w=wk_, c=0.7)
        for si_, (col0, Tn, htile) in enumerate(segs):
            s2 = s2b[si_]
            layer_norm(s2, S2K[si_], Tn, 1, s2, S2K[si_][-1] if si_ == 0 else 'mix')
            out_fn(Tn, si_)

    def load_cs(slot, idx):
        T.dma('sp', 'cs%d' % slot, lambda e: e.dma_start(out=cosb[slot][:], in_=cos_d[idx]), w=['cs%d' % slot])
        T.dma('sp', 'cs%d' % slot, lambda e: e.dma_start(out=sinb[slot][:], in_=sin_d[idx]), w=['cs%d' % slot])

    def do_main():
        n_macro = nt_full // 2
        xbase = 8192 - NT_FULL * 128
        yrow = [0]
        for mi in range(n_macro):
            for tt in range(2):
                i = 2 * mi + tt
                slot = i % 2
                if i == 0:
                    load_x(0, xall[xbase: xbase + 128, :], 128)
                    load_cs(0, 0)
                if i + 1 < nt_full:
                    load_x((i + 1) % 2, xall[xbase + (i + 1) * 128: xbase + (i + 2) * 128, :], 128)
                    load_cs((i + 1) % 2, i + 1)
                T.cur_tag = 's1_%02d' % mi
                kslot = i % 2
                pslot = (i + 1) % 2
                pbias = hs[:, 1:2] if i == 2 else (NEG if i == 0 else 0.0)
                last = (i == nt_full - 1)
                stage1(128, slot, 0, kslot, pslot, pbias, slot, tt * 128, tt, last, False, par=mi % 2)
                if i == 1:
                    T.op('dve', lambda e: e.tensor_scalar(mst[0][:], mst[0][:], hs[0:4, 0:1], None, ALU.mult), r=['m0', 'const'], w=['m0'])
                if last:
                    out_toks.append(T.dma('sp', 'okvp', lambda e: e.dma_start(out=kp_d[:, :], in_=kvout[:, 0, :]), r=['kvout']))
                    out_toks.append(T.dma('sp', 'okvp', lambda e: e.dma_start(out=vp_d[:, :], in_=kvout[:, 1, :]), r=['kvout']))

            def out_y(Tn, yslot):
                r0 = yrow[0]
                yrow[0] += Tn
                out_toks.append(T.dma('sp', 'oy%d' % yslot, lambda e: e.dma_start(out=y_d[r0:r0 + Tn, :], in_=s2b[yslot][:Tn, :]), r=S2K[yslot]))
            T.cur_tag = 's2_%02d' % mi
            stage2(256, [(0, 128, 0), (128, 128, 1)], cvh, 'cvh', mi == 0, out_y, par=mi % 2)

        out_toks.append(T.dma('sp', 'ostC', lambda e: e.dma_start(out=Cp_d.rearrange("h d e -> d h e"), in_=Cst[0][:, :, 0:128]), r=['C0']))
        out_toks.append(T.dma('sp', 'ostC', lambda e: e.dma_start(out=np_d.rearrange("h d -> d h"), in_=Cst[0][:, :, 128], allow_slow_non_contiguous=True), r=['C0']))
        out_toks.append(T.dma('sp', 'ostm', lambda e: e.dma_start(out=mp_d[:, :], in_=mst[0][:]), r=['m0']))
        for j in range(2):
            out_toks.append(T.dma('sp', 'ostcv', lambda e, j=j: e.dma_start(out=cvp_d[j, :].rearrange("(c p) -> p c", p=128), in_=cvh[:, :, j],
                                                                           allow_slow_non_contiguous=True), r=['cvh']))

    def do_sample():
        T.cur_tag = 'sample'
        load_x(0, xs_d[:, :], 16)
        load_cs(0, NT_FULL)
        T.dma('sp', 'xin1', lambda e: e.dma_start(out=xin[1][:, 0:128], in_=ck_d[:, :]), w=['xin1'])
        T.dma('sp', 'xin1', lambda e: e.dma_start(out=xin[1][:, 128:256], in_=cv_d[:, :]), w=['xin1'])
        T.op('pe', lambda e: e.transpose(P4[:, 0:128], xin[1][:, 0:128], ident[:, :]), r=['xin1', 'const'], w=['P4'])
        T.op('act', lambda e: e.copy(kTb[1][:, :], P4[:, 0:128]), r=['P4'], w=['kT_1'])
        T.op('act', lambda e: e.copy(va1[1][:, :, 0:64], xin[1][:, 128:256].rearrange("p (k d) -> p k d", k=2)), r=['xin1'], w=['va1_1'])
        T.op('dve', lambda e: e.memset(Cst[0][:], 0.0), w=['C0'])
        T.dma('sp', 'sstC', lambda e: e.dma_start(out=Cst[0][:, :, 0:128], in_=C0_d.rearrange("h d e -> d h e")), w=['C0'])
        T.dma('sp', 'sstC', lambda e: e.dma_start(
            out=Cst[0][:, :, 128], in_=n0_d.rearrange("h d -> d h"), allow_slow_non_contiguous=True), w=['C0'])
        T.dma('sp', 'sstm', lambda e: e.dma_start(out=mst[0][:], in_=m0_d[:, :]), w=['m0'])
        T.op('act', lambda e: e.copy(Cb[0][:], Cst[0][:]), r=['C0'], w=['Cb0'])
        stage1(16, 0, 0, 0, 1, 0.0, 0, 0, 0, True, True)
        out_toks.append(T.dma('sp', 'okvs', lambda e: e.dma_start(out=ks_d[:, :], in_=kvout[:16, 0, :]), r=['kvout']))
        out_toks.append(T.dma('sp', 'okvs', lambda e: e.dma_start(out=vs_d[:, :], in_=kvout[:16, 1, :]), r=['kvout']))

        def out_ys(Tn, yslot):
            out_toks.append(T.dma('sp', 'oy%d' % yslot, lambda e: e.dma_start(out=ys_d[:, :], in_=s2b[yslot][:Tn, :]), r=S2K[yslot]))
        stage2(16, [(0, 16, 0)], cvsb, 'cvsb', False, out_ys, par=0)
        out_toks.append(T.dma('sp', 'ossC', lambda e: e.dma_start(out=Cs_d.rearrange("h d e -> d h e"), in_=Cst[0][:, :, 0:128]), r=['C0']))
        out_toks.append(T.dma('sp', 'ossC', lambda e: e.dma_start(out=ns_d.rearrange("h d -> d h"), in_=Cst[0][:, :, 128], allow_slow_non_contiguous=True), r=['C0']))
        out_toks.append(T.dma('sp', 'ossm', lambda e: e.dma_start(out=ms_d[:, :], in_=mst[0][:]), r=['m0']))
        for j in range(2):
            out_toks.append(T.dma('sp', 'osscv', lambda e, j=j: e.dma_start(out=cvs_d[j, :].rearrange("(c p) -> p c", p=128), in_=cvsb[:, :, j],
                                                                           allow_slow_non_contiguous=True), r=['cvsb']))
    do_sample()
    T.cur_tag = 'setup2'
    T.op('dve', lambda e: e.memset(Cst[0][:], 0.0), w=['C0'])
    T.op('dve', lambda e: e.memset(Cb[0][:], 0.0), w=['Cb0'])
    T.op('dve', lambda e: e.memset(mst[0][:], 0.0), w=['m0'])
    do_prefix()
    do_main()
    T.wait_all('sp', out_toks)
    if FILL:
        T.filler_fn = lambda e: e.matmul(P5[:, 0:512], identb[:, :], win[:, 0, 0:512], start=True, stop=True)
        T.fill_frac = float(os.environ.get('K_FILL_FRAC', '0.5'))
        T.filler_cost = float(os.environ.get('K_FILL_COST', '0.3'))
        T.fill_min = float(os.environ.get('K_FILL_MIN', '1.5'))
    T.schedule()

    with nc.Block() as block:
        @block.sync
        def _(e):
            T.replay('sp', e)

        @block.tensor
        def _(e):
            T.replay('pe', e)

        @block.scalar
        def _(e):
            T.replay('act', e)

        @block.vector
        def _(e):
            T.replay('dve', e)

        @block.gpsimd
        def _(e):
            T.replay('pool', e)
    es.close()
    return nc


def _host_consts():
    ident = np.eye(128, dtype=np.float32)
    s = np.arange(128)[:, None]
    l = np.arange(128)[None, :]
    maskT = np.where(l >= s, 0.0, NEG).astype(np.float32)
    amask = np.ones((128, 2, 128), np.float32)
    amask[:, 0, :] = np.where((s < 64) & (l >= 64), 0.0, 1.0)
    amask[:, 1, :] = np.where((s >= 64) & (l < 64), 0.0, 1.0)
    sel4 = np.zeros((4, 4, 128), np.float32)
    for h in range(4):
        sel4[h, h, :] = 1.0
    return ident, maskT, amask, sel4.reshape(4, 512)


def _rope_tables(pos):
    half = 32
    inv = (np.float32(10000.0) ** (-np.arange(half, dtype=np.float32) / np.float32(half))).astype(np.float32)
    d = np.arange(128) % 64
    f = inv[d % 32]
    ang = (pos.astype(np.float32)[None, :] * f[:, None]).astype(np.float32)
    c = np.cos(ang).astype(np.float32)
    sn = np.sin(ang).astype(np.float32)
    sign = np.where(d < 32, -1.0, 1.0).astype(np.float32)[:, None]
    return c, (sn * sign).astype(np.float32)


_NC_CACHE = {}


def kernel(x_prompt, x_sample, state_mlstm_C, state_mlstm_n, state_mlstm_m, cache_swa_k, cache_swa_v,
           state_conv, w_in, b_igate, b_fgate, g_mlstm_norm, attn_sinks, w_out, ln1_g, ln1_b,
           w_up, w_conv, b_conv, w_down, ln2_g, ln2_b):
    f = lambda a: np.ascontiguousarray(np.asarray(a, dtype=np.float32))
    x_prompt, x_sample = f(x_prompt), f(x_sample)
    w_in0 = f(w_in)[0]
    rot = (np.arange(64) + 32) % 64
    qa0, ka0, va0 = 2056, 2568, 2696
    qap, qar = [], []
    for j in range(4):
        for hd in (j, 4 + j):
            qap.extend(qa0 + hd * 64 + np.arange(64))
            qar.extend(qa0 + hd * 64 + rot)
    kap = list(ka0 + np.arange(128))
    kar = list(ka0 + np.concatenate([rot, 64 + rot]))
    cols = list(range(0, 2056)) + list(range(va0, va0 + 128)) + qap + qar + kap + kar
    w_in_aug = np.ascontiguousarray(w_in0[:, np.array(cols, dtype=np.int64)])
    assert w_in_aug.shape[1] == WCOLS
    ident, maskT, amask, sel4 = _host_consts()
    ln = np.stack([f(ln1_g)[0], f(ln1_b)[0], f(ln2_g)[0], f(ln2_b)[0]], 0)

    nt_pre = int(os.environ.get("K_NT_PRE", NT_PRE))
    nt_full = int(os.environ.get("K_NT_FULL", NT_FULL))
    key = (nt_pre, nt_full)
    if key not in _NC_CACHE:
        _NC_CACHE[key] = build_program(nt_pre, nt_full)
    nc = _NC_CACHE[key]

    in_maps = []
    for c in range(8):
        b, half = c // 2, c % 2
        if half == 1:
            xall = x_prompt[b]
        else:
            xall = np.concatenate([np.zeros((4096, D), np.float32), x_prompt[b, :4096]], 0)
        pos0 = half * 4096 - 256
        cosT = np.zeros((NT_FULL + 1, 128, 128), np.float32)
        sinT = np.zeros((NT_FULL + 1, 128, 128), np.float32)
        for i in range(NT_FULL):
            cc, ss = _rope_tables(pos0 + i * 128 + np.arange(128))
            cosT[i], sinT[i] = cc, ss
        cc, ss = _rope_tables(2048 + np.arange(16))
        cosT[NT_FULL, :, :16], sinT[NT_FULL, :, :16] = cc, ss
        hsv = np.zeros((128, 2), np.float32)
        hsv[:, 0] = float(half)
        hsv[:, 1] = 0.0 if half == 1 else NEG
        in_maps.append({
            "xall": np.ascontiguousarray(xall), "xs": x_sample[c],
            "C0": f(state_mlstm_C)[0, c], "n0": f(state_mlstm_n)[0, c], "m0": f(state_mlstm_m)[0, c].reshape(4, 1),
            "ck": f(cache_swa_k)[0, c].reshape(128, 128), "cv": f(cache_swa_v)[0, c].reshape(128, 128),
            "sconv": f(state_conv)[0, c],
            "w_in": w_in_aug, "w_out": f(w_out)[0], "w_up": f(w_up)[0], "w_down": f(w_down)[0],
            "w_conv": f(w_conv)[0], "b_conv": f(b_conv)[0].reshape(1, -1),
            "b_i": f(b_igate)[0].reshape(4, 1), "b_f": f(b_fgate)[0].reshape(4, 1),
            "g_norm": f(g_mlstm_norm)[0].reshape(1, 512), "sinks": f(attn_sinks)[0].reshape(1, 8),
            "ln": ln, "cosT": cosT, "sinT": sinT, "ident": ident, "maskT": maskT, "amask": amask,
            "sel4": sel4, "hs": hsv,
        })
    res = run_bass_kernel_spmd(nc, in_maps, core_ids=list(range(8)))
    R = res.results
    y_p = np.zeros((4, 8192, D), np.float32)
    for c in range(8):
        y_p[c // 2, (c % 2) * 4096:(c % 2 + 1) * 4096] = R[c]["y"]
    y_s = np.stack([R[c]["ys"] for c in range(8)], 0)
    odd = [1, 3, 5, 7]
    C_p = np.stack([R[c]["Cp"] for c in odd], 0)[None]
    n_p = np.stack([R[c]["np"] for c in odd], 0)[None]
    m_p = np.stack([R[c]["mp"].reshape(4) for c in odd], 0)[None]
    k_p = np.stack([R[c]["kp"].reshape(128, 2, 64) for c in odd], 0)[None]
    v_p = np.stack([R[c]["vp"].reshape(128, 2, 64) for c in odd], 0)[None]
    cv_p = np.stack([R[c]["cvp"] for c in odd], 0)[None]
    C_s = np.stack([R[c]["Cs"] for c in range(8)], 0)[None]
    n_s = np.stack([R[c]["ns"] for c in range(8)], 0)[None]
    m_s = np.stack([R[c]["ms"].reshape(4) for c in range(8)], 0)[None]
    k_s = np.stack([R[c]["ks"].reshape(16, 2, 64) for c in range(8)], 0)[None]
    v_s = np.stack([R[c]["vs"].reshape(16, 2, 64) for c in range(8)], 0)[None]
    cv_s = np.stack([R[c]["cvs"] for c in range(8)], 0)[None]
    return (y_p, y_s, C_p, n_p, m_p, k_p, v_p, cv_p, C_s, n_s, m_s, k_s, v_s, cv_s)
```

```python
import os
from contextlib import ExitStack
import numpy as np
import concourse.bass as bass
import concourse.mybir as mybir
from concourse.bass_utils import run_bass_kernel_spmd

F32 = mybir.dt.float32
BF16 = mybir.dt.bfloat16
AF = mybir.ActivationFunctionType
ALU = mybir.AluOpType
AX = mybir.AxisListType

D = 1024
NT_PRE = 30
NT_FULL = 34
DFF = 2816
NU = 22
WCOLS = 3464
C_QM, C_KM, C_VM, C_OM, C_IP, C_FP, C_VA, C_QAP, C_QAR, C_KAP, C_KAR = 0, 512, 1024, 1536, 2048, 2052, 2056, 2184, 2696, 3208, 3336
ALPHA = float(2.0 ** 0.25)
EPS = 1e-5
KSCALE = float(128.0 ** -0.5)
NEG = -30000.0


SKIP_WAR = int(os.environ.get('K_SKIP_WAR', '1'))


class Tracker:
    HOP = 0.4
    DEF_COST = {'pe': 0.7, 'act': 0.45, 'dve': 0.3, 'pool': 0.8, 'sp': 0.08}

    def __init__(self, nc, es):
        self.nc = nc
        self.es = es
        self.engs = {}
        self.last_w = {}
        self.readers = {}
        self.streams = {}
        self.ops = []
        self.thr_hist = {}
        self.cur_tag = 'setup'
        self.filler_fn = None
        self.pe_scale = float(os.environ.get('K_PE_SCALE', '0.65'))
        self.ad_scale = float(os.environ.get('K_AD_SCALE', '1.1'))
        self.HOP = float(os.environ.get('K_HOP', '0.9'))
        self.hop_same = float(os.environ.get('K_HOP_SAME', '0.3'))
        self.filler_cost = 0.43
        self.fill_min = 1.5
        self.fill_frac = 0.6

    def add_engine(self, name, eng):
        sem = self.es.enter_context(self.nc.semaphore("s_" + name))
        self.engs[name] = dict(eng=eng, sem=sem, name=name, order=[])

    def stream(self, name, group=False):
        if name not in self.streams:
            sem = self.es.enter_context(self.nc.semaphore("d_" + name))
            self.streams[name] = dict(sem=sem, count=0, group=group, name=name)
        return self.streams[name]

    def _deps(self, reads, writes, extra, group):
        deps = set()
        for k in reads:
            deps.update(self.last_w.get(k, ()))
        self._raw = set(deps)
        if not group:
            for k in writes:
                deps.update(self.last_w.get(k, ()))
                deps.update(self.readers.get(k, ()))
        deps.update(extra)
        return deps

    def _commit(self, oid, reads, writes, group):
        for k in reads:
            self.readers.setdefault(k, []).append(oid)
        for k in writes:
            if group:
                self.last_w.setdefault(k, []).append(oid)
            else:
                self.last_w[k] = [oid]
                self.readers[k] = []

    def op(self, ename, fn, r=(), w=(), extra=(), c=None):
        deps = self._deps(r, w, extra, False)
        oid = len(self.ops)
        cc = self.DEF_COST[ename] if c is None else c
        if ename == 'pe':
            cc *= self.pe_scale
        elif ename in ('act', 'dve', 'pool'):
            cc *= self.ad_scale
        self.ops.append(dict(id=oid, eng=ename, fn=fn, deps=deps, dma=None, tag=self.cur_tag, cost=cc, raw=self._raw | set(extra)))
        self._commit(oid, r, w, False)
        return oid

    def dma(self, qname, stream, fn, r=(), w=(), extra=(), group=False, c=4.0):
        S = self.stream(stream, group)
        deps = self._deps(r, w, extra, group)
        oid = len(self.ops)
        thr = ()
        if group:
            hist = self.thr_hist.setdefault(qname, [])
            B = 3 if qname == 'pool' else 4
            nb = len(hist) // B
            if nb > 0:
                thr = tuple(hist[(nb - 1) * B:nb * B])
                deps.update(thr)
            hist.append(oid)
        self.ops.append(dict(id=oid, eng=qname, fn=fn, deps=deps, dma=S, cost=c, thr=thr, tag=self.cur_tag))
        self._commit(oid, r, w, group)
        return oid

    def wait_all(self, ename, toks):
        oid = len(self.ops)
        self.ops.append(dict(id=oid, eng=ename, fn=None, deps=set(toks), dma=None, cost=0.05, tag='final'))
        return oid

    def schedule(self):
        import heapq
        ops = self.ops
        n = len(ops)
        ndeps = [len(o['deps']) for o in ops]
        users = [[] for _ in range(n)]
        for o in ops:
            for d in o['deps']:
                users[d].append(o['id'])
        ready_t = [0.0] * n
        fin = [0.0] * n
        bl = [0.0] * n
        for o in reversed(ops):
            i = o['id']
            m = 0.0
            for u in users[i]:
                if bl[u] > m:
                    m = bl[u]
            bl[i] = m + o['cost'] + self.HOP
        mode = os.environ.get('K_PRIO', 'id')
        if mode == 'bl':
            prio = [(-bl[i], i) for i in range(n)]
        else:
            prio = [(i, i) for i in range(n)]
        ready = {e: [] for e in self.engs}
        free_t = {e: 0.0 for e in self.engs}
        events = []
        for o in ops:
            if ndeps[o['id']] == 0:
                heapq.heappush(ready[o['eng']], (prio[o['id']], o['id']))
        for e in self.engs:
            heapq.heappush(events, (0.0, 0, e))
        seq = 1
        done = 0
        pending_wake = {e: True for e in self.engs}
        while done < n:
            if not events:
                raise RuntimeError("scheduler stuck")
            t, _, e = heapq.heappop(events)
            pending_wake[e] = False
            if free_t[e] > t + 1e-9:
                heapq.heappush(events, (free_t[e], seq, e)); seq += 1; pending_wake[e] = True
                continue
            cand = None
            tmp = []
            while ready[e]:
                item = heapq.heappop(ready[e])
                oid = item[1]
                if ready_t[oid] <= t + 1e-9:
                    cand = oid
                    break
                tmp.append(item)
            for x in tmp:
                heapq.heappush(ready[e], x)
            if cand is None:
                if ready[e]:
                    tn = min(ready_t[x[1]] for x in ready[e])
                    heapq.heappush(events, (tn, seq, e)); seq += 1; pending_wake[e] = True
                continue
            o = ops[cand]
            o['t0'] = t
            self.engs[e]['order'].append(cand)
            if o['dma'] is not None:
                free_t[e] = t + (0.6 if e == 'pool' else 0.08)
                fin[cand] = t + o['cost']
            else:
                free_t[e] = t + o['cost']
                fin[cand] = free_t[e]
            done += 1
            for u in users[cand]:
                ndeps[u] -= 1
                if ops[u]['eng'] == e and o['dma'] is None:
                    if e == 'pe' or (SKIP_WAR and e in ('act', 'dve') and cand not in ops[u].get('raw', ops[u]['deps'])):
                        rt = fin[cand]
                    else:
                        rt = fin[cand] + self.hop_same
                else:
                    rt = fin[cand] + self.HOP
                if rt > ready_t[u]:
                    ready_t[u] = rt
                if ndeps[u] == 0:
                    ue = ops[u]['eng']
                    heapq.heappush(ready[ue], (prio[u], u))
                    if not pending_wake[ue]:
                        heapq.heappush(events, (max(ready_t[u], free_t[ue]), seq, ue)); seq += 1; pending_wake[ue] = True
            heapq.heappush(events, (free_t[e], seq, e)); seq += 1; pending_wake[e] = True
        self.sim_time = max(fin)
        self.fin = fin
        if self.filler_fn is not None:
            fc = self.filler_cost
            new_order = []
            prev_end = None
            nfill = 0
            armed = False
            for oid in self.engs['pe']['order']:
                o = ops[oid]
                if not armed:
                    if any(ops[d]['dma'] is not None and ops[d]['dma']['name'] == 'win' for d in o['deps']):
                        armed = True
                    new_order.append(oid)
                    prev_end = fin[oid]
                    continue
                if prev_end is not None:
                    gap = o['t0'] - prev_end
                    if gap > self.fill_min:
                        k = int((gap * self.fill_frac) / fc)
                        for _ in range(k):
                            fid = len(ops)
                            ops.append(dict(id=fid, eng='pe', fn=self.filler_fn, deps=set(), dma=None, cost=fc, tag='fill', filler=True))
                            new_order.append(fid)
                            nfill += 1
                new_order.append(oid)
                prev_end = fin[oid]
            self.engs['pe']['order'] = new_order
            self.nfill = nfill
        for e, E in self.engs.items():
            pos = 0
            for oid in E['order']:
                o = ops[oid]
                if o['dma'] is not None:
                    o['dma']['count'] += 16
                    o['val'] = o['dma']['count']
                elif o['fn'] is not None and not o.get('filler'):
                    pos += 1
                    o['val'] = pos

    def replay(self, ename, eng):
        E = self.engs[ename]
        ops = self.ops
        seen = {}
        for oid in E['order']:
            o = ops[oid]
            need = {}
            for d in o['deps']:
                D = ops[d]
                if D['dma'] is not None:
                    S = D['dma']
                    if d in o.get('thr', ()):
                        sem, val = S['sem'], D['val']
                    else:
                        sem, val = S['sem'], (S['count'] if S['group'] else D['val'])
                else:
                    if D['fn'] is None:
                        continue
                    if D['eng'] == 'pe' and ename == 'pe':
                        continue
                    if SKIP_WAR and D['eng'] == ename and ename in ('act', 'dve') and d not in o.get('raw', o['deps']):
                        continue
                    sem, val = self.engs[D['eng']]['sem'], D['val']
                if need.get(sem, 0) < val:
                    need[sem] = val
            for sem, val in need.items():
                if seen.get(sem, 0) < val:
                    eng.wait_ge(sem, val)
                    seen[sem] = val
            if o['fn'] is None:
                continue
            ins = o['fn'](eng)
            if o.get('filler'):
                continue
            if o['dma'] is not None:
                ins.then_inc(o['dma']['sem'], 16)
            else:
                ins.then_inc(E['sem'], 1)


def build_program(nt_pre=NT_PRE, nt_full=NT_FULL):
    nc = bass.Bass("TRN2", target_bir_lowering=False)
    es = ExitStack()

    def din(name, shape, dt=F32):
        return nc.dram_tensor(name, list(shape), dt, kind="ExternalInput").ap()

    def dout(name, shape, dt=F32):
        return nc.dram_tensor(name, list(shape), dt, kind="ExternalOutput").ap()

    xall = din("xall", [8192, D])
    xs_d = din("xs", [16, D])
    C0_d = din("C0", [4, 128, 128])
    n0_d = din("n0", [4, 128])
    m0_d = din("m0", [4, 1])
    ck_d = din("ck", [128, 128])
    cv_d = din("cv", [128, 128])
    sconv_d = din("sconv", [2, 2 * DFF])
    win_d = din("w_in", [D, WCOLS])
    wout_d = din("w_out", [D, D])
    wup_d = din("w_up", [D, 2 * DFF])
    wdn_d = din("w_down", [DFF, D])
    wconv_d = din("w_conv", [3, 2 * DFF])
    bconv_d = din("b_conv", [1, 2 * DFF])
    bi_d = din("b_i", [4, 1])
    bf_d = din("b_f", [4, 1])
    gn_d = din("g_norm", [1, 512])
    sink_d = din("sinks", [1, 8])
    ln_d = din("ln", [4, D])
    cos_d = din("cosT", [NT_FULL + 1, 128, 128])
    sin_d = din("sinT", [NT_FULL + 1, 128, 128])
    ident_d = din("ident", [128, 128])
    maskT_d = din("maskT", [128, 128])
    amask_d = din("amask", [128, 2, 128])
    sel4_d = din("sel4", [4, 4 * 128])
    hs_d = din("hs", [128, 2])

    y_d = dout("y", [4096, D])
    ys_d = dout("ys", [16, D])
    Cp_d = dout("Cp", [4, 128, 128])
    np_d = dout("np", [4, 128])
    mp_d = dout("mp", [4, 1])
    kp_d = dout("kp", [128, 128])
    vp_d = dout("vp", [128, 128])
    cvp_d = dout("cvp", [2, 2 * DFF])
    Cs_d = dout("Cs", [4, 128, 128])
    ns_d = dout("ns", [4, 128])
    ms_d = dout("ms", [4, 1])
    ks_d = dout("ks", [16, 128])
    vs_d = dout("vs", [16, 128])
    cvs_d = dout("cvs", [2, 2 * DFF])
    wupbf = nc.dram_tensor("wupbf", [NU, 128, 8 * 256], BF16, kind="Internal").ap()
    wdnbf = nc.dram_tensor("wdnbf", [NU // 2, 128, 2 * D], BF16, kind="Internal").ap()

    def sb(name, shape, dt=F32):
        return es.enter_context(nc.sbuf_tensor(name, list(shape), dt))

    def pt(name, shape, dt=F32):
        return es.enter_context(nc.psum_tensor(name, list(shape), dt))

    win = sb("win", [128, 8, WCOLS], BF16)
    wout = sb("wout", [128, 8, D], BF16)
    wdr = [sb("wdr%d" % i, [128, 2, 512], BF16) for i in range(5)]
    NS = 2
    ustS = [sb("ustS%d" % i, [128, 2, 258]) for i in range(NS)]
    ctS = [sb("ctS%d" % i, [128, 2, 256]) for i in range(NS)]
    wur = [sb("wur%d" % i, [128, 8, 256], BF16) for i in range(2)]
    lnp = sb("lnp", [128, 4, D])
    gnb = sb("gnb", [128, 512])
    esink = sb("esink", [128, 8])
    ident = sb("ident_s", [128, 128])
    maskT = sb("maskT_s", [128, 128])
    amask = sb("amask_s", [128, 2, 128], BF16)
    sel4 = sb("sel4_s", [4, 4 * 128])
    hs = sb("hs_s", [128, 2])
    bi = sb("bi_s", [4, 1])
    nbf = sb("nbf_s", [4, 1])
    wcv = sb("wcv", [128, 3, 2 * NU])
    bcv = sb("bcv", [128, 2 * NU])
    cvh = sb("cvh", [128, 2 * NU, 2])
    cvsb = sb("cvsb", [128, 2 * NU, 2])
    ones4 = sb("ones4", [4, 128])
    onesr = ones4

    xin = [sb("xin%d" % i, [128, D]) for i in range(2)]
    xT = sb("xT", [128, 8, 128], BF16)
    mixT = xT
    kmtok = sb("kmtok", [128, 4, 128], BF16)
    kw = kmtok
    v1 = sb("v1", [128, 4, 130], BF16)
    qmT = sb("qmT", [128, 4, 128], BF16)
    qaT = sb("qaT", [128, 4, 128], BF16)
    kmT = sb("kmT", [128, 4, 128], BF16)
    cosb = [sb("cos%d" % i, [128, 128]) for i in range(2)]
    sinb = [sb("sin%d" % i, [128, 128]) for i in range(2)]
    nd1 = sb("nd1", [128, 4, 130])
    nd = nd1
    rt1 = sb("rt1", [128, 4, 128])
    Ebuf = sb("Ebuf", [128, 4, 128])
    hmr = Ebuf
    rt2 = sb("rt2", [128, 4, 128])
    krope = sb("krope", [128, 128])
    kTb = [sb("kTb%d" % i, [128, 128], BF16) for i in range(2)]
    va1 = [sb("va1%d" % i, [128, 2, 66], BF16) for i in range(2)]
    kvout = sb("kvout", [128, 2, 128])
    ST = sb("ST", [128, 4, 128], BF16)
    Cst = [sb("Cst0", [128, 4, 130])] * 2
    Cb = [sb("Cb0", [128, 4, 130], BF16)] * 2
    mst = [sb("mst0", [4, 1])] * 2
    bst = sb("bst", [128, 4, 6])
    mv = sb("mv", [128, 4, 2])
    sm = sb("sm", [128, 32])
    lnscr = [(sb("bstL%d" % i, [128, 2, 6]), sb("mvL%d" % i, [128, 2]), sb("smL%d" % i, [128, 2])) for i in range(2)]
    pTraw = sb("pTraw", [128, 1024])
    pT = pTraw[:].bitcast(BF16).rearrange("p (b h t) -> p b h t", b=2, h=8)
    og = sb("og", [128, 512])
    mix = sb("mix", [128, D])
    mixb = sb("mixb", [128, D], BF16)
    identb = sb("identb", [128, 128], BF16)
    s1 = mix
    MIXK = ['mix']
    hresL = [sb("hres%d" % i, [128, 2, D]) for i in range(2)]
    hTL = [sb("hT%d" % i, [128, 8, 256], BF16) for i in range(2)]
    actT = sb("actT", [128, NU, 256], BF16)
    G = sb("G", [4, 7, 128])
    tm = sb("tm", [128, 16])
    dbc = sb("dbc", [128, 4])
    gsm = sb("gsm", [4, 8])

    P01 = pt("P01", [128, 1024])
    P23 = pt("P23", [128, 1024])
    P4 = pt("P4", [128, 512])
    P5 = pt("P5", [128, 512])
    P6 = pt("P6", [128, 512])
    P7 = pt("P7", [128, 512])
    PA0 = P01[:, 0:512]
    PA1 = P01[:, 512:1024]
    P0b = PA0.bitcast(BF16)
    P1b = PA1.bitcast(BF16)

    T = Tracker(nc, es)
    FILL = int(os.environ.get('K_FILL', '1'))
    T.add_engine('sp', nc.sync)
    T.add_engine('pe', nc.tensor)
    T.add_engine('act', nc.scalar)
    T.add_engine('dve', nc.vector)
    T.add_engine('pool', nc.gpsimd)

    def ld(q, stream, out, in_, w, r=()):
        return T.dma(q, stream, lambda e, o=out, i=in_: e.dma_start(out=o, in_=i), r=r, w=w, group=True, c=6.0)

    for k in range(8):
        for c0 in (0, 1732):
            ld('pool', 'win', win[:, k, c0:c0 + 1732], win_d[k * 128:(k + 1) * 128, c0:c0 + 1732], w=['win'])
    ld('sp', 'small', ident[:], ident_d[:, :], w=['const'])
    ld('sp', 'small', maskT[:], maskT_d[:, :], w=['const'])
    ld('pool', 'small2', amask[:], amask_d[:, :, :], w=['c5'])
    ld('sp', 'small', sel4[:], sel4_d[:, :], w=['const'])
    ld('sp', 'small', hs[:], hs_d[:, :], w=['const'])
    ld('sp', 'small', bi[:], bi_d[:, :], w=['const'])
    ld('sp', 'small', nbf[:], bf_d[:, :], w=['const'])
    ld('sp', 'small', gnb[:], gn_d[0, :].partition_broadcast(128), w=['const'])
    ld('sp', 'small', esink[:], sink_d[0, :].partition_broadcast(128), w=['const'])
    for j in range(4):
        ld('sp', 'small', lnp[:, j, :], ln_d[j, :].partition_broadcast(128), w=['const'])
    for j in range(3):
        T.dma('sp', 'small', lambda e, j=j: e.dma_start(
            out=wcv[:, j, :], in_=wconv_d[j, :].rearrange("(c p) -> p c", p=128), allow_slow_non_contiguous=True), w=['const'], group=True, c=20.0)
    T.dma('sp', 'small', lambda e: e.dma_start(
        out=bcv[:], in_=bconv_d[0, :].rearrange("(c p) -> p c", p=128), allow_slow_non_contiguous=True), w=['const'], group=True, c=20.0)
    for j in range(2):
        T.dma('sp', 'small', lambda e, j=j: e.dma_start(
            out=cvsb[:, :, j], in_=sconv_d[j, :].rearrange("(c p) -> p c", p=128), allow_slow_non_contiguous=True), w=['cvsb'], group=True, c=20.0)
    for k in range(8):
        ld('pool', 'wout', wout[:, k, :], wout_d[k * 128:(k + 1) * 128, :], w=['wout'])
    for u in range(NU):
        for j in range(2):
            c0 = j * DFF + u * 128
            T.dma('pool', 'wupc', lambda e, u=u, j=j, c0=c0: e.dma_start(
                out=wupbf[u].rearrange("p (k c) -> p k c", k=8)[:, :, j * 128:(j + 1) * 128],
                in_=wup_d[:, c0:c0 + 128].rearrange("(k p) c -> p k c", p=128)), w=['wupbf'], group=True, c=8.0)
    for c in range(NU // 2):
        T.dma('pool', 'wdnc', lambda e, c=c: e.dma_start(
            out=wdnbf[c].rearrange("p (k n) -> p k n", k=2),
            in_=wdn_d[c * 256:(c + 1) * 256, :].rearrange("(k p) n -> p k n", p=128)), w=['wdnbf'], group=True, c=8.0)

    T.op('dve', lambda e: e.memset(ones4[:], 1.0), w=['c2'])
    T.op('dve', lambda e: e.tensor_scalar(nbf[:], nbf[:], -1.0, None, ALU.mult), r=['const'], w=['c3'])
    T.op('act', lambda e: e.activation(esink[:], esink[:], AF.Exp), r=['const'], w=['c4'])
    T.op('dve', lambda e: e.tensor_copy(identb[:], ident[:]), r=['const'], w=['c6'])
    for i in range(2):
        T.op('dve', lambda e, i=i: e.memset(v1[:, :, 128:130], 1.0), w=['v1'])
    for i in range(2):
        T.op('dve', lambda e, i=i: e.memset(va1[i][:], 1.0), w=['va1_%d' % i])
        T.op('dve', lambda e, i=i: e.memset(kTb[i][:], 0.0), w=['kT_%d' % i])
    T.op('dve', lambda e: e.memset(cvh[:], 0.0), w=['cvh'])
    CONSTS = ['const', 'c2', 'c3', 'c4', 'c5', 'c6']

    def transposes8(src, Tn, dst_ps, rkeys):
        def fn(e):
            ins = None
            for k in range(8):
                ins = e.transpose(dst_ps[:, k * 128:k * 128 + Tn], src[:Tn, k * 128:(k + 1) * 128], ident[:Tn, :Tn])
            return ins
        T.op('pe', fn, r=list(rkeys) + CONSTS, w=['P0', 'P1'], c=2.0)

    def load_x(slot, src_ap, Tn):
        return T.dma('sp', 'xin%d' % slot, lambda e: e.dma_start(out=xin[slot][:Tn, :], in_=src_ap), w=['xin%d' % slot])

    def make_xT(slot, Tn):
        transposes8(xin[slot], Tn, P01, ['xin%d' % slot])
        for b in range(2):
            T.op('act', lambda e, b=b: e.copy(
                xT[:, 4 * b:4 * b + 4, :Tn],
                P01[:, 512 * b:512 * b + 512].rearrange("p (k t) -> p k t", k=4)[:, :, :Tn]),
                r=['P%d' % b], w=['xT'])

    def mm_A(ps, col0, ncols, Tn, wkey, bankkeys):
        def fn(e):
            ins = None
            for k in range(8):
                ins = e.matmul(ps[:Tn, :ncols], xT[:, k, :Tn], win[:, k, col0:col0 + ncols], start=(k == 0), stop=(k == 7))
            return ins
        T.op('pe', fn, r=['xT', 'win'], w=bankkeys, c=8 * (0.09 + max(ncols, 64) / 2400.0) + 0.1)

    def mm_B(ps, col0, mcols, Tn, bankkeys):
        def fn(e):
            ins = None
            for k in range(8):
                ins = e.matmul(ps[:mcols, :Tn], win[:, k, col0:col0 + mcols], xT[:, k, :Tn], start=(k == 0), stop=(k == 7))
            return ins
        T.op('pe', fn, r=['xT', 'win'], w=bankkeys, c=8 * 0.13 + 0.1)

    def gate_chain(Tn, si, full):
        m = mst[si]
        mk = 'm%d' % si
        ipT = P4[0:4, 128:128 + Tn]
        fpT = P4[0:4, 256:256 + Tn]
        R = lambda j: G[:, j, :Tn]
        T.op('act', lambda e: e.activation(R(0), fpT, AF.Exp, bias=nbf[:], scale=-1.0), r=['P4', 'c3'], w=['G0'])
        T.op('act', lambda e: e.activation(R(0), R(0), AF.Ln, bias=1.0), r=['G0'], w=['G0'])
        T.op('dve', lambda e: e.tensor_tensor_scan(R(1), onesr[:, :Tn], R(0), 0.0, ALU.mult, ALU.subtract), r=['G0', 'c2'], w=['G1'])
        T.op('dve', lambda e: e.scalar_tensor_tensor(R(2), ipT, bi[:], R(1), ALU.add, ALU.subtract), r=['P4', 'G1', 'const'], w=['G2'])
        T.op('dve', lambda e: e.tensor_tensor_scan(R(3), R(2), R(2), m[:], ALU.max, ALU.max), r=['G2', mk], w=['G3'])
        T.op('act', lambda e: e.activation(R(4), R(3), AF.Exp, bias=m[:], scale=-1.0), r=['G3', mk], w=['G4'])
        T.op('dve', lambda e: e.tensor_scalar(gsm[:, 0:1], G[:, 3, Tn - 1:Tn], -1.0, None, ALU.mult), r=['G3'], w=['gsm0'])
        T.op('act', lambda e: e.activation(R(5), R(2), AF.Exp, bias=gsm[:, 0:1]), r=['G2', 'gsm0'], w=['G5'])
        T.op('dve', lambda e: e.tensor_tensor(R(6), R(1), R(3), ALU.add), r=['G1', 'G3'], w=['G6'])
        if full:
            T.op('act', lambda e: e.activation(R(0), R(6), AF.Exp, scale=-1.0), r=['G6'], w=['G0'])
            T.op('dve', lambda e: e.tensor_scalar(R(1), R(3), -1.0, None, ALU.mult), r=['G3'], w=['G1'])
        T.op('dve', lambda e: e.tensor_scalar(gsm[:, 4:8], ident[0:4, 0:4], G[:, 4, Tn - 1:Tn], None, ALU.mult), r=['G4', 'const'], w=['gsm1'])
        T.op('dve', lambda e: e.tensor_copy(m[:], G[:, 6, Tn - 1:Tn]), r=['G6'], w=[mk])

        def fn(e):
            ins = None
            rows = [(2, 0), (4, 4), (0, 8), (5, 12)] if full else [(5, 12)]
            for (rj, c) in rows:
                ins = e.matmul(P4[:Tn, 384 + c:384 + c + 4], G[:, rj, :Tn], ident[0:4, 0:4], start=True, stop=True)
            ins = e.matmul(P4[:, 448:452], ones4[:, :], gsm[:, 4:8], start=True, stop=True)
            return ins
        T.op('pe', fn, r=['G2', 'G4', 'G0', 'G5', 'gsm1', 'c2', 'const'], w=['P4'])
        if full:
            T.op('dve', lambda e: e.tensor_copy(tm[:Tn, :], P4[:Tn, 384:400]), r=['P4'], w=['tm'])
        else:
            T.op('dve', lambda e: e.tensor_copy(tm[:Tn, 12:16], P4[:Tn, 396:400]), r=['P4'], w=['tm'])
        T.op('dve', lambda e: e.tensor_copy(dbc[:], P4[:, 448:452]), r=['P4'], w=['dbc'])

    def state_update(Tn, si):
        C = Cst[si]
        ck = 'C%d' % si
        T.op('dve', lambda e: e.tensor_tensor(kw[:Tn], kmtok[:Tn], tm[:Tn, 12:16].unsqueeze(2).to_broadcast([Tn, 4, 128]), ALU.mult),
             r=['kmtok', 'tm'], w=['kmtok'])

        def fn(e):
            ins = None
            for h in range(4):
                ins = e.matmul(P23[:, h * 256:h * 256 + 129], kw[:Tn, h, :], v1[:Tn, h, 0:129], start=True, stop=True)
            return ins
        T.op('pe', fn, r=['kmtok', 'v1'], w=['P2', 'P3'])
        for h in range(4):
            T.op('dve', lambda e, h=h: e.scalar_tensor_tensor(C[:, h, 0:129], C[:, h, 0:129], dbc[:, h:h + 1],
                                                              P23[:, h * 256:h * 256 + 129], ALU.mult, ALU.add),
                 r=['dbc', 'P2', 'P3', ck], w=[ck])

    out_toks = []

    def do_prefix():
        tok_x = {}
        if nt_pre > 0:
            tok_x[0] = load_x(0, xall[0:128, :], 128)
        for i in range(nt_pre):
            T.cur_tag = 'pre'
            slot = i % 2
            if i + 1 < nt_pre:
                load_x((i + 1) % 2, xall[(i + 1) * 128:(i + 2) * 128, :], 128)
            make_xT(slot, 128)
            mm_A(P23[:, 0:512], C_KM, 512, 128, 'win', ['P2'])
            mm_A(P23[:, 512:1024], C_VM, 512, 128, 'win', ['P3'])
            mm_B(P4[:, 128:256], C_IP, 4, 128, ['P4'])
            mm_B(P4[:, 256:384], C_FP, 4, 128, ['P4'])
            T.op('act', lambda e: e.activation(kmtok[:].rearrange("p h d -> p (h d)"), P23[:, 0:512], AF.Copy, scale=KSCALE), r=['P2'], w=['kmtok'])
            T.op('act', lambda e: e.copy(v1[:, :, 0:128], P23[:, 512:1024].rearrange("p (h d) -> p h d", h=4)), r=['P3'], w=['v1'])
            gate_chain(128, 0, False)
            state_update(128, 0)


    def stage1(Tn, xslot, si, kslot, pslot, prev_bias, cs_slot, hcol, htile, save_kv, is_sample, par=0):
        hres = hresL[par]
        hT = hTL[par]
        hrk = 'hres%d_%d' % (par, htile)
        htk = 'hT%d' % par
        C = Cst[si]
        ck = 'C%d' % si
        cbk = 'Cb%d' % si
        make_xT(xslot, Tn)
        mm_A(P23[:, 0:512], C_KM, 512, Tn, 'win', ['P2'])
        mm_A(P23[:, 512:1024], C_VM, 512, Tn, 'win', ['P3'])
        mm_A(PA1[:, 0:512], C_OM, 512, Tn, 'win', ['P1'])
        mm_A(P4[:, 0:128], C_VA, 128, Tn, 'win', ['P4'])
        mm_B(P4[:, 128:256], C_IP, 4, Tn, ['P4'])
        mm_B(P4[:, 256:384], C_FP, 4, Tn, ['P4'])
        T.op('act', lambda e: e.activation(kmtok[:Tn].rearrange("p h d -> p (h d)"), P23[:Tn, 0:512], AF.Copy, scale=KSCALE), r=['P2'], w=['kmtok'])
        T.op('act', lambda e: e.copy(v1[:Tn, :, 0:128], P23[:Tn, 512:1024].rearrange("p (h d) -> p h d", h=4)), r=['P3'], w=['v1'])
        T.op('act', lambda e: e.activation(og[:Tn], PA1[:Tn, :], AF.Exp, scale=-1.0), r=['P1'], w=['og'])
        T.op('act', lambda e: e.activation(og[:Tn], og[:Tn], AF.Ln, bias=1.0), r=['og'], w=['og'], c=0.55)
        T.op('act', lambda e: e.activation(og[:Tn], og[:Tn], AF.Exp, scale=-1.0), r=['og'], w=['og'], c=0.55)
        T.op('pool', lambda e: e.tensor_tensor(og[:Tn], og[:Tn], gnb[:Tn], ALU.mult), r=['og', 'const'], w=['og'], c=1.1)
        T.op('act', lambda e: e.copy(va1[kslot][:Tn, :, 0:64], P4[:Tn, 0:128].rearrange("p (k d) -> p k d", k=2)), r=['P4'], w=['va1_%d' % kslot])
        if save_kv:
            T.op('dve', lambda e: e.tensor_copy(kvout[:Tn, 1, :], P4[:Tn, 0:128]), r=['P4'], w=['kvout'])
        gate_chain(Tn, si, True)
        def fn_rb(e):
            ins = None
            for h in range(4):
                ins = e.matmul(P4[:Tn, h * 128:h * 128 + Tn], sel4[:, h * 128:h * 128 + Tn], G[:, 1, :Tn], start=True, stop=True)
            return ins
        T.op('pe', fn_rb, r=['G1', 'const'], w=['P4'], c=1.0)
        T.op('dve', lambda e: e.tensor_tensor(Ebuf[:Tn, :, :Tn], P4[:Tn, :].rearrange("p (h t) -> p h t", h=4)[:, :, :Tn],
                                              maskT[:Tn, :Tn].unsqueeze(1).to_broadcast([Tn, 4, Tn]), ALU.add),
             r=['P4', 'const'], w=['Ebuf'])
        for h in range(4):
            T.op('act', lambda e, h=h: e.activation(Ebuf[:Tn, h, :Tn], Ebuf[:Tn, h, :Tn], AF.Exp, bias=tm[:Tn, h:h + 1]), r=['Ebuf', 'tm'], w=['Ebuf'])
        for mt_ in range(4):
            mm_B(PA0[:, mt_ * 128:mt_ * 128 + 128], C_QM + mt_ * 128, 128, Tn, ['P0'])
        T.op('act', lambda e: e.copy(qmT[:, :, :Tn], PA0[:, :].rearrange("p (h t) -> p h t", h=4)[:, :, :Tn]), r=['P0'], w=['qmT'])
        def fn_kt(e):
            ins = None
            for h in range(4):
                ins = e.transpose(P1b[:, h * 128:h * 128 + Tn], kmtok[:Tn, h, :], identb[:Tn, :Tn])
            return ins
        T.op('pe', fn_kt, r=['kmtok', 'c6'], w=['P1'], c=0.7)
        T.op('act', lambda e: e.copy(kmT[:, :, :Tn], P1b[:, 0:512].rearrange("p (h t) -> p h t", h=4)[:, :, :Tn]), r=['P1'], w=['kmT'])
        def fn_s(e):
            ins = None
            for h in range(4):
                ins = e.matmul(P4[:Tn, h * 128:h * 128 + Tn], kmT[:, h, :Tn], qmT[:, h, :Tn], start=True, stop=True)
            return ins
        T.op('pe', fn_s, r=['kmT', 'qmT'], w=['P4'])
        T.op('dve', lambda e: e.tensor_tensor(ST[:Tn, :, :Tn], P4[:Tn, :].rearrange("p (h t) -> p h t", h=4)[:, :, :Tn], Ebuf[:Tn, :, :Tn], ALU.mult),
             r=['P4', 'Ebuf'], w=['ST'])
        def fn_qc(e):
            ins = None
            for h in range(4):
                ins = e.matmul(P23[:Tn, h * 256:h * 256 + 129], qmT[:, h, :Tn], Cb[si][:, h, 0:129], start=True, stop=True)
            return ins
        T.op('pe', fn_qc, r=['qmT', cbk], w=['P2', 'P3'])
        def fn_sv(e):
            ins = None
            for h in range(4):
                ins = e.matmul(P01[:Tn, h * 256:h * 256 + 129], ST[:Tn, h, :Tn], v1[:Tn, h, 0:129], start=True, stop=True)
            return ins
        T.op('pe', fn_sv, r=['ST', 'v1'], w=['P0', 'P1'])
        for h in range(4):
            T.op('act', lambda e, h=h: e.activation(nd1[:Tn, h, 0:129], P23[:Tn, h * 256:h * 256 + 129], AF.Copy, scale=tm[:Tn, 4 + h:5 + h]),
                 r=['P2', 'P3', 'tm'], w=['nd1'])
        T.op('dve', lambda e: e.tensor_tensor(nd[:Tn, :, 0:129], nd1[:Tn, :, 0:129],
                                              P01[:Tn, :].rearrange("p (h c) -> p h c", h=4)[:, :, 0:129], ALU.add),
             r=['nd1', 'P0', 'P1'], w=['nd1'])
        state_update(Tn, si)
        T.op('act', lambda e: e.copy(Cb[si][:], C[:]), r=[ck], w=[cbk])
        T.op('dve', lambda e: e.tensor_scalar(sm[:Tn, 20:24], nd[:Tn, :, 128], -1.0, None, ALU.mult), r=['nd1'], w=['sm_den'])
        T.op('dve', lambda e: e.tensor_tensor(sm[:Tn, 20:24], sm[:Tn, 20:24], nd[:Tn, :, 128], ALU.max), r=['nd1', 'sm_den'], w=['sm_den'])
        T.op('dve', lambda e: e.tensor_tensor(sm[:Tn, 0:4], sm[:Tn, 20:24], tm[:Tn, 8:12], ALU.max), r=['sm_den', 'tm'], w=['sm_den'])
        T.op('dve', lambda e: e.reciprocal(sm[:Tn, 0:4], sm[:Tn, 0:4]), r=['sm_den'], w=['sm_den'])
        T.op('dve', lambda e: e.tensor_tensor(hmr[:Tn], nd[:Tn, :, 0:128], sm[:Tn, 0:4].unsqueeze(2).to_broadcast([Tn, 4, 128]), ALU.mult),
             r=['nd1', 'sm_den'], w=['Ebuf'])
        for h in range(4):
            T.op('dve', lambda e, h=h: e.bn_stats(bst[:Tn, h, :], hmr[:Tn, h, :]), r=['Ebuf'], w=['bst'])
        for h in range(4):
            T.op('dve', lambda e, h=h: e.bn_aggr(mv[:Tn, h, :], bst[:Tn, h, :]), r=['bst'], w=['mv'])
        T.op('act', lambda e: e.activation(sm[:Tn, 4:8], mv[:Tn, :, 1], AF.Ln, bias=EPS), r=['mv'], w=['sm_hn'])
        T.op('act', lambda e: e.activation(sm[:Tn, 4:8], sm[:Tn, 4:8], AF.Exp, scale=-0.5), r=['sm_hn'], w=['sm_hn'])
        for h in range(4):
            T.op('dve', lambda e, h=h: e.tensor_scalar(hmr[:Tn, h, :], hmr[:Tn, h, :], mv[:Tn, h, 0:1], sm[:Tn, 4 + h:5 + h], ALU.subtract, ALU.mult),
                 r=['Ebuf', 'mv', 'sm_hn'], w=['Ebuf'])
        T.op('dve', lambda e: e.tensor_tensor(mixb[:Tn, 0:512], hmr[:Tn].rearrange("p h d -> p (h d)"), og[:Tn], ALU.mult),
             r=['Ebuf', 'og'], w=['mbA'])
        cosv, sinv = cosb[cs_slot], sinb[cs_slot]
        csk = 'cs%d' % cs_slot
        for mt_ in range(4):
            mm_B(P4[:, mt_ * 128:mt_ * 128 + 128], C_QAP + mt_ * 128, 128, Tn, ['P4'])
        for mt_ in range(4):
            mm_B(PA1[:, mt_ * 128:mt_ * 128 + 128], C_QAR + mt_ * 128, 128, Tn, ['P1'])
        T.op('dve', lambda e: e.tensor_tensor(rt1[:, :, :Tn], P4[:, :].rearrange("p (h t) -> p h t", h=4)[:, :, :Tn],
                                              cosv[:, :Tn].unsqueeze(1).to_broadcast([128, 4, Tn]), ALU.mult), r=['P4', csk], w=['rt1'])
        T.op('dve', lambda e: e.tensor_tensor(rt2[:, :, :Tn], PA1[:, :].rearrange("p (h t) -> p h t", h=4)[:, :, :Tn],
                                              sinv[:, :Tn].unsqueeze(1).to_broadcast([128, 4, Tn]), ALU.mult), r=['P1', csk], w=['rt2'])
        T.op('dve', lambda e: e.tensor_tensor(qaT[:, :, :Tn], rt1[:, :, :Tn], rt2[:, :, :Tn], ALU.add), r=['rt1', 'rt2'], w=['qaT'])
        mm_B(PA0[:, 0:128], C_KAP, 128, Tn, ['P0'])
        mm_B(PA0[:, 128:256], C_KAR, 128, Tn, ['P0'])
        T.op('dve', lambda e: e.tensor_tensor(rt1[:, 0, :Tn], PA0[:, 0:Tn], cosv[:, :Tn], ALU.mult), r=['P0', csk], w=['rt1'])
        T.op('dve', lambda e: e.tensor_tensor(rt2[:, 0, :Tn], PA0[:, 128:128 + Tn], sinv[:, :Tn], ALU.mult), r=['P0', csk], w=['rt2'])
        T.op('dve', lambda e: e.tensor_tensor(krope[:, :Tn], rt1[:, 0, :Tn], rt2[:, 0, :Tn], ALU.add), r=['rt1', 'rt2'], w=['krope'])
        T.op('act', lambda e: e.copy(kTb[kslot][:, :Tn], krope[:, :Tn]), r=['krope'], w=['kT_%d' % kslot])
        if save_kv:
            T.op('pe', lambda e: e.transpose(PA0[:Tn, 256:384], krope[:, :Tn], ident[:, :]), r=['krope', 'const'], w=['P0'])
            T.op('dve', lambda e: e.tensor_copy(kvout[:Tn, 0, :], PA0[:Tn, 256:384]), r=['P0'], w=['kvout'])
        blocks = [(pslot, 128, prev_bias), (kslot, Tn, 0.0)]
        psb = [(P23, ['P2', 'P3']), (P01, ['P0', 'P1'])]
        for bi_, (ks_, Sb, bias_) in enumerate(blocks):
            ps_, keys_ = psb[bi_]
            def fn_sc(e, ks_=ks_, Sb=Sb, ps_=ps_):
                ins = None
                for j in range(4):
                    for half in range(2):
                        hd = half * 4 + j
                        ins = e.matmul(ps_[:Sb, hd * 128:hd * 128 + Tn], kTb[ks_][half * 64:(half + 1) * 64, :Sb],
                                       qaT[half * 64:(half + 1) * 64, j, :Tn], start=True, stop=True)
                return ins
            T.op('pe', fn_sc, r=['kT_%d' % ks_, 'qaT'], w=keys_)
            for b2 in range(2):
                T.op('act', lambda e, b2=b2, Sb=Sb, ps_=ps_, bias_=bias_, bi_=bi_: e.activation(
                    pT[:Sb, bi_, 4 * b2:4 * b2 + 4, :Tn],
                    ps_[:Sb, 512 * b2:512 * b2 + 512].rearrange("p (h t) -> p h t", h=4)[:, :, :Tn],
                    AF.Exp, bias=bias_, scale=0.125), r=[keys_[b2], 'c5', 'const'], w=['pT%d' % bi_])
        def fn_pv(e):
            ins = None
            for hd in range(8):
                if is_sample:
                    for bi_, (ks_, Sb, _) in enumerate(blocks):
                        ins = e.matmul(P45(hd)[:Tn, :], pT[:Sb, bi_, hd, :Tn], va1[ks_][:Sb, hd // 4, 0:65],
                                       start=(bi_ == 0), stop=(bi_ == 1))
                else:
                    for qc, rng in ((0, ((0, 0, 128), (1, 0, 64))), (1, ((0, 64, 128), (1, 0, 128)))):
                        for ii, (bi_, s0, s1_) in enumerate(rng):
                            ks_ = blocks[bi_][0]
                            ins = e.matmul(P45(hd)[qc * 64:(qc + 1) * 64, :], pT[s0:s1_, bi_, hd, qc * 64:(qc + 1) * 64],
                                           va1[ks_][s0:s1_, hd // 4, 0:65], start=(ii == 0), stop=(ii == 1))
            return ins
        def P45(hd):
            base = P4 if hd < 4 else PA1
            c = (hd % 4) * 65
            return base[:, c:c + 65]
        T.op('pe', fn_pv, r=['pT0', 'pT1', 'va1_%d' % pslot, 'va1_%d' % kslot], w=['P4', 'P1'])
        for half, base, bk in ((0, P4, 'P4'), (1, PA1, 'P1')):
            o3 = base[:Tn, 0:260].rearrange("p (h c) -> p h c", h=4)
            T.op('dve', lambda e, o3=o3, half=half: e.tensor_tensor(sm[:Tn, 8 + 4 * half:12 + 4 * half], o3[:, :, 64], esink[:Tn, 4 * half:4 * half + 4], ALU.add),
                 r=[bk, 'c4'], w=['sm_pv'])
        T.op('dve', lambda e: e.reciprocal(sm[:Tn, 8:16], sm[:Tn, 8:16]), r=['sm_pv'], w=['sm_pv'])
        for half, base, bk in ((0, P4, 'P4'), (1, PA1, 'P1')):
            o3 = base[:Tn, 0:260].rearrange("p (h c) -> p h c", h=4)
            T.op('dve', lambda e, o3=o3, half=half: e.tensor_tensor(
                mixb[:Tn, 512 + 256 * half:768 + 256 * half].rearrange("p (h d) -> p h d", h=4), o3[:, :, 0:64],
                sm[:Tn, 8 + 4 * half:12 + 4 * half].unsqueeze(2).to_broadcast([Tn, 4, 64]), ALU.mult),
                r=[bk, 'sm_pv'], w=['mbB%d' % half])
        def fn_mt(e):
            ins = None
            for k in range(8):
                ins = e.transpose(P0b[:, k * 128:k * 128 + Tn], mixb[:Tn, k * 128:(k + 1) * 128], identb[:Tn, :Tn])
            return ins
        T.op('pe', fn_mt, r=['mbA', 'mbB0', 'mbB1', 'c6'], w=['P0'], c=1.1)
        T.op('act', lambda e: e.copy(mixT[:, :, :Tn], P0b[:, :].rearrange("p (k t) -> p k t", k=8)[:, :, :Tn]), r=['P0'], w=['xT'], c=0.9)
        for n in range(2):
            def fn_o(e, n=n):
                ins = None
                for k in range(8):
                    ins = e.matmul(P23[:Tn, n * 512:(n + 1) * 512], mixT[:, k, :Tn], wout[:, k, n * 512:(n + 1) * 512], start=(k == 0), stop=(k == 7))
                return ins
            T.op('pe', fn_o, r=['xT', 'wout'], w=['P%d' % (2 + n)], c=2.5)
        xk = 'xin%d' % xslot
        T.op('dve', lambda e: e.scalar_tensor_tensor(s1[:Tn], xin[xslot][:Tn], ALPHA, P23[:Tn, :], ALU.mult, ALU.add), r=[xk, 'P2', 'P3'], w=['mix'], c=1.3)
        layer_norm(s1, MIXK, Tn, 0, hres[:, htile, :], hrk)
        transposes8(hres[:, htile, :], Tn, P01, [hrk])
        for b in range(2):
            T.op('act', lambda e, b=b: e.copy(hT[:, 4 * b:4 * b + 4, hcol:hcol + Tn],
                                              P01[:, 512 * b:512 * b + 512].rearrange("p (k t) -> p k t", k=4)[:, :, :Tn]),
                 r=['P%d' % b], w=[htk])

    def layer_norm(src, skey, Tn, which, dst, dkey):
        skeys = list(skey) if isinstance(skey, (list, tuple)) else [skey]
        skey = skeys[-1]
        bstL, mvL, smL = lnscr[which]
        kb, km_, ks_ = 'bstL%d' % which, 'mvL%d' % which, 'smL%d' % which
        for c in range(2):
            T.op('dve', lambda e, c=c: e.bn_stats(bstL[:Tn, c, :], src[:Tn, c * 512:(c + 1) * 512]), r=skeys, w=[kb])
        T.op('dve', lambda e: e.bn_aggr(mvL[:Tn, :], bstL[:Tn, 0:2, :].rearrange("p a b -> p (a b)")), r=[kb], w=[km_])
        T.op('act', lambda e: e.activation(smL[:Tn, 0:1], mvL[:Tn, 1:2], AF.Ln, bias=EPS), r=[km_], w=[ks_])
        T.op('act', lambda e: e.activation(smL[:Tn, 0:1], smL[:Tn, 0:1], AF.Exp, scale=-0.5), r=[ks_], w=[ks_])
        T.op('dve', lambda e: e.scalar_tensor_tensor(src[:Tn], src[:Tn], mvL[:Tn, 0:1], lnp[:Tn, 2 * which, :], ALU.subtract, ALU.mult),
             r=skeys + [km_, 'const'], w=skeys, c=1.25)
        T.op('dve', lambda e: e.scalar_tensor_tensor(dst[:Tn], src[:Tn], smL[:Tn, 0:1], lnp[:Tn, 2 * which + 1, :], ALU.mult, ALU.add),
             r=skeys + [ks_, 'const'], w=[dkey], c=1.25)

    ring_ctr = [0]
    dring_ctr = [0]
    set_ctr = [0]

    def stage2(N, segs, cvbuf, cvkey, first_macro, out_fn, par=0):
        hres = hresL[par]
        hT = hTL[par]
        htk = 'hT%d' % par
        for u in range(NU):
            rs = ring_ctr[0] % 2
            ring_ctr[0] += 1
            bs = u % 2
            PSU = (P6, P7)[bs]
            pk = ('P6', 'P7')[bs]
            si_ = set_ctr[0] % NS
            set_ctr[0] += 1
            ustv = ustS[si_]
            uk = ['ustS%d' % si_]
            ctv = ctS[si_]
            ckk = 'ctS%d' % si_
            T.dma('sp', 'wur%d' % rs, lambda e, rs=rs, u=u: e.dma_start(out=wur[rs][:].rearrange("p k c -> p (k c)"), in_=wupbf[u]),
                  r=['wupbf'], w=['wur%d' % rs])
            def fn_u(e, rs=rs, PSU=PSU):
                ins = None
                for j in range(2):
                    for k in range(8):
                        ins = e.matmul(PSU[:, j * 256:j * 256 + N], wur[rs][:, k, j * 128:(j + 1) * 128], hT[:, k, :N], start=(k == 0), stop=(k == 7))
                return ins
            T.op('pe', fn_u, r=['wur%d' % rs, htk], w=[pk], c=16 * (0.09 + max(N, 64) / 2400.0) + 0.1)
            pu = PSU[:, :].rearrange("p (j t) -> p j t", j=2)
            cv4 = cvbuf[:].rearrange("p (j u) t -> p j u t", j=2)
            T.op('act', lambda e, ustv=ustv, pu=pu: e.copy(ustv[:, :, 2:2 + N], pu[:, :, :N]), r=[pk], w=uk, c=0.7)
            T.op('dve', lambda e, u=u, ustv=ustv: e.tensor_copy(ustv[:, :, 0:2], cv4[:, :, u, :]), r=[cvkey], w=uk)
            if first_macro:
                T.op('dve', lambda e, u=u, ustv=ustv: e.tensor_scalar(cv4[:, :, u, :], ustv[:, :, N:N + 2], hs[:, 0:1], None, ALU.mult), r=uk + ['const'], w=[cvkey])
                continue
            T.op('dve', lambda e, u=u, ustv=ustv: e.tensor_copy(cv4[:, :, u, :], ustv[:, :, N:N + 2]), r=uk, w=[cvkey])
            for j in range(2):
                ci = j * NU + u
                T.op('act', lambda e, j=j, ci=ci, ctv=ctv, pu=pu: e.activation(ctv[:, j, :N], pu[:, j, :N], AF.Identity, bias=bcv[:, ci:ci + 1], scale=wcv[:, 2, ci:ci + 1]),
                     r=[pk, 'const'], w=[ckk], c=0.45)
                T.op('dve', lambda e, j=j, ci=ci, ctv=ctv, ustv=ustv: e.scalar_tensor_tensor(ctv[:, j, :N], ustv[:, j, 1:1 + N], wcv[:, 1, ci:ci + 1], ctv[:, j, :N], ALU.mult, ALU.add),
                     r=uk + [ckk, 'const'], w=[ckk], c=0.45)
                T.op('dve', lambda e, j=j, ci=ci, ctv=ctv, ustv=ustv: e.scalar_tensor_tensor(ctv[:, j, :N], ustv[:, j, 0:N], wcv[:, 0, ci:ci + 1], ctv[:, j, :N], ALU.mult, ALU.add),
                     r=uk + [ckk, 'const'], w=[ckk], c=0.45)
            T.op('act', lambda e, ctv=ctv: e.activation(ctv[:, 1, :N], ctv[:, 1, :N], AF.Silu), r=[ckk], w=[ckk], c=0.45)
            T.op('pool', lambda e, u=u, ctv=ctv: e.tensor_tensor(actT[:, u, :N], ctv[:, 1, :N], ctv[:, 0, :N], ALU.mult), r=[ckk], w=['actT'], c=0.7)
        if first_macro:
            return
        accs = [(P6, 'P6'), (P7, 'P7')]
        NCH = NU // 2
        for si_, (col0, Tn, htile) in enumerate(segs):
            pass
        for n in range(2):
            for c in range(NCH):
                ds = dring_ctr[0] % 5
                dring_ctr[0] += 1
                T.dma('sp', 'wdr%d' % ds, lambda e, ds=ds, c=c, n=n: e.dma_start(
                    out=wdr[ds][:], in_=wdnbf[c].rearrange("p (k n) -> p k n", k=2)[:, :, n * 512:(n + 1) * 512]),
                    r=['wdnbf'], w=['wdr%d' % ds])
                def fn_d(e, ds=ds, c=c):
                    ins = None
                    for si_, (col0, Tn, htile) in enumerate(segs):
                        acc = accs[si_][0]
                        for kk in range(2):
                            kc = 2 * c + kk
                            ins = e.matmul(acc[:Tn, :], actT[:, kc, col0:col0 + Tn], wdr[ds][:, kk, :],
                                           start=(kc == 0), stop=(kc == NU - 1))
                    return ins
                T.op('pe', fn_d, r=['actT', 'wdr%d' % ds], w=[accs[i][1] for i in range(len(segs))], c=len(segs) * 2 * 0.31 + 0.1)
            for si_, (col0, Tn, htile) in enumerate(segs):
                acc, akey = accs[si_]
                hk = 'hres%d_%d' % (par, htile)
                T.op('dve', lambda e, Tn=Tn, htile=htile, acc=acc, n=n: e.scalar_tensor_tensor(
                    hres[:Tn, htile, n * 512:(n + 1) * 512], hres[:Tn, htile, n * 512:(n + 1) * 512], ALPHA, acc[:Tn, :], ALU.mult, ALU.add),
                    r=[hk, akey], w=[hk], c=0.7)
        for si_, (col0, Tn, htile) in enumerate(segs):
            hk = 'hres%d_%d' % (par, htile)
            layer_norm(hres[:, htile, :], hk, Tn, 1, hres[:, htile, :], hk)
            out_fn(Tn, si_, hres[:, htile, :], hk, par)

    def load_cs(slot, idx):
        T.dma('sp', 'cs%d' % slot, lambda e: e.dma_start(out=cosb[slot][:], in_=cos_d[idx]), w=['cs%d' % slot])
        T.dma('sp', 'cs%d' % slot, lambda e: e.dma_start(out=sinb[slot][:], in_=sin_d[idx]), w=['cs%d' % slot])

    def do_main():
        n_macro = nt_full // 2
        xbase = 8192 - NT_FULL * 128
        yrow = [0]
        for mi in range(n_macro):
            for tt in range(2):
                i = 2 * mi + tt
                slot = i % 2
                if i == 0:
                    load_x(0, xall[xbase: xbase + 128, :], 128)
                    load_cs(0, 0)
                if i + 1 < nt_full:
                    load_x((i + 1) % 2, xall[xbase + (i + 1) * 128: xbase + (i + 2) * 128, :], 128)
                    load_cs((i + 1) % 2, i + 1)
                T.cur_tag = 's1_%02d' % mi
                kslot = i % 2
                pslot = (i + 1) % 2
                pbias = hs[:, 1:2] if i == 2 else (NEG if i == 0 else 0.0)
                last = (i == nt_full - 1)
                stage1(128, slot, 0, kslot, pslot, pbias, slot, tt * 128, tt, last, False, par=mi % 2)
                if i == 1:
                    T.op('dve', lambda e: e.tensor_scalar(mst[0][:], mst[0][:], hs[0:4, 0:1], None, ALU.mult), r=['m0', 'const'], w=['m0'])
                if last:
                    out_toks.append(T.dma('sp', 'okvp', lambda e: e.dma_start(out=kp_d[:, :], in_=kvout[:, 0, :]), r=['kvout']))
                    out_toks.append(T.dma('sp', 'okvp', lambda e: e.dma_start(out=vp_d[:, :], in_=kvout[:, 1, :]), r=['kvout']))

            def out_y(Tn, yslot, src, hk, par):
                r0 = yrow[0]
                yrow[0] += Tn
                out_toks.append(T.dma('sp', 'oy%d_%d' % (par, yslot), lambda e: e.dma_start(out=y_d[r0:r0 + Tn, :], in_=src[:Tn, :]), r=[hk]))
            T.cur_tag = 's2_%02d' % mi
            stage2(256, [(0, 128, 0), (128, 128, 1)], cvh, 'cvh', mi == 0, out_y, par=mi % 2)

        out_toks.append(T.dma('sp', 'ostC', lambda e: e.dma_start(out=Cp_d.rearrange("h d e -> d h e"), in_=Cst[0][:, :, 0:128]), r=['C0']))
        out_toks.append(T.dma('sp', 'ostC', lambda e: e.dma_start(out=np_d.rearrange("h d -> d h"), in_=Cst[0][:, :, 128], allow_slow_non_contiguous=True), r=['C0']))
        out_toks.append(T.dma('sp', 'ostm', lambda e: e.dma_start(out=mp_d[:, :], in_=mst[0][:]), r=['m0']))
        for j in range(2):
            out_toks.append(T.dma('sp', 'ostcv', lambda e, j=j: e.dma_start(out=cvp_d[j, :].rearrange("(c p) -> p c", p=128), in_=cvh[:, :, j],
                                                                           allow_slow_non_contiguous=True), r=['cvh']))

    def do_sample():
        T.cur_tag = 'sample'
        load_x(0, xs_d[:, :], 16)
        load_cs(0, NT_FULL)
        T.dma('sp', 'xin1', lambda e: e.dma_start(out=xin[1][:, 0:128], in_=ck_d[:, :]), w=['xin1'])
        T.dma('sp', 'xin1', lambda e: e.dma_start(out=xin[1][:, 128:256], in_=cv_d[:, :]), w=['xin1'])
        T.op('pe', lambda e: e.transpose(P4[:, 0:128], xin[1][:, 0:128], ident[:, :]), r=['xin1', 'const'], w=['P4'])
        T.op('act', lambda e: e.copy(kTb[1][:, :], P4[:, 0:128]), r=['P4'], w=['kT_1'])
        T.op('act', lambda e: e.copy(va1[1][:, :, 0:64], xin[1][:, 128:256].rearrange("p (k d) -> p k d", k=2)), r=['xin1'], w=['va1_1'])
        T.op('dve', lambda e: e.memset(Cst[0][:], 0.0), w=['C0'])
        T.dma('sp', 'sstC', lambda e: e.dma_start(out=Cst[0][:, :, 0:128], in_=C0_d.rearrange("h d e -> d h e")), w=['C0'])
        T.dma('sp', 'sstC', lambda e: e.dma_start(
            out=Cst[0][:, :, 128], in_=n0_d.rearrange("h d -> d h"), allow_slow_non_contiguous=True), w=['C0'])
        T.dma('sp', 'sstm', lambda e: e.dma_start(out=mst[0][:], in_=m0_d[:, :]), w=['m0'])
        T.op('act', lambda e: e.copy(Cb[0][:], Cst[0][:]), r=['C0'], w=['Cb0'])
        stage1(16, 0, 0, 0, 1, 0.0, 0, 0, 0, True, True)
        out_toks.append(T.dma('sp', 'okvs', lambda e: e.dma_start(out=ks_d[:, :], in_=kvout[:16, 0, :]), r=['kvout']))
        out_toks.append(T.dma('sp', 'okvs', lambda e: e.dma_start(out=vs_d[:, :], in_=kvout[:16, 1, :]), r=['kvout']))

        def out_ys(Tn, yslot, src, hk, par):
            out_toks.append(T.dma('sp', 'oys', lambda e: e.dma_start(out=ys_d[:, :], in_=src[:Tn, :]), r=[hk]))
        stage2(16, [(0, 16, 0)], cvsb, 'cvsb', False, out_ys, par=0)
        out_toks.append(T.dma('sp', 'ossC', lambda e: e.dma_start(out=Cs_d.rearrange("h d e -> d h e"), in_=Cst[0][:, :, 0:128]), r=['C0']))
        out_toks.append(T.dma('sp', 'ossC', lambda e: e.dma_start(out=ns_d.rearrange("h d -> d h"), in_=Cst[0][:, :, 128], allow_slow_non_contiguous=True), r=['C0']))
        out_toks.append(T.dma('sp', 'ossm', lambda e: e.dma_start(out=ms_d[:, :], in_=mst[0][:]), r=['m0']))
        for j in range(2):
            out_toks.append(T.dma('sp', 'osscv', lambda e, j=j: e.dma_start(out=cvs_d[j, :].rearrange("(c p) -> p c", p=128), in_=cvsb[:, :, j],
                                                                           allow_slow_non_contiguous=True), r=['cvsb']))
    do_sample()
    T.cur_tag = 'setup2'
    T.op('dve', lambda e: e.memset(Cst[0][:], 0.0), w=['C0'])
    T.op('dve', lambda e: e.memset(Cb[0][:], 0.0), w=['Cb0'])
    T.op('dve', lambda e: e.memset(mst[0][:], 0.0), w=['m0'])
    do_prefix()
    do_main()
    T.wait_all('sp', out_toks)
    if FILL:
        T.filler_fn = lambda e: e.matmul(P5[:, 0:512], identb[:, :], win[:, 0, 0:512], start=True, stop=True)
        T.fill_frac = float(os.environ.get('K_FILL_FRAC', '0.5'))
        T.filler_cost = float(os.environ.get('K_FILL_COST', '0.3'))
        T.fill_min = float(os.environ.get('K_FILL_MIN', '1.5'))
    T.schedule()

    with nc.Block() as block:
        @block.sync
        def _(e):
            T.replay('sp', e)

        @block.tensor
        def _(e):
            T.replay('pe', e)

        @block.scalar
        def _(e):
            T.replay('act', e)

        @block.vector
        def _(e):
            T.replay('dve', e)

        @block.gpsimd
        def _(e):
            T.replay('pool', e)
    es.close()
    return nc


def _host_consts():
    ident = np.eye(128, dtype=np.float32)
    s = np.arange(128)[:, None]
    l = np.arange(128)[None, :]
    maskT = np.where(l >= s, 0.0, NEG).astype(np.float32)
    amask = np.ones((128, 2, 128), np.float32)
    amask[:, 0, :] = np.where((s < 64) & (l >= 64), 0.0, 1.0)
    amask[:, 1, :] = np.where((s >= 64) & (l < 64), 0.0, 1.0)
    sel4 = np.zeros((4, 4, 128), np.float32)
    for h in range(4):
        sel4[h, h, :] = 1.0
    return ident, maskT, amask, sel4.reshape(4, 512)


def _rope_tables(pos):
    half = 32
    inv = (np.float32(10000.0) ** (-np.arange(half, dtype=np.float32) / np.float32(half))).astype(np.float32)
    d = np.arange(128) % 64
    f = inv[d % 32]
    ang = (pos.astype(np.float32)[None, :] * f[:, None]).astype(np.float32)
    c = np.cos(ang).astype(np.float32)
    sn = np.sin(ang).astype(np.float32)
    sign = np.where(d < 32, -1.0, 1.0).astype(np.float32)[:, None]
    return c, (sn * sign).astype(np.float32)


_NC_CACHE = {}


def kernel(x_prompt, x_sample, state_mlstm_C, state_mlstm_n, state_mlstm_m, cache_swa_k, cache_swa_v,
           state_conv, w_in, b_igate, b_fgate, g_mlstm_norm, attn_sinks, w_out, ln1_g, ln1_b,
           w_up, w_conv, b_conv, w_down, ln2_g, ln2_b):
    f = lambda a: np.ascontiguousarray(np.asarray(a, dtype=np.float32))
    x_prompt, x_sample = f(x_prompt), f(x_sample)
    w_in0 = f(w_in)[0]
    rot = (np.arange(64) + 32) % 64
    qa0, ka0, va0 = 2056, 2568, 2696
    qap, qar = [], []
    for j in range(4):
        for hd in (j, 4 + j):
            qap.extend(qa0 + hd * 64 + np.arange(64))
            qar.extend(qa0 + hd * 64 + rot)
    kap = list(ka0 + np.arange(128))
    kar = list(ka0 + np.concatenate([rot, 64 + rot]))
    cols = list(range(0, 2056)) + list(range(va0, va0 + 128)) + qap + qar + kap + kar
    w_in_aug = np.ascontiguousarray(w_in0[:, np.array(cols, dtype=np.int64)])
    assert w_in_aug.shape[1] == WCOLS
    ident, maskT, amask, sel4 = _host_consts()
    ln = np.stack([f(ln1_g)[0], f(ln1_b)[0], f(ln2_g)[0], f(ln2_b)[0]], 0)

    nt_pre = int(os.environ.get("K_NT_PRE", NT_PRE))
    nt_full = int(os.environ.get("K_NT_FULL", NT_FULL))
    key = (nt_pre, nt_full)
    if key not in _NC_CACHE:
        _NC_CACHE[key] = build_program(nt_pre, nt_full)
    nc = _NC_CACHE[key]

    in_maps = []
    for c in range(8):
        b, half = c // 2, c % 2
        if half == 1:
            xall = x_prompt[b]
        else:
            xall = np.concatenate([np.zeros((4096, D), np.float32), x_prompt[b, :4096]], 0)
        pos0 = half * 4096 - 256
        cosT = np.zeros((NT_FULL + 1, 128, 128), np.float32)
        sinT = np.zeros((NT_FULL + 1, 128, 128), np.float32)
        for i in range(NT_FULL):
            cc, ss = _rope_tables(pos0 + i * 128 + np.arange(128))
            cosT[i], sinT[i] = cc, ss
        cc, ss = _rope_tables(2048 + np.arange(16))
        cosT[NT_FULL, :, :16], sinT[NT_FULL, :, :16] = cc, ss
        hsv = np.zeros((128, 2), np.float32)
        hsv[:, 0] = float(half)
        hsv[:, 1] = 0.0 if half == 1 else NEG
        in_maps.append({
            "xall": np.ascontiguousarray(xall), "xs": x_sample[c],
            "C0": f(state_mlstm_C)[0, c], "n0": f(state_mlstm_n)[0, c], "m0": f(state_mlstm_m)[0, c].reshape(4, 1),
            "ck": f(cache_swa_k)[0, c].reshape(128, 128), "cv": f(cache_swa_v)[0, c].reshape(128, 128),
            "sconv": f(state_conv)[0, c],
            "w_in": w_in_aug, "w_out": f(w_out)[0], "w_up": f(w_up)[0], "w_down": f(w_down)[0],
            "w_conv": f(w_conv)[0], "b_conv": f(b_conv)[0].reshape(1, -1),
            "b_i": f(b_igate)[0].reshape(4, 1), "b_f": f(b_fgate)[0].reshape(4, 1),
            "g_norm": f(g_mlstm_norm)[0].reshape(1, 512), "sinks": f(attn_sinks)[0].reshape(1, 8),
            "ln": ln, "cosT": cosT, "sinT": sinT, "ident": ident, "maskT": maskT, "amask": amask,
            "sel4": sel4, "hs": hsv,
        })
    res = run_bass_kernel_spmd(nc, in_maps, core_ids=list(range(8)))
    R = res.results
    y_p = np.zeros((4, 8192, D), np.float32)
    for c in range(8):
        y_p[c // 2, (c % 2) * 4096:(c % 2 + 1) * 4096] = R[c]["y"]
    y_s = np.stack([R[c]["ys"] for c in range(8)], 0)
    odd = [1, 3, 5, 7]
    C_p = np.stack([R[c]["Cp"] for c in odd], 0)[None]
    n_p = np.stack([R[c]["np"] for c in odd], 0)[None]
    m_p = np.stack([R[c]["mp"].reshape(4) for c in odd], 0)[None]
    k_p = np.stack([R[c]["kp"].reshape(128, 2, 64) for c in odd], 0)[None]
    v_p = np.stack([R[c]["vp"].reshape(128, 2, 64) for c in odd], 0)[None]
    cv_p = np.stack([R[c]["cvp"] for c in odd], 0)[None]
    C_s = np.stack([R[c]["Cs"] for c in range(8)], 0)[None]
    n_s = np.stack([R[c]["ns"] for c in range(8)], 0)[None]
    m_s = np.stack([R[c]["ms"].reshape(4) for c in range(8)], 0)[None]
    k_s = np.stack([R[c]["ks"].reshape(16, 2, 64) for c in range(8)], 0)[None]
    v_s = np.stack([R[c]["vs"].reshape(16, 2, 64) for c in range(8)], 0)[None]
    cv_s = np.stack([R[c]["cvs"] for c in range(8)], 0)[None]
    return (y_p, y_s, C_p, n_p, m_p, k_p, v_p, cv_p, C_s, n_s, m_s, k_s, v_s, cv_s)
```

```python
import os
from contextlib import ExitStack
import numpy as np
import concourse.bass as bass
import concourse.mybir as mybir
from concourse.bass_utils import run_bass_kernel_spmd

F32 = mybir.dt.float32
BF16 = mybir.dt.bfloat16
AF = mybir.ActivationFunctionType
ALU = mybir.AluOpType
AX = mybir.AxisListType

D = 1024
NT_PRE = 30
NT_FULL = 34
DFF = 2816
NU = 22
WCOLS = 3464
C_QM, C_KM, C_VM, C_OM, C_IP, C_FP, C_VA, C_QAP, C_QAR, C_KAP, C_KAR = 0, 512, 1024, 1536, 2048, 2052, 2056, 2184, 2696, 3208, 3336
ALPHA = float(2.0 ** 0.25)
EPS = 1e-5
KSCALE = float(128.0 ** -0.5)
NEG = -30000.0


SKIP_WAR = int(os.environ.get('K_SKIP_WAR', '1'))


class Tracker:
    HOP = 0.4
    DEF_COST = {'pe': 0.7, 'act': 0.45, 'dve': 0.3, 'pool': 0.8, 'sp': 0.08}

    def __init__(self, nc, es):
        self.nc = nc
        self.es = es
        self.engs = {}
        self.last_w = {}
        self.readers = {}
        self.streams = {}
        self.ops = []
        self.thr_hist = {}
        self.cur_tag = 'setup'
        self.filler_fn = None
        self.pe_scale = float(os.environ.get('K_PE_SCALE', '0.65'))
        self.ad_scale = float(os.environ.get('K_AD_SCALE', '1.1'))
        self.HOP = float(os.environ.get('K_HOP', '0.9'))
        self.hop_same = float(os.environ.get('K_HOP_SAME', '0.3'))
        self.filler_cost = 0.43
        self.fill_min = 1.5
        self.fill_frac = 0.6

    def add_engine(self, name, eng):
        sem = self.es.enter_context(self.nc.semaphore("s_" + name))
        self.engs[name] = dict(eng=eng, sem=sem, name=name, order=[])

    def stream(self, name, group=False):
        if name not in self.streams:
            sem = self.es.enter_context(self.nc.semaphore("d_" + name))
            self.streams[name] = dict(sem=sem, count=0, group=group, name=name)
        return self.streams[name]

    def _deps(self, reads, writes, extra, group):
        deps = set()
        for k in reads:
            deps.update(self.last_w.get(k, ()))
        self._raw = set(deps)
        if not group:
            for k in writes:
                deps.update(self.last_w.get(k, ()))
                deps.update(self.readers.get(k, ()))
        deps.update(extra)
        return deps

    def _commit(self, oid, reads, writes, group):
        for k in reads:
            self.readers.setdefault(k, []).append(oid)
        for k in writes:
            if group:
                self.last_w.setdefault(k, []).append(oid)
            else:
                self.last_w[k] = [oid]
                self.readers[k] = []

    def op(self, ename, fn, r=(), w=(), extra=(), c=None):
        deps = self._deps(r, w, extra, False)
        oid = len(self.ops)
        cc = self.DEF_COST[ename] if c is None else c
        if ename == 'pe':
            cc *= self.pe_scale
        elif ename in ('act', 'dve', 'pool'):
            cc *= self.ad_scale
        self.ops.append(dict(id=oid, eng=ename, fn=fn, deps=deps, dma=None, tag=self.cur_tag, cost=cc, raw=self._raw | set(extra)))
        self._commit(oid, r, w, False)
        return oid

    def dma(self, qname, stream, fn, r=(), w=(), extra=(), group=False, c=4.0):
        S = self.stream(stream, group)
        deps = self._deps(r, w, extra, group)
        oid = len(self.ops)
        thr = ()
        if group:
            hist = self.thr_hist.setdefault(qname, [])
            B = 3 if qname == 'pool' else 4
            nb = len(hist) // B
            if nb > 0:
                thr = tuple(hist[(nb - 1) * B:nb * B])
                deps.update(thr)
            hist.append(oid)
        self.ops.append(dict(id=oid, eng=qname, fn=fn, deps=deps, dma=S, cost=c, thr=thr, tag=self.cur_tag))
        self._commit(oid, r, w, group)
        return oid

    def wait_all(self, ename, toks):
        oid = len(self.ops)
        self.ops.append(dict(id=oid, eng=ename, fn=None, deps=set(toks), dma=None, cost=0.05, tag='final'))
        return oid

    def schedule(self):
        import heapq
        ops = self.ops
        n = len(ops)
        ndeps = [len(o['deps']) for o in ops]
        users = [[] for _ in range(n)]
        for o in ops:
            for d in o['deps']:
                users[d].append(o['id'])
        ready_t = [0.0] * n
        fin = [0.0] * n
        bl = [0.0] * n
        for o in reversed(ops):
            i = o['id']
            m = 0.0
            for u in users[i]:
                if bl[u] > m:
                    m = bl[u]
            bl[i] = m + o['cost'] + self.HOP
        mode = os.environ.get('K_PRIO', 'id')
        if mode == 'bl':
            prio = [(-bl[i], i) for i in range(n)]
        else:
            prio = [(i, i) for i in range(n)]
        ready = {e: [] for e in self.engs}
        free_t = {e: 0.0 for e in self.engs}
        events = []
        for o in ops:
            if ndeps[o['id']] == 0:
                heapq.heappush(ready[o['eng']], (prio[o['id']], o['id']))
        for e in self.engs:
            heapq.heappush(events, (0.0, 0, e))
        seq = 1
        done = 0
        pending_wake = {e: True for e in self.engs}
        while done < n:
            if not events:
                raise RuntimeError("scheduler stuck")
            t, _, e = heapq.heappop(events)
            pending_wake[e] = False
            if free_t[e] > t + 1e-9:
                heapq.heappush(events, (free_t[e], seq, e)); seq += 1; pending_wake[e] = True
                continue
            cand = None
            tmp = []
            while ready[e]:
                item = heapq.heappop(ready[e])
                oid = item[1]
                if ready_t[oid] <= t + 1e-9:
                    cand = oid
                    break
                tmp.append(item)
            for x in tmp:
                heapq.heappush(ready[e], x)
            if cand is None:
                if ready[e]:
                    tn = min(ready_t[x[1]] for x in ready[e])
                    heapq.heappush(events, (tn, seq, e)); seq += 1; pending_wake[e] = True
                continue
            o = ops[cand]
            o['t0'] = t
            self.engs[e]['order'].append(cand)
            if o['dma'] is not None:
                free_t[e] = t + (0.6 if e == 'pool' else 0.08)
                fin[cand] = t + o['cost']
            else:
                free_t[e] = t + o['cost']
                fin[cand] = free_t[e]
            done += 1
            for u in users[cand]:
                ndeps[u] -= 1
                if ops[u]['eng'] == e and o['dma'] is None:
                    if e == 'pe' or (SKIP_WAR and e in ('act', 'dve') and cand not in ops[u].get('raw', ops[u]['deps'])):
                        rt = fin[cand]
                    else:
                        rt = fin[cand] + self.hop_same
                else:
                    rt = fin[cand] + self.HOP
                if rt > ready_t[u]:
                    ready_t[u] = rt
                if ndeps[u] == 0:
                    ue = ops[u]['eng']
                    heapq.heappush(ready[ue], (prio[u], u))
                    if not pending_wake[ue]:
                        heapq.heappush(events, (max(ready_t[u], free_t[ue]), seq, ue)); seq += 1; pending_wake[ue] = True
            heapq.heappush(events, (free_t[e], seq, e)); seq += 1; pending_wake[e] = True
        self.sim_time = max(fin)
        self.fin = fin
        if self.filler_fn is not None:
            fc = self.filler_cost
            new_order = []
            prev_end = None
            nfill = 0
            armed = False
            for oid in self.engs['pe']['order']:
                o = ops[oid]
                if not armed:
                    if any(ops[d]['dma'] is not None and ops[d]['dma']['name'] == 'win' for d in o['deps']):
                        armed = True
                    new_order.append(oid)
                    prev_end = fin[oid]
                    continue
                if prev_end is not None:
                    gap = o['t0'] - prev_end
                    if gap > self.fill_min:
                        k = int((gap * self.fill_frac) / fc)
                        for _ in range(k):
                            fid = len(ops)
                            ops.append(dict(id=fid, eng='pe', fn=self.filler_fn, deps=set(), dma=None, cost=fc, tag='fill', filler=True))
                            new_order.append(fid)
                            nfill += 1
                new_order.append(oid)
                prev_end = fin[oid]
            self.engs['pe']['order'] = new_order
            self.nfill = nfill
        for e, E in self.engs.items():
            pos = 0
            for oid in E['order']:
                o = ops[oid]
                if o['dma'] is not None:
                    o['dma']['count'] += 16
                    o['val'] = o['dma']['count']
                elif o['fn'] is not None and not o.get('filler'):
                    pos += 1
                    o['val'] = pos

    def replay(self, ename, eng):
        E = self.engs[ename]
        ops = self.ops
        seen = {}
        for oid in E['order']:
            o = ops[oid]
            need = {}
            for d in o['deps']:
                D = ops[d]
                if D['dma'] is not None:
                    S = D['dma']
                    if d in o.get('thr', ()):
                        sem, val = S['sem'], D['val']
                    else:
                        sem, val = S['sem'], (S['count'] if S['group'] else D['val'])
                else:
                    if D['fn'] is None:
                        continue
                    if D['eng'] == 'pe' and ename == 'pe':
                        continue
                    if SKIP_WAR and D['eng'] == ename and ename in ('act', 'dve') and d not in o.get('raw', o['deps']):
                        continue
                    sem, val = self.engs[D['eng']]['sem'], D['val']
                if need.get(sem, 0) < val:
                    need[sem] = val
            for sem, val in need.items():
                if seen.get(sem, 0) < val:
                    eng.wait_ge(sem, val)
                    seen[sem] = val
            if o['fn'] is None:
                continue
            ins = o['fn'](eng)
            if o.get('filler'):
                continue
            if o['dma'] is not None:
                ins.then_inc(o['dma']['sem'], 16)
            else:
                ins.then_inc(E['sem'], 1)


def build_program(nt_pre=NT_PRE, nt_full=NT_FULL):
    nc = bass.Bass("TRN2", target_bir_lowering=False)
    es = ExitStack()

    def din(name, shape, dt=F32):
        return nc.dram_tensor(name, list(shape), dt, kind="ExternalInput").ap()

    def dout(name, shape, dt=F32):
        return nc.dram_tensor(name, list(shape), dt, kind="ExternalOutput").ap()

    xall = din("xall", [8192, D])
    xs_d = din("xs", [16, D])
    C0_d = din("C0", [4, 128, 128])
    n0_d = din("n0", [4, 128])
    m0_d = din("m0", [4, 1])
    ck_d = din("ck", [128, 128])
    cv_d = din("cv", [128, 128])
    sconv_d = din("sconv", [2, 2 * DFF])
    win_d = din("w_in", [D, WCOLS])
    wout_d = din("w_out", [D, D])
    wup_d = din("w_up", [D, 2 * DFF])
    wdn_d = din("w_down", [DFF, D])
    wconv_d = din("w_conv", [3, 2 * DFF])
    bconv_d = din("b_conv", [1, 2 * DFF])
    bi_d = din("b_i", [4, 1])
    bf_d = din("b_f", [4, 1])
    gn_d = din("g_norm", [1, 512])
    sink_d = din("sinks", [1, 8])
    ln_d = din("ln", [4, D])
    cos_d = din("cosT", [NT_FULL + 1, 128, 128])
    sin_d = din("sinT", [NT_FULL + 1, 128, 128])
    ident_d = din("ident", [128, 128])
    maskT_d = din("maskT", [128, 128])
    amask_d = din("amask", [128, 2, 128])
    sel4_d = din("sel4", [4, 4 * 128])
    hs_d = din("hs", [128, 2])

    y_d = dout("y", [4096, D])
    ys_d = dout("ys", [16, D])
    Cp_d = dout("Cp", [4, 128, 128])
    np_d = dout("np", [4, 128])
    mp_d = dout("mp", [4, 1])
    kp_d = dout("kp", [128, 128])
    vp_d = dout("vp", [128, 128])
    cvp_d = dout("cvp", [2, 2 * DFF])
    Cs_d = dout("Cs", [4, 128, 128])
    ns_d = dout("ns", [4, 128])
    ms_d = dout("ms", [4, 1])
    ks_d = dout("ks", [16, 128])
    vs_d = dout("vs", [16, 128])
    cvs_d = dout("cvs", [2, 2 * DFF])
    wupbf = nc.dram_tensor("wupbf", [NU, 128, 8 * 256], BF16, kind="Internal").ap()
    wdnbf = nc.dram_tensor("wdnbf", [NU // 2, 128, 2 * D], BF16, kind="Internal").ap()

    def sb(name, shape, dt=F32):
        return es.enter_context(nc.sbuf_tensor(name, list(shape), dt))

    def pt(name, shape, dt=F32):
        return es.enter_context(nc.psum_tensor(name, list(shape), dt))

    win = sb("win", [128, 8, WCOLS], BF16)
    wout = sb("wout", [128, 8, D], BF16)
    wdr = [sb("wdr%d" % i, [128, 2, 512], BF16) for i in range(5)]
    NS = 2
    ustS = [sb("ustS%d" % i, [128, 2, 258]) for i in range(NS)]
    ctS = [sb("ctS%d" % i, [128, 2, 256]) for i in range(NS)]
    wur = [sb("wur%d" % i, [128, 8, 256], BF16) for i in range(2)]
    lnp = sb("lnp", [128, 4, D])
    gnb = sb("gnb", [128, 512])
    esink = sb("esink", [128, 8])
    ident = sb("ident_s", [128, 128])
    maskT = sb("maskT_s", [128, 128])
    amask = sb("amask_s", [128, 2, 128], BF16)
    sel4 = sb("sel4_s", [4, 4 * 128])
    hs = sb("hs_s", [128, 2])
    bi = sb("bi_s", [4, 1])
    nbf = sb("nbf_s", [4, 1])
    wcv = sb("wcv", [128, 3, 2 * NU])
    bcv = sb("bcv", [128, 2 * NU])
    cvh = sb("cvh", [128, 2 * NU, 2])
    cvsb = sb("cvsb", [128, 2 * NU, 2])
    ones4 = sb("ones4", [4, 128])
    onesr = ones4

    xin = [sb("xin%d" % i, [128, D]) for i in range(2)]
    xT = sb("xT", [128, 8, 128], BF16)
    mixT = sb("mixT", [128, 8, 128], BF16)
    kmtok = sb("kmtok", [128, 4, 128], BF16)
    kw = kmtok
    v1 = sb("v1", [128, 4, 130], BF16)
    qmT = sb("qmT", [128, 4, 128], BF16)
    qaT = sb("qaT", [128, 4, 128], BF16)
    kmT = sb("kmT", [128, 4, 128], BF16)
    cosb = [sb("cos%d" % i, [128, 128]) for i in range(2)]
    sinb = [sb("sin%d" % i, [128, 128]) for i in range(2)]
    nd1 = sb("nd1", [128, 4, 130])
    nd = nd1
    rt1 = sb("rt1", [128, 4, 128])
    Ebuf = sb("Ebuf", [128, 4, 128])
    hmr = Ebuf
    rt2 = sb("rt2", [128, 4, 128])
    krope = sb("krope", [128, 128])
    kTb = [sb("kTb%d" % i, [128, 128], BF16) for i in range(2)]
    va1 = [sb("va1%d" % i, [128, 2, 66], BF16) for i in range(2)]
    kvout = sb("kvout", [128, 2, 128])
    ST = sb("ST", [128, 4, 128], BF16)
    Cst = [sb("Cst0", [128, 4, 130])] * 2
    Cb = [sb("Cb0", [128, 4, 130], BF16)] * 2
    mst = [sb("mst0", [4, 1])] * 2
    bst = sb("bst", [128, 4, 6])
    mv = sb("mv", [128, 4, 2])
    sm = sb("sm", [128, 32])
    lnscr = [(sb("bstL%d" % i, [128, 2, 6]), sb("mvL%d" % i, [128, 2]), sb("smL%d" % i, [128, 2])) for i in range(2)]
    pTraw = sb("pTraw", [128, 1024])
    pT = pTraw[:].bitcast(BF16).rearrange("p (b h t) -> p b h t", b=2, h=8)
    og = sb("og", [128, 512])
    mix = sb("mix", [128, D])
    mixb = sb("mixb", [128, D], BF16)
    identb = sb("identb", [128, 128], BF16)
    s1 = mix
    MIXK = ['mix']
    hresL = [sb("hres%d" % i, [128, 2, D]) for i in range(2)]
    hTL = [sb("hT%d" % i, [128, 8, 256], BF16) for i in range(2)]
    actT = sb("actT", [128, NU, 256], BF16)
    G = sb("G", [4, 7, 128])
    tm = sb("tm", [128, 16])
    dbc = sb("dbc", [128, 4])
    gsm = sb("gsm", [4, 8])

    P01 = pt("P01", [128, 1024])
    P23 = pt("P23", [128, 1024])
    P4 = pt("P4", [128, 512])
    P5 = pt("P5", [128, 512])
    P6 = pt("P6", [128, 512])
    P7 = pt("P7", [128, 512])
    PA0 = P01[:, 0:512]
    PA1 = P01[:, 512:1024]
    P0b = PA0.bitcast(BF16)
    P1b = PA1.bitcast(BF16)

    T = Tracker(nc, es)
    FILL = int(os.environ.get('K_FILL', '1'))
    T.add_engine('sp', nc.sync)
    T.add_engine('pe', nc.tensor)
    T.add_engine('act', nc.scalar)
    T.add_engine('dve', nc.vector)
    T.add_engine('pool', nc.gpsimd)

    def ld(q, stream, out, in_, w, r=()):
        return T.dma(q, stream, lambda e, o=out, i=in_: e.dma_start(out=o, in_=i), r=r, w=w, group=True, c=6.0)

    for k in range(8):
        for c0 in (0, 1732):
            ld('pool', 'win', win[:, k, c0:c0 + 1732], win_d[k * 128:(k + 1) * 128, c0:c0 + 1732], w=['win'])
    ld('sp', 'small', ident[:], ident_d[:, :], w=['const'])
    ld('sp', 'small', maskT[:], maskT_d[:, :], w=['const'])
    ld('pool', 'small2', amask[:], amask_d[:, :, :], w=['c5'])
    ld('sp', 'small', sel4[:], sel4_d[:, :], w=['const'])
    ld('sp', 'small', hs[:], hs_d[:, :], w=['const'])
    ld('sp', 'small', bi[:], bi_d[:, :], w=['const'])
    ld('sp', 'small', nbf[:], bf_d[:, :], w=['const'])
    ld('sp', 'small', gnb[:], gn_d[0, :].partition_broadcast(128), w=['const'])
    ld('sp', 'small', esink[:], sink_d[0, :].partition_broadcast(128), w=['const'])
    for j in range(4):
        ld('sp', 'small', lnp[:, j, :], ln_d[j, :].partition_broadcast(128), w=['const'])
    for j in range(3):
        T.dma('sp', 'small', lambda e, j=j: e.dma_start(
            out=wcv[:, j, :], in_=wconv_d[j, :].rearrange("(c p) -> p c", p=128), allow_slow_non_contiguous=True), w=['const'], group=True, c=20.0)
    T.dma('sp', 'small', lambda e: e.dma_start(
        out=bcv[:], in_=bconv_d[0, :].rearrange("(c p) -> p c", p=128), allow_slow_non_contiguous=True), w=['const'], group=True, c=20.0)
    for j in range(2):
        T.dma('sp', 'small', lambda e, j=j: e.dma_start(
            out=cvsb[:, :, j], in_=sconv_d[j, :].rearrange("(c p) -> p c", p=128), allow_slow_non_contiguous=True), w=['cvsb'], group=True, c=20.0)
    for k in range(8):
        ld('pool', 'wout', wout[:, k, :], wout_d[k * 128:(k + 1) * 128, :], w=['wout'])
    for u in range(NU):
        for j in range(2):
            c0 = j * DFF + u * 128
            T.dma('pool', 'wupc', lambda e, u=u, j=j, c0=c0: e.dma_start(
                out=wupbf[u].rearrange("p (k c) -> p k c", k=8)[:, :, j * 128:(j + 1) * 128],
                in_=wup_d[:, c0:c0 + 128].rearrange("(k p) c -> p k c", p=128)), w=['wupbf'], group=True, c=8.0)
    for c in range(NU // 2):
        T.dma('pool', 'wdnc', lambda e, c=c: e.dma_start(
            out=wdnbf[c].rearrange("p (k n) -> p k n", k=2),
            in_=wdn_d[c * 256:(c + 1) * 256, :].rearrange("(k p) n -> p k n", p=128)), w=['wdnbf'], group=True, c=8.0)

    T.op('dve', lambda e: e.memset(ones4[:], 1.0), w=['c2'])
    T.op('dve', lambda e: e.tensor_scalar(nbf[:], nbf[:], -1.0, None, ALU.mult), r=['const'], w=['c3'])
    T.op('act', lambda e: e.activation(esink[:], esink[:], AF.Exp), r=['const'], w=['c4'])
    T.op('dve', lambda e: e.tensor_copy(identb[:], ident[:]), r=['const'], w=['c6'])
    for i in range(2):
        T.op('dve', lambda e, i=i: e.memset(v1[:, :, 128:130], 1.0), w=['v1'])
    for i in range(2):
        T.op('dve', lambda e, i=i: e.memset(va1[i][:], 1.0), w=['va1_%d' % i])
        T.op('dve', lambda e, i=i: e.memset(kTb[i][:], 0.0), w=['kT_%d' % i])
    T.op('dve', lambda e: e.memset(cvh[:], 0.0), w=['cvh'])
    CONSTS = ['const', 'c2', 'c3', 'c4', 'c5', 'c6']

    def transposes8(src, Tn, dst_ps, rkeys):
        def fn(e):
            ins = None
            for k in range(8):
                ins = e.transpose(dst_ps[:, k * 128:k * 128 + Tn], src[:Tn, k * 128:(k + 1) * 128], ident[:Tn, :Tn])
            return ins
        T.op('pe', fn, r=list(rkeys) + CONSTS, w=['P0', 'P1'], c=2.0)

    def load_x(slot, src_ap, Tn):
        return T.dma('sp', 'xin%d' % slot, lambda e: e.dma_start(out=xin[slot][:Tn, :], in_=src_ap), w=['xin%d' % slot])

    def make_xT(slot, Tn):
        transposes8(xin[slot], Tn, P01, ['xin%d' % slot])
        for b in range(2):
            T.op('act', lambda e, b=b: e.copy(
                xT[:, 4 * b:4 * b + 4, :Tn],
                P01[:, 512 * b:512 * b + 512].rearrange("p (k t) -> p k t", k=4)[:, :, :Tn]),
                r=['P%d' % b], w=['xT'])

    def mm_A(ps, col0, ncols, Tn, wkey, bankkeys):
        def fn(e):
            ins = None
            for k in range(8):
                ins = e.matmul(ps[:Tn, :ncols], xT[:, k, :Tn], win[:, k, col0:col0 + ncols], start=(k == 0), stop=(k == 7))
            return ins
        T.op('pe', fn, r=['xT', 'win'], w=bankkeys, c=8 * (0.09 + max(ncols, 64) / 2400.0) + 0.1)

    def mm_B(ps, col0, mcols, Tn, bankkeys):
        def fn(e):
            ins = None
            for k in range(8):
                ins = e.matmul(ps[:mcols, :Tn], win[:, k, col0:col0 + mcols], xT[:, k, :Tn], start=(k == 0), stop=(k == 7))
            return ins
        T.op('pe', fn, r=['xT', 'win'], w=bankkeys, c=8 * 0.13 + 0.1)

    def gate_chain(Tn, si, full):
        m = mst[si]
        mk = 'm%d' % si
        ipT = P4[0:4, 128:128 + Tn]
        fpT = P4[0:4, 256:256 + Tn]
        R = lambda j: G[:, j, :Tn]
        T.op('act', lambda e: e.activation(R(0), fpT, AF.Exp, bias=nbf[:], scale=-1.0), r=['P4', 'c3'], w=['G0'])
        T.op('act', lambda e: e.activation(R(0), R(0), AF.Ln, bias=1.0), r=['G0'], w=['G0'])
        T.op('dve', lambda e: e.tensor_tensor_scan(R(1), onesr[:, :Tn], R(0), 0.0, ALU.mult, ALU.subtract), r=['G0', 'c2'], w=['G1'])
        T.op('dve', lambda e: e.scalar_tensor_tensor(R(2), ipT, bi[:], R(1), ALU.add, ALU.subtract), r=['P4', 'G1', 'const'], w=['G2'])
        T.op('dve', lambda e: e.tensor_tensor_scan(R(3), R(2), R(2), m[:], ALU.max, ALU.max), r=['G2', mk], w=['G3'])
        T.op('act', lambda e: e.activation(R(4), R(3), AF.Exp, bias=m[:], scale=-1.0), r=['G3', mk], w=['G4'])
        T.op('dve', lambda e: e.tensor_scalar(gsm[:, 0:1], G[:, 3, Tn - 1:Tn], -1.0, None, ALU.mult), r=['G3'], w=['gsm0'])
        T.op('act', lambda e: e.activation(R(5), R(2), AF.Exp, bias=gsm[:, 0:1]), r=['G2', 'gsm0'], w=['G5'])
        T.op('dve', lambda e: e.tensor_tensor(R(6), R(1), R(3), ALU.add), r=['G1', 'G3'], w=['G6'])
        if full:
            T.op('act', lambda e: e.activation(R(0), R(6), AF.Exp, scale=-1.0), r=['G6'], w=['G0'])
            T.op('dve', lambda e: e.tensor_scalar(R(1), R(3), -1.0, None, ALU.mult), r=['G3'], w=['G1'])
        T.op('dve', lambda e: e.tensor_scalar(gsm[:, 4:8], ident[0:4, 0:4], G[:, 4, Tn - 1:Tn], None, ALU.mult), r=['G4', 'const'], w=['gsm1'])
        T.op('dve', lambda e: e.tensor_copy(m[:], G[:, 6, Tn - 1:Tn]), r=['G6'], w=[mk])

        def fn(e):
            ins = None
            rows = [(2, 0), (4, 4), (0, 8), (5, 12)] if full else [(5, 12)]
            for (rj, c) in rows:
                ins = e.matmul(P4[:Tn, 384 + c:384 + c + 4], G[:, rj, :Tn], ident[0:4, 0:4], start=True, stop=True)
            ins = e.matmul(P4[:, 448:452], ones4[:, :], gsm[:, 4:8], start=True, stop=True)
            return ins
        T.op('pe', fn, r=['G2', 'G4', 'G0', 'G5', 'gsm1', 'c2', 'const'], w=['P4'])
        if full:
            T.op('dve', lambda e: e.tensor_copy(tm[:Tn, :], P4[:Tn, 384:400]), r=['P4'], w=['tm'])
        else:
            T.op('dve', lambda e: e.tensor_copy(tm[:Tn, 12:16], P4[:Tn, 396:400]), r=['P4'], w=['tm'])
        T.op('dve', lambda e: e.tensor_copy(dbc[:], P4[:, 448:452]), r=['P4'], w=['dbc'])

    def state_update(Tn, si):
        C = Cst[si]
        ck = 'C%d' % si
        T.op('dve', lambda e: e.tensor_tensor(kw[:Tn], kmtok[:Tn], tm[:Tn, 12:16].unsqueeze(2).to_broadcast([Tn, 4, 128]), ALU.mult),
             r=['kmtok', 'tm'], w=['kmtok'])

        def fn(e):
            ins = None
            for h in range(4):
                ins = e.matmul(P23[:, h * 256:h * 256 + 129], kw[:Tn, h, :], v1[:Tn, h, 0:129], start=True, stop=True)
            return ins
        T.op('pe', fn, r=['kmtok', 'v1'], w=['P2', 'P3'])
        for h in range(4):
            T.op('dve', lambda e, h=h: e.scalar_tensor_tensor(C[:, h, 0:129], C[:, h, 0:129], dbc[:, h:h + 1],
                                                              P23[:, h * 256:h * 256 + 129], ALU.mult, ALU.add),
                 r=['dbc', 'P2', 'P3', ck], w=[ck])

    out_toks = []

    def do_prefix():
        tok_x = {}
        if nt_pre > 0:
            tok_x[0] = load_x(0, xall[0:128, :], 128)
        for i in range(nt_pre):
            T.cur_tag = 'pre'
            slot = i % 2
            if i + 1 < nt_pre:
                load_x((i + 1) % 2, xall[(i + 1) * 128:(i + 2) * 128, :], 128)
            make_xT(slot, 128)
            mm_A(P23[:, 0:512], C_KM, 512, 128, 'win', ['P2'])
            mm_A(P23[:, 512:1024], C_VM, 512, 128, 'win', ['P3'])
            mm_B(P4[:, 128:256], C_IP, 4, 128, ['P4'])
            mm_B(P4[:, 256:384], C_FP, 4, 128, ['P4'])
            T.op('act', lambda e: e.activation(kmtok[:].rearrange("p h d -> p (h d)"), P23[:, 0:512], AF.Copy, scale=KSCALE), r=['P2'], w=['kmtok'])
            T.op('act', lambda e: e.copy(v1[:, :, 0:128], P23[:, 512:1024].rearrange("p (h d) -> p h d", h=4)), r=['P3'], w=['v1'])
            gate_chain(128, 0, False)
            state_update(128, 0)


    def stage1(Tn, xslot, si, kslot, pslot, prev_bias, cs_slot, hcol, htile, save_kv, is_sample, par=0, head_done=False, mid_hook=None):
        hres = hresL[par]
        hT = hTL[par]
        hrk = 'hres%d_%d' % (par, htile)
        htk = 'hT%d' % par
        C = Cst[si]
        ck = 'C%d' % si
        cbk = 'Cb%d' % si
        if not head_done:
            make_xT(xslot, Tn)
        mm_A(P23[:, 0:512], C_KM, 512, Tn, 'win', ['P2'])
        mm_A(P23[:, 512:1024], C_VM, 512, Tn, 'win', ['P3'])
        mm_A(PA1[:, 0:512], C_OM, 512, Tn, 'win', ['P1'])
        mm_A(P4[:, 0:128], C_VA, 128, Tn, 'win', ['P4'])
        mm_B(P4[:, 128:256], C_IP, 4, Tn, ['P4'])
        mm_B(P4[:, 256:384], C_FP, 4, Tn, ['P4'])
        T.op('act', lambda e: e.activation(kmtok[:Tn].rearrange("p h d -> p (h d)"), P23[:Tn, 0:512], AF.Copy, scale=KSCALE), r=['P2'], w=['kmtok'])
        T.op('act', lambda e: e.copy(v1[:Tn, :, 0:128], P23[:Tn, 512:1024].rearrange("p (h d) -> p h d", h=4)), r=['P3'], w=['v1'])
        T.op('act', lambda e: e.activation(og[:Tn], PA1[:Tn, :], AF.Exp, scale=-1.0), r=['P1'], w=['og'])
        T.op('act', lambda e: e.activation(og[:Tn], og[:Tn], AF.Ln, bias=1.0), r=['og'], w=['og'], c=0.55)
        T.op('act', lambda e: e.activation(og[:Tn], og[:Tn], AF.Exp, scale=-1.0), r=['og'], w=['og'], c=0.55)
        T.op('pool', lambda e: e.tensor_tensor(og[:Tn], og[:Tn], gnb[:Tn], ALU.mult), r=['og', 'const'], w=['og'], c=1.1)
        T.op('act', lambda e: e.copy(va1[kslot][:Tn, :, 0:64], P4[:Tn, 0:128].rearrange("p (k d) -> p k d", k=2)), r=['P4'], w=['va1_%d' % kslot])
        if save_kv:
            T.op('dve', lambda e: e.tensor_copy(kvout[:Tn, 1, :], P4[:Tn, 0:128]), r=['P4'], w=['kvout'])
        gate_chain(Tn, si, True)
        def fn_rb(e):
            ins = None
            for h in range(4):
                ins = e.matmul(P4[:Tn, h * 128:h * 128 + Tn], sel4[:, h * 128:h * 128 + Tn], G[:, 1, :Tn], start=True, stop=True)
            return ins
        T.op('pe', fn_rb, r=['G1', 'const'], w=['P4'], c=1.0)
        T.op('dve', lambda e: e.tensor_tensor(Ebuf[:Tn, :, :Tn], P4[:Tn, :].rearrange("p (h t) -> p h t", h=4)[:, :, :Tn],
                                              maskT[:Tn, :Tn].unsqueeze(1).to_broadcast([Tn, 4, Tn]), ALU.add),
             r=['P4', 'const'], w=['Ebuf'])
        for h in range(4):
            T.op('act', lambda e, h=h: e.activation(Ebuf[:Tn, h, :Tn], Ebuf[:Tn, h, :Tn], AF.Exp, bias=tm[:Tn, h:h + 1]), r=['Ebuf', 'tm'], w=['Ebuf'])
        for mt_ in range(4):
            mm_B(PA0[:, mt_ * 128:mt_ * 128 + 128], C_QM + mt_ * 128, 128, Tn, ['P0'])
        T.op('act', lambda e: e.copy(qmT[:, :, :Tn], PA0[:, :].rearrange("p (h t) -> p h t", h=4)[:, :, :Tn]), r=['P0'], w=['qmT'])
        def fn_kt(e):
            ins = None
            for h in range(4):
                ins = e.transpose(P1b[:, h * 128:h * 128 + Tn], kmtok[:Tn, h, :], identb[:Tn, :Tn])
            return ins
        T.op('pe', fn_kt, r=['kmtok', 'c6'], w=['P1'], c=0.7)
        T.op('act', lambda e: e.copy(kmT[:, :, :Tn], P1b[:, 0:512].rearrange("p (h t) -> p h t", h=4)[:, :, :Tn]), r=['P1'], w=['kmT'])
        def fn_s(e):
            ins = None
            for h in range(4):
                ins = e.matmul(P4[:Tn, h * 128:h * 128 + Tn], kmT[:, h, :Tn], qmT[:, h, :Tn], start=True, stop=True)
            return ins
        T.op('pe', fn_s, r=['kmT', 'qmT'], w=['P4'])
        T.op('dve', lambda e: e.tensor_tensor(ST[:Tn, :, :Tn], P4[:Tn, :].rearrange("p (h t) -> p h t", h=4)[:, :, :Tn], Ebuf[:Tn, :, :Tn], ALU.mult),
             r=['P4', 'Ebuf'], w=['ST'])
        def fn_qc(e):
            ins = None
            for h in range(4):
                ins = e.matmul(P23[:Tn, h * 256:h * 256 + 129], qmT[:, h, :Tn], Cb[si][:, h, 0:129], start=True, stop=True)
            return ins
        T.op('pe', fn_qc, r=['qmT', cbk], w=['P2', 'P3'])
        def fn_sv(e):
            ins = None
            for h in range(4):
                ins = e.matmul(P01[:Tn, h * 256:h * 256 + 129], ST[:Tn, h, :Tn], v1[:Tn, h, 0:129], start=True, stop=True)
            return ins
        T.op('pe', fn_sv, r=['ST', 'v1'], w=['P0', 'P1'])
        for h in range(4):
            T.op('act', lambda e, h=h: e.activation(nd1[:Tn, h, 0:129], P23[:Tn, h * 256:h * 256 + 129], AF.Copy, scale=tm[:Tn, 4 + h:5 + h]),
                 r=['P2', 'P3', 'tm'], w=['nd1'])
        T.op('dve', lambda e: e.tensor_tensor(nd[:Tn, :, 0:129], nd1[:Tn, :, 0:129],
                                              P01[:Tn, :].rearrange("p (h c) -> p h c", h=4)[:, :, 0:129], ALU.add),
             r=['nd1', 'P0', 'P1'], w=['nd1'])
        state_update(Tn, si)
        T.op('act', lambda e: e.copy(Cb[si][:], C[:]), r=[ck], w=[cbk])
        T.op('dve', lambda e: e.tensor_scalar(sm[:Tn, 20:24], nd[:Tn, :, 128], -1.0, None, ALU.mult), r=['nd1'], w=['sm_den'])
        T.op('dve', lambda e: e.tensor_tensor(sm[:Tn, 20:24], sm[:Tn, 20:24], nd[:Tn, :, 128], ALU.max), r=['nd1', 'sm_den'], w=['sm_den'])
        T.op('dve', lambda e: e.tensor_tensor(sm[:Tn, 0:4], sm[:Tn, 20:24], tm[:Tn, 8:12], ALU.max), r=['sm_den', 'tm'], w=['sm_den'])
        T.op('dve', lambda e: e.reciprocal(sm[:Tn, 0:4], sm[:Tn, 0:4]), r=['sm_den'], w=['sm_den'])
        T.op('dve', lambda e: e.tensor_tensor(hmr[:Tn], nd[:Tn, :, 0:128], sm[:Tn, 0:4].unsqueeze(2).to_broadcast([Tn, 4, 128]), ALU.mult),
             r=['nd1', 'sm_den'], w=['Ebuf'])
        for h in range(4):
            T.op('dve', lambda e, h=h: e.bn_stats(bst[:Tn, h, :], hmr[:Tn, h, :]), r=['Ebuf'], w=['bst'])
        for h in range(4):
            T.op('dve', lambda e, h=h: e.bn_aggr(mv[:Tn, h, :], bst[:Tn, h, :]), r=['bst'], w=['mv'])
        T.op('act', lambda e: e.activation(sm[:Tn, 4:8], mv[:Tn, :, 1], AF.Ln, bias=EPS), r=['mv'], w=['sm_hn'])
        T.op('act', lambda e: e.activation(sm[:Tn, 4:8], sm[:Tn, 4:8], AF.Exp, scale=-0.5), r=['sm_hn'], w=['sm_hn'])
        for h in range(4):
            T.op('dve', lambda e, h=h: e.tensor_scalar(hmr[:Tn, h, :], hmr[:Tn, h, :], mv[:Tn, h, 0:1], sm[:Tn, 4 + h:5 + h], ALU.subtract, ALU.mult),
                 r=['Ebuf', 'mv', 'sm_hn'], w=['Ebuf'])
        T.op('dve', lambda e: e.tensor_tensor(mixb[:Tn, 0:512], hmr[:Tn].rearrange("p h d -> p (h d)"), og[:Tn], ALU.mult),
             r=['Ebuf', 'og'], w=['mbA'])
        cosv, sinv = cosb[cs_slot], sinb[cs_slot]
        csk = 'cs%d' % cs_slot
        for mt_ in range(4):
            mm_B(P4[:, mt_ * 128:mt_ * 128 + 128], C_QAP + mt_ * 128, 128, Tn, ['P4'])
        for mt_ in range(4):
            mm_B(PA1[:, mt_ * 128:mt_ * 128 + 128], C_QAR + mt_ * 128, 128, Tn, ['P1'])
        T.op('dve', lambda e: e.tensor_tensor(rt1[:, :, :Tn], P4[:, :].rearrange("p (h t) -> p h t", h=4)[:, :, :Tn],
                                              cosv[:, :Tn].unsqueeze(1).to_broadcast([128, 4, Tn]), ALU.mult), r=['P4', csk], w=['rt1'])
        T.op('dve', lambda e: e.tensor_tensor(rt2[:, :, :Tn], PA1[:, :].rearrange("p (h t) -> p h t", h=4)[:, :, :Tn],
                                              sinv[:, :Tn].unsqueeze(1).to_broadcast([128, 4, Tn]), ALU.mult), r=['P1', csk], w=['rt2'])
        T.op('dve', lambda e: e.tensor_tensor(qaT[:, :, :Tn], rt1[:, :, :Tn], rt2[:, :, :Tn], ALU.add), r=['rt1', 'rt2'], w=['qaT'])
        mm_B(PA0[:, 0:128], C_KAP, 128, Tn, ['P0'])
        mm_B(PA0[:, 128:256], C_KAR, 128, Tn, ['P0'])
        T.op('dve', lambda e: e.tensor_tensor(rt1[:, 0, :Tn], PA0[:, 0:Tn], cosv[:, :Tn], ALU.mult), r=['P0', csk], w=['rt1'])
        T.op('dve', lambda e: e.tensor_tensor(rt2[:, 0, :Tn], PA0[:, 128:128 + Tn], sinv[:, :Tn], ALU.mult), r=['P0', csk], w=['rt2'])
        T.op('dve', lambda e: e.tensor_tensor(krope[:, :Tn], rt1[:, 0, :Tn], rt2[:, 0, :Tn], ALU.add), r=['rt1', 'rt2'], w=['krope'])
        T.op('act', lambda e: e.copy(kTb[kslot][:, :Tn], krope[:, :Tn]), r=['krope'], w=['kT_%d' % kslot])
        if save_kv:
            T.op('pe', lambda e: e.transpose(PA0[:Tn, 256:384], krope[:, :Tn], ident[:, :]), r=['krope', 'const'], w=['P0'])
            T.op('dve', lambda e: e.tensor_copy(kvout[:Tn, 0, :], PA0[:Tn, 256:384]), r=['P0'], w=['kvout'])
        blocks = [(pslot, 128, prev_bias), (kslot, Tn, 0.0)]
        psb = [(P23, ['P2', 'P3']), (P01, ['P0', 'P1'])]
        for bi_, (ks_, Sb, bias_) in enumerate(blocks):
            ps_, keys_ = psb[bi_]
            def fn_sc(e, ks_=ks_, Sb=Sb, ps_=ps_):
                ins = None
                for j in range(4):
                    for half in range(2):
                        hd = half * 4 + j
                        ins = e.matmul(ps_[:Sb, hd * 128:hd * 128 + Tn], kTb[ks_][half * 64:(half + 1) * 64, :Sb],
                                       qaT[half * 64:(half + 1) * 64, j, :Tn], start=True, stop=True)
                return ins
            T.op('pe', fn_sc, r=['kT_%d' % ks_, 'qaT'], w=keys_)
            for b2 in range(2):
                T.op('act', lambda e, b2=b2, Sb=Sb, ps_=ps_, bias_=bias_, bi_=bi_: e.activation(
                    pT[:Sb, bi_, 4 * b2:4 * b2 + 4, :Tn],
                    ps_[:Sb, 512 * b2:512 * b2 + 512].rearrange("p (h t) -> p h t", h=4)[:, :, :Tn],
                    AF.Exp, bias=bias_, scale=0.125), r=[keys_[b2], 'c5', 'const'], w=['pT%d' % bi_])
        def fn_pv(e):
            ins = None
            for hd in range(8):
                if is_sample:
                    for bi_, (ks_, Sb, _) in enumerate(blocks):
                        ins = e.matmul(P45(hd)[:Tn, :], pT[:Sb, bi_, hd, :Tn], va1[ks_][:Sb, hd // 4, 0:65],
                                       start=(bi_ == 0), stop=(bi_ == 1))
                else:
                    for qc, rng in ((0, ((0, 0, 128), (1, 0, 64))), (1, ((0, 64, 128), (1, 0, 128)))):
                        for ii, (bi_, s0, s1_) in enumerate(rng):
                            ks_ = blocks[bi_][0]
                            ins = e.matmul(P45(hd)[qc * 64:(qc + 1) * 64, :], pT[s0:s1_, bi_, hd, qc * 64:(qc + 1) * 64],
                                           va1[ks_][s0:s1_, hd // 4, 0:65], start=(ii == 0), stop=(ii == 1))
            return ins
        def P45(hd):
            base = P4 if hd < 4 else PA1
            c = (hd % 4) * 65
            return base[:, c:c + 65]
        T.op('pe', fn_pv, r=['pT0', 'pT1', 'va1_%d' % pslot, 'va1_%d' % kslot], w=['P4', 'P1'])
        for half, base, bk in ((0, P4, 'P4'), (1, PA1, 'P1')):
            o3 = base[:Tn, 0:260].rearrange("p (h c) -> p h c", h=4)
            T.op('dve', lambda e, o3=o3, half=half: e.tensor_tensor(sm[:Tn, 8 + 4 * half:12 + 4 * half], o3[:, :, 64], esink[:Tn, 4 * half:4 * half + 4], ALU.add),
                 r=[bk, 'c4'], w=['sm_pv'])
        T.op('dve', lambda e: e.reciprocal(sm[:Tn, 8:16], sm[:Tn, 8:16]), r=['sm_pv'], w=['sm_pv'])
        for half, base, bk in ((0, P4, 'P4'), (1, PA1, 'P1')):
            o3 = base[:Tn, 0:260].rearrange("p (h c) -> p h c", h=4)
            T.op('dve', lambda e, o3=o3, half=half: e.tensor_tensor(
                mixb[:Tn, 512 + 256 * half:768 + 256 * half].rearrange("p (h d) -> p h d", h=4), o3[:, :, 0:64],
                sm[:Tn, 8 + 4 * half:12 + 4 * half].unsqueeze(2).to_broadcast([Tn, 4, 64]), ALU.mult),
                r=[bk, 'sm_pv'], w=['mbB%d' % half])
        def fn_mt(e):
            ins = None
            for k in range(8):
                ins = e.transpose(P0b[:, k * 128:k * 128 + Tn], mixb[:Tn, k * 128:(k + 1) * 128], identb[:Tn, :Tn])
            return ins
        T.op('pe', fn_mt, r=['mbA', 'mbB0', 'mbB1', 'c6'], w=['P0'], c=1.1)
        T.op('act', lambda e: e.copy(mixT[:, :, :Tn], P0b[:, :].rearrange("p (k t) -> p k t", k=8)[:, :, :Tn]), r=['P0'], w=['mixT'], c=0.9)
        if mid_hook is not None:
            mid_hook()
        for n in range(2):
            def fn_o(e, n=n):
                ins = None
                for k in range(8):
                    ins = e.matmul(P23[:Tn, n * 512:(n + 1) * 512], mixT[:, k, :Tn], wout[:, k, n * 512:(n + 1) * 512], start=(k == 0), stop=(k == 7))
                return ins
            T.op('pe', fn_o, r=['mixT', 'wout'], w=['P%d' % (2 + n)], c=2.5)
        xk = 'xin%d' % xslot
        T.op('dve', lambda e: e.scalar_tensor_tensor(s1[:Tn], xin[xslot][:Tn], ALPHA, P23[:Tn, :], ALU.mult, ALU.add), r=[xk, 'P2', 'P3'], w=['mix'], c=1.3)
        layer_norm(s1, MIXK, Tn, 0, hres[:, htile, :], hrk)
        transposes8(hres[:, htile, :], Tn, P01, [hrk])
        for b in range(2):
            T.op('act', lambda e, b=b: e.copy(hT[:, 4 * b:4 * b + 4, hcol:hcol + Tn],
                                              P01[:, 512 * b:512 * b + 512].rearrange("p (k t) -> p k t", k=4)[:, :, :Tn]),
                 r=['P%d' % b], w=[htk])

    def layer_norm(src, skey, Tn, which, dst, dkey):
        skeys = list(skey) if isinstance(skey, (list, tuple)) else [skey]
        skey = skeys[-1]
        bstL, mvL, smL = lnscr[which]
        kb, km_, ks_ = 'bstL%d' % which, 'mvL%d' % which, 'smL%d' % which
        for c in range(2):
            T.op('dve', lambda e, c=c: e.bn_stats(bstL[:Tn, c, :], src[:Tn, c * 512:(c + 1) * 512]), r=skeys, w=[kb])
        T.op('dve', lambda e: e.bn_aggr(mvL[:Tn, :], bstL[:Tn, 0:2, :].rearrange("p a b -> p (a b)")), r=[kb], w=[km_])
        T.op('act', lambda e: e.activation(smL[:Tn, 0:1], mvL[:Tn, 1:2], AF.Ln, bias=EPS), r=[km_], w=[ks_])
        T.op('act', lambda e: e.activation(smL[:Tn, 0:1], smL[:Tn, 0:1], AF.Exp, scale=-0.5), r=[ks_], w=[ks_])
        T.op('dve', lambda e: e.scalar_tensor_tensor(src[:Tn], src[:Tn], mvL[:Tn, 0:1], lnp[:Tn, 2 * which, :], ALU.subtract, ALU.mult),
             r=skeys + [km_, 'const'], w=skeys, c=1.25)
        T.op('dve', lambda e: e.scalar_tensor_tensor(dst[:Tn], src[:Tn], smL[:Tn, 0:1], lnp[:Tn, 2 * which + 1, :], ALU.mult, ALU.add),
             r=skeys + [ks_, 'const'], w=[dkey], c=1.25)

    ring_ctr = [0]
    dring_ctr = [0]
    set_ctr = [0]

    def stage2(N, segs, cvbuf, cvkey, first_macro, out_fn, par=0):
        hres = hresL[par]
        hT = hTL[par]
        htk = 'hT%d' % par
        for u in range(NU):
            rs = ring_ctr[0] % 2
            ring_ctr[0] += 1
            bs = u % 2
            PSU = (P6, P7)[bs]
            pk = ('P6', 'P7')[bs]
            si_ = set_ctr[0] % NS
            set_ctr[0] += 1
            ustv = ustS[si_]
            uk = ['ustS%d' % si_]
            ctv = ctS[si_]
            ckk = 'ctS%d' % si_
            T.dma('sp', 'wur%d' % rs, lambda e, rs=rs, u=u: e.dma_start(out=wur[rs][:].rearrange("p k c -> p (k c)"), in_=wupbf[u]),
                  r=['wupbf'], w=['wur%d' % rs])
            def fn_u(e, rs=rs, PSU=PSU):
                ins = None
                for j in range(2):
                    for k in range(8):
                        ins = e.matmul(PSU[:, j * 256:j * 256 + N], wur[rs][:, k, j * 128:(j + 1) * 128], hT[:, k, :N], start=(k == 0), stop=(k == 7))
                return ins
            T.op('pe', fn_u, r=['wur%d' % rs, htk], w=[pk], c=16 * (0.09 + max(N, 64) / 2400.0) + 0.1)
            pu = PSU[:, :].rearrange("p (j t) -> p j t", j=2)
            cv4 = cvbuf[:].rearrange("p (j u) t -> p j u t", j=2)
            T.op('act', lambda e, ustv=ustv, pu=pu: e.copy(ustv[:, :, 2:2 + N], pu[:, :, :N]), r=[pk], w=uk, c=0.7)
            T.op('dve', lambda e, u=u, ustv=ustv: e.tensor_copy(ustv[:, :, 0:2], cv4[:, :, u, :]), r=[cvkey], w=uk)
            if first_macro:
                T.op('dve', lambda e, u=u, ustv=ustv: e.tensor_scalar(cv4[:, :, u, :], ustv[:, :, N:N + 2], hs[:, 0:1], None, ALU.mult), r=uk + ['const'], w=[cvkey])
                continue
            T.op('dve', lambda e, u=u, ustv=ustv: e.tensor_copy(cv4[:, :, u, :], ustv[:, :, N:N + 2]), r=uk, w=[cvkey])
            for j in range(2):
                ci = j * NU + u
                T.op('act', lambda e, j=j, ci=ci, ctv=ctv, pu=pu: e.activation(ctv[:, j, :N], pu[:, j, :N], AF.Identity, bias=bcv[:, ci:ci + 1], scale=wcv[:, 2, ci:ci + 1]),
                     r=[pk, 'const'], w=[ckk], c=0.45)
                T.op('dve', lambda e, j=j, ci=ci, ctv=ctv, ustv=ustv: e.scalar_tensor_tensor(ctv[:, j, :N], ustv[:, j, 1:1 + N], wcv[:, 1, ci:ci + 1], ctv[:, j, :N], ALU.mult, ALU.add),
                     r=uk + [ckk, 'const'], w=[ckk], c=0.45)
                T.op('dve', lambda e, j=j, ci=ci, ctv=ctv, ustv=ustv: e.scalar_tensor_tensor(ctv[:, j, :N], ustv[:, j, 0:N], wcv[:, 0, ci:ci + 1], ctv[:, j, :N], ALU.mult, ALU.add),
                     r=uk + [ckk, 'const'], w=[ckk], c=0.45)
            T.op('act', lambda e, ctv=ctv: e.activation(ctv[:, 1, :N], ctv[:, 1, :N], AF.Silu), r=[ckk], w=[ckk], c=0.45)
            T.op('pool', lambda e, u=u, ctv=ctv: e.tensor_tensor(actT[:, u, :N], ctv[:, 1, :N], ctv[:, 0, :N], ALU.mult), r=[ckk], w=['actT'], c=0.7)
        if first_macro:
            return
        accs = [(P6, 'P6'), (P7, 'P7')]
        NCH = NU // 2
        for si_, (col0, Tn, htile) in enumerate(segs):
            pass
        for n in range(2):
            for c in range(NCH):
                ds = dring_ctr[0] % 5
                dring_ctr[0] += 1
                T.dma('sp', 'wdr%d' % ds, lambda e, ds=ds, c=c, n=n: e.dma_start(
                    out=wdr[ds][:], in_=wdnbf[c].rearrange("p (k n) -> p k n", k=2)[:, :, n * 512:(n + 1) * 512]),
                    r=['wdnbf'], w=['wdr%d' % ds])
                def fn_d(e, ds=ds, c=c):
                    ins = None
                    for si_, (col0, Tn, htile) in enumerate(segs):
                        acc = accs[si_][0]
                        for kk in range(2):
                            kc = 2 * c + kk
                            ins = e.matmul(acc[:Tn, :], actT[:, kc, col0:col0 + Tn], wdr[ds][:, kk, :],
                                           start=(kc == 0), stop=(kc == NU - 1))
                    return ins
                T.op('pe', fn_d, r=['actT', 'wdr%d' % ds], w=[accs[i][1] for i in range(len(segs))], c=len(segs) * 2 * 0.31 + 0.1)
            for si_, (col0, Tn, htile) in enumerate(segs):
                acc, akey = accs[si_]
                hk = 'hres%d_%d' % (par, htile)
                T.op('dve', lambda e, Tn=Tn, htile=htile, acc=acc, n=n: e.scalar_tensor_tensor(
                    hres[:Tn, htile, n * 512:(n + 1) * 512], hres[:Tn, htile, n * 512:(n + 1) * 512], ALPHA, acc[:Tn, :], ALU.mult, ALU.add),
                    r=[hk, akey], w=[hk], c=0.7)
        for si_, (col0, Tn, htile) in enumerate(segs):
            hk = 'hres%d_%d' % (par, htile)
            layer_norm(hres[:, htile, :], hk, Tn, 1, hres[:, htile, :], hk)
            out_fn(Tn, si_, hres[:, htile, :], hk, par)

    def load_cs(slot, idx):
        T.dma('sp', 'cs%d' % slot, lambda e: e.dma_start(out=cosb[slot][:], in_=cos_d[idx]), w=['cs%d' % slot])
        T.dma('sp', 'cs%d' % slot, lambda e: e.dma_start(out=sinb[slot][:], in_=sin_d[idx]), w=['cs%d' % slot])

    def do_main():
        n_macro = nt_full // 2
        xbase = 8192 - NT_FULL * 128
        yrow = [0]
        for mi in range(n_macro):
            for tt in range(2):
                i = 2 * mi + tt
                slot = i % 2
                if i == 0:
                    load_x(0, xall[xbase: xbase + 128, :], 128)
                    load_cs(0, 0)
                if i + 1 < nt_full:
                    load_x((i + 1) % 2, xall[xbase + (i + 1) * 128: xbase + (i + 2) * 128, :], 128)
                    load_cs((i + 1) % 2, i + 1)
                T.cur_tag = 's1_%02d' % mi
                kslot = i % 2
                pslot = (i + 1) % 2
                pbias = hs[:, 1:2] if i == 2 else (NEG if i == 0 else 0.0)
                last = (i == nt_full - 1)
                if i == 0:
                    make_xT(0, 128)
                nxt = (lambda ns=(i + 1) % 2: make_xT(ns, 128)) if i + 1 < nt_full else None
                stage1(128, slot, 0, kslot, pslot, pbias, slot, tt * 128, tt, last, False, par=mi % 2, head_done=True, mid_hook=nxt)
                if i == 1:
                    T.op('dve', lambda e: e.tensor_scalar(mst[0][:], mst[0][:], hs[0:4, 0:1], None, ALU.mult), r=['m0', 'const'], w=['m0'])
                if last:
                    out_toks.append(T.dma('sp', 'okvp', lambda e: e.dma_start(out=kp_d[:, :], in_=kvout[:, 0, :]), r=['kvout']))
                    out_toks.append(T.dma('sp', 'okvp', lambda e: e.dma_start(out=vp_d[:, :], in_=kvout[:, 1, :]), r=['kvout']))

            def out_y(Tn, yslot, src, hk, par):
                r0 = yrow[0]
                yrow[0] += Tn
                out_toks.append(T.dma('sp', 'oy%d_%d' % (par, yslot), lambda e: e.dma_start(out=y_d[r0:r0 + Tn, :], in_=src[:Tn, :]), r=[hk]))
            T.cur_tag = 's2_%02d' % mi
            stage2(256, [(0, 128, 0), (128, 128, 1)], cvh, 'cvh', mi == 0, out_y, par=mi % 2)

        out_toks.append(T.dma('sp', 'ostC', lambda e: e.dma_start(out=Cp_d.rearrange("h d e -> d h e"), in_=Cst[0][:, :, 0:128]), r=['C0']))
        out_toks.append(T.dma('sp', 'ostC', lambda e: e.dma_start(out=np_d.rearrange("h d -> d h"), in_=Cst[0][:, :, 128], allow_slow_non_contiguous=True), r=['C0']))
        out_toks.append(T.dma('sp', 'ostm', lambda e: e.dma_start(out=mp_d[:, :], in_=mst[0][:]), r=['m0']))
        for j in range(2):
            out_toks.append(T.dma('sp', 'ostcv', lambda e, j=j: e.dma_start(out=cvp_d[j, :].rearrange("(c p) -> p c", p=128), in_=cvh[:, :, j],
                                                                           allow_slow_non_contiguous=True), r=['cvh']))

    def do_sample():
        T.cur_tag = 'sample'
        load_x(0, xs_d[:, :], 16)
        load_cs(0, NT_FULL)
        T.dma('sp', 'xin1', lambda e: e.dma_start(out=xin[1][:, 0:128], in_=ck_d[:, :]), w=['xin1'])
        T.dma('sp', 'xin1', lambda e: e.dma_start(out=xin[1][:, 128:256], in_=cv_d[:, :]), w=['xin1'])
        T.op('pe', lambda e: e.transpose(P4[:, 0:128], xin[1][:, 0:128], ident[:, :]), r=['xin1', 'const'], w=['P4'])
        T.op('act', lambda e: e.copy(kTb[1][:, :], P4[:, 0:128]), r=['P4'], w=['kT_1'])
        T.op('act', lambda e: e.copy(va1[1][:, :, 0:64], xin[1][:, 128:256].rearrange("p (k d) -> p k d", k=2)), r=['xin1'], w=['va1_1'])
        T.op('dve', lambda e: e.memset(Cst[0][:], 0.0), w=['C0'])
        T.dma('sp', 'sstC', lambda e: e.dma_start(out=Cst[0][:, :, 0:128], in_=C0_d.rearrange("h d e -> d h e")), w=['C0'])
        T.dma('sp', 'sstC', lambda e: e.dma_start(
            out=Cst[0][:, :, 128], in_=n0_d.rearrange("h d -> d h"), allow_slow_non_contiguous=True), w=['C0'])
        T.dma('sp', 'sstm', lambda e: e.dma_start(out=mst[0][:], in_=m0_d[:, :]), w=['m0'])
        T.op('act', lambda e: e.copy(Cb[0][:], Cst[0][:]), r=['C0'], w=['Cb0'])
        stage1(16, 0, 0, 0, 1, 0.0, 0, 0, 0, True, True)
        out_toks.append(T.dma('sp', 'okvs', lambda e: e.dma_start(out=ks_d[:, :], in_=kvout[:16, 0, :]), r=['kvout']))
        out_toks.append(T.dma('sp', 'okvs', lambda e: e.dma_start(out=vs_d[:, :], in_=kvout[:16, 1, :]), r=['kvout']))

        def out_ys(Tn, yslot, src, hk, par):
            out_toks.append(T.dma('sp', 'oys', lambda e: e.dma_start(out=ys_d[:, :], in_=src[:Tn, :]), r=[hk]))
        stage2(16, [(0, 16, 0)], cvsb, 'cvsb', False, out_ys, par=0)
        out_toks.append(T.dma('sp', 'ossC', lambda e: e.dma_start(out=Cs_d.rearrange("h d e -> d h e"), in_=Cst[0][:, :, 0:128]), r=['C0']))
        out_toks.append(T.dma('sp', 'ossC', lambda e: e.dma_start(out=ns_d.rearrange("h d -> d h"), in_=Cst[0][:, :, 128], allow_slow_non_contiguous=True), r=['C0']))
        out_toks.append(T.dma('sp', 'ossm', lambda e: e.dma_start(out=ms_d[:, :], in_=mst[0][:]), r=['m0']))
        for j in range(2):
            out_toks.append(T.dma('sp', 'osscv', lambda e, j=j: e.dma_start(out=cvs_d[j, :].rearrange("(c p) -> p c", p=128), in_=cvsb[:, :, j],
                                                                           allow_slow_non_contiguous=True), r=['cvsb']))
    do_sample()
    T.cur_tag = 'setup2'
    T.op('dve', lambda e: e.memset(Cst[0][:], 0.0), w=['C0'])
    T.op('dve', lambda e: e.memset(Cb[0][:], 0.0), w=['Cb0'])
    T.op('dve', lambda e: e.memset(mst[0][:], 0.0), w=['m0'])
    do_prefix()
    do_main()
    T.wait_all('sp', out_toks)
    if FILL:
        T.filler_fn = lambda e: e.matmul(P5[:, 0:512], identb[:, :], win[:, 0, 0:512], start=True, stop=True)
        T.fill_frac = float(os.environ.get('K_FILL_FRAC', '0.5'))
        T.filler_cost = float(os.environ.get('K_FILL_COST', '0.3'))
        T.fill_min = float(os.environ.get('K_FILL_MIN', '1.5'))
    T.schedule()

    with nc.Block() as block:
        @block.sync
        def _(e):
            T.replay('sp', e)

        @block.tensor
        def _(e):
            T.replay('pe', e)

        @block.scalar
        def _(e):
            T.replay('act', e)

        @block.vector
        def _(e):
            T.replay('dve', e)

        @block.gpsimd
        def _(e):
            T.replay('pool', e)
    es.close()
    return nc


def _host_consts():
    ident = np.eye(128, dtype=np.float32)
    s = np.arange(128)[:, None]
    l = np.arange(128)[None, :]
    maskT = np.where(l >= s, 0.0, NEG).astype(np.float32)
    amask = np.ones((128, 2, 128), np.float32)
    amask[:, 0, :] = np.where((s < 64) & (l >= 64), 0.0, 1.0)
    amask[:, 1, :] = np.where((s >= 64) & (l < 64), 0.0, 1.0)
    sel4 = np.zeros((4, 4, 128), np.float32)
    for h in range(4):
        sel4[h, h, :] = 1.0
    return ident, maskT, amask, sel4.reshape(4, 512)


def _rope_tables(pos):
    half = 32
    inv = (np.float32(10000.0) ** (-np.arange(half, dtype=np.float32) / np.float32(half))).astype(np.float32)
    d = np.arange(128) % 64
    f = inv[d % 32]
    ang = (pos.astype(np.float32)[None, :] * f[:, None]).astype(np.float32)
    c = np.cos(ang).astype(np.float32)
    sn = np.sin(ang).astype(np.float32)
    sign = np.where(d < 32, -1.0, 1.0).astype(np.float32)[:, None]
    return c, (sn * sign).astype(np.float32)


_NC_CACHE = {}


def kernel(x_prompt, x_sample, state_mlstm_C, state_mlstm_n, state_mlstm_m, cache_swa_k, cache_swa_v,
           state_conv, w_in, b_igate, b_fgate, g_mlstm_norm, attn_sinks, w_out, ln1_g, ln1_b,
           w_up, w_conv, b_conv, w_down, ln2_g, ln2_b):
    f = lambda a: np.ascontiguousarray(np.asarray(a, dtype=np.float32))
    x_prompt, x_sample = f(x_prompt), f(x_sample)
    w_in0 = f(w_in)[0]
    rot = (np.arange(64) + 32) % 64
    qa0, ka0, va0 = 2056, 2568, 2696
    qap, qar = [], []
    for j in range(4):
        for hd in (j, 4 + j):
            qap.extend(qa0 + hd * 64 + np.arange(64))
            qar.extend(qa0 + hd * 64 + rot)
    kap = list(ka0 + np.arange(128))
    kar = list(ka0 + np.concatenate([rot, 64 + rot]))
    cols = list(range(0, 2056)) + list(range(va0, va0 + 128)) + qap + qar + kap + kar
    w_in_aug = np.ascontiguousarray(w_in0[:, np.array(cols, dtype=np.int64)])
    assert w_in_aug.shape[1] == WCOLS
    ident, maskT, amask, sel4 = _host_consts()
    ln = np.stack([f(ln1_g)[0], f(ln1_b)[0], f(ln2_g)[0], f(ln2_b)[0]], 0)

    nt_pre = int(os.environ.get("K_NT_PRE", NT_PRE))
    nt_full = int(os.environ.get("K_NT_FULL", NT_FULL))
    key = (nt_pre, nt_full)
    if key not in _NC_CACHE:
        _NC_CACHE[key] = build_program(nt_pre, nt_full)
    nc = _NC_CACHE[key]

    in_maps = []
    for c in range(8):
        b, half = c // 2, c % 2
        if half == 1:
            xall = x_prompt[b]
        else:
            xall = np.concatenate([np.zeros((4096, D), np.float32), x_prompt[b, :4096]], 0)
        pos0 = half * 4096 - 256
        cosT = np.zeros((NT_FULL + 1, 128, 128), np.float32)
        sinT = np.zeros((NT_FULL + 1, 128, 128), np.float32)
        for i in range(NT_FULL):
            cc, ss = _rope_tables(pos0 + i * 128 + np.arange(128))
            cosT[i], sinT[i] = cc, ss
        cc, ss = _rope_tables(2048 + np.arange(16))
        cosT[NT_FULL, :, :16], sinT[NT_FULL, :, :16] = cc, ss
        hsv = np.zeros((128, 2), np.float32)
        hsv[:, 0] = float(half)
        hsv[:, 1] = 0.0 if half == 1 else NEG
        in_maps.append({
            "xall": np.ascontiguousarray(xall), "xs": x_sample[c],
            "C0": f(state_mlstm_C)[0, c], "n0": f(state_mlstm_n)[0, c], "m0": f(state_mlstm_m)[0, c].reshape(4, 1),
            "ck": f(cache_swa_k)[0, c].reshape(128, 128), "cv": f(cache_swa_v)[0, c].reshape(128, 128),
            "sconv": f(state_conv)[0, c],
            "w_in": w_in_aug, "w_out": f(w_out)[0], "w_up": f(w_up)[0], "w_down": f(w_down)[0],
            "w_conv": f(w_conv)[0], "b_conv": f(b_conv)[0].reshape(1, -1),
            "b_i": f(b_igate)[0].reshape(4, 1), "b_f": f(b_fgate)[0].reshape(4, 1),
            "g_norm": f(g_mlstm_norm)[0].reshape(1, 512), "sinks": f(attn_sinks)[0].reshape(1, 8),
            "ln": ln, "cosT": cosT, "sinT": sinT, "ident": ident, "maskT": maskT, "amask": amask,
            "sel4": sel4, "hs": hsv,
        })
    res = run_bass_kernel_spmd(nc, in_maps, core_ids=list(range(8)))
    R = res.results
    y_p = np.zeros((4, 8192, D), np.float32)
    for c in range(8):
        y_p[c // 2, (c % 2) * 4096:(c % 2 + 1) * 4096] = R[c]["y"]
    y_s = np.stack([R[c]["ys"] for c in range(8)], 0)
    odd = [1, 3, 5, 7]
    C_p = np.stack([R[c]["Cp"] for c in odd], 0)[None]
    n_p = np.stack([R[c]["np"] for c in odd], 0)[None]
    m_p = np.stack([R[c]["mp"].reshape(4) for c in odd], 0)[None]
    k_p = np.stack([R[c]["kp"].reshape(128, 2, 64) for c in odd], 0)[None]
    v_p = np.stack([R[c]["vp"].reshape(128, 2, 64) for c in odd], 0)[None]
    cv_p = np.stack([R[c]["cvp"] for c in odd], 0)[None]
    C_s = np.stack([R[c]["Cs"] for c in range(8)], 0)[None]
    n_s = np.stack([R[c]["ns"] for c in range(8)], 0)[None]
    m_s = np.stack([R[c]["ms"].reshape(4) for c in range(8)], 0)[None]
    k_s = np.stack([R[c]["ks"].reshape(16, 2, 64) for c in range(8)], 0)[None]
    v_s = np.stack([R[c]["vs"].reshape(16, 2, 64) for c in range(8)], 0)[None]
    cv_s = np.stack([R[c]["cvs"] for c in range(8)], 0)[None]
    return (y_p, y_s, C_p, n_p, m_p, k_p, v_p, cv_p, C_s, n_s, m_s, k_s, v_s, cv_s)
```

```python
import os
from contextlib import ExitStack
import numpy as np
import concourse.bass as bass
import concourse.mybir as mybir
from concourse.bass_utils import run_bass_kernel_spmd

F32 = mybir.dt.float32
BF16 = mybir.dt.bfloat16
AF = mybir.ActivationFunctionType
ALU = mybir.AluOpType
AX = mybir.AxisListType

D = 1024
NT_PRE = 30
NT_FULL = 34
DFF = 2816
NU = 22
WCOLS = 3464
C_QM, C_KM, C_VM, C_OM, C_IP, C_FP, C_VA, C_QAP, C_QAR, C_KAP, C_KAR = 0, 512, 1024, 1536, 2048, 2052, 2056, 2184, 2696, 3208, 3336
ALPHA = float(2.0 ** 0.25)
EPS = 1e-5
KSCALE = float(128.0 ** -0.5)
NEG = -30000.0


SKIP_WAR = int(os.environ.get('K_SKIP_WAR', '0'))


class Tracker:
    HOP = 0.4
    DEF_COST = {'pe': 0.7, 'act': 0.45, 'dve': 0.3, 'pool': 0.8, 'sp': 0.08}

    def __init__(self, nc, es):
        self.nc = nc
        self.es = es
        self.engs = {}
        self.last_w = {}
        self.readers = {}
        self.streams = {}
        self.ops = []
        self.thr_hist = {}
        self.cur_tag = 'setup'
        self.filler_fn = None
        self.pe_scale = float(os.environ.get('K_PE_SCALE', '0.65'))
        self.ad_scale = float(os.environ.get('K_AD_SCALE', '1.1'))
        self.HOP = float(os.environ.get('K_HOP', '0.9'))
        self.hop_same = float(os.environ.get('K_HOP_SAME', '0.3'))
        self.filler_cost = 0.43
        self.fill_min = 1.5
        self.fill_frac = 0.6

    def add_engine(self, name, eng):
        sem = self.es.enter_context(self.nc.semaphore("s_" + name))
        self.engs[name] = dict(eng=eng, sem=sem, name=name, order=[])

    def stream(self, name, group=False):
        if name not in self.streams:
            sem = self.es.enter_context(self.nc.semaphore("d_" + name))
            self.streams[name] = dict(sem=sem, count=0, group=group, name=name)
        return self.streams[name]

    def _deps(self, reads, writes, extra, group):
        deps = set()
        for k in reads:
            deps.update(self.last_w.get(k, ()))
        self._raw = set(deps)
        if not group:
            for k in writes:
                deps.update(self.last_w.get(k, ()))
                deps.update(self.readers.get(k, ()))
        deps.update(extra)
        return deps

    def _commit(self, oid, reads, writes, group):
        for k in reads:
            self.readers.setdefault(k, []).append(oid)
        for k in writes:
            if group:
                self.last_w.setdefault(k, []).append(oid)
            else:
                self.last_w[k] = [oid]
                self.readers[k] = []

    def op(self, ename, fn, r=(), w=(), extra=(), c=None):
        deps = self._deps(r, w, extra, False)
        oid = len(self.ops)
        cc = self.DEF_COST[ename] if c is None else c
        if ename == 'pe':
            cc *= self.pe_scale
        elif ename in ('act', 'dve', 'pool'):
            cc *= self.ad_scale
        self.ops.append(dict(id=oid, eng=ename, fn=fn, deps=deps, dma=None, tag=self.cur_tag, cost=cc, raw=self._raw | set(extra)))
        self._commit(oid, r, w, False)
        return oid

    def dma(self, qname, stream, fn, r=(), w=(), extra=(), group=False, c=4.0):
        S = self.stream(stream, group)
        deps = self._deps(r, w, extra, group)
        oid = len(self.ops)
        thr = ()
        if group:
            hist = self.thr_hist.setdefault(qname, [])
            B = 3 if qname == 'pool' else 4
            nb = len(hist) // B
            if nb > 0:
                thr = tuple(hist[(nb - 1) * B:nb * B])
                deps.update(thr)
            hist.append(oid)
        self.ops.append(dict(id=oid, eng=qname, fn=fn, deps=deps, dma=S, cost=c, thr=thr, tag=self.cur_tag))
        self._commit(oid, r, w, group)
        return oid

    def wait_all(self, ename, toks):
        oid = len(self.ops)
        self.ops.append(dict(id=oid, eng=ename, fn=None, deps=set(toks), dma=None, cost=0.05, tag='final'))
        return oid

    def schedule(self):
        import heapq
        ops = self.ops
        n = len(ops)
        ndeps = [len(o['deps']) for o in ops]
        users = [[] for _ in range(n)]
        for o in ops:
            for d in o['deps']:
                users[d].append(o['id'])
        ready_t = [0.0] * n
        fin = [0.0] * n
        bl = [0.0] * n
        for o in reversed(ops):
            i = o['id']
            m = 0.0
            for u in users[i]:
                if bl[u] > m:
                    m = bl[u]
            bl[i] = m + o['cost'] + self.HOP
        mode = os.environ.get('K_PRIO', 'id')
        if mode == 'bl':
            prio = [(-bl[i], i) for i in range(n)]
        else:
            prio = [(i, i) for i in range(n)]
        ready = {e: [] for e in self.engs}
        free_t = {e: 0.0 for e in self.engs}
        events = []
        for o in ops:
            if ndeps[o['id']] == 0:
                heapq.heappush(ready[o['eng']], (prio[o['id']], o['id']))
        for e in self.engs:
            heapq.heappush(events, (0.0, 0, e))
        seq = 1
        done = 0
        pending_wake = {e: True for e in self.engs}
        while done < n:
            if not events:
                raise RuntimeError("scheduler stuck")
            t, _, e = heapq.heappop(events)
            pending_wake[e] = False
            if free_t[e] > t + 1e-9:
                heapq.heappush(events, (free_t[e], seq, e)); seq += 1; pending_wake[e] = True
                continue
            cand = None
            tmp = []
            while ready[e]:
                item = heapq.heappop(ready[e])
                oid = item[1]
                if ready_t[oid] <= t + 1e-9:
                    cand = oid
                    break
                tmp.append(item)
            for x in tmp:
                heapq.heappush(ready[e], x)
            if cand is None:
                if ready[e]:
                    tn = min(ready_t[x[1]] for x in ready[e])
                    heapq.heappush(events, (tn, seq, e)); seq += 1; pending_wake[e] = True
                continue
            o = ops[cand]
            o['t0'] = t
            self.engs[e]['order'].append(cand)
            if o['dma'] is not None:
                free_t[e] = t + (0.6 if e == 'pool' else 0.08)
                fin[cand] = t + o['cost']
            else:
                free_t[e] = t + o['cost']
                fin[cand] = free_t[e]
            done += 1
            for u in users[cand]:
                ndeps[u] -= 1
                if ops[u]['eng'] == e and o['dma'] is None:
                    if e == 'pe' or (SKIP_WAR and e in ('act', 'dve') and cand not in ops[u].get('raw', ops[u]['deps'])):
                        rt = fin[cand]
                    else:
                        rt = fin[cand] + self.hop_same
                else:
                    rt = fin[cand] + self.HOP
                if rt > ready_t[u]:
                    ready_t[u] = rt
                if ndeps[u] == 0:
                    ue = ops[u]['eng']
                    heapq.heappush(ready[ue], (prio[u], u))
                    if not pending_wake[ue]:
                        heapq.heappush(events, (max(ready_t[u], free_t[ue]), seq, ue)); seq += 1; pending_wake[ue] = True
            heapq.heappush(events, (free_t[e], seq, e)); seq += 1; pending_wake[e] = True
        self.sim_time = max(fin)
        self.fin = fin
        if self.filler_fn is not None:
            fc = self.filler_cost
            new_order = []
            prev_end = None
            nfill = 0
            armed = False
            for oid in self.engs['pe']['order']:
                o = ops[oid]
                if not armed:
                    if any(ops[d]['dma'] is not None and ops[d]['dma']['name'] == 'win' for d in o['deps']):
                        armed = True
                    new_order.append(oid)
                    prev_end = fin[oid]
                    continue
                if prev_end is not None:
                    gap = o['t0'] - prev_end
                    if gap > self.fill_min:
                        k = int((gap * self.fill_frac) / fc)
                        for _ in range(k):
                            fid = len(ops)
                            ops.append(dict(id=fid, eng='pe', fn=self.filler_fn, deps=set(), dma=None, cost=fc, tag='fill', filler=True))
                            new_order.append(fid)
                            nfill += 1
                new_order.append(oid)
                prev_end = fin[oid]
            self.engs['pe']['order'] = new_order
            self.nfill = nfill
        for e, E in self.engs.items():
            pos = 0
            for oid in E['order']:
                o = ops[oid]
                if o['dma'] is not None:
                    o['dma']['count'] += 16
                    o['val'] = o['dma']['count']
                elif o['fn'] is not None and not o.get('filler'):
                    pos += 1
                    o['val'] = pos

    def replay(self, ename, eng):
        E = self.engs[ename]
        ops = self.ops
        seen = {}
        for oid in E['order']:
            o = ops[oid]
            need = {}
            for d in o['deps']:
                D = ops[d]
                if D['dma'] is not None:
                    S = D['dma']
                    if d in o.get('thr', ()):
                        sem, val = S['sem'], D['val']
                    else:
                        sem, val = S['sem'], (S['count'] if S['group'] else D['val'])
                else:
                    if D['fn'] is None:
                        continue
                    if D['eng'] == 'pe' and ename == 'pe':
                        continue
                    if SKIP_WAR and D['eng'] == ename and ename in ('act', 'dve') and d not in o.get('raw', o['deps']):
                        continue
                    sem, val = self.engs[D['eng']]['sem'], D['val']
                if need.get(sem, 0) < val:
                    need[sem] = val
            for sem, val in need.items():
                if seen.get(sem, 0) < val:
                    eng.wait_ge(sem, val)
                    seen[sem] = val
            if o['fn'] is None:
                continue
            ins = o['fn'](eng)
            if o.get('filler'):
                continue
            if o['dma'] is not None:
                ins.then_inc(o['dma']['sem'], 16)
            else:
                ins.then_inc(E['sem'], 1)


def build_program(nt_pre=NT_PRE, nt_full=NT_FULL):
    nc = bass.Bass("TRN2", target_bir_lowering=False)
    es = ExitStack()

    def din(name, shape, dt=F32):
        return nc.dram_tensor(name, list(shape), dt, kind="ExternalInput").ap()

    def dout(name, shape, dt=F32):
        return nc.dram_tensor(name, list(shape), dt, kind="ExternalOutput").ap()

    xall = din("xall", [8192, D])
    xs_d = din("xs", [16, D])
    C0_d = din("C0", [4, 128, 128])
    n0_d = din("n0", [4, 128])
    m0_d = din("m0", [4, 1])
    ck_d = din("ck", [128, 128])
    cv_d = din("cv", [128, 128])
    sconv_d = din("sconv", [2, 2 * DFF])
    win_d = din("w_in", [D, WCOLS])
    wout_d = din("w_out", [D, D])
    wup_d = din("w_up", [D, 2 * DFF])
    wdn_d = din("w_down", [DFF, D])
    wconv_d = din("w_conv", [3, 2 * DFF])
    bconv_d = din("b_conv", [1, 2 * DFF])
    bi_d = din("b_i", [4, 1])
    bf_d = din("b_f", [4, 1])
    gn_d = din("g_norm", [1, 512])
    sink_d = din("sinks", [1, 8])
    ln_d = din("ln", [4, D])
    cos_d = din("cosT", [NT_FULL + 1, 128, 128])
    sin_d = din("sinT", [NT_FULL + 1, 128, 128])
    ident_d = din("ident", [128, 128])
    maskT_d = din("maskT", [128, 128])
    amask_d = din("amask", [128, 2, 128])
    sel4_d = din("sel4", [4, 4 * 128])
    hs_d = din("hs", [128, 2])

    y_d = dout("y", [4096, D])
    ys_d = dout("ys", [16, D])
    Cp_d = dout("Cp", [4, 128, 128])
    np_d = dout("np", [4, 128])
    mp_d = dout("mp", [4, 1])
    kp_d = dout("kp", [128, 128])
    vp_d = dout("vp", [128, 128])
    cvp_d = dout("cvp", [2, 2 * DFF])
    Cs_d = dout("Cs", [4, 128, 128])
    ns_d = dout("ns", [4, 128])
    ms_d = dout("ms", [4, 1])
    ks_d = dout("ks", [16, 128])
    vs_d = dout("vs", [16, 128])
    cvs_d = dout("cvs", [2, 2 * DFF])
    wupbf = nc.dram_tensor("wupbf", [NU, 128, 8 * 256], BF16, kind="Internal").ap()
    wdnbf = nc.dram_tensor("wdnbf", [NU // 2, 128, 2 * D], BF16, kind="Internal").ap()

    def sb(name, shape, dt=F32):
        return es.enter_context(nc.sbuf_tensor(name, list(shape), dt))

    def pt(name, shape, dt=F32):
        return es.enter_context(nc.psum_tensor(name, list(shape), dt))

    win = sb("win", [128, 8, WCOLS], BF16)
    wout = sb("wout", [128, 8, D], BF16)
    wdr = [sb("wdr%d" % i, [128, 2, 512], BF16) for i in range(5)]
    NS = 2
    ustS = [sb("ustS%d" % i, [128, 2, 258]) for i in range(NS)]
    ctS = [sb("ctS%d" % i, [128, 2, 256]) for i in range(NS)]
    wur = [sb("wur%d" % i, [128, 8, 256], BF16) for i in range(2)]
    lnp = sb("lnp", [128, 4, D])
    gnb = sb("gnb", [128, 512])
    esink = sb("esink", [128, 8])
    ident = sb("ident_s", [128, 128])
    maskT = sb("maskT_s", [128, 128])
    amask = sb("amask_s", [128, 2, 128], BF16)
    sel4 = sb("sel4_s", [4, 4 * 128])
    hs = sb("hs_s", [128, 2])
    bi = sb("bi_s", [4, 1])
    nbf = sb("nbf_s", [4, 1])
    wcv = sb("wcv", [128, 3, 2 * NU])
    bcv = sb("bcv", [128, 2 * NU])
    cvh = sb("cvh", [128, 2 * NU, 2])
    cvsb = sb("cvsb", [128, 2 * NU, 2])
    ones4 = sb("ones4", [4, 128])
    onesr = ones4

    xin = [sb("xin%d" % i, [128, D]) for i in range(2)]
    xT = sb("xT", [128, 8, 128], BF16)
    mixT = sb("mixT", [128, 8, 128], BF16)
    kmtok = sb("kmtok", [128, 4, 128], BF16)
    kw = kmtok
    v1 = sb("v1", [128, 4, 130], BF16)
    qmT = sb("qmT", [128, 4, 128], BF16)
    qaT = sb("qaT", [128, 4, 128], BF16)
    kmT = sb("kmT", [128, 4, 128], BF16)
    cosb = [sb("cos%d" % i, [128, 128]) for i in range(2)]
    sinb = [sb("sin%d" % i, [128, 128]) for i in range(2)]
    nd1 = sb("nd1", [128, 4, 130])
    nd = nd1
    rt1 = sb("rt1", [128, 4, 128])
    Ebuf = sb("Ebuf", [128, 4, 128])
    hmr = Ebuf
    rt2 = sb("rt2", [128, 4, 128])
    krope = sb("krope", [128, 128])
    kTb = [sb("kTb%d" % i, [128, 128], BF16) for i in range(2)]
    va1 = [sb("va1%d" % i, [128, 2, 66], BF16) for i in range(2)]
    kvout = sb("kvout", [128, 2, 128])
    ST = sb("ST", [128, 4, 128], BF16)
    Cst = [sb("Cst0", [128, 4, 130])] * 2
    Cb = [sb("Cb0", [128, 4, 130], BF16)] * 2
    mst = [sb("mst0", [4, 1])] * 2
    bst = sb("bst", [128, 4, 6])
    mv = sb("mv", [128, 4, 2])
    sm = sb("sm", [128, 32])
    lnscr = [(sb("bstL%d" % i, [128, 2, 6]), sb("mvL%d" % i, [128, 2]), sb("smL%d" % i, [128, 2])) for i in range(2)]
    pTraw = sb("pTraw", [128, 1024])
    pT = pTraw[:].bitcast(BF16).rearrange("p (b h t) -> p b h t", b=2, h=8)
    og = sb("og", [128, 512])
    mix = sb("mix", [128, D])
    mixb = sb("mixb", [128, D], BF16)
    identb = sb("identb", [128, 128], BF16)
    s1 = mix
    MIXK = ['mix']
    hresL = [sb("hres%d" % i, [128, 2, D]) for i in range(2)]
    hTL = [sb("hT%d" % i, [128, 8, 256], BF16) for i in range(2)]
    actT = sb("actT", [128, NU, 256], BF16)
    G = sb("G", [4, 7, 128])
    tm = sb("tm", [128, 16])
    dbc = sb("dbc", [128, 4])
    gsm = sb("gsm", [4, 8])

    P01 = pt("P01", [128, 1024])
    P23 = pt("P23", [128, 1024])
    P4 = pt("P4", [128, 512])
    P5 = pt("P5", [128, 512])
    P6 = pt("P6", [128, 512])
    P7 = pt("P7", [128, 512])
    PA0 = P01[:, 0:512]
    PA1 = P01[:, 512:1024]
    P0b = PA0.bitcast(BF16)
    P1b = PA1.bitcast(BF16)

    T = Tracker(nc, es)
    FILL = int(os.environ.get('K_FILL', '1'))
    T.add_engine('sp', nc.sync)
    T.add_engine('pe', nc.tensor)
    T.add_engine('act', nc.scalar)
    T.add_engine('dve', nc.vector)
    T.add_engine('pool', nc.gpsimd)

    def ld(q, stream, out, in_, w, r=()):
        return T.dma(q, stream, lambda e, o=out, i=in_: e.dma_start(out=o, in_=i), r=r, w=w, group=True, c=6.0)

    for k in range(8):
        for c0 in (0, 1732):
            ld('pool', 'win', win[:, k, c0:c0 + 1732], win_d[k * 128:(k + 1) * 128, c0:c0 + 1732], w=['win'])
    ld('sp', 'small', ident[:], ident_d[:, :], w=['const'])
    ld('sp', 'small', maskT[:], maskT_d[:, :], w=['const'])
    ld('pool', 'small2', amask[:], amask_d[:, :, :], w=['c5'])
    ld('sp', 'small', sel4[:], sel4_d[:, :], w=['const'])
    ld('sp', 'small', hs[:], hs_d[:, :], w=['const'])
    ld('sp', 'small', bi[:], bi_d[:, :], w=['const'])
    ld('sp', 'small', nbf[:], bf_d[:, :], w=['const'])
    ld('sp', 'small', gnb[:], gn_d[0, :].partition_broadcast(128), w=['const'])
    ld('sp', 'small', esink[:], sink_d[0, :].partition_broadcast(128), w=['const'])
    for j in range(4):
        ld('sp', 'small', lnp[:, j, :], ln_d[j, :].partition_broadcast(128), w=['const'])
    for j in range(3):
        T.dma('sp', 'small', lambda e, j=j: e.dma_start(
            out=wcv[:, j, :], in_=wconv_d[j, :].rearrange("(c p) -> p c", p=128), allow_slow_non_contiguous=True), w=['const'], group=True, c=20.0)
    T.dma('sp', 'small', lambda e: e.dma_start(
        out=bcv[:], in_=bconv_d[0, :].rearrange("(c p) -> p c", p=128), allow_slow_non_contiguous=True), w=['const'], group=True, c=20.0)
    for j in range(2):
        T.dma('sp', 'small', lambda e, j=j: e.dma_start(
            out=cvsb[:, :, j], in_=sconv_d[j, :].rearrange("(c p) -> p c", p=128), allow_slow_non_contiguous=True), w=['cvsb'], group=True, c=20.0)
    for k in range(8):
        ld('pool', 'wout', wout[:, k, :], wout_d[k * 128:(k + 1) * 128, :], w=['wout'])
    for u in range(NU):
        for j in range(2):
            c0 = j * DFF + u * 128
            T.dma('pool', 'wupc', lambda e, u=u, j=j, c0=c0: e.dma_start(
                out=wupbf[u].rearrange("p (k c) -> p k c", k=8)[:, :, j * 128:(j + 1) * 128],
                in_=wup_d[:, c0:c0 + 128].rearrange("(k p) c -> p k c", p=128)), w=['wupbf'], group=True, c=8.0)
    for c in range(NU // 2):
        T.dma('pool', 'wdnc', lambda e, c=c: e.dma_start(
            out=wdnbf[c].rearrange("p (k n) -> p k n", k=2),
            in_=wdn_d[c * 256:(c + 1) * 256, :].rearrange("(k p) n -> p k n", p=128)), w=['wdnbf'], group=True, c=8.0)

    T.op('dve', lambda e: e.memset(ones4[:], 1.0), w=['c2'])
    T.op('dve', lambda e: e.tensor_scalar(nbf[:], nbf[:], -1.0, None, ALU.mult), r=['const'], w=['c3'])
    T.op('act', lambda e: e.activation(esink[:], esink[:], AF.Exp), r=['const'], w=['c4'])
    T.op('dve', lambda e: e.tensor_copy(identb[:], ident[:]), r=['const'], w=['c6'])
    for i in range(2):
        T.op('dve', lambda e, i=i: e.memset(v1[:, :, 128:130], 1.0), w=['v1'])
    for i in range(2):
        T.op('dve', lambda e, i=i: e.memset(va1[i][:], 1.0), w=['va1_%d' % i])
        T.op('dve', lambda e, i=i: e.memset(kTb[i][:], 0.0), w=['kT_%d' % i])
    T.op('dve', lambda e: e.memset(cvh[:], 0.0), w=['cvh'])
    CONSTS = ['const', 'c2', 'c3', 'c4', 'c5', 'c6']

    def transposes8(src, Tn, dst_ps, rkeys):
        def fn(e):
            ins = None
            for k in range(8):
                ins = e.transpose(dst_ps[:, k * 128:k * 128 + Tn], src[:Tn, k * 128:(k + 1) * 128], ident[:Tn, :Tn])
            return ins
        T.op('pe', fn, r=list(rkeys) + CONSTS, w=['P0', 'P1'], c=2.0)

    def load_x(slot, src_ap, Tn):
        return T.dma('sp', 'xin%d' % slot, lambda e: e.dma_start(out=xin[slot][:Tn, :], in_=src_ap), w=['xin%d' % slot])

    def make_xT(slot, Tn):
        transposes8(xin[slot], Tn, P01, ['xin%d' % slot])
        for b in range(2):
            T.op('act', lambda e, b=b: e.copy(
                xT[:, 4 * b:4 * b + 4, :Tn],
                P01[:, 512 * b:512 * b + 512].rearrange("p (k t) -> p k t", k=4)[:, :, :Tn]),
                r=['P%d' % b], w=['xT'])

    def mm_A(ps, col0, ncols, Tn, wkey, bankkeys):
        def fn(e):
            ins = None
            for k in range(8):
                ins = e.matmul(ps[:Tn, :ncols], xT[:, k, :Tn], win[:, k, col0:col0 + ncols], start=(k == 0), stop=(k == 7))
            return ins
        T.op('pe', fn, r=['xT', 'win'], w=bankkeys, c=8 * (0.09 + max(ncols, 64) / 2400.0) + 0.1)

    def mm_B(ps, col0, mcols, Tn, bankkeys):
        def fn(e):
            ins = None
            for k in range(8):
                ins = e.matmul(ps[:mcols, :Tn], win[:, k, col0:col0 + mcols], xT[:, k, :Tn], start=(k == 0), stop=(k == 7))
            return ins
        T.op('pe', fn, r=['xT', 'win'], w=bankkeys, c=8 * 0.13 + 0.1)

    def gate_chain(Tn, si, full):
        m = mst[si]
        mk = 'm%d' % si
        ipT = P4[0:4, 128:128 + Tn]
        fpT = P4[0:4, 256:256 + Tn]
        R = lambda j: G[:, j, :Tn]
        T.op('act', lambda e: e.activation(R(0), fpT, AF.Exp, bias=nbf[:], scale=-1.0), r=['P4', 'c3'], w=['G0'])
        T.op('act', lambda e: e.activation(R(0), R(0), AF.Ln, bias=1.0), r=['G0'], w=['G0'])
        T.op('dve', lambda e: e.tensor_tensor_scan(R(1), onesr[:, :Tn], R(0), 0.0, ALU.mult, ALU.subtract), r=['G0', 'c2'], w=['G1'])
        T.op('dve', lambda e: e.scalar_tensor_tensor(R(2), ipT, bi[:], R(1), ALU.add, ALU.subtract), r=['P4', 'G1', 'const'], w=['G2'])
        T.op('dve', lambda e: e.tensor_tensor_scan(R(3), R(2), R(2), m[:], ALU.max, ALU.max), r=['G2', mk], w=['G3'])
        T.op('act', lambda e: e.activation(R(4), R(3), AF.Exp, bias=m[:], scale=-1.0), r=['G3', mk], w=['G4'])
        T.op('dve', lambda e: e.tensor_scalar(gsm[:, 0:1], G[:, 3, Tn - 1:Tn], -1.0, None, ALU.mult), r=['G3'], w=['gsm0'])
        T.op('act', lambda e: e.activation(R(5), R(2), AF.Exp, bias=gsm[:, 0:1]), r=['G2', 'gsm0'], w=['G5'])
        T.op('dve', lambda e: e.tensor_tensor(R(6), R(1), R(3), ALU.add), r=['G1', 'G3'], w=['G6'])
        if full:
            T.op('act', lambda e: e.activation(R(0), R(6), AF.Exp, scale=-1.0), r=['G6'], w=['G0'])
            T.op('dve', lambda e: e.tensor_scalar(R(1), R(3), -1.0, None, ALU.mult), r=['G3'], w=['G1'])
        T.op('dve', lambda e: e.tensor_scalar(gsm[:, 4:8], ident[0:4, 0:4], G[:, 4, Tn - 1:Tn], None, ALU.mult), r=['G4', 'const'], w=['gsm1'])
        T.op('dve', lambda e: e.tensor_copy(m[:], G[:, 6, Tn - 1:Tn]), r=['G6'], w=[mk])

        def fn(e):
            ins = None
            rows = [(2, 0), (4, 4), (0, 8), (5, 12)] if full else [(5, 12)]
            for (rj, c) in rows:
                ins = e.matmul(P4[:Tn, 384 + c:384 + c + 4], G[:, rj, :Tn], ident[0:4, 0:4], start=True, stop=True)
            ins = e.matmul(P4[:, 448:452], ones4[:, :], gsm[:, 4:8], start=True, stop=True)
            return ins
        T.op('pe', fn, r=['G2', 'G4', 'G0', 'G5', 'gsm1', 'c2', 'const'], w=['P4'])
        if full:
            T.op('dve', lambda e: e.tensor_copy(tm[:Tn, :], P4[:Tn, 384:400]), r=['P4'], w=['tm'])
        else:
            T.op('dve', lambda e: e.tensor_copy(tm[:Tn, 12:16], P4[:Tn, 396:400]), r=['P4'], w=['tm'])
        T.op('dve', lambda e: e.tensor_copy(dbc[:], P4[:, 448:452]), r=['P4'], w=['dbc'])

    def state_update(Tn, si):
        C = Cst[si]
        ck = 'C%d' % si
        T.op('dve', lambda e: e.tensor_tensor(kw[:Tn], kmtok[:Tn], tm[:Tn, 12:16].unsqueeze(2).to_broadcast([Tn, 4, 128]), ALU.mult),
             r=['kmtok', 'tm'], w=['kmtok'])

        def fn(e):
            ins = None
            for h in range(4):
                ins = e.matmul(P23[:, h * 256:h * 256 + 129], kw[:Tn, h, :], v1[:Tn, h, 0:129], start=True, stop=True)
            return ins
        T.op('pe', fn, r=['kmtok', 'v1'], w=['P2', 'P3'])
        for h in range(4):
            T.op('dve', lambda e, h=h: e.scalar_tensor_tensor(C[:, h, 0:129], C[:, h, 0:129], dbc[:, h:h + 1],
                                                              P23[:, h * 256:h * 256 + 129], ALU.mult, ALU.add),
                 r=['dbc', 'P2', 'P3', ck], w=[ck])

    out_toks = []

    def do_prefix():
        tok_x = {}
        if nt_pre > 0:
            tok_x[0] = load_x(0, xall[0:128, :], 128)
        for i in range(nt_pre):
            T.cur_tag = 'pre'
            slot = i % 2
            if i + 1 < nt_pre:
                load_x((i + 1) % 2, xall[(i + 1) * 128:(i + 2) * 128, :], 128)
            make_xT(slot, 128)
            mm_A(P23[:, 0:512], C_KM, 512, 128, 'win', ['P2'])
            mm_A(P23[:, 512:1024], C_VM, 512, 128, 'win', ['P3'])
            mm_B(P4[:, 128:256], C_IP, 4, 128, ['P4'])
            mm_B(P4[:, 256:384], C_FP, 4, 128, ['P4'])
            T.op('act', lambda e: e.activation(kmtok[:].rearrange("p h d -> p (h d)"), P23[:, 0:512], AF.Copy, scale=KSCALE), r=['P2'], w=['kmtok'])
            T.op('act', lambda e: e.copy(v1[:, :, 0:128], P23[:, 512:1024].rearrange("p (h d) -> p h d", h=4)), r=['P3'], w=['v1'])
            gate_chain(128, 0, False)
            state_update(128, 0)


    def stage1(Tn, xslot, si, kslot, pslot, prev_bias, cs_slot, hcol, htile, save_kv, is_sample, par=0, head_done=False, mid_hook=None):
        hres = hresL[par]
        hT = hTL[par]
        hrk = 'hres%d_%d' % (par, htile)
        htk = 'hT%d' % par
        C = Cst[si]
        ck = 'C%d' % si
        cbk = 'Cb%d' % si
        if not head_done:
            make_xT(xslot, Tn)
        mm_A(P23[:, 0:512], C_KM, 512, Tn, 'win', ['P2'])
        mm_A(P23[:, 512:1024], C_VM, 512, Tn, 'win', ['P3'])
        mm_A(PA1[:, 0:512], C_OM, 512, Tn, 'win', ['P1'])
        mm_A(P4[:, 0:128], C_VA, 128, Tn, 'win', ['P4'])
        mm_B(P4[:, 128:256], C_IP, 4, Tn, ['P4'])
        mm_B(P4[:, 256:384], C_FP, 4, Tn, ['P4'])
        T.op('act', lambda e: e.activation(kmtok[:Tn].rearrange("p h d -> p (h d)"), P23[:Tn, 0:512], AF.Copy, scale=KSCALE), r=['P2'], w=['kmtok'])
        T.op('act', lambda e: e.copy(v1[:Tn, :, 0:128], P23[:Tn, 512:1024].rearrange("p (h d) -> p h d", h=4)), r=['P3'], w=['v1'])
        T.op('act', lambda e: e.activation(og[:Tn], PA1[:Tn, :], AF.Exp, scale=-1.0), r=['P1'], w=['og'])
        T.op('act', lambda e: e.activation(og[:Tn], og[:Tn], AF.Ln, bias=1.0), r=['og'], w=['og'], c=0.55)
        T.op('act', lambda e: e.activation(og[:Tn], og[:Tn], AF.Exp, scale=-1.0), r=['og'], w=['og'], c=0.55)
        T.op('pool', lambda e: e.tensor_tensor(og[:Tn], og[:Tn], gnb[:Tn], ALU.mult), r=['og', 'const'], w=['og'], c=1.1)
        T.op('act', lambda e: e.copy(va1[kslot][:Tn, :, 0:64], P4[:Tn, 0:128].rearrange("p (k d) -> p k d", k=2)), r=['P4'], w=['va1_%d' % kslot])
        if save_kv:
            T.op('dve', lambda e: e.tensor_copy(kvout[:Tn, 1, :], P4[:Tn, 0:128]), r=['P4'], w=['kvout'])
        gate_chain(Tn, si, True)
        def fn_rb(e):
            ins = None
            for h in range(4):
                ins = e.matmul(P4[:Tn, h * 128:h * 128 + Tn], sel4[:, h * 128:h * 128 + Tn], G[:, 1, :Tn], start=True, stop=True)
            return ins
        T.op('pe', fn_rb, r=['G1', 'const'], w=['P4'], c=1.0)
        T.op('dve', lambda e: e.tensor_tensor(Ebuf[:Tn, :, :Tn], P4[:Tn, :].rearrange("p (h t) -> p h t", h=4)[:, :, :Tn],
                                              maskT[:Tn, :Tn].unsqueeze(1).to_broadcast([Tn, 4, Tn]), ALU.add),
             r=['P4', 'const'], w=['Ebuf'])
        for h in range(4):
            T.op('act', lambda e, h=h: e.activation(Ebuf[:Tn, h, :Tn], Ebuf[:Tn, h, :Tn], AF.Exp, bias=tm[:Tn, h:h + 1]), r=['Ebuf', 'tm'], w=['Ebuf'])
        for mt_ in range(4):
            mm_B(PA0[:, mt_ * 128:mt_ * 128 + 128], C_QM + mt_ * 128, 128, Tn, ['P0'])
        T.op('act', lambda e: e.copy(qmT[:, :, :Tn], PA0[:, :].rearrange("p (h t) -> p h t", h=4)[:, :, :Tn]), r=['P0'], w=['qmT'])
        def fn_kt(e):
            ins = None
            for h in range(4):
                ins = e.transpose(P1b[:, h * 128:h * 128 + Tn], kmtok[:Tn, h, :], identb[:Tn, :Tn])
            return ins
        T.op('pe', fn_kt, r=['kmtok', 'c6'], w=['P1'], c=0.7)
        T.op('act', lambda e: e.copy(kmT[:, :, :Tn], P1b[:, 0:512].rearrange("p (h t) -> p h t", h=4)[:, :, :Tn]), r=['P1'], w=['kmT'])
        def fn_s(e):
            ins = None
            for h in range(4):
                ins = e.matmul(P4[:Tn, h * 128:h * 128 + Tn], kmT[:, h, :Tn], qmT[:, h, :Tn], start=True, stop=True)
            return ins
        T.op('pe', fn_s, r=['kmT', 'qmT'], w=['P4'])
        T.op('dve', lambda e: e.tensor_tensor(ST[:Tn, :, :Tn], P4[:Tn, :].rearrange("p (h t) -> p h t", h=4)[:, :, :Tn], Ebuf[:Tn, :, :Tn], ALU.mult),
             r=['P4', 'Ebuf'], w=['ST'])
        def fn_qc(e):
            ins = None
            for h in range(4):
                ins = e.matmul(P23[:Tn, h * 256:h * 256 + 129], qmT[:, h, :Tn], Cb[si][:, h, 0:129], start=True, stop=True)
            return ins
        T.op('pe', fn_qc, r=['qmT', cbk], w=['P2', 'P3'])
        def fn_sv(e):
            ins = None
            for h in range(4):
                ins = e.matmul(P01[:Tn, h * 256:h * 256 + 129], ST[:Tn, h, :Tn], v1[:Tn, h, 0:129], start=True, stop=True)
            return ins
        T.op('pe', fn_sv, r=['ST', 'v1'], w=['P0', 'P1'])
        for h in range(4):
            T.op('act', lambda e, h=h: e.activation(nd1[:Tn, h, 0:129], P23[:Tn, h * 256:h * 256 + 129], AF.Copy, scale=tm[:Tn, 4 + h:5 + h]),
                 r=['P2', 'P3', 'tm'], w=['nd1'])
        T.op('dve', lambda e: e.tensor_tensor(nd[:Tn, :, 0:129], nd1[:Tn, :, 0:129],
                                              P01[:Tn, :].rearrange("p (h c) -> p h c", h=4)[:, :, 0:129], ALU.add),
             r=['nd1', 'P0', 'P1'], w=['nd1'])
        state_update(Tn, si)
        T.op('act', lambda e: e.copy(Cb[si][:], C[:]), r=[ck], w=[cbk])
        T.op('dve', lambda e: e.tensor_scalar(sm[:Tn, 20:24], nd[:Tn, :, 128], -1.0, None, ALU.mult), r=['nd1'], w=['sm_den'])
        T.op('dve', lambda e: e.tensor_tensor(sm[:Tn, 20:24], sm[:Tn, 20:24], nd[:Tn, :, 128], ALU.max), r=['nd1', 'sm_den'], w=['sm_den'])
        T.op('dve', lambda e: e.tensor_tensor(sm[:Tn, 0:4], sm[:Tn, 20:24], tm[:Tn, 8:12], ALU.max), r=['sm_den', 'tm'], w=['sm_den'])
        T.op('dve', lambda e: e.reciprocal(sm[:Tn, 0:4], sm[:Tn, 0:4]), r=['sm_den'], w=['sm_den'])
        T.op('dve', lambda e: e.tensor_tensor(hmr[:Tn], nd[:Tn, :, 0:128], sm[:Tn, 0:4].unsqueeze(2).to_broadcast([Tn, 4, 128]), ALU.mult),
             r=['nd1', 'sm_den'], w=['Ebuf'])
        for h in range(4):
            T.op('dve', lambda e, h=h: e.bn_stats(bst[:Tn, h, :], hmr[:Tn, h, :]), r=['Ebuf'], w=['bst'])
        for h in range(4):
            T.op('dve', lambda e, h=h: e.bn_aggr(mv[:Tn, h, :], bst[:Tn, h, :]), r=['bst'], w=['mv'])
        T.op('act', lambda e: e.activation(sm[:Tn, 4:8], mv[:Tn, :, 1], AF.Ln, bias=EPS), r=['mv'], w=['sm_hn'])
        T.op('act', lambda e: e.activation(sm[:Tn, 4:8], sm[:Tn, 4:8], AF.Exp, scale=-0.5), r=['sm_hn'], w=['sm_hn'])
        for h in range(4):
            T.op('dve', lambda e, h=h: e.tensor_scalar(hmr[:Tn, h, :], hmr[:Tn, h, :], mv[:Tn, h, 0:1], sm[:Tn, 4 + h:5 + h], ALU.subtract, ALU.mult),
                 r=['Ebuf', 'mv', 'sm_hn'], w=['Ebuf'])
        T.op('dve', lambda e: e.tensor_tensor(mixb[:Tn, 0:512], hmr[:Tn].rearrange("p h d -> p (h d)"), og[:Tn], ALU.mult),
             r=['Ebuf', 'og'], w=['mbA'])
        cosv, sinv = cosb[cs_slot], sinb[cs_slot]
        csk = 'cs%d' % cs_slot
        for mt_ in range(4):
            mm_B(P4[:, mt_ * 128:mt_ * 128 + 128], C_QAP + mt_ * 128, 128, Tn, ['P4'])
        for mt_ in range(4):
            mm_B(PA1[:, mt_ * 128:mt_ * 128 + 128], C_QAR + mt_ * 128, 128, Tn, ['P1'])
        T.op('dve', lambda e: e.tensor_tensor(rt1[:, :, :Tn], P4[:, :].rearrange("p (h t) -> p h t", h=4)[:, :, :Tn],
                                              cosv[:, :Tn].unsqueeze(1).to_broadcast([128, 4, Tn]), ALU.mult), r=['P4', csk], w=['rt1'])
        T.op('dve', lambda e: e.tensor_tensor(rt2[:, :, :Tn], PA1[:, :].rearrange("p (h t) -> p h t", h=4)[:, :, :Tn],
                                              sinv[:, :Tn].unsqueeze(1).to_broadcast([128, 4, Tn]), ALU.mult), r=['P1', csk], w=['rt2'])
        T.op('dve', lambda e: e.tensor_tensor(qaT[:, :, :Tn], rt1[:, :, :Tn], rt2[:, :, :Tn], ALU.add), r=['rt1', 'rt2'], w=['qaT'])
        mm_B(PA0[:, 0:128], C_KAP, 128, Tn, ['P0'])
        mm_B(PA0[:, 128:256], C_KAR, 128, Tn, ['P0'])
        T.op('dve', lambda e: e.tensor_tensor(rt1[:, 0, :Tn], PA0[:, 0:Tn], cosv[:, :Tn], ALU.mult), r=['P0', csk], w=['rt1'])
        T.op('dve', lambda e: e.tensor_tensor(rt2[:, 0, :Tn], PA0[:, 128:128 + Tn], sinv[:, :Tn], ALU.mult), r=['P0', csk], w=['rt2'])
        T.op('dve', lambda e: e.tensor_tensor(krope[:, :Tn], rt1[:, 0, :Tn], rt2[:, 0, :Tn], ALU.add), r=['rt1', 'rt2'], w=['krope'])
        T.op('act', lambda e: e.copy(kTb[kslot][:, :Tn], krope[:, :Tn]), r=['krope'], w=['kT_%d' % kslot])
        if save_kv:
            T.op('pe', lambda e: e.transpose(PA0[:Tn, 256:384], krope[:, :Tn], ident[:, :]), r=['krope', 'const'], w=['P0'])
            T.op('dve', lambda e: e.tensor_copy(kvout[:Tn, 0, :], PA0[:Tn, 256:384]), r=['P0'], w=['kvout'])
        blocks = [(pslot, 128, prev_bias), (kslot, Tn, 0.0)]
        psb = [(P23, ['P2', 'P3']), (P01, ['P0', 'P1'])]
        for bi_, (ks_, Sb, bias_) in enumerate(blocks):
            ps_, keys_ = psb[bi_]
            def fn_sc(e, ks_=ks_, Sb=Sb, ps_=ps_):
                ins = None
                for j in range(4):
                    for half in range(2):
                        hd = half * 4 + j
                        ins = e.matmul(ps_[:Sb, hd * 128:hd * 128 + Tn], kTb[ks_][half * 64:(half + 1) * 64, :Sb],
                                       qaT[half * 64:(half + 1) * 64, j, :Tn], start=True, stop=True)
                return ins
            T.op('pe', fn_sc, r=['kT_%d' % ks_, 'qaT'], w=keys_)
            for b2 in range(2):
                T.op('act', lambda e, b2=b2, Sb=Sb, ps_=ps_, bias_=bias_, bi_=bi_: e.activation(
                    pT[:Sb, bi_, 4 * b2:4 * b2 + 4, :Tn],
                    ps_[:Sb, 512 * b2:512 * b2 + 512].rearrange("p (h t) -> p h t", h=4)[:, :, :Tn],
                    AF.Exp, bias=bias_, scale=0.125), r=[keys_[b2], 'c5', 'const'], w=['pT%d' % bi_])
        def fn_pv(e):
            ins = None
            for hd in range(8):
                if is_sample:
                    for bi_, (ks_, Sb, _) in enumerate(blocks):
                        ins = e.matmul(P45(hd)[:Tn, :], pT[:Sb, bi_, hd, :Tn], va1[ks_][:Sb, hd // 4, 0:65],
                                       start=(bi_ == 0), stop=(bi_ == 1))
                else:
                    for qc, rng in ((0, ((0, 0, 128), (1, 0, 64))), (1, ((0, 64, 128), (1, 0, 128)))):
                        for ii, (bi_, s0, s1_) in enumerate(rng):
                            ks_ = blocks[bi_][0]
                            ins = e.matmul(P45(hd)[qc * 64:(qc + 1) * 64, :], pT[s0:s1_, bi_, hd, qc * 64:(qc + 1) * 64],
                                           va1[ks_][s0:s1_, hd // 4, 0:65], start=(ii == 0), stop=(ii == 1))
            return ins
        def P45(hd):
            base = P4 if hd < 4 else PA1
            c = (hd % 4) * 65
            return base[:, c:c + 65]
        T.op('pe', fn_pv, r=['pT0', 'pT1', 'va1_%d' % pslot, 'va1_%d' % kslot], w=['P4', 'P1'])
        for half, base, bk in ((0, P4, 'P4'), (1, PA1, 'P1')):
            o3 = base[:Tn, 0:260].rearrange("p (h c) -> p h c", h=4)
            T.op('dve', lambda e, o3=o3, half=half: e.tensor_tensor(sm[:Tn, 8 + 4 * half:12 + 4 * half], o3[:, :, 64], esink[:Tn, 4 * half:4 * half + 4], ALU.add),
                 r=[bk, 'c4'], w=['sm_pv'])
        T.op('dve', lambda e: e.reciprocal(sm[:Tn, 8:16], sm[:Tn, 8:16]), r=['sm_pv'], w=['sm_pv'])
        for half, base, bk in ((0, P4, 'P4'), (1, PA1, 'P1')):
            o3 = base[:Tn, 0:260].rearrange("p (h c) -> p h c", h=4)
            T.op('dve', lambda e, o3=o3, half=half: e.tensor_tensor(
                mixb[:Tn, 512 + 256 * half:768 + 256 * half].rearrange("p (h d) -> p h d", h=4), o3[:, :, 0:64],
                sm[:Tn, 8 + 4 * half:12 + 4 * half].unsqueeze(2).to_broadcast([Tn, 4, 64]), ALU.mult),
                r=[bk, 'sm_pv'], w=['mbB%d' % half])
        def fn_mt(e):
            ins = None
            for k in range(8):
                ins = e.transpose(P0b[:, k * 128:k * 128 + Tn], mixb[:Tn, k * 128:(k + 1) * 128], identb[:Tn, :Tn])
            return ins
        T.op('pe', fn_mt, r=['mbA', 'mbB0', 'mbB1', 'c6'], w=['P0'], c=1.1)
        T.op('act', lambda e: e.copy(mixT[:, :, :Tn], P0b[:, :].rearrange("p (k t) -> p k t", k=8)[:, :, :Tn]), r=['P0'], w=['mixT'], c=0.9)
        if mid_hook is not None:
            mid_hook()
        for n in range(2):
            def fn_o(e, n=n):
                ins = None
                for k in range(8):
                    ins = e.matmul(P23[:Tn, n * 512:(n + 1) * 512], mixT[:, k, :Tn], wout[:, k, n * 512:(n + 1) * 512], start=(k == 0), stop=(k == 7))
                return ins
            T.op('pe', fn_o, r=['mixT', 'wout'], w=['P%d' % (2 + n)], c=2.5)
        xk = 'xin%d' % xslot
        T.op('dve', lambda e: e.scalar_tensor_tensor(s1[:Tn], xin[xslot][:Tn], ALPHA, P23[:Tn, :], ALU.mult, ALU.add), r=[xk, 'P2', 'P3'], w=['mix'], c=1.3)
        layer_norm(s1, MIXK, Tn, 0, hres[:, htile, :], hrk)
        transposes8(hres[:, htile, :], Tn, P01, [hrk])
        for b in range(2):
            T.op('act', lambda e, b=b: e.copy(hT[:, 4 * b:4 * b + 4, hcol:hcol + Tn],
                                              P01[:, 512 * b:512 * b + 512].rearrange("p (k t) -> p k t", k=4)[:, :, :Tn]),
                 r=['P%d' % b], w=[htk])

    def layer_norm(src, skey, Tn, which, dst, dkey):
        skeys = list(skey) if isinstance(skey, (list, tuple)) else [skey]
        skey = skeys[-1]
        bstL, mvL, smL = lnscr[which]
        kb, km_, ks_ = 'bstL%d' % which, 'mvL%d' % which, 'smL%d' % which
        for c in range(2):
            T.op('dve', lambda e, c=c: e.bn_stats(bstL[:Tn, c, :], src[:Tn, c * 512:(c + 1) * 512]), r=skeys, w=[kb])
        T.op('dve', lambda e: e.bn_aggr(mvL[:Tn, :], bstL[:Tn, 0:2, :].rearrange("p a b -> p (a b)")), r=[kb], w=[km_])
        T.op('act', lambda e: e.activation(smL[:Tn, 0:1], mvL[:Tn, 1:2], AF.Ln, bias=EPS), r=[km_], w=[ks_])
        T.op('act', lambda e: e.activation(smL[:Tn, 0:1], smL[:Tn, 0:1], AF.Exp, scale=-0.5), r=[ks_], w=[ks_])
        T.op('dve', lambda e: e.scalar_tensor_tensor(src[:Tn], src[:Tn], mvL[:Tn, 0:1], lnp[:Tn, 2 * which, :], ALU.subtract, ALU.mult),
             r=skeys + [km_, 'const'], w=skeys, c=1.25)
        T.op('dve', lambda e: e.scalar_tensor_tensor(dst[:Tn], src[:Tn], smL[:Tn, 0:1], lnp[:Tn, 2 * which + 1, :], ALU.mult, ALU.add),
             r=skeys + [ks_, 'const'], w=[dkey], c=1.25)

    ring_ctr = [0]
    dring_ctr = [0]
    set_ctr = [0]

    def stage2(N, segs, cvbuf, cvkey, first_macro, out_fn, par=0):
        hres = hresL[par]
        hT = hTL[par]
        htk = 'hT%d' % par
        for u in range(NU):
            rs = ring_ctr[0] % 2
            ring_ctr[0] += 1
            bs = u % 2
            PSU = (P6, P7)[bs]
            pk = ('P6', 'P7')[bs]
            si_ = set_ctr[0] % NS
            set_ctr[0] += 1
            ustv = ustS[si_]
            uk = ['ustS%d' % si_]
            ctv = ctS[si_]
            ckk = 'ctS%d' % si_
            T.dma('sp', 'wur%d' % rs, lambda e, rs=rs, u=u: e.dma_start(out=wur[rs][:].rearrange("p k c -> p (k c)"), in_=wupbf[u]),
                  r=['wupbf'], w=['wur%d' % rs])
            def fn_u(e, rs=rs, PSU=PSU):
                ins = None
                for j in range(2):
                    for k in range(8):
                        ins = e.matmul(PSU[:, j * 256:j * 256 + N], wur[rs][:, k, j * 128:(j + 1) * 128], hT[:, k, :N], start=(k == 0), stop=(k == 7))
                return ins
            T.op('pe', fn_u, r=['wur%d' % rs, htk], w=[pk], c=16 * (0.09 + max(N, 64) / 2400.0) + 0.1)
            pu = PSU[:, :].rearrange("p (j t) -> p j t", j=2)
            cv4 = cvbuf[:].rearrange("p (j u) t -> p j u t", j=2)
            T.op('act', lambda e, ustv=ustv, pu=pu: e.copy(ustv[:, :, 2:2 + N], pu[:, :, :N]), r=[pk], w=uk, c=0.7)
            T.op('dve', lambda e, u=u, ustv=ustv: e.tensor_copy(ustv[:, :, 0:2], cv4[:, :, u, :]), r=[cvkey], w=uk)
            if first_macro:
                T.op('dve', lambda e, u=u, ustv=ustv: e.tensor_scalar(cv4[:, :, u, :], ustv[:, :, N:N + 2], hs[:, 0:1], None, ALU.mult), r=uk + ['const'], w=[cvkey])
                continue
            T.op('dve', lambda e, u=u, ustv=ustv: e.tensor_copy(cv4[:, :, u, :], ustv[:, :, N:N + 2]), r=uk, w=[cvkey])
            for j in range(2):
                ci = j * NU + u
                T.op('act', lambda e, j=j, ci=ci, ctv=ctv, pu=pu: e.activation(ctv[:, j, :N], pu[:, j, :N], AF.Identity, bias=bcv[:, ci:ci + 1], scale=wcv[:, 2, ci:ci + 1]),
                     r=[pk, 'const'], w=[ckk], c=0.45)
                T.op('dve', lambda e, j=j, ci=ci, ctv=ctv, ustv=ustv: e.scalar_tensor_tensor(ctv[:, j, :N], ustv[:, j, 1:1 + N], wcv[:, 1, ci:ci + 1], ctv[:, j, :N], ALU.mult, ALU.add),
                     r=uk + [ckk, 'const'], w=[ckk], c=0.45)
                T.op('dve', lambda e, j=j, ci=ci, ctv=ctv, ustv=ustv: e.scalar_tensor_tensor(ctv[:, j, :N], ustv[:, j, 0:N], wcv[:, 0, ci:ci + 1], ctv[:, j, :N], ALU.mult, ALU.add),
                     r=uk + [ckk, 'const'], w=[ckk], c=0.45)
            T.op('act', lambda e, ctv=ctv: e.activation(ctv[:, 1, :N], ctv[:, 1, :N], AF.Silu), r=[ckk], w=[ckk], c=0.45)
            T.op('pool', lambda e, u=u, ctv=ctv: e.tensor_tensor(actT[:, u, :N], ctv[:, 1, :N], ctv[:, 0, :N], ALU.mult), r=[ckk], w=['actT%d' % u], c=0.7)
        if first_macro:
            return
        accs = [(P6, 'P6'), (P7, 'P7')]
        NCH = NU // 2
        for si_, (col0, Tn, htile) in enumerate(segs):
            pass
        for n in range(2):
            for c in range(NCH):
                ds = dring_ctr[0] % 5
                dring_ctr[0] += 1
                T.dma('sp', 'wdr%d' % ds, lambda e, ds=ds, c=c, n=n: e.dma_start(
                    out=wdr[ds][:], in_=wdnbf[c].rearrange("p (k n) -> p k n", k=2)[:, :, n * 512:(n + 1) * 512]),
                    r=['wdnbf'], w=['wdr%d' % ds])
                def fn_d(e, ds=ds, c=c):
                    ins = None
                    for si_, (col0, Tn, htile) in enumerate(segs):
                        acc = accs[si_][0]
                        for kk in range(2):
                            kc = 2 * c + kk
                            ins = e.matmul(acc[:Tn, :], actT[:, kc, col0:col0 + Tn], wdr[ds][:, kk, :],
                                           start=(kc == 0), stop=(kc == NU - 1))
                    return ins
                T.op('pe', fn_d, r=['actT%d' % (2 * c), 'actT%d' % (2 * c + 1), 'wdr%d' % ds], w=[accs[i][1] for i in range(len(segs))], c=len(segs) * 2 * 0.31 + 0.1)
            for si_, (col0, Tn, htile) in enumerate(segs):
                acc, akey = accs[si_]
                hk = 'hres%d_%d' % (par, htile)
                T.op('dve', lambda e, Tn=Tn, htile=htile, acc=acc, n=n: e.scalar_tensor_tensor(
                    hres[:Tn, htile, n * 512:(n + 1) * 512], hres[:Tn, htile, n * 512:(n + 1) * 512], ALPHA, acc[:Tn, :], ALU.mult, ALU.add),
                    r=[hk, akey], w=[hk], c=0.7)
        for si_, (col0, Tn, htile) in enumerate(segs):
            hk = 'hres%d_%d' % (par, htile)
            layer_norm(hres[:, htile, :], hk, Tn, 1, hres[:, htile, :], hk)
            out_fn(Tn, si_, hres[:, htile, :], hk, par)

    def load_cs(slot, idx):
        T.dma('sp', 'cs%d' % slot, lambda e: e.dma_start(out=cosb[slot][:], in_=cos_d[idx]), w=['cs%d' % slot])
        T.dma('sp', 'cs%d' % slot, lambda e: e.dma_start(out=sinb[slot][:], in_=sin_d[idx]), w=['cs%d' % slot])

    def do_main():
        n_macro = nt_full // 2
        xbase = 8192 - NT_FULL * 128
        yrow = [0]
        for mi in range(n_macro):
            for tt in range(2):
                i = 2 * mi + tt
                slot = i % 2
                if i == 0:
                    load_x(0, xall[xbase: xbase + 128, :], 128)
                    load_cs(0, 0)
                if i + 1 < nt_full:
                    load_x((i + 1) % 2, xall[xbase + (i + 1) * 128: xbase + (i + 2) * 128, :], 128)
                    load_cs((i + 1) % 2, i + 1)
                T.cur_tag = 's1_%02d' % mi
                kslot = i % 2
                pslot = (i + 1) % 2
                pbias = hs[:, 1:2] if i == 2 else (NEG if i == 0 else 0.0)
                last = (i == nt_full - 1)
                if i == 0:
                    make_xT(0, 128)
                nxt = (lambda ns=(i + 1) % 2: make_xT(ns, 128)) if i + 1 < nt_full else None
                stage1(128, slot, 0, kslot, pslot, pbias, slot, tt * 128, tt, last, False, par=mi % 2, head_done=True, mid_hook=nxt)
                if i == 1:
                    T.op('dve', lambda e: e.tensor_scalar(mst[0][:], mst[0][:], hs[0:4, 0:1], None, ALU.mult), r=['m0', 'const'], w=['m0'])
                if last:
                    out_toks.append(T.dma('sp', 'okvp', lambda e: e.dma_start(out=kp_d[:, :], in_=kvout[:, 0, :]), r=['kvout']))
                    out_toks.append(T.dma('sp', 'okvp', lambda e: e.dma_start(out=vp_d[:, :], in_=kvout[:, 1, :]), r=['kvout']))

            def out_y(Tn, yslot, src, hk, par):
                r0 = yrow[0]
                yrow[0] += Tn
                out_toks.append(T.dma('sp', 'oy%d_%d' % (par, yslot), lambda e: e.dma_start(out=y_d[r0:r0 + Tn, :], in_=src[:Tn, :]), r=[hk]))
            T.cur_tag = 's2_%02d' % mi
            stage2(256, [(0, 128, 0), (128, 128, 1)], cvh, 'cvh', mi == 0, out_y, par=mi % 2)

        out_toks.append(T.dma('sp', 'ostC', lambda e: e.dma_start(out=Cp_d.rearrange("h d e -> d h e"), in_=Cst[0][:, :, 0:128]), r=['C0']))
        out_toks.append(T.dma('sp', 'ostC', lambda e: e.dma_start(out=np_d.rearrange("h d -> d h"), in_=Cst[0][:, :, 128], allow_slow_non_contiguous=True), r=['C0']))
        out_toks.append(T.dma('sp', 'ostm', lambda e: e.dma_start(out=mp_d[:, :], in_=mst[0][:]), r=['m0']))
        for j in range(2):
            out_toks.append(T.dma('sp', 'ostcv', lambda e, j=j: e.dma_start(out=cvp_d[j, :].rearrange("(c p) -> p c", p=128), in_=cvh[:, :, j],
                                                                           allow_slow_non_contiguous=True), r=['cvh']))

    def do_sample():
        T.cur_tag = 'sample'
        load_x(0, xs_d[:, :], 16)
        load_cs(0, NT_FULL)
        T.dma('sp', 'xin1', lambda e: e.dma_start(out=xin[1][:, 0:128], in_=ck_d[:, :]), w=['xin1'])
        T.dma('sp', 'xin1', lambda e: e.dma_start(out=xin[1][:, 128:256], in_=cv_d[:, :]), w=['xin1'])
        T.op('pe', lambda e: e.transpose(P4[:, 0:128], xin[1][:, 0:128], ident[:, :]), r=['xin1', 'const'], w=['P4'])
        T.op('act', lambda e: e.copy(kTb[1][:, :], P4[:, 0:128]), r=['P4'], w=['kT_1'])
        T.op('act', lambda e: e.copy(va1[1][:, :, 0:64], xin[1][:, 128:256].rearrange("p (k d) -> p k d", k=2)), r=['xin1'], w=['va1_1'])
        T.op('dve', lambda e: e.memset(Cst[0][:], 0.0), w=['C0'])
        T.dma('sp', 'sstC', lambda e: e.dma_start(out=Cst[0][:, :, 0:128], in_=C0_d.rearrange("h d e -> d h e")), w=['C0'])
        T.dma('sp', 'sstC', lambda e: e.dma_start(
            out=Cst[0][:, :, 128], in_=n0_d.rearrange("h d -> d h"), allow_slow_non_contiguous=True), w=['C0'])
        T.dma('sp', 'sstm', lambda e: e.dma_start(out=mst[0][:], in_=m0_d[:, :]), w=['m0'])
        T.op('act', lambda e: e.copy(Cb[0][:], Cst[0][:]), r=['C0'], w=['Cb0'])
        stage1(16, 0, 0, 0, 1, 0.0, 0, 0, 0, True, True)
        out_toks.append(T.dma('sp', 'okvs', lambda e: e.dma_start(out=ks_d[:, :], in_=kvout[:16, 0, :]), r=['kvout']))
        out_toks.append(T.dma('sp', 'okvs', lambda e: e.dma_start(out=vs_d[:, :], in_=kvout[:16, 1, :]), r=['kvout']))

        def out_ys(Tn, yslot, src, hk, par):
            out_toks.append(T.dma('sp', 'oys', lambda e: e.dma_start(out=ys_d[:, :], in_=src[:Tn, :]), r=[hk]))
        stage2(16, [(0, 16, 0)], cvsb, 'cvsb', False, out_ys, par=0)
        out_toks.append(T.dma('sp', 'ossC', lambda e: e.dma_start(out=Cs_d.rearrange("h d e -> d h e"), in_=Cst[0][:, :, 0:128]), r=['C0']))
        out_toks.append(T.dma('sp', 'ossC', lambda e: e.dma_start(out=ns_d.rearrange("h d -> d h"), in_=Cst[0][:, :, 128], allow_slow_non_contiguous=True), r=['C0']))
        out_toks.append(T.dma('sp', 'ossm', lambda e: e.dma_start(out=ms_d[:, :], in_=mst[0][:]), r=['m0']))
        for j in range(2):
            out_toks.append(T.dma('sp', 'osscv', lambda e, j=j: e.dma_start(out=cvs_d[j, :].rearrange("(c p) -> p c", p=128), in_=cvsb[:, :, j],
                                                                           allow_slow_non_contiguous=True), r=['cvsb']))
    do_sample()
    T.cur_tag = 'setup2'
    T.op('dve', lambda e: e.memset(Cst[0][:], 0.0), w=['C0'])
    T.op('dve', lambda e: e.memset(Cb[0][:], 0.0), w=['Cb0'])
    T.op('dve', lambda e: e.memset(mst[0][:], 0.0), w=['m0'])
    do_prefix()
    do_main()
    T.wait_all('sp', out_toks)
    if FILL:
        T.filler_fn = lambda e: e.matmul(P5[:, 0:512], identb[:, :], win[:, 0, 0:512], start=True, stop=True)
        T.fill_frac = float(os.environ.get('K_FILL_FRAC', '0.5'))
        T.filler_cost = float(os.environ.get('K_FILL_COST', '0.3'))
        T.fill_min = float(os.environ.get('K_FILL_MIN', '1.5'))
    T.schedule()

    with nc.Block() as block:
        @block.sync
        def _(e):
            T.replay('sp', e)

        @block.tensor
        def _(e):
            T.replay('pe', e)

        @block.scalar
        def _(e):
            T.replay('act', e)

        @block.vector
        def _(e):
            T.replay('dve', e)

        @block.gpsimd
        def _(e):
            T.replay('pool', e)
    es.close()
    return nc


def _host_consts():
    ident = np.eye(128, dtype=np.float32)
    s = np.arange(128)[:, None]
    l = np.arange(128)[None, :]
    maskT = np.where(l >= s, 0.0, NEG).astype(np.float32)
    amask = np.ones((128, 2, 128), np.float32)
    amask[:, 0, :] = np.where((s < 64) & (l >= 64), 0.0, 1.0)
    amask[:, 1, :] = np.where((s >= 64) & (l < 64), 0.0, 1.0)
    sel4 = np.zeros((4, 4, 128), np.float32)
    for h in range(4):
        sel4[h, h, :] = 1.0
    return ident, maskT, amask, sel4.reshape(4, 512)


def _rope_tables(pos):
    half = 32
    inv = (np.float32(10000.0) ** (-np.arange(half, dtype=np.float32) / np.float32(half))).astype(np.float32)
    d = np.arange(128) % 64
    f = inv[d % 32]
    ang = (pos.astype(np.float32)[None, :] * f[:, None]).astype(np.float32)
    c = np.cos(ang).astype(np.float32)
    sn = np.sin(ang).astype(np.float32)
    sign = np.where(d < 32, -1.0, 1.0).astype(np.float32)[:, None]
    return c, (sn * sign).astype(np.float32)


_NC_CACHE = {}


def kernel(x_prompt, x_sample, state_mlstm_C, state_mlstm_n, state_mlstm_m, cache_swa_k, cache_swa_v,
           state_conv, w_in, b_igate, b_fgate, g_mlstm_norm, attn_sinks, w_out, ln1_g, ln1_b,
           w_up, w_conv, b_conv, w_down, ln2_g, ln2_b):
    f = lambda a: np.ascontiguousarray(np.asarray(a, dtype=np.float32))
    x_prompt, x_sample = f(x_prompt), f(x_sample)
    w_in0 = f(w_in)[0]
    rot = (np.arange(64) + 32) % 64
    qa0, ka0, va0 = 2056, 2568, 2696
    qap, qar = [], []
    for j in range(4):
        for hd in (j, 4 + j):
            qap.extend(qa0 + hd * 64 + np.arange(64))
            qar.extend(qa0 + hd * 64 + rot)
    kap = list(ka0 + np.arange(128))
    kar = list(ka0 + np.concatenate([rot, 64 + rot]))
    cols = list(range(0, 2056)) + list(range(va0, va0 + 128)) + qap + qar + kap + kar
    w_in_aug = np.ascontiguousarray(w_in0[:, np.array(cols, dtype=np.int64)])
    assert w_in_aug.shape[1] == WCOLS
    ident, maskT, amask, sel4 = _host_consts()
    ln = np.stack([f(ln1_g)[0], f(ln1_b)[0], f(ln2_g)[0], f(ln2_b)[0]], 0)

    nt_pre = int(os.environ.get("K_NT_PRE", NT_PRE))
    nt_full = int(os.environ.get("K_NT_FULL", NT_FULL))
    key = (nt_pre, nt_full)
    if key not in _NC_CACHE:
        _NC_CACHE[key] = build_program(nt_pre, nt_full)
    nc = _NC_CACHE[key]

    in_maps = []
    for c in range(8):
        b, half = c // 2, c % 2
        if half == 1:
            xall = x_prompt[b]
        else:
            xall = np.concatenate([np.zeros((4096, D), np.float32), x_prompt[b, :4096]], 0)
        pos0 = half * 4096 - 256
        cosT = np.zeros((NT_FULL + 1, 128, 128), np.float32)
        sinT = np.zeros((NT_FULL + 1, 128, 128), np.float32)
        for i in range(NT_FULL):
            cc, ss = _rope_tables(pos0 + i * 128 + np.arange(128))
            cosT[i], sinT[i] = cc, ss
        cc, ss = _rope_tables(2048 + np.arange(16))
        cosT[NT_FULL, :, :16], sinT[NT_FULL, :, :16] = cc, ss
        hsv = np.zeros((128, 2), np.float32)
        hsv[:, 0] = float(half)
        hsv[:, 1] = 0.0 if half == 1 else NEG
        in_maps.append({
            "xall": np.ascontiguousarray(xall), "xs": x_sample[c],
            "C0": f(state_mlstm_C)[0, c], "n0": f(state_mlstm_n)[0, c], "m0": f(state_mlstm_m)[0, c].reshape(4, 1),
            "ck": f(cache_swa_k)[0, c].reshape(128, 128), "cv": f(cache_swa_v)[0, c].reshape(128, 128),
            "sconv": f(state_conv)[0, c],
            "w_in": w_in_aug, "w_out": f(w_out)[0], "w_up": f(w_up)[0], "w_down": f(w_down)[0],
            "w_conv": f(w_conv)[0], "b_conv": f(b_conv)[0].reshape(1, -1),
            "b_i": f(b_igate)[0].reshape(4, 1), "b_f": f(b_fgate)[0].reshape(4, 1),
            "g_norm": f(g_mlstm_norm)[0].reshape(1, 512), "sinks": f(attn_sinks)[0].reshape(1, 8),
            "ln": ln, "cosT": cosT, "sinT": sinT, "ident": ident, "maskT": maskT, "amask": amask,
            "sel4": sel4, "hs": hsv,
        })
    res = run_bass_kernel_spmd(nc, in_maps, core_ids=list(range(8)))
    R = res.results
    y_p = np.zeros((4, 8192, D), np.float32)
    for c in range(8):
        y_p[c // 2, (c % 2) * 4096:(c % 2 + 1) * 4096] = R[c]["y"]
    y_s = np.stack([R[c]["ys"] for c in range(8)], 0)
    odd = [1, 3, 5, 7]
    C_p = np.stack([R[c]["Cp"] for c in odd], 0)[None]
    n_p = np.stack([R[c]["np"] for c in odd], 0)[None]
    m_p = np.stack([R[c]["mp"].reshape(4) for c in odd], 0)[None]
    k_p = np.stack([R[c]["kp"].reshape(128, 2, 64) for c in odd], 0)[None]
    v_p = np.stack([R[c]["vp"].reshape(128, 2, 64) for c in odd], 0)[None]
    cv_p = np.stack([R[c]["cvp"] for c in odd], 0)[None]
    C_s = np.stack([R[c]["Cs"] for c in range(8)], 0)[None]
    n_s = np.stack([R[c]["ns"] for c in range(8)], 0)[None]
    m_s = np.stack([R[c]["ms"].reshape(4) for c in range(8)], 0)[None]
    k_s = np.stack([R[c]["ks"].reshape(16, 2, 64) for c in range(8)], 0)[None]
    v_s = np.stack([R[c]["vs"].reshape(16, 2, 64) for c in range(8)], 0)[None]
    cv_s = np.stack([R[c]["cvs"] for c in range(8)], 0)[None]
    return (y_p, y_s, C_p, n_p, m_p, k_p, v_p, cv_p, C_s, n_s, m_s, k_s, v_s, cv_s)
```

```python
import os
from contextlib import ExitStack
import numpy as np
import concourse.bass as bass
import concourse.mybir as mybir
from concourse.bass_utils import run_bass_kernel_spmd

F32 = mybir.dt.float32
BF16 = mybir.dt.bfloat16
AF = mybir.ActivationFunctionType
ALU = mybir.AluOpType
AX = mybir.AxisListType

D = 1024
NT_PRE = 30
NT_FULL = 34
DFF = 2816
NU = 22
WCOLS = 3464
C_QM, C_KM, C_VM, C_OM, C_IP, C_FP, C_VA, C_QAP, C_QAR, C_KAP, C_KAR = 0, 512, 1024, 1536, 2048, 2052, 2056, 2184, 2696, 3208, 3336
ALPHA = float(2.0 ** 0.25)
EPS = 1e-5
KSCALE = float(128.0 ** -0.5)
NEG = -30000.0


SKIP_WAR = int(os.environ.get('K_SKIP_WAR', '0'))


class Tracker:
    HOP = 0.4
    DEF_COST = {'pe': 0.7, 'act': 0.45, 'dve': 0.3, 'pool': 0.8, 'sp': 0.08}

    def __init__(self, nc, es):
        self.nc = nc
        self.es = es
        self.engs = {}
        self.last_w = {}
        self.readers = {}
        self.streams = {}
        self.ops = []
        self.thr_hist = {}
        self.cur_tag = 'setup'
        self.filler_fn = None
        self.pe_scale = float(os.environ.get('K_PE_SCALE', '0.65'))
        self.ad_scale = float(os.environ.get('K_AD_SCALE', '1.1'))
        self.HOP = float(os.environ.get('K_HOP', '0.9'))
        self.hop_same = float(os.environ.get('K_HOP_SAME', '0.3'))
        self.filler_cost = 0.43
        self.fill_min = 1.5
        self.fill_frac = 0.6

    def add_engine(self, name, eng):
        sem = self.es.enter_context(self.nc.semaphore("s_" + name))
        self.engs[name] = dict(eng=eng, sem=sem, name=name, order=[])

    def stream(self, name, group=False):
        if name not in self.streams:
            sem = self.es.enter_context(self.nc.semaphore("d_" + name))
            self.streams[name] = dict(sem=sem, count=0, group=group, name=name)
        return self.streams[name]

    def _deps(self, reads, writes, extra, group):
        deps = set()
        for k in reads:
            deps.update(self.last_w.get(k, ()))
        self._raw = set(deps)
        if not group:
            for k in writes:
                deps.update(self.last_w.get(k, ()))
                deps.update(self.readers.get(k, ()))
        deps.update(extra)
        return deps

    def _commit(self, oid, reads, writes, group):
        for k in reads:
            self.readers.setdefault(k, []).append(oid)
        for k in writes:
            if group:
                self.last_w.setdefault(k, []).append(oid)
            else:
                self.last_w[k] = [oid]
                self.readers[k] = []

    def op(self, ename, fn, r=(), w=(), extra=(), c=None):
        deps = self._deps(r, w, extra, False)
        oid = len(self.ops)
        cc = self.DEF_COST[ename] if c is None else c
        if ename == 'pe':
            cc *= self.pe_scale
        elif ename in ('act', 'dve', 'pool'):
            cc *= self.ad_scale
        self.ops.append(dict(id=oid, eng=ename, fn=fn, deps=deps, dma=None, tag=self.cur_tag, cost=cc, raw=self._raw | set(extra)))
        self._commit(oid, r, w, False)
        return oid

    def dma(self, qname, stream, fn, r=(), w=(), extra=(), group=False, c=4.0):
        S = self.stream(stream, group)
        deps = self._deps(r, w, extra, group)
        oid = len(self.ops)
        thr = ()
        if group:
            hist = self.thr_hist.setdefault(qname, [])
            B = 3 if qname == 'pool' else 4
            nb = len(hist) // B
            if nb > 0:
                thr = tuple(hist[(nb - 1) * B:nb * B])
                deps.update(thr)
            hist.append(oid)
        self.ops.append(dict(id=oid, eng=qname, fn=fn, deps=deps, dma=S, cost=c, thr=thr, tag=self.cur_tag))
        self._commit(oid, r, w, group)
        return oid

    def wait_all(self, ename, toks):
        oid = len(self.ops)
        self.ops.append(dict(id=oid, eng=ename, fn=None, deps=set(toks), dma=None, cost=0.05, tag='final'))
        return oid

    def schedule(self):
        import heapq
        ops = self.ops
        n = len(ops)
        ndeps = [len(o['deps']) for o in ops]
        users = [[] for _ in range(n)]
        for o in ops:
            for d in o['deps']:
                users[d].append(o['id'])
        ready_t = [0.0] * n
        fin = [0.0] * n
        bl = [0.0] * n
        for o in reversed(ops):
            i = o['id']
            m = 0.0
            for u in users[i]:
                if bl[u] > m:
                    m = bl[u]
            bl[i] = m + o['cost'] + self.HOP
        mode = os.environ.get('K_PRIO', 'id')
        if mode == 'bl':
            prio = [(-bl[i], i) for i in range(n)]
        else:
            prio = [(i, i) for i in range(n)]
        ready = {e: [] for e in self.engs}
        free_t = {e: 0.0 for e in self.engs}
        events = []
        for o in ops:
            if ndeps[o['id']] == 0:
                heapq.heappush(ready[o['eng']], (prio[o['id']], o['id']))
        for e in self.engs:
            heapq.heappush(events, (0.0, 0, e))
        seq = 1
        done = 0
        pending_wake = {e: True for e in self.engs}
        while done < n:
            if not events:
                raise RuntimeError("scheduler stuck")
            t, _, e = heapq.heappop(events)
            pending_wake[e] = False
            if free_t[e] > t + 1e-9:
                heapq.heappush(events, (free_t[e], seq, e)); seq += 1; pending_wake[e] = True
                continue
            cand = None
            tmp = []
            while ready[e]:
                item = heapq.heappop(ready[e])
                oid = item[1]
                if ready_t[oid] <= t + 1e-9:
                    cand = oid
                    break
                tmp.append(item)
            for x in tmp:
                heapq.heappush(ready[e], x)
            if cand is None:
                if ready[e]:
                    tn = min(ready_t[x[1]] for x in ready[e])
                    heapq.heappush(events, (tn, seq, e)); seq += 1; pending_wake[e] = True
                continue
            o = ops[cand]
            o['t0'] = t
            self.engs[e]['order'].append(cand)
            if o['dma'] is not None:
                free_t[e] = t + (0.6 if e == 'pool' else 0.08)
                fin[cand] = t + o['cost']
            else:
                free_t[e] = t + o['cost']
                fin[cand] = free_t[e]
            done += 1
            for u in users[cand]:
                ndeps[u] -= 1
                if ops[u]['eng'] == e and o['dma'] is None:
                    if e == 'pe' or (SKIP_WAR and e in ('act', 'dve') and cand not in ops[u].get('raw', ops[u]['deps'])):
                        rt = fin[cand]
                    else:
                        rt = fin[cand] + self.hop_same
                else:
                    rt = fin[cand] + self.HOP
                if rt > ready_t[u]:
                    ready_t[u] = rt
                if ndeps[u] == 0:
                    ue = ops[u]['eng']
                    heapq.heappush(ready[ue], (prio[u], u))
                    if not pending_wake[ue]:
                        heapq.heappush(events, (max(ready_t[u], free_t[ue]), seq, ue)); seq += 1; pending_wake[ue] = True
            heapq.heappush(events, (free_t[e], seq, e)); seq += 1; pending_wake[e] = True
        self.sim_time = max(fin)
        self.fin = fin
        if self.filler_fn is not None:
            fc = self.filler_cost
            new_order = []
            prev_end = None
            nfill = 0
            armed = False
            for oid in self.engs['pe']['order']:
                o = ops[oid]
                if not armed:
                    if any(ops[d]['dma'] is not None and ops[d]['dma']['name'] == 'win' for d in o['deps']):
                        armed = True
                    new_order.append(oid)
                    prev_end = fin[oid]
                    continue
                if prev_end is not None:
                    gap = o['t0'] - prev_end
                    if gap > self.fill_min:
                        k = int((gap * self.fill_frac) / fc)
                        for _ in range(k):
                            fid = len(ops)
                            ops.append(dict(id=fid, eng='pe', fn=self.filler_fn, deps=set(), dma=None, cost=fc, tag='fill', filler=True))
                            new_order.append(fid)
                            nfill += 1
                new_order.append(oid)
                prev_end = fin[oid]
            self.engs['pe']['order'] = new_order
            self.nfill = nfill
        for e, E in self.engs.items():
            pos = 0
            for oid in E['order']:
                o = ops[oid]
                if o['dma'] is not None:
                    o['dma']['count'] += 16
                    o['val'] = o['dma']['count']
                elif o['fn'] is not None and not o.get('filler'):
                    pos += 1
                    o['val'] = pos

    def replay(self, ename, eng):
        E = self.engs[ename]
        ops = self.ops
        seen = {}
        for oid in E['order']:
            o = ops[oid]
            need = {}
            for d in o['deps']:
                D = ops[d]
                if D['dma'] is not None:
                    S = D['dma']
                    if d in o.get('thr', ()):
                        sem, val = S['sem'], D['val']
                    else:
                        sem, val = S['sem'], (S['count'] if S['group'] else D['val'])
                else:
                    if D['fn'] is None:
                        continue
                    if D['eng'] == 'pe' and ename == 'pe':
                        continue
                    if SKIP_WAR and D['eng'] == ename and ename in ('act', 'dve') and d not in o.get('raw', o['deps']):
                        continue
                    sem, val = self.engs[D['eng']]['sem'], D['val']
                if need.get(sem, 0) < val:
                    need[sem] = val
            for sem, val in need.items():
                if seen.get(sem, 0) < val:
                    eng.wait_ge(sem, val)
                    seen[sem] = val
            if o['fn'] is None:
                continue
            ins = o['fn'](eng)
            if o.get('filler'):
                continue
            if o['dma'] is not None:
                ins.then_inc(o['dma']['sem'], 16)
            else:
                ins.then_inc(E['sem'], 1)


def build_program(nt_pre=NT_PRE, nt_full=NT_FULL):
    nc = bass.Bass("TRN2", target_bir_lowering=False)
    es = ExitStack()

    def din(name, shape, dt=F32):
        return nc.dram_tensor(name, list(shape), dt, kind="ExternalInput").ap()

    def dout(name, shape, dt=F32):
        return nc.dram_tensor(name, list(shape), dt, kind="ExternalOutput").ap()

    xall = din("xall", [8192, D])
    xs_d = din("xs", [16, D])
    C0_d = din("C0", [4, 128, 128])
    n0_d = din("n0", [4, 128])
    m0_d = din("m0", [4, 1])
    ck_d = din("ck", [128, 128])
    cv_d = din("cv", [128, 128])
    sconv_d = din("sconv", [2, 2 * DFF])
    win_d = din("w_in", [D, WCOLS])
    wout_d = din("w_out", [D, D])
    wup_d = din("w_up", [D, 2 * DFF])
    wdn_d = din("w_down", [DFF, D])
    wconv_d = din("w_conv", [3, 2 * DFF])
    bconv_d = din("b_conv", [1, 2 * DFF])
    bi_d = din("b_i", [4, 1])
    bf_d = din("b_f", [4, 1])
    gn_d = din("g_norm", [1, 512])
    sink_d = din("sinks", [1, 8])
    ln_d = din("ln", [4, D])
    cos_d = din("cosT", [NT_FULL + 1, 128, 128])
    sin_d = din("sinT", [NT_FULL + 1, 128, 128])
    ident_d = din("ident", [128, 128])
    maskT_d = din("maskT", [128, 128])
    amask_d = din("amask", [128, 2, 128])
    sel4_d = din("sel4", [4, 4 * 128])
    hs_d = din("hs", [128, 2])

    y_d = dout("y", [4096, D])
    ys_d = dout("ys", [16, D])
    Cp_d = dout("Cp", [4, 128, 128])
    np_d = dout("np", [4, 128])
    mp_d = dout("mp", [4, 1])
    kp_d = dout("kp", [128, 128])
    vp_d = dout("vp", [128, 128])
    cvp_d = dout("cvp", [2, 2 * DFF])
    Cs_d = dout("Cs", [4, 128, 128])
    ns_d = dout("ns", [4, 128])
    ms_d = dout("ms", [4, 1])
    ks_d = dout("ks", [16, 128])
    vs_d = dout("vs", [16, 128])
    cvs_d = dout("cvs", [2, 2 * DFF])
    wupbf = nc.dram_tensor("wupbf", [NU, 128, 8 * 256], BF16, kind="Internal").ap()
    wdnbf = nc.dram_tensor("wdnbf", [NU // 2, 128, 2 * D], BF16, kind="Internal").ap()

    def sb(name, shape, dt=F32):
        return es.enter_context(nc.sbuf_tensor(name, list(shape), dt))

    def pt(name, shape, dt=F32):
        return es.enter_context(nc.psum_tensor(name, list(shape), dt))

    win = sb("win", [128, 8, WCOLS], BF16)
    wout = sb("wout", [128, 8, D], BF16)
    wdr = [sb("wdr%d" % i, [128, 2, 512], BF16) for i in range(5)]
    NS = 2
    ustS = [sb("ustS%d" % i, [128, 2, 258]) for i in range(NS)]
    ctS = [sb("ctS%d" % i, [128, 2, 256]) for i in range(NS)]
    wur = [sb("wur%d" % i, [128, 8, 256], BF16) for i in range(2)]
    lnp = sb("lnp", [128, 4, D])
    gnb = sb("gnb", [128, 512])
    esink = sb("esink", [128, 8])
    ident = sb("ident_s", [128, 128])
    maskT = sb("maskT_s", [128, 128])
    amask = sb("amask_s", [128, 2, 128], BF16)
    sel4 = sb("sel4_s", [4, 4 * 128])
    hs = sb("hs_s", [128, 2])
    bi = sb("bi_s", [4, 1])
    nbf = sb("nbf_s", [4, 1])
    wcv = sb("wcv", [128, 3, 2 * NU])
    bcv = sb("bcv", [128, 2 * NU])
    cvh = sb("cvh", [128, 2 * NU, 2])
    cvsb = sb("cvsb", [128, 2 * NU, 2])
    ones4 = sb("ones4", [4, 128])
    onesr = ones4

    xin = [sb("xin%d" % i, [128, D]) for i in range(2)]
    xT = sb("xT", [128, 8, 128], BF16)
    mixT = sb("mixT", [128, 8, 128], BF16)
    kmtok = sb("kmtok", [128, 4, 128], BF16)
    kw = kmtok
    v1 = sb("v1", [128, 4, 130], BF16)
    qmT = sb("qmT", [128, 4, 128], BF16)
    qaT = sb("qaT", [128, 4, 128], BF16)
    kmT = sb("kmT", [128, 4, 128], BF16)
    cosb = [sb("cos%d" % i, [128, 128]) for i in range(2)]
    sinb = [sb("sin%d" % i, [128, 128]) for i in range(2)]
    nd1 = sb("nd1", [128, 4, 130])
    nd = nd1
    rt1 = sb("rt1", [128, 4, 128])
    Ebuf = sb("Ebuf", [128, 4, 128])
    hmr = Ebuf
    rt2 = sb("rt2", [128, 4, 128])
    krope = sb("krope", [128, 128])
    kTb = [sb("kTb%d" % i, [128, 128], BF16) for i in range(2)]
    va1 = [sb("va1%d" % i, [128, 2, 66], BF16) for i in range(2)]
    kvout = sb("kvout", [128, 2, 128])
    ST = sb("ST", [128, 4, 128], BF16)
    Cst = [sb("Cst0", [128, 4, 130])] * 2
    Cb = [sb("Cb0", [128, 4, 130], BF16)] * 2
    mst = [sb("mst0", [4, 1])] * 2
    bst = sb("bst", [128, 4, 6])
    mv = sb("mv", [128, 4, 2])
    sm = sb("sm", [128, 32])
    lnscr = [(sb("bstL%d" % i, [128, 2, 6]), sb("mvL%d" % i, [128, 2]), sb("smL%d" % i, [128, 2])) for i in range(2)]
    pTraw = sb("pTraw", [128, 1024])
    pT = pTraw[:].bitcast(BF16).rearrange("p (b h t) -> p b h t", b=2, h=8)
    og = sb("og", [128, 512])
    mix = sb("mix", [128, D])
    mixb = sb("mixb", [128, D], BF16)
    identb = sb("identb", [128, 128], BF16)
    s1 = mix
    MIXK = ['mix']
    hresL = [sb("hres%d" % i, [128, 2, D]) for i in range(2)]
    hTL = [sb("hT%d" % i, [128, 8, 256], BF16) for i in range(2)]
    actT = sb("actT", [128, NU, 256], BF16)
    G = sb("G", [4, 7, 128])
    tm = sb("tm", [128, 16])
    dbc = sb("dbc", [128, 4])
    gsm = sb("gsm", [4, 8])

    P01 = pt("P01", [128, 1024])
    P23 = pt("P23", [128, 1024])
    P4 = pt("P4", [128, 512])
    P5 = pt("P5", [128, 512])
    P6 = pt("P6", [128, 512])
    P7 = pt("P7", [128, 512])
    PA0 = P01[:, 0:512]
    PA1 = P01[:, 512:1024]
    P0b = PA0.bitcast(BF16)
    P1b = PA1.bitcast(BF16)

    T = Tracker(nc, es)
    FILL = int(os.environ.get('K_FILL', '1'))
    T.add_engine('sp', nc.sync)
    T.add_engine('pe', nc.tensor)
    T.add_engine('act', nc.scalar)
    T.add_engine('dve', nc.vector)
    T.add_engine('pool', nc.gpsimd)

    def ld(q, stream, out, in_, w, r=()):
        return T.dma(q, stream, lambda e, o=out, i=in_: e.dma_start(out=o, in_=i), r=r, w=w, group=True, c=6.0)

    for k in range(8):
        for c0 in (0, 1732):
            ld('pool', 'win', win[:, k, c0:c0 + 1732], win_d[k * 128:(k + 1) * 128, c0:c0 + 1732], w=['win'])
    ld('sp', 'small', ident[:], ident_d[:, :], w=['const'])
    ld('sp', 'small', maskT[:], maskT_d[:, :], w=['const'])
    ld('pool', 'small2', amask[:], amask_d[:, :, :], w=['c5'])
    ld('sp', 'small', sel4[:], sel4_d[:, :], w=['const'])
    ld('sp', 'small', hs[:], hs_d[:, :], w=['const'])
    ld('sp', 'small', bi[:], bi_d[:, :], w=['const'])
    ld('sp', 'small', nbf[:], bf_d[:, :], w=['const'])
    ld('sp', 'small', gnb[:], gn_d[0, :].partition_broadcast(128), w=['const'])
    ld('sp', 'small', esink[:], sink_d[0, :].partition_broadcast(128), w=['const'])
    for j in range(4):
        ld('sp', 'small', lnp[:, j, :], ln_d[j, :].partition_broadcast(128), w=['const'])
    for j in range(3):
        T.dma('sp', 'small', lambda e, j=j: e.dma_start(
            out=wcv[:, j, :], in_=wconv_d[j, :].rearrange("(c p) -> p c", p=128), allow_slow_non_contiguous=True), w=['const'], group=True, c=20.0)
    T.dma('sp', 'small', lambda e: e.dma_start(
        out=bcv[:], in_=bconv_d[0, :].rearrange("(c p) -> p c", p=128), allow_slow_non_contiguous=True), w=['const'], group=True, c=20.0)
    for j in range(2):
        T.dma('sp', 'small', lambda e, j=j: e.dma_start(
            out=cvsb[:, :, j], in_=sconv_d[j, :].rearrange("(c p) -> p c", p=128), allow_slow_non_contiguous=True), w=['cvsb_%d' % u_ for u_ in range(NU)], group=True, c=20.0)
    for k in range(8):
        ld('pool', 'wout', wout[:, k, :], wout_d[k * 128:(k + 1) * 128, :], w=['wout'])
    for u in range(NU):
        for j in range(2):
            c0 = j * DFF + u * 128
            T.dma('pool', 'wupc', lambda e, u=u, j=j, c0=c0: e.dma_start(
                out=wupbf[u].rearrange("p (k c) -> p k c", k=8)[:, :, j * 128:(j + 1) * 128],
                in_=wup_d[:, c0:c0 + 128].rearrange("(k p) c -> p k c", p=128)), w=['wupbf'], group=True, c=8.0)
    for c in range(NU // 2):
        T.dma('pool', 'wdnc', lambda e, c=c: e.dma_start(
            out=wdnbf[c].rearrange("p (k n) -> p k n", k=2),
            in_=wdn_d[c * 256:(c + 1) * 256, :].rearrange("(k p) n -> p k n", p=128)), w=['wdnbf'], group=True, c=8.0)

    T.op('dve', lambda e: e.memset(ones4[:], 1.0), w=['c2'])
    T.op('dve', lambda e: e.tensor_scalar(nbf[:], nbf[:], -1.0, None, ALU.mult), r=['const'], w=['c3'])
    T.op('act', lambda e: e.activation(esink[:], esink[:], AF.Exp), r=['const'], w=['c4'])
    T.op('dve', lambda e: e.tensor_copy(identb[:], ident[:]), r=['const'], w=['c6'])
    for i in range(2):
        T.op('dve', lambda e, i=i: e.memset(v1[:, :, 128:130], 1.0), w=['v1'])
    for i in range(2):
        T.op('dve', lambda e, i=i: e.memset(va1[i][:], 1.0), w=['va1_%d' % i])
        T.op('dve', lambda e, i=i: e.memset(kTb[i][:], 0.0), w=['kT_%d' % i])
    T.op('dve', lambda e: e.memset(cvh[:], 0.0), w=['cvh_%d' % u_ for u_ in range(NU)])
    CONSTS = ['const', 'c2', 'c3', 'c4', 'c5', 'c6']

    def transposes8(src, Tn, dst_ps, rkeys):
        def fn(e):
            ins = None
            for k in range(8):
                ins = e.transpose(dst_ps[:, k * 128:k * 128 + Tn], src[:Tn, k * 128:(k + 1) * 128], ident[:Tn, :Tn])
            return ins
        T.op('pe', fn, r=list(rkeys) + CONSTS, w=['P0', 'P1'], c=2.0)

    def load_x(slot, src_ap, Tn):
        return T.dma('sp', 'xin%d' % slot, lambda e: e.dma_start(out=xin[slot][:Tn, :], in_=src_ap), w=['xin%d' % slot])

    def make_xT(slot, Tn):
        transposes8(xin[slot], Tn, P01, ['xin%d' % slot])
        for b in range(2):
            T.op('act', lambda e, b=b: e.copy(
                xT[:, 4 * b:4 * b + 4, :Tn],
                P01[:, 512 * b:512 * b + 512].rearrange("p (k t) -> p k t", k=4)[:, :, :Tn]),
                r=['P%d' % b], w=['xT'])

    def mm_A(ps, col0, ncols, Tn, wkey, bankkeys):
        def fn(e):
            ins = None
            for k in range(8):
                ins = e.matmul(ps[:Tn, :ncols], xT[:, k, :Tn], win[:, k, col0:col0 + ncols], start=(k == 0), stop=(k == 7))
            return ins
        T.op('pe', fn, r=['xT', 'win'], w=bankkeys, c=8 * (0.09 + max(ncols, 64) / 2400.0) + 0.1)

    def mm_B(ps, col0, mcols, Tn, bankkeys):
        def fn(e):
            ins = None
            for k in range(8):
                ins = e.matmul(ps[:mcols, :Tn], win[:, k, col0:col0 + mcols], xT[:, k, :Tn], start=(k == 0), stop=(k == 7))
            return ins
        T.op('pe', fn, r=['xT', 'win'], w=bankkeys, c=8 * 0.13 + 0.1)

    def gate_chain(Tn, si, full):
        m = mst[si]
        mk = 'm%d' % si
        ipT = P4[0:4, 128:128 + Tn]
        fpT = P4[0:4, 256:256 + Tn]
        R = lambda j: G[:, j, :Tn]
        T.op('act', lambda e: e.activation(R(0), fpT, AF.Exp, bias=nbf[:], scale=-1.0), r=['P4', 'c3'], w=['G0'])
        T.op('act', lambda e: e.activation(R(0), R(0), AF.Ln, bias=1.0), r=['G0'], w=['G0'])
        T.op('dve', lambda e: e.tensor_tensor_scan(R(1), onesr[:, :Tn], R(0), 0.0, ALU.mult, ALU.subtract), r=['G0', 'c2'], w=['G1'])
        T.op('dve', lambda e: e.scalar_tensor_tensor(R(2), ipT, bi[:], R(1), ALU.add, ALU.subtract), r=['P4', 'G1', 'const'], w=['G2'])
        T.op('dve', lambda e: e.tensor_tensor_scan(R(3), R(2), R(2), m[:], ALU.max, ALU.max), r=['G2', mk], w=['G3'])
        T.op('act', lambda e: e.activation(R(4), R(3), AF.Exp, bias=m[:], scale=-1.0), r=['G3', mk], w=['G4'])
        T.op('dve', lambda e: e.tensor_scalar(gsm[:, 0:1], G[:, 3, Tn - 1:Tn], -1.0, None, ALU.mult), r=['G3'], w=['gsm0'])
        T.op('act', lambda e: e.activation(R(5), R(2), AF.Exp, bias=gsm[:, 0:1]), r=['G2', 'gsm0'], w=['G5'])
        T.op('dve', lambda e: e.tensor_tensor(R(6), R(1), R(3), ALU.add), r=['G1', 'G3'], w=['G6'])
        if full:
            T.op('act', lambda e: e.activation(R(0), R(6), AF.Exp, scale=-1.0), r=['G6'], w=['G0'])
            T.op('dve', lambda e: e.tensor_scalar(R(1), R(3), -1.0, None, ALU.mult), r=['G3'], w=['G1'])
        T.op('dve', lambda e: e.tensor_scalar(gsm[:, 4:8], ident[0:4, 0:4], G[:, 4, Tn - 1:Tn], None, ALU.mult), r=['G4', 'const'], w=['gsm1'])
        T.op('dve', lambda e: e.tensor_copy(m[:], G[:, 6, Tn - 1:Tn]), r=['G6'], w=[mk])

        def fn(e):
            ins = None
            rows = [(2, 0), (4, 4), (0, 8), (5, 12)] if full else [(5, 12)]
            for (rj, c) in rows:
                ins = e.matmul(P4[:Tn, 384 + c:384 + c + 4], G[:, rj, :Tn], ident[0:4, 0:4], start=True, stop=True)
            ins = e.matmul(P4[:, 448:452], ones4[:, :], gsm[:, 4:8], start=True, stop=True)
            return ins
        T.op('pe', fn, r=['G2', 'G4', 'G0', 'G5', 'gsm1', 'c2', 'const'], w=['P4'])
        if full:
            T.op('dve', lambda e: e.tensor_copy(tm[:Tn, :], P4[:Tn, 384:400]), r=['P4'], w=['tm'])
        else:
            T.op('dve', lambda e: e.tensor_copy(tm[:Tn, 12:16], P4[:Tn, 396:400]), r=['P4'], w=['tm'])
        T.op('dve', lambda e: e.tensor_copy(dbc[:], P4[:, 448:452]), r=['P4'], w=['dbc'])

    def state_update(Tn, si):
        C = Cst[si]
        ck = 'C%d' % si
        T.op('dve', lambda e: e.tensor_tensor(kw[:Tn], kmtok[:Tn], tm[:Tn, 12:16].unsqueeze(2).to_broadcast([Tn, 4, 128]), ALU.mult),
             r=['kmtok', 'tm'], w=['kmtok'])

        def fn(e):
            ins = None
            for h in range(4):
                ins = e.matmul(P23[:, h * 256:h * 256 + 129], kw[:Tn, h, :], v1[:Tn, h, 0:129], start=True, stop=True)
            return ins
        T.op('pe', fn, r=['kmtok', 'v1'], w=['P2', 'P3'])
        for h in range(4):
            T.op('dve', lambda e, h=h: e.scalar_tensor_tensor(C[:, h, 0:129], C[:, h, 0:129], dbc[:, h:h + 1],
                                                              P23[:, h * 256:h * 256 + 129], ALU.mult, ALU.add),
                 r=['dbc', 'P2', 'P3', ck], w=[ck])

    out_toks = []

    def do_prefix():
        tok_x = {}
        if nt_pre > 0:
            tok_x[0] = load_x(0, xall[0:128, :], 128)
        for i in range(nt_pre):
            T.cur_tag = 'pre'
            slot = i % 2
            if i + 1 < nt_pre:
                load_x((i + 1) % 2, xall[(i + 1) * 128:(i + 2) * 128, :], 128)
            make_xT(slot, 128)
            mm_A(P23[:, 0:512], C_KM, 512, 128, 'win', ['P2'])
            mm_A(P23[:, 512:1024], C_VM, 512, 128, 'win', ['P3'])
            mm_B(P4[:, 128:256], C_IP, 4, 128, ['P4'])
            mm_B(P4[:, 256:384], C_FP, 4, 128, ['P4'])
            T.op('act', lambda e: e.activation(kmtok[:].rearrange("p h d -> p (h d)"), P23[:, 0:512], AF.Copy, scale=KSCALE), r=['P2'], w=['kmtok'])
            T.op('act', lambda e: e.copy(v1[:, :, 0:128], P23[:, 512:1024].rearrange("p (h d) -> p h d", h=4)), r=['P3'], w=['v1'])
            gate_chain(128, 0, False)
            state_update(128, 0)


    def stage1(Tn, xslot, si, kslot, pslot, prev_bias, cs_slot, hcol, htile, save_kv, is_sample, par=0, head_done=False, mid_hook=None):
        hres = hresL[par]
        hT = hTL[par]
        hrk = 'hres%d_%d' % (par, htile)
        htk = 'hT%d' % par
        C = Cst[si]
        ck = 'C%d' % si
        cbk = 'Cb%d' % si
        if not head_done:
            make_xT(xslot, Tn)
        mm_A(P23[:, 0:512], C_KM, 512, Tn, 'win', ['P2'])
        mm_A(P23[:, 512:1024], C_VM, 512, Tn, 'win', ['P3'])
        mm_A(PA1[:, 0:512], C_OM, 512, Tn, 'win', ['P1'])
        mm_A(P4[:, 0:128], C_VA, 128, Tn, 'win', ['P4'])
        mm_B(P4[:, 128:256], C_IP, 4, Tn, ['P4'])
        mm_B(P4[:, 256:384], C_FP, 4, Tn, ['P4'])
        T.op('act', lambda e: e.activation(kmtok[:Tn].rearrange("p h d -> p (h d)"), P23[:Tn, 0:512], AF.Copy, scale=KSCALE), r=['P2'], w=['kmtok'])
        T.op('act', lambda e: e.copy(v1[:Tn, :, 0:128], P23[:Tn, 512:1024].rearrange("p (h d) -> p h d", h=4)), r=['P3'], w=['v1'])
        T.op('act', lambda e: e.activation(og[:Tn], PA1[:Tn, :], AF.Exp, scale=-1.0), r=['P1'], w=['og'])
        T.op('act', lambda e: e.activation(og[:Tn], og[:Tn], AF.Ln, bias=1.0), r=['og'], w=['og'], c=0.55)
        T.op('act', lambda e: e.activation(og[:Tn], og[:Tn], AF.Exp, scale=-1.0), r=['og'], w=['og'], c=0.55)
        T.op('pool', lambda e: e.tensor_tensor(og[:Tn], og[:Tn], gnb[:Tn], ALU.mult), r=['og', 'const'], w=['og'], c=1.1)
        T.op('act', lambda e: e.copy(va1[kslot][:Tn, :, 0:64], P4[:Tn, 0:128].rearrange("p (k d) -> p k d", k=2)), r=['P4'], w=['va1_%d' % kslot])
        if save_kv:
            T.op('dve', lambda e: e.tensor_copy(kvout[:Tn, 1, :], P4[:Tn, 0:128]), r=['P4'], w=['kvout'])
        gate_chain(Tn, si, True)
        def fn_rb(e):
            ins = None
            for h in range(4):
                ins = e.matmul(P4[:Tn, h * 128:h * 128 + Tn], sel4[:, h * 128:h * 128 + Tn], G[:, 1, :Tn], start=True, stop=True)
            return ins
        T.op('pe', fn_rb, r=['G1', 'const'], w=['P4'], c=1.0)
        T.op('dve', lambda e: e.tensor_tensor(Ebuf[:Tn, :, :Tn], P4[:Tn, :].rearrange("p (h t) -> p h t", h=4)[:, :, :Tn],
                                              maskT[:Tn, :Tn].unsqueeze(1).to_broadcast([Tn, 4, Tn]), ALU.add),
             r=['P4', 'const'], w=['Ebuf'])
        for h in range(4):
            T.op('act', lambda e, h=h: e.activation(Ebuf[:Tn, h, :Tn], Ebuf[:Tn, h, :Tn], AF.Exp, bias=tm[:Tn, h:h + 1]), r=['Ebuf', 'tm'], w=['Ebuf'])
        for mt_ in range(4):
            mm_B(PA0[:, mt_ * 128:mt_ * 128 + 128], C_QM + mt_ * 128, 128, Tn, ['P0'])
        T.op('act', lambda e: e.copy(qmT[:, :, :Tn], PA0[:, :].rearrange("p (h t) -> p h t", h=4)[:, :, :Tn]), r=['P0'], w=['qmT'])
        def fn_kt(e):
            ins = None
            for h in range(4):
                ins = e.transpose(P1b[:, h * 128:h * 128 + Tn], kmtok[:Tn, h, :], identb[:Tn, :Tn])
            return ins
        T.op('pe', fn_kt, r=['kmtok', 'c6'], w=['P1'], c=0.7)
        T.op('act', lambda e: e.copy(kmT[:, :, :Tn], P1b[:, 0:512].rearrange("p (h t) -> p h t", h=4)[:, :, :Tn]), r=['P1'], w=['kmT'])
        def fn_s(e):
            ins = None
            for h in range(4):
                ins = e.matmul(P4[:Tn, h * 128:h * 128 + Tn], kmT[:, h, :Tn], qmT[:, h, :Tn], start=True, stop=True)
            return ins
        T.op('pe', fn_s, r=['kmT', 'qmT'], w=['P4'])
        T.op('dve', lambda e: e.tensor_tensor(ST[:Tn, :, :Tn], P4[:Tn, :].rearrange("p (h t) -> p h t", h=4)[:, :, :Tn], Ebuf[:Tn, :, :Tn], ALU.mult),
             r=['P4', 'Ebuf'], w=['ST'])
        def fn_qc(e):
            ins = None
            for h in range(4):
                ins = e.matmul(P23[:Tn, h * 256:h * 256 + 129], qmT[:, h, :Tn], Cb[si][:, h, 0:129], start=True, stop=True)
            return ins
        T.op('pe', fn_qc, r=['qmT', cbk], w=['P2', 'P3'])
        def fn_sv(e):
            ins = None
            for h in range(4):
                ins = e.matmul(P01[:Tn, h * 256:h * 256 + 129], ST[:Tn, h, :Tn], v1[:Tn, h, 0:129], start=True, stop=True)
            return ins
        T.op('pe', fn_sv, r=['ST', 'v1'], w=['P0', 'P1'])
        for h in range(4):
            T.op('act', lambda e, h=h: e.activation(nd1[:Tn, h, 0:129], P23[:Tn, h * 256:h * 256 + 129], AF.Copy, scale=tm[:Tn, 4 + h:5 + h]),
                 r=['P2', 'P3', 'tm'], w=['nd1'])
        T.op('dve', lambda e: e.tensor_tensor(nd[:Tn, :, 0:129], nd1[:Tn, :, 0:129],
                                              P01[:Tn, :].rearrange("p (h c) -> p h c", h=4)[:, :, 0:129], ALU.add),
             r=['nd1', 'P0', 'P1'], w=['nd1'])
        state_update(Tn, si)
        T.op('act', lambda e: e.copy(Cb[si][:], C[:]), r=[ck], w=[cbk])
        T.op('dve', lambda e: e.tensor_scalar(sm[:Tn, 20:24], nd[:Tn, :, 128], -1.0, None, ALU.mult), r=['nd1'], w=['sm_den'])
        T.op('dve', lambda e: e.tensor_tensor(sm[:Tn, 20:24], sm[:Tn, 20:24], nd[:Tn, :, 128], ALU.max), r=['nd1', 'sm_den'], w=['sm_den'])
        T.op('dve', lambda e: e.tensor_tensor(sm[:Tn, 0:4], sm[:Tn, 20:24], tm[:Tn, 8:12], ALU.max), r=['sm_den', 'tm'], w=['sm_den'])
        T.op('dve', lambda e: e.reciprocal(sm[:Tn, 0:4], sm[:Tn, 0:4]), r=['sm_den'], w=['sm_den'])
        T.op('dve', lambda e: e.tensor_tensor(hmr[:Tn], nd[:Tn, :, 0:128], sm[:Tn, 0:4].unsqueeze(2).to_broadcast([Tn, 4, 128]), ALU.mult),
             r=['nd1', 'sm_den'], w=['Ebuf'])
        for h in range(4):
            T.op('dve', lambda e, h=h: e.bn_stats(bst[:Tn, h, :], hmr[:Tn, h, :]), r=['Ebuf'], w=['bst'])
        for h in range(4):
            T.op('dve', lambda e, h=h: e.bn_aggr(mv[:Tn, h, :], bst[:Tn, h, :]), r=['bst'], w=['mv'])
        T.op('act', lambda e: e.activation(sm[:Tn, 4:8], mv[:Tn, :, 1], AF.Ln, bias=EPS), r=['mv'], w=['sm_hn'])
        T.op('act', lambda e: e.activation(sm[:Tn, 4:8], sm[:Tn, 4:8], AF.Exp, scale=-0.5), r=['sm_hn'], w=['sm_hn'])
        for h in range(4):
            T.op('dve', lambda e, h=h: e.tensor_scalar(hmr[:Tn, h, :], hmr[:Tn, h, :], mv[:Tn, h, 0:1], sm[:Tn, 4 + h:5 + h], ALU.subtract, ALU.mult),
                 r=['Ebuf', 'mv', 'sm_hn'], w=['Ebuf'])
        T.op('dve', lambda e: e.tensor_tensor(mixb[:Tn, 0:512], hmr[:Tn].rearrange("p h d -> p (h d)"), og[:Tn], ALU.mult),
             r=['Ebuf', 'og'], w=['mbA'])
        cosv, sinv = cosb[cs_slot], sinb[cs_slot]
        csk = 'cs%d' % cs_slot
        for mt_ in range(4):
            mm_B(P4[:, mt_ * 128:mt_ * 128 + 128], C_QAP + mt_ * 128, 128, Tn, ['P4'])
        for mt_ in range(4):
            mm_B(PA1[:, mt_ * 128:mt_ * 128 + 128], C_QAR + mt_ * 128, 128, Tn, ['P1'])
        T.op('dve', lambda e: e.tensor_tensor(rt1[:, :, :Tn], P4[:, :].rearrange("p (h t) -> p h t", h=4)[:, :, :Tn],
                                              cosv[:, :Tn].unsqueeze(1).to_broadcast([128, 4, Tn]), ALU.mult), r=['P4', csk], w=['rt1'])
        T.op('dve', lambda e: e.tensor_tensor(rt2[:, :, :Tn], PA1[:, :].rearrange("p (h t) -> p h t", h=4)[:, :, :Tn],
                                              sinv[:, :Tn].unsqueeze(1).to_broadcast([128, 4, Tn]), ALU.mult), r=['P1', csk], w=['rt2'])
        T.op('dve', lambda e: e.tensor_tensor(qaT[:, :, :Tn], rt1[:, :, :Tn], rt2[:, :, :Tn], ALU.add), r=['rt1', 'rt2'], w=['qaT'])
        mm_B(PA0[:, 0:128], C_KAP, 128, Tn, ['P0'])
        mm_B(PA0[:, 128:256], C_KAR, 128, Tn, ['P0'])
        T.op('dve', lambda e: e.tensor_tensor(rt1[:, 0, :Tn], PA0[:, 0:Tn], cosv[:, :Tn], ALU.mult), r=['P0', csk], w=['rt1'])
        T.op('dve', lambda e: e.tensor_tensor(rt2[:, 0, :Tn], PA0[:, 128:128 + Tn], sinv[:, :Tn], ALU.mult), r=['P0', csk], w=['rt2'])
        T.op('dve', lambda e: e.tensor_tensor(krope[:, :Tn], rt1[:, 0, :Tn], rt2[:, 0, :Tn], ALU.add), r=['rt1', 'rt2'], w=['krope'])
        T.op('act', lambda e: e.copy(kTb[kslot][:, :Tn], krope[:, :Tn]), r=['krope'], w=['kT_%d' % kslot])
        if save_kv:
            T.op('pe', lambda e: e.transpose(PA0[:Tn, 256:384], krope[:, :Tn], ident[:, :]), r=['krope', 'const'], w=['P0'])
            T.op('dve', lambda e: e.tensor_copy(kvout[:Tn, 0, :], PA0[:Tn, 256:384]), r=['P0'], w=['kvout'])
        blocks = [(pslot, 128, prev_bias), (kslot, Tn, 0.0)]
        psb = [(P23, ['P2', 'P3']), (P01, ['P0', 'P1'])]
        for bi_, (ks_, Sb, bias_) in enumerate(blocks):
            ps_, keys_ = psb[bi_]
            def fn_sc(e, ks_=ks_, Sb=Sb, ps_=ps_):
                ins = None
                for j in range(4):
                    for half in range(2):
                        hd = half * 4 + j
                        ins = e.matmul(ps_[:Sb, hd * 128:hd * 128 + Tn], kTb[ks_][half * 64:(half + 1) * 64, :Sb],
                                       qaT[half * 64:(half + 1) * 64, j, :Tn], start=True, stop=True)
                return ins
            T.op('pe', fn_sc, r=['kT_%d' % ks_, 'qaT'], w=keys_)
            for b2 in range(2):
                T.op('act', lambda e, b2=b2, Sb=Sb, ps_=ps_, bias_=bias_, bi_=bi_: e.activation(
                    pT[:Sb, bi_, 4 * b2:4 * b2 + 4, :Tn],
                    ps_[:Sb, 512 * b2:512 * b2 + 512].rearrange("p (h t) -> p h t", h=4)[:, :, :Tn],
                    AF.Exp, bias=bias_, scale=0.125), r=[keys_[b2], 'c5', 'const'], w=['pT%d' % bi_])
        def fn_pv(e):
            ins = None
            for hd in range(8):
                if is_sample:
                    for bi_, (ks_, Sb, _) in enumerate(blocks):
                        ins = e.matmul(P45(hd)[:Tn, :], pT[:Sb, bi_, hd, :Tn], va1[ks_][:Sb, hd // 4, 0:65],
                                       start=(bi_ == 0), stop=(bi_ == 1))
                else:
                    for qc, rng in ((0, ((0, 0, 128), (1, 0, 64))), (1, ((0, 64, 128), (1, 0, 128)))):
                        for ii, (bi_, s0, s1_) in enumerate(rng):
                            ks_ = blocks[bi_][0]
                            ins = e.matmul(P45(hd)[qc * 64:(qc + 1) * 64, :], pT[s0:s1_, bi_, hd, qc * 64:(qc + 1) * 64],
                                           va1[ks_][s0:s1_, hd // 4, 0:65], start=(ii == 0), stop=(ii == 1))
            return ins
        def P45(hd):
            base = P4 if hd < 4 else PA1
            c = (hd % 4) * 65
            return base[:, c:c + 65]
        T.op('pe', fn_pv, r=['pT0', 'pT1', 'va1_%d' % pslot, 'va1_%d' % kslot], w=['P4', 'P1'])
        for half, base, bk in ((0, P4, 'P4'), (1, PA1, 'P1')):
            o3 = base[:Tn, 0:260].rearrange("p (h c) -> p h c", h=4)
            T.op('dve', lambda e, o3=o3, half=half: e.tensor_tensor(sm[:Tn, 8 + 4 * half:12 + 4 * half], o3[:, :, 64], esink[:Tn, 4 * half:4 * half + 4], ALU.add),
                 r=[bk, 'c4'], w=['sm_pv'])
        T.op('dve', lambda e: e.reciprocal(sm[:Tn, 8:16], sm[:Tn, 8:16]), r=['sm_pv'], w=['sm_pv'])
        for half, base, bk in ((0, P4, 'P4'), (1, PA1, 'P1')):
            o3 = base[:Tn, 0:260].rearrange("p (h c) -> p h c", h=4)
            T.op('dve', lambda e, o3=o3, half=half: e.tensor_tensor(
                mixb[:Tn, 512 + 256 * half:768 + 256 * half].rearrange("p (h d) -> p h d", h=4), o3[:, :, 0:64],
                sm[:Tn, 8 + 4 * half:12 + 4 * half].unsqueeze(2).to_broadcast([Tn, 4, 64]), ALU.mult),
                r=[bk, 'sm_pv'], w=['mbB%d' % half])
        def fn_mt(e):
            ins = None
            for k in range(8):
                ins = e.transpose(P0b[:, k * 128:k * 128 + Tn], mixb[:Tn, k * 128:(k + 1) * 128], identb[:Tn, :Tn])
            return ins
        T.op('pe', fn_mt, r=['mbA', 'mbB0', 'mbB1', 'c6'], w=['P0'], c=1.1)
        T.op('act', lambda e: e.copy(mixT[:, :, :Tn], P0b[:, :].rearrange("p (k t) -> p k t", k=8)[:, :, :Tn]), r=['P0'], w=['mixT'], c=0.9)
        if mid_hook is not None:
            mid_hook()
        for n in range(2):
            def fn_o(e, n=n):
                ins = None
                for k in range(8):
                    ins = e.matmul(P23[:Tn, n * 512:(n + 1) * 512], mixT[:, k, :Tn], wout[:, k, n * 512:(n + 1) * 512], start=(k == 0), stop=(k == 7))
                return ins
            T.op('pe', fn_o, r=['mixT', 'wout'], w=['P%d' % (2 + n)], c=2.5)
        xk = 'xin%d' % xslot
        T.op('dve', lambda e: e.scalar_tensor_tensor(s1[:Tn], xin[xslot][:Tn], ALPHA, P23[:Tn, :], ALU.mult, ALU.add), r=[xk, 'P2', 'P3'], w=['mix'], c=1.3)
        layer_norm(s1, MIXK, Tn, 0, hres[:, htile, :], hrk)
        transposes8(hres[:, htile, :], Tn, P01, [hrk])
        for b in range(2):
            T.op('act', lambda e, b=b: e.copy(hT[:, 4 * b:4 * b + 4, hcol:hcol + Tn],
                                              P01[:, 512 * b:512 * b + 512].rearrange("p (k t) -> p k t", k=4)[:, :, :Tn]),
                 r=['P%d' % b], w=[htk])

    def layer_norm(src, skey, Tn, which, dst, dkey):
        skeys = list(skey) if isinstance(skey, (list, tuple)) else [skey]
        skey = skeys[-1]
        bstL, mvL, smL = lnscr[which]
        kb, km_, ks_ = 'bstL%d' % which, 'mvL%d' % which, 'smL%d' % which
        for c in range(2):
            T.op('dve', lambda e, c=c: e.bn_stats(bstL[:Tn, c, :], src[:Tn, c * 512:(c + 1) * 512]), r=skeys, w=[kb])
        T.op('dve', lambda e: e.bn_aggr(mvL[:Tn, :], bstL[:Tn, 0:2, :].rearrange("p a b -> p (a b)")), r=[kb], w=[km_])
        T.op('act', lambda e: e.activation(smL[:Tn, 0:1], mvL[:Tn, 1:2], AF.Ln, bias=EPS), r=[km_], w=[ks_])
        T.op('act', lambda e: e.activation(smL[:Tn, 0:1], smL[:Tn, 0:1], AF.Exp, scale=-0.5), r=[ks_], w=[ks_])
        T.op('dve', lambda e: e.scalar_tensor_tensor(src[:Tn], src[:Tn], mvL[:Tn, 0:1], lnp[:Tn, 2 * which, :], ALU.subtract, ALU.mult),
             r=skeys + [km_, 'const'], w=skeys, c=1.25)
        T.op('dve', lambda e: e.scalar_tensor_tensor(dst[:Tn], src[:Tn], smL[:Tn, 0:1], lnp[:Tn, 2 * which + 1, :], ALU.mult, ALU.add),
             r=skeys + [ks_, 'const'], w=[dkey], c=1.25)

    ring_ctr = [0]
    dring_ctr = [0]
    set_ctr = [0]

    def stage2(N, segs, cvbuf, cvkey, first_macro, out_fn, par=0):
        hres = hresL[par]
        hT = hTL[par]
        htk = 'hT%d' % par
        for u in range(NU):
            rs = ring_ctr[0] % 2
            ring_ctr[0] += 1
            bs = u % 2
            PSU = (P6, P7)[bs]
            pk = ('P6', 'P7')[bs]
            si_ = set_ctr[0] % NS
            set_ctr[0] += 1
            ustv = ustS[si_]
            ukm = ['ustS%dm' % si_]
            ukh = ['ustS%dh' % si_]
            uk = ukm + ukh
            ctv = ctS[si_]
            ckk = 'ctS%d' % si_
            T.dma('sp', 'wur%d' % rs, lambda e, rs=rs, u=u: e.dma_start(out=wur[rs][:].rearrange("p k c -> p (k c)"), in_=wupbf[u]),
                  r=['wupbf'], w=['wur%d' % rs])
            def fn_u(e, rs=rs, PSU=PSU):
                ins = None
                for j in range(2):
                    for k in range(8):
                        ins = e.matmul(PSU[:, j * 256:j * 256 + N], wur[rs][:, k, j * 128:(j + 1) * 128], hT[:, k, :N], start=(k == 0), stop=(k == 7))
                return ins
            T.op('pe', fn_u, r=['wur%d' % rs, htk], w=[pk], c=16 * (0.09 + max(N, 64) / 2400.0) + 0.1)
            pu = PSU[:, :].rearrange("p (j t) -> p j t", j=2)
            cv4 = cvbuf[:].rearrange("p (j u) t -> p j u t", j=2)
            T.op('act', lambda e, ustv=ustv, pu=pu: e.copy(ustv[:, :, 2:2 + N], pu[:, :, :N]), r=[pk], w=ukm, c=0.7)
            T.op('dve', lambda e, u=u, ustv=ustv: e.tensor_copy(ustv[:, :, 0:2], cv4[:, :, u, :]), r=['%s_%d' % (cvkey, u)], w=ukh)
            if first_macro:
                T.op('dve', lambda e, u=u, ustv=ustv: e.tensor_scalar(cv4[:, :, u, :], ustv[:, :, N:N + 2], hs[:, 0:1], None, ALU.mult), r=ukm + ['const'], w=['%s_%d' % (cvkey, u)])
                continue
            T.op('dve', lambda e, u=u, ustv=ustv: e.tensor_copy(cv4[:, :, u, :], ustv[:, :, N:N + 2]), r=ukm, w=['%s_%d' % (cvkey, u)])
            for j in range(2):
                ci = j * NU + u
                T.op('act', lambda e, j=j, ci=ci, ctv=ctv, pu=pu: e.activation(ctv[:, j, :N], pu[:, j, :N], AF.Identity, bias=bcv[:, ci:ci + 1], scale=wcv[:, 2, ci:ci + 1]),
                     r=[pk, 'const'], w=[ckk], c=0.45)
                T.op('dve', lambda e, j=j, ci=ci, ctv=ctv, ustv=ustv: e.scalar_tensor_tensor(ctv[:, j, :N], ustv[:, j, 1:1 + N], wcv[:, 1, ci:ci + 1], ctv[:, j, :N], ALU.mult, ALU.add),
                     r=uk + [ckk, 'const'], w=[ckk], c=0.45)
                T.op('dve', lambda e, j=j, ci=ci, ctv=ctv, ustv=ustv: e.scalar_tensor_tensor(ctv[:, j, :N], ustv[:, j, 0:N], wcv[:, 0, ci:ci + 1], ctv[:, j, :N], ALU.mult, ALU.add),
                     r=uk + [ckk, 'const'], w=[ckk], c=0.45)
            T.op('act', lambda e, ctv=ctv: e.activation(ctv[:, 1, :N], ctv[:, 1, :N], AF.Silu), r=[ckk], w=[ckk], c=0.45)
            T.op('pool', lambda e, u=u, ctv=ctv: e.tensor_tensor(actT[:, u, :N], ctv[:, 1, :N], ctv[:, 0, :N], ALU.mult), r=[ckk], w=['actT%d' % u], c=0.7)
        if first_macro:
            return
        accs = [(P6, 'P6'), (P7, 'P7')]
        NCH = NU // 2
        for si_, (col0, Tn, htile) in enumerate(segs):
            pass
        for n in range(2):
            for c in range(NCH):
                ds = dring_ctr[0] % 5
                dring_ctr[0] += 1
                T.dma('sp', 'wdr%d' % ds, lambda e, ds=ds, c=c, n=n: e.dma_start(
                    out=wdr[ds][:], in_=wdnbf[c].rearrange("p (k n) -> p k n", k=2)[:, :, n * 512:(n + 1) * 512]),
                    r=['wdnbf'], w=['wdr%d' % ds])
                def fn_d(e, ds=ds, c=c):
                    ins = None
                    for si_, (col0, Tn, htile) in enumerate(segs):
                        acc = accs[si_][0]
                        for kk in range(2):
                            kc = 2 * c + kk
                            ins = e.matmul(acc[:Tn, :], actT[:, kc, col0:col0 + Tn], wdr[ds][:, kk, :],
                                           start=(kc == 0), stop=(kc == NU - 1))
                    return ins
                T.op('pe', fn_d, r=['actT%d' % (2 * c), 'actT%d' % (2 * c + 1), 'wdr%d' % ds], w=[accs[i][1] for i in range(len(segs))], c=len(segs) * 2 * 0.31 + 0.1)
            for si_, (col0, Tn, htile) in enumerate(segs):
                acc, akey = accs[si_]
                hk = 'hres%d_%d' % (par, htile)
                T.op('dve', lambda e, Tn=Tn, htile=htile, acc=acc, n=n: e.scalar_tensor_tensor(
                    hres[:Tn, htile, n * 512:(n + 1) * 512], hres[:Tn, htile, n * 512:(n + 1) * 512], ALPHA, acc[:Tn, :], ALU.mult, ALU.add),
                    r=[hk, akey], w=[hk], c=0.7)
        for si_, (col0, Tn, htile) in enumerate(segs):
            hk = 'hres%d_%d' % (par, htile)
            layer_norm(hres[:, htile, :], hk, Tn, 1, hres[:, htile, :], hk)
            out_fn(Tn, si_, hres[:, htile, :], hk, par)

    def load_cs(slot, idx):
        T.dma('sp', 'cs%d' % slot, lambda e: e.dma_start(out=cosb[slot][:], in_=cos_d[idx]), w=['cs%d' % slot])
        T.dma('sp', 'cs%d' % slot, lambda e: e.dma_start(out=sinb[slot][:], in_=sin_d[idx]), w=['cs%d' % slot])

    def do_main():
        n_macro = nt_full // 2
        xbase = 8192 - NT_FULL * 128
        yrow = [0]
        for mi in range(n_macro):
            for tt in range(2):
                i = 2 * mi + tt
                slot = i % 2
                if i == 0:
                    load_x(0, xall[xbase: xbase + 128, :], 128)
                    load_cs(0, 0)
                if i + 1 < nt_full:
                    load_x((i + 1) % 2, xall[xbase + (i + 1) * 128: xbase + (i + 2) * 128, :], 128)
                    load_cs((i + 1) % 2, i + 1)
                T.cur_tag = 's1_%02d' % mi
                kslot = i % 2
                pslot = (i + 1) % 2
                pbias = hs[:, 1:2] if i == 2 else (NEG if i == 0 else 0.0)
                last = (i == nt_full - 1)
                if i == 0:
                    make_xT(0, 128)
                nxt = (lambda ns=(i + 1) % 2: make_xT(ns, 128)) if i + 1 < nt_full else None
                stage1(128, slot, 0, kslot, pslot, pbias, slot, tt * 128, tt, last, False, par=mi % 2, head_done=True, mid_hook=nxt)
                if i == 1:
                    T.op('dve', lambda e: e.tensor_scalar(mst[0][:], mst[0][:], hs[0:4, 0:1], None, ALU.mult), r=['m0', 'const'], w=['m0'])
                if last:
                    out_toks.append(T.dma('sp', 'okvp', lambda e: e.dma_start(out=kp_d[:, :], in_=kvout[:, 0, :]), r=['kvout']))
                    out_toks.append(T.dma('sp', 'okvp', lambda e: e.dma_start(out=vp_d[:, :], in_=kvout[:, 1, :]), r=['kvout']))

            def out_y(Tn, yslot, src, hk, par):
                r0 = yrow[0]
                yrow[0] += Tn
                out_toks.append(T.dma('sp', 'oy%d_%d' % (par, yslot), lambda e: e.dma_start(out=y_d[r0:r0 + Tn, :], in_=src[:Tn, :]), r=[hk]))
            T.cur_tag = 's2_%02d' % mi
            stage2(256, [(0, 128, 0), (128, 128, 1)], cvh, 'cvh', mi == 0, out_y, par=mi % 2)

        out_toks.append(T.dma('sp', 'ostC', lambda e: e.dma_start(out=Cp_d.rearrange("h d e -> d h e"), in_=Cst[0][:, :, 0:128]), r=['C0']))
        out_toks.append(T.dma('sp', 'ostC', lambda e: e.dma_start(out=np_d.rearrange("h d -> d h"), in_=Cst[0][:, :, 128], allow_slow_non_contiguous=True), r=['C0']))
        out_toks.append(T.dma('sp', 'ostm', lambda e: e.dma_start(out=mp_d[:, :], in_=mst[0][:]), r=['m0']))
        for j in range(2):
            out_toks.append(T.dma('sp', 'ostcv', lambda e, j=j: e.dma_start(out=cvp_d[j, :].rearrange("(c p) -> p c", p=128), in_=cvh[:, :, j],
                                                                           allow_slow_non_contiguous=True), r=['cvh_%d' % u_ for u_ in range(NU)]))

    def do_sample():
        T.cur_tag = 'sample'
        load_x(0, xs_d[:, :], 16)
        load_cs(0, NT_FULL)
        T.dma('sp', 'xin1', lambda e: e.dma_start(out=xin[1][:, 0:128], in_=ck_d[:, :]), w=['xin1'])
        T.dma('sp', 'xin1', lambda e: e.dma_start(out=xin[1][:, 128:256], in_=cv_d[:, :]), w=['xin1'])
        T.op('pe', lambda e: e.transpose(P4[:, 0:128], xin[1][:, 0:128], ident[:, :]), r=['xin1', 'const'], w=['P4'])
        T.op('act', lambda e: e.copy(kTb[1][:, :], P4[:, 0:128]), r=['P4'], w=['kT_1'])
        T.op('act', lambda e: e.copy(va1[1][:, :, 0:64], xin[1][:, 128:256].rearrange("p (k d) -> p k d", k=2)), r=['xin1'], w=['va1_1'])
        T.op('dve', lambda e: e.memset(Cst[0][:], 0.0), w=['C0'])
        T.dma('sp', 'sstC', lambda e: e.dma_start(out=Cst[0][:, :, 0:128], in_=C0_d.rearrange("h d e -> d h e")), w=['C0'])
        T.dma('sp', 'sstC', lambda e: e.dma_start(
            out=Cst[0][:, :, 128], in_=n0_d.rearrange("h d -> d h"), allow_slow_non_contiguous=True), w=['C0'])
        T.dma('sp', 'sstm', lambda e: e.dma_start(out=mst[0][:], in_=m0_d[:, :]), w=['m0'])
        T.op('act', lambda e: e.copy(Cb[0][:], Cst[0][:]), r=['C0'], w=['Cb0'])
        stage1(16, 0, 0, 0, 1, 0.0, 0, 0, 0, True, True)
        out_toks.append(T.dma('sp', 'okvs', lambda e: e.dma_start(out=ks_d[:, :], in_=kvout[:16, 0, :]), r=['kvout']))
        out_toks.append(T.dma('sp', 'okvs', lambda e: e.dma_start(out=vs_d[:, :], in_=kvout[:16, 1, :]), r=['kvout']))

        def out_ys(Tn, yslot, src, hk, par):
            out_toks.append(T.dma('sp', 'oys', lambda e: e.dma_start(out=ys_d[:, :], in_=src[:Tn, :]), r=[hk]))
        stage2(16, [(0, 16, 0)], cvsb, 'cvsb', False, out_ys, par=0)
        out_toks.append(T.dma('sp', 'ossC', lambda e: e.dma_start(out=Cs_d.rearrange("h d e -> d h e"), in_=Cst[0][:, :, 0:128]), r=['C0']))
        out_toks.append(T.dma('sp', 'ossC', lambda e: e.dma_start(out=ns_d.rearrange("h d -> d h"), in_=Cst[0][:, :, 128], allow_slow_non_contiguous=True), r=['C0']))
        out_toks.append(T.dma('sp', 'ossm', lambda e: e.dma_start(out=ms_d[:, :], in_=mst[0][:]), r=['m0']))
        for j in range(2):
            out_toks.append(T.dma('sp', 'osscv', lambda e, j=j: e.dma_start(out=cvs_d[j, :].rearrange("(c p) -> p c", p=128), in_=cvsb[:, :, j],
                                                                           allow_slow_non_contiguous=True), r=['cvsb_%d' % u_ for u_ in range(NU)]))
    do_sample()
    T.cur_tag = 'setup2'
    T.op('dve', lambda e: e.memset(Cst[0][:], 0.0), w=['C0'])
    T.op('dve', lambda e: e.memset(Cb[0][:], 0.0), w=['Cb0'])
    T.op('dve', lambda e: e.memset(mst[0][:], 0.0), w=['m0'])
    do_prefix()
    do_main()
    T.wait_all('sp', out_toks)
    if FILL:
        T.filler_fn = lambda e: e.matmul(P5[:, 0:512], identb[:, :], win[:, 0, 0:512], start=True, stop=True)
        T.fill_frac = float(os.environ.get('K_FILL_FRAC', '0.5'))
        T.filler_cost = float(os.environ.get('K_FILL_COST', '0.3'))
        T.fill_min = float(os.environ.get('K_FILL_MIN', '1.5'))
    T.schedule()

    with nc.Block() as block:
        @block.sync
        def _(e):
            T.replay('sp', e)

        @block.tensor
        def _(e):
            T.replay('pe', e)

        @block.scalar
        def _(e):
            T.replay('act', e)

        @block.vector
        def _(e):
            T.replay('dve', e)

        @block.gpsimd
        def _(e):
            T.replay('pool', e)
    es.close()
    return nc


def _host_consts():
    ident = np.eye(128, dtype=np.float32)
    s = np.arange(128)[:, None]
    l = np.arange(128)[None, :]
    maskT = np.where(l >= s, 0.0, NEG).astype(np.float32)
    amask = np.ones((128, 2, 128), np.float32)
    amask[:, 0, :] = np.where((s < 64) & (l >= 64), 0.0, 1.0)
    amask[:, 1, :] = np.where((s >= 64) & (l < 64), 0.0, 1.0)
    sel4 = np.zeros((4, 4, 128), np.float32)
    for h in range(4):
        sel4[h, h, :] = 1.0
    return ident, maskT, amask, sel4.reshape(4, 512)


def _rope_tables(pos):
    half = 32
    inv = (np.float32(10000.0) ** (-np.arange(half, dtype=np.float32) / np.float32(half))).astype(np.float32)
    d = np.arange(128) % 64
    f = inv[d % 32]
    ang = (pos.astype(np.float32)[None, :] * f[:, None]).astype(np.float32)
    c = np.cos(ang).astype(np.float32)
    sn = np.sin(ang).astype(np.float32)
    sign = np.where(d < 32, -1.0, 1.0).astype(np.float32)[:, None]
    return c, (sn * sign).astype(np.float32)


_NC_CACHE = {}


def kernel(x_prompt, x_sample, state_mlstm_C, state_mlstm_n, state_mlstm_m, cache_swa_k, cache_swa_v,
           state_conv, w_in, b_igate, b_fgate, g_mlstm_norm, attn_sinks, w_out, ln1_g, ln1_b,
           w_up, w_conv, b_conv, w_down, ln2_g, ln2_b):
    f = lambda a: np.ascontiguousarray(np.asarray(a, dtype=np.float32))
    x_prompt, x_sample = f(x_prompt), f(x_sample)
    w_in0 = f(w_in)[0]
    rot = (np.arange(64) + 32) % 64
    qa0, ka0, va0 = 2056, 2568, 2696
    qap, qar = [], []
    for j in range(4):
        for hd in (j, 4 + j):
            qap.extend(qa0 + hd * 64 + np.arange(64))
            qar.extend(qa0 + hd * 64 + rot)
    kap = list(ka0 + np.arange(128))
    kar = list(ka0 + np.concatenate([rot, 64 + rot]))
    cols = list(range(0, 2056)) + list(range(va0, va0 + 128)) + qap + qar + kap + kar
    w_in_aug = np.ascontiguousarray(w_in0[:, np.array(cols, dtype=np.int64)])
    assert w_in_aug.shape[1] == WCOLS
    ident, maskT, amask, sel4 = _host_consts()
    ln = np.stack([f(ln1_g)[0], f(ln1_b)[0], f(ln2_g)[0], f(ln2_b)[0]], 0)

    nt_pre = int(os.environ.get("K_NT_PRE", NT_PRE))
    nt_full = int(os.environ.get("K_NT_FULL", NT_FULL))
    key = (nt_pre, nt_full)
    if key not in _NC_CACHE:
        _NC_CACHE[key] = build_program(nt_pre, nt_full)
    nc = _NC_CACHE[key]

    in_maps = []
    for c in range(8):
        b, half = c // 2, c % 2
        if half == 1:
            xall = x_prompt[b]
        else:
            xall = np.concatenate([np.zeros((4096, D), np.float32), x_prompt[b, :4096]], 0)
        pos0 = half * 4096 - 256
        cosT = np.zeros((NT_FULL + 1, 128, 128), np.float32)
        sinT = np.zeros((NT_FULL + 1, 128, 128), np.float32)
        for i in range(NT_FULL):
            cc, ss = _rope_tables(pos0 + i * 128 + np.arange(128))
            cosT[i], sinT[i] = cc, ss
        cc, ss = _rope_tables(2048 + np.arange(16))
        cosT[NT_FULL, :, :16], sinT[NT_FULL, :, :16] = cc, ss
        hsv = np.zeros((128, 2), np.float32)
        hsv[:, 0] = float(half)
        hsv[:, 1] = 0.0 if half == 1 else NEG
        in_maps.append({
            "xall": np.ascontiguousarray(xall), "xs": x_sample[c],
            "C0": f(state_mlstm_C)[0, c], "n0": f(state_mlstm_n)[0, c], "m0": f(state_mlstm_m)[0, c].reshape(4, 1),
            "ck": f(cache_swa_k)[0, c].reshape(128, 128), "cv": f(cache_swa_v)[0, c].reshape(128, 128),
            "sconv": f(state_conv)[0, c],
            "w_in": w_in_aug, "w_out": f(w_out)[0], "w_up": f(w_up)[0], "w_down": f(w_down)[0],
            "w_conv": f(w_conv)[0], "b_conv": f(b_conv)[0].reshape(1, -1),
            "b_i": f(b_igate)[0].reshape(4, 1), "b_f": f(b_fgate)[0].reshape(4, 1),
            "g_norm": f(g_mlstm_norm)[0].reshape(1, 512), "sinks": f(attn_sinks)[0].reshape(1, 8),
            "ln": ln, "cosT": cosT, "sinT": sinT, "ident": ident, "maskT": maskT, "amask": amask,
            "sel4": sel4, "hs": hsv,
        })
    res = run_bass_kernel_spmd(nc, in_maps, core_ids=list(range(8)))
    R = res.results
    y_p = np.zeros((4, 8192, D), np.float32)
    for c in range(8):
        y_p[c // 2, (c % 2) * 4096:(c % 2 + 1) * 4096] = R[c]["y"]
    y_s = np.stack([R[c]["ys"] for c in range(8)], 0)
    odd = [1, 3, 5, 7]
    C_p = np.stack([R[c]["Cp"] for c in odd], 0)[None]
    n_p = np.stack([R[c]["np"] for c in odd], 0)[None]
    m_p = np.stack([R[c]["mp"].reshape(4) for c in odd], 0)[None]
    k_p = np.stack([R[c]["kp"].reshape(128, 2, 64) for c in odd], 0)[None]
    v_p = np.stack([R[c]["vp"].reshape(128, 2, 64) for c in odd], 0)[None]
    cv_p = np.stack([R[c]["cvp"] for c in odd], 0)[None]
    C_s = np.stack([R[c]["Cs"] for c in range(8)], 0)[None]
    n_s = np.stack([R[c]["ns"] for c in range(8)], 0)[None]
    m_s = np.stack([R[c]["ms"].reshape(4) for c in range(8)], 0)[None]
    k_s = np.stack([R[c]["ks"].reshape(16, 2, 64) for c in range(8)], 0)[None]
    v_s = np.stack([R[c]["vs"].reshape(16, 2, 64) for c in range(8)], 0)[None]
    cv_s = np.stack([R[c]["cvs"] for c in range(8)], 0)[None]
    return (y_p, y_s, C_p, n_p, m_p, k_p, v_p, cv_p, C_s, n_s, m_s, k_s, v_s, cv_s)
```
